# Optimizing a Trainium2 kernel written in Bass

```python
import math
import jax
import jax.numpy as jnp
from jax import lax
import numpy as np

D_MODEL = 1024
BATCH = 32
SEQ = 256
DEPTH = 2
DEC_BATCH = 2
DEC_SEQ = 1024
PAST_LEN = 512

GRID_W = 64
D_FF = 2816
N_MOD = 9
NORM_EPS = 1e-6
CHUNK = 128
D_SSM = 256
SSM_HEADS = 4
SSM_HEAD_DIM = 64
SSM_STATE = 128
SSM_GROUPS = 2
SSM_CONV = 3
SSM_CONV_CH = D_SSM + 2 * SSM_GROUPS * SSM_STATE
SSM_IN = D_SSM + SSM_CONV_CH + 2 * SSM_HEADS
D_HY = 256
HY_ORDER = 2
HY_CONV = 3
HY_BANDS = 16
HY_EMB = 1 + 2 * HY_BANDS
HY_HIDDEN = 64
HY_FAST_DECAY = 0.3
HY_SLOW_DECAY = 1.5
HY_TARGET = 1e-2
HY_IN = (HY_ORDER + 1) * D_HY
HY_FILT = 2 * HY_ORDER * D_HY
D_RET = 256
RET_HEADS = 4
RET_HEAD_DIM = 64
RET_IN = 4 * D_RET
ATT_HEADS = 4
ATT_KV_HEADS = 2
HEAD_DIM = 64
D_ATT = ATT_HEADS * HEAD_DIM
ATT_IN = D_ATT + 2 * ATT_KV_HEADS * HEAD_DIM
WINDOW = 128
ATT_BLOCK = 128
ROPE_BASE = 10000.0

D_MIX = D_SSM + D_HY + D_RET + D_ATT
D_IN = SSM_IN + HY_IN + RET_IN + ATT_IN

kernel_name = 'hybrid_ssd_hyena_retention_swa_prefix_trunk'


def rmsnorm(x, w):
    xf = x.astype(jnp.float32)
    y = xf * lax.rsqrt(jnp.mean(xf * xf, axis=-1, keepdims=True) + NORM_EPS)
    return y.astype(x.dtype) * w


def modulate(x, shift, scale):
    return x * (1 + scale) + shift


def swiglu(h, w_in, w_out):
    gate, up = jnp.split(h @ w_in, 2, axis=-1)
    return (jax.nn.silu(gate) * up) @ w_out


def centred_conv(x, w, b):
    k, l = w.shape[0], x.shape[1]
    half = k // 2
    xp = jnp.pad(x, ((0, 0), (half, half), (0, 0)))
    out = xp[:, 0:l] * w[0]
    for i in range(1, k):
        out = out + xp[:, i:i + l] * w[i]
    return out + b


def chunked_scan(q, k, v, log_a, s0):
    f32 = jnp.float32
    b, l, h, n = q.shape
    p = v.shape[-1]
    nc = l // CHUNK
    qc = q.astype(f32).reshape(b, nc, CHUNK, h, n)
    kc = k.astype(f32).reshape(b, nc, CHUNK, h, n)
    vc = v.astype(f32).reshape(b, nc, CHUNK, h, p)
    cum = jnp.cumsum(log_a.astype(f32).reshape(b, nc, CHUNK, h), axis=2)
    mask = jnp.tril(jnp.ones((CHUNK, CHUNK), dtype=bool))[None, None, :, :, None]
    seg = cum[:, :, :, None, :] - cum[:, :, None, :, :]
    decay = jnp.exp(jnp.where(mask, seg, -jnp.inf))
    scores = jnp.einsum('bcihn,bcjhn->bcijh', qc, kc) * decay
    y_intra = jnp.einsum('bcijh,bcjhp->bcihp', scores, vc)
    tail = jnp.exp(cum[:, :, -1:, :] - cum)
    inc = jnp.einsum('bcjhn,bcjh,bcjhp->bchnp', kc, tail, vc)
    chunk_decay = jnp.exp(cum[:, :, -1, :])

    def step(s, inp):
        d, dinc = inp
        return d[..., None, None] * s + dinc, s

    s_final, s_prev = lax.scan(step, s0.astype(f32), (jnp.moveaxis(chunk_decay, 1, 0), jnp.moveaxis(inc, 1, 0)))
    s_prev = jnp.moveaxis(s_prev, 0, 1)
    y_inter = jnp.einsum('bcihn,bcih,bchnp->bcihp', qc, jnp.exp(cum), s_prev)
    y = (y_intra + y_inter).reshape(b, l, h, p).astype(v.dtype)
    return y, s_final


def bidir_scan(q, k_f, k_b, v, la_f, la_b, s0):
    y_f, s_f = chunked_scan(q, k_f, v, la_f, s0[:, 0])
    fl = lambda a: jnp.flip(a, axis=1)
    y_b, s_b = chunked_scan(fl(q), fl(k_b), fl(v), fl(la_b), s0[:, 1])
    return y_f + fl(y_b), jnp.stack([s_f, s_b], axis=1)


def ssd_mixer(u, conv_w, conv_b, dt_bias, a_log, d_skip, norm_w, s0):
    b, l, _ = u.shape
    z = u[..., :D_SSM]
    xbc = jax.nn.silu(centred_conv(u[..., D_SSM:D_SSM + SSM_CONV_CH], conv_w, conv_b))
    dt_raw = u[..., D_SSM + SSM_CONV_CH:].reshape(b, l, 2, SSM_HEADS)
    xs = xbc[..., :D_SSM].reshape(b, l, SSM_HEADS, SSM_HEAD_DIM)
    gs = SSM_GROUPS * SSM_STATE
    rep = SSM_HEADS // SSM_GROUPS
    bm = jnp.repeat(xbc[..., D_SSM:D_SSM + gs].reshape(b, l, SSM_GROUPS, SSM_STATE), rep, axis=2)
    cm = jnp.repeat(xbc[..., D_SSM + gs:].reshape(b, l, SSM_GROUPS, SSM_STATE), rep, axis=2)
    dt = jax.nn.softplus(dt_raw.astype(jnp.float32) + dt_bias.astype(jnp.float32))
    log_a = dt * (-jnp.exp(a_log.astype(jnp.float32)))
    k_f = bm * dt[:, :, 0, :, None]
    k_b = bm * dt[:, :, 1, :, None]
    y, s = bidir_scan(cm, k_f, k_b, xs, log_a[:, :, 0], log_a[:, :, 1], s0)
    y = (y + d_skip[:, None] * xs).reshape(b, l, D_SSM)
    return rmsnorm(y * jax.nn.silu(z), norm_w), s


def hyena_filters(l, w1, b1, w2, b2, w3, freq):
    f32 = jnp.float32
    pos = jnp.arange(l, dtype=f32)
    t = pos / (l - 1)
    bands = jnp.linspace(1e-4, HY_BANDS - 1, HY_BANDS, dtype=f32)
    ang = (2.0 * math.pi / l) * pos[:, None] * bands[None, :]
    feats = jnp.concatenate([t[:, None], jnp.cos(ang), -jnp.sin(ang)], axis=-1)
    fr = freq.astype(f32)
    h = jnp.sin(fr * (feats @ w1.astype(f32) + b1.astype(f32)))
    h = jnp.sin(fr * (h @ w2.astype(f32) + b2.astype(f32)))
    h = (h @ w3.astype(f32)).reshape(l, 2, HY_ORDER, D_HY)
    max_decay = math.log(HY_TARGET) / HY_FAST_DECAY
    min_decay = math.log(HY_TARGET) / HY_SLOW_DECAY
    deltas = jnp.abs(jnp.linspace(min_decay, max_decay, D_HY, dtype=f32))
    h = h * jnp.exp(-t[:, None] * deltas[None, :])[:, None, None, :]
    h_fwd, h_bwd = h[:, 0], h[:, 1]
    return jnp.concatenate([h_fwd, jnp.zeros((1, HY_ORDER, D_HY), f32), h_bwd[:0:-1]], axis=0)


def fft_long_conv(z, kern, d_bias):
    l = z.shape[1]
    zf = z.astype(jnp.float32)
    zk = jnp.fft.rfft(zf, n=2 * l, axis=1)
    kk = jnp.fft.rfft(kern, axis=0)
    y = jnp.fft.irfft(zk * kk[None], n=2 * l, axis=1)[:, :l]
    return (y + zf * d_bias.astype(jnp.float32)).astype(z.dtype)


def hyena_mixer(u, conv_w, conv_b, w1, b1, w2, b2, w3, freq, d_bias):
    l = u.shape[1]
    uc = centred_conv(u, conv_w, conv_b)
    v, x1, x2 = jnp.split(uc, 3, axis=-1)
    kern = hyena_filters(l, w1, b1, w2, b2, w3, freq)
    z = x1 * fft_long_conv(v, kern[:, 0], d_bias[0])
    z = x2 * fft_long_conv(z, kern[:, 1], d_bias[1])
    return z


def retention_mixer(u, decay_logit, gn_w, s0):
    b, l, _ = u.shape
    q, k, v, g = jnp.split(u, 4, axis=-1)
    hs = lambda a: a.reshape(b, l, RET_HEADS, RET_HEAD_DIM)
    q, k, v = hs(q), hs(k) * RET_HEAD_DIM ** -0.5, hs(v)
    log_g = jax.nn.log_sigmoid(decay_logit.astype(jnp.float32))
    la_f = jnp.broadcast_to(log_g[0], (b, l, RET_HEADS))
    la_b = jnp.broadcast_to(log_g[1], (b, l, RET_HEADS))
    y, s = bidir_scan(q, k, k, v, la_f, la_b, s0)
    yf = y.astype(jnp.float32)
    mu = jnp.mean(yf, axis=-1, keepdims=True)
    var = jnp.mean(jnp.square(yf - mu), axis=-1, keepdims=True)
    yn = ((yf - mu) * lax.rsqrt(var + NORM_EPS)).reshape(b, l, D_RET).astype(u.dtype) * gn_w
    return yn * jax.nn.silu(g), s


def axial_rope(x):
    l = x.shape[1]
    n_rows = l // GRID_W
    rows = jnp.repeat(jnp.arange(n_rows, dtype=jnp.float32), GRID_W)
    cols = jnp.tile(jnp.arange(GRID_W, dtype=jnp.float32), n_rows)
    nf = HEAD_DIM // 4
    inv = ROPE_BASE ** (-jnp.arange(nf, dtype=jnp.float32) / nf)

    def rot(xh, pos):
        ang = pos[:, None] * inv[None, :]
        c = jnp.cos(ang)[None, :, None, :]
        s = jnp.sin(ang)[None, :, None, :]
        x1 = xh[..., :nf].astype(jnp.float32)
        x2 = xh[..., nf:].astype(jnp.float32)
        return jnp.concatenate([x1 * c - x2 * s, x1 * s + x2 * c], axis=-1)

    half = HEAD_DIM // 2
    return jnp.concatenate([rot(x[..., :half], rows), rot(x[..., half:], cols)], axis=-1).astype(x.dtype)


def context_attention(q, k, v, sink):
    b, l = q.shape[:2]
    g = ATT_HEADS // ATT_KV_HEADS
    nb = l // ATT_BLOCK
    qb = q.reshape(b, nb, ATT_BLOCK, ATT_KV_HEADS, g, HEAD_DIM)
    sink_l = sink.astype(jnp.float32).reshape(ATT_KV_HEADS, g)[None, :, :, None, None]
    scale = HEAD_DIM ** -0.5

    def block(i):
        qi = lax.dynamic_index_in_dim(qb, i, axis=1, keepdims=False)
        s = jnp.einsum('bqkgd,bskd->bkgqs', qi, k).astype(jnp.float32) * scale
        s = jnp.concatenate([jnp.broadcast_to(sink_l, s.shape[:-1] + (1,)), s], axis=-1)
        p = jax.nn.softmax(s, axis=-1)[..., 1:].astype(v.dtype)
        return jnp.einsum('bkgqs,bskd->bqkgd', p, v)

    o = lax.map(block, jnp.arange(nb))
    return jnp.moveaxis(o, 0, 1).reshape(b, l, D_ATT)


def latent_attention(q, k, v, ck, cv, sink):
    b, l = q.shape[:2]
    lc = ck.shape[1]
    g = ATT_HEADS // ATT_KV_HEADS
    nb = l // ATT_BLOCK
    qb = q.reshape(b, nb, ATT_BLOCK, ATT_KV_HEADS, g, HEAD_DIM)
    pad = ((0, 0), (ATT_BLOCK, ATT_BLOCK), (0, 0), (0, 0))
    kp, vp = jnp.pad(k, pad), jnp.pad(v, pad)
    sink_l = sink.astype(jnp.float32).reshape(ATT_KV_HEADS, g)[None, :, :, None, None]
    scale = HEAD_DIM ** -0.5
    q_off = jnp.arange(ATT_BLOCK)
    k_off = jnp.arange(3 * ATT_BLOCK)

    def block(i):
        qi = lax.dynamic_index_in_dim(qb, i, axis=1, keepdims=False)
        ki = lax.dynamic_slice_in_dim(kp, i * ATT_BLOCK, 3 * ATT_BLOCK, axis=1)
        vi = lax.dynamic_slice_in_dim(vp, i * ATT_BLOCK, 3 * ATT_BLOCK, axis=1)
        qpos = i * ATT_BLOCK + q_off
        kpos = (i - 1) * ATT_BLOCK + k_off
        valid = (jnp.abs(qpos[:, None] - kpos[None, :]) <= WINDOW) & ((kpos >= 0) & (kpos < l))[None, :]
        s_loc = jnp.einsum('bqkgd,bskd->bkgqs', qi, ki).astype(jnp.float32) * scale
        s_loc = jnp.where(valid, s_loc, -jnp.inf)
        s_ctx = jnp.einsum('bqkgd,bskd->bkgqs', qi, ck).astype(jnp.float32) * scale
        s = jnp.concatenate([jnp.broadcast_to(sink_l, s_loc.shape[:-1] + (1,)), s_ctx, s_loc], axis=-1)
        p = jax.nn.softmax(s, axis=-1).astype(v.dtype)
        return (jnp.einsum('bkgqs,bskd->bqkgd', p[..., 1:1 + lc], cv)
                + jnp.einsum('bkgqs,bskd->bqkgd', p[..., 1 + lc:], vi))

    o = lax.map(block, jnp.arange(nb))
    return jnp.moveaxis(o, 0, 1).reshape(b, l, D_ATT)


def token_mixing(h, lp, ssd_s0, ret_s0, ctx_kv):
    u = h @ lp['mix_w_in']
    o1 = SSM_IN
    o2 = o1 + HY_IN
    o3 = o2 + RET_IN
    y_ssd, s_ssd = ssd_mixer(u[..., :o1], lp['ssd_conv_w'], lp['ssd_conv_b'], lp['ssd_dt_bias'],
                             lp['ssd_a_log'], lp['ssd_d'], lp['ssd_norm_w'], ssd_s0)
    y_hy = hyena_mixer(u[..., o1:o2], lp['hy_conv_w'], lp['hy_conv_b'], lp['hy_w1'], lp['hy_b1'],
                       lp['hy_w2'], lp['hy_b2'], lp['hy_w3'], lp['hy_freq'], lp['hy_bias'])
    y_ret, s_ret = retention_mixer(u[..., o2:o3], lp['ret_decay_logit'], lp['ret_gn_w'], ret_s0)
    ua = u[..., o3:]
    b, l = ua.shape[:2]
    kvw = ATT_KV_HEADS * HEAD_DIM
    q = rmsnorm(ua[..., :D_ATT].reshape(b, l, ATT_HEADS, HEAD_DIM), lp['attn_q_norm'])
    k = rmsnorm(ua[..., D_ATT:D_ATT + kvw].reshape(b, l, ATT_KV_HEADS, HEAD_DIM), lp['attn_k_norm'])
    v = ua[..., D_ATT + kvw:].reshape(b, l, ATT_KV_HEADS, HEAD_DIM)
    if ctx_kv is None:
        y_att = context_attention(q, k, v, lp['attn_sink'])
    else:
        y_att = latent_attention(axial_rope(q), axial_rope(k), v, ctx_kv[0], ctx_kv[1], lp['attn_sink'])
    y = jnp.concatenate([y_ssd, y_hy, y_ret, y_att], axis=-1) @ lp['mix_w_out']
    return y, s_ssd, s_ret, k, v


def trunk_layer(x, cond, lp, ssd_s0, ret_s0, ctx_kv):
    mod = (jax.nn.silu(cond) @ lp['w_mod'] + lp['b_mod'])[:, None, :]
    sh1, sc1, g1, sh2, sc2, g2, sh3, sc3, g3 = jnp.split(mod, N_MOD, axis=-1)
    nw = lp['norm_w']
    x = x + 0.5 * g1 * swiglu(modulate(rmsnorm(x, nw[0]), sh1, sc1), lp['ffn_w_in'][0], lp['ffn_w_out'][0])
    y, s_ssd, s_ret, k, v = token_mixing(modulate(rmsnorm(x, nw[1]), sh2, sc2), lp, ssd_s0, ret_s0, ctx_kv)
    x = x + g2 * y
    x = x + 0.5 * g3 * swiglu(modulate(rmsnorm(x, nw[2]), sh3, sc3), lp['ffn_w_in'][1], lp['ffn_w_out'][1])
    return x, s_ssd, s_ret, k, v


def setup_inputs(seed: int = 0) -> dict:
    key = jax.random.key(seed)
    ks = iter(jax.random.split(key, 48))
    f32 = jnp.float32

    def nrm(shape, s):
        return jax.random.normal(next(ks), shape, f32) * s

    x_prompt = nrm((BATCH, SEQ, D_MODEL), 1.0)
    x_sample = nrm((DEC_BATCH, DEC_SEQ, D_MODEL), 1.0)
    cache_k = nrm((DEC_BATCH, DEPTH, PAST_LEN, ATT_KV_HEADS, HEAD_DIM), 1.0)
    cache_v = nrm((DEC_BATCH, DEPTH, PAST_LEN, ATT_KV_HEADS, HEAD_DIM), 1.0)
    state_ssd = nrm((DEC_BATCH, DEPTH, 2, SSM_HEADS, SSM_STATE, SSM_HEAD_DIM), 0.1)
    state_ret = nrm((DEC_BATCH, DEPTH, 2, RET_HEADS, RET_HEAD_DIM, RET_HEAD_DIM), 1.0)
    c = nrm((DEC_BATCH, D_MODEL), 1.0)
    c_ctx = nrm((D_MODEL,), 1.0)
    w_mod = nrm((DEPTH, D_MODEL, N_MOD * D_MODEL), 0.5 * D_MODEL ** -0.5)
    b_mod = nrm((DEPTH, N_MOD * D_MODEL), 0.02)
    norm_w = 1.0 + nrm((DEPTH, 3, D_MODEL), 0.02)
    ffn_w_in = nrm((DEPTH, 2, D_MODEL, 2 * D_FF), D_MODEL ** -0.5)
    ffn_w_out = nrm((DEPTH, 2, D_FF, D_MODEL), D_FF ** -0.5)
    mix_w_in = nrm((DEPTH, D_MODEL, D_IN), D_MODEL ** -0.5)
    mix_w_out = nrm((DEPTH, D_MIX, D_MODEL), D_MIX ** -0.5)
    ssd_conv_w = nrm((DEPTH, SSM_CONV, SSM_CONV_CH), SSM_CONV ** -0.5)
    ssd_conv_b = nrm((DEPTH, SSM_CONV_CH), 0.02)
    dt0 = jnp.exp(jax.random.uniform(next(ks), (DEPTH, 2, SSM_HEADS), f32, math.log(1e-3), math.log(1e-1)))
    ssd_dt_bias = dt0 + jnp.log(-jnp.expm1(-dt0))
    ssd_a_log = jnp.log(jax.random.uniform(next(ks), (DEPTH, 2, SSM_HEADS), f32, 1.0, 16.0))
    ssd_d = 1.0 + nrm((DEPTH, SSM_HEADS), 0.1)
    ssd_norm_w = 1.0 + nrm((DEPTH, D_SSM), 0.02)
    hy_conv_w = nrm((DEPTH, HY_CONV, HY_IN), HY_CONV ** -0.5)
    hy_conv_b = nrm((DEPTH, HY_IN), 0.02)
    hy_w1 = nrm((DEPTH, HY_EMB, HY_HIDDEN), HY_EMB ** -0.5)
    hy_b1 = nrm((DEPTH, HY_HIDDEN), 0.1)
    hy_w2 = nrm((DEPTH, HY_HIDDEN, HY_HIDDEN), HY_HIDDEN ** -0.5)
    hy_b2 = nrm((DEPTH, HY_HIDDEN), 0.1)
    hy_w3 = nrm((DEPTH, HY_HIDDEN, HY_FILT), 0.05 * HY_HIDDEN ** -0.5)
    hy_freq = 1.0 + nrm((DEPTH, HY_HIDDEN), 0.1)
    hy_bias = 1.0 + nrm((DEPTH, HY_ORDER, D_HY), 0.1)
    gamma0 = 1.0 - 2.0 ** (-5.0 - jnp.arange(RET_HEADS, dtype=f32))
    ret_decay_logit = jnp.log(gamma0 / (1.0 - gamma0)) + nrm((DEPTH, 2, RET_HEADS), 0.05)
    ret_gn_w = 1.0 + nrm((DEPTH, D_RET), 0.02)
    attn_q_norm = 1.0 + nrm((DEPTH, HEAD_DIM), 0.02)
    attn_k_norm = 1.0 + nrm((DEPTH, HEAD_DIM), 0.02)
    attn_sink = nrm((DEPTH, ATT_HEADS), 0.5)
    return {'x_prompt': x_prompt, 'x_sample': x_sample, 'cache_k': cache_k, 'cache_v': cache_v,
            'state_ssd': state_ssd, 'state_ret': state_ret, 'c': c, 'c_ctx': c_ctx,
            'w_mod': w_mod, 'b_mod': b_mod, 'norm_w': norm_w, 'ffn_w_in': ffn_w_in, 'ffn_w_out': ffn_w_out,
            'mix_w_in': mix_w_in, 'mix_w_out': mix_w_out, 'ssd_conv_w': ssd_conv_w, 'ssd_conv_b': ssd_conv_b,
            'ssd_dt_bias': ssd_dt_bias, 'ssd_a_log': ssd_a_log, 'ssd_d': ssd_d, 'ssd_norm_w': ssd_norm_w,
            'hy_conv_w': hy_conv_w, 'hy_conv_b': hy_conv_b, 'hy_w1': hy_w1, 'hy_b1': hy_b1, 'hy_w2': hy_w2,
            'hy_b2': hy_b2, 'hy_w3': hy_w3, 'hy_freq': hy_freq, 'hy_bias': hy_bias,
            'ret_decay_logit': ret_decay_logit, 'ret_gn_w': ret_gn_w, 'attn_q_norm': attn_q_norm,
            'attn_k_norm': attn_k_norm, 'attn_sink': attn_sink}


def reference(x_prompt, x_sample, cache_k, cache_v, state_ssd, state_ret, c, c_ctx,
              w_mod, b_mod, norm_w, ffn_w_in, ffn_w_out, mix_w_in, mix_w_out,
              ssd_conv_w, ssd_conv_b, ssd_dt_bias, ssd_a_log, ssd_d, ssd_norm_w,
              hy_conv_w, hy_conv_b, hy_w1, hy_b1, hy_w2, hy_b2, hy_w3, hy_freq, hy_bias,
              ret_decay_logit, ret_gn_w, attn_q_norm, attn_k_norm, attn_sink):
    bp = x_prompt.shape[0]
    yp, ys = x_prompt, x_sample
    new_k, new_v, new_ssd, new_ret = [], [], [], []
    for li in range(DEPTH):
        lp = dict(w_mod=w_mod[li], b_mod=b_mod[li], norm_w=norm_w[li], ffn_w_in=ffn_w_in[li],
                  ffn_w_out=ffn_w_out[li], mix_w_in=mix_w_in[li], mix_w_out=mix_w_out[li],
                  ssd_conv_w=ssd_conv_w[li], ssd_conv_b=ssd_conv_b[li], ssd_dt_bias=ssd_dt_bias[li],
                  ssd_a_log=ssd_a_log[li], ssd_d=ssd_d[li], ssd_norm_w=ssd_norm_w[li],
                  hy_conv_w=hy_conv_w[li], hy_conv_b=hy_conv_b[li], hy_w1=hy_w1[li], hy_b1=hy_b1[li],
                  hy_w2=hy_w2[li], hy_b2=hy_b2[li], hy_w3=hy_w3[li], hy_freq=hy_freq[li], hy_bias=hy_bias[li],
                  ret_decay_logit=ret_decay_logit[li], ret_gn_w=ret_gn_w[li], attn_q_norm=attn_q_norm[li],
                  attn_k_norm=attn_k_norm[li], attn_sink=attn_sink[li])
        zs_ssd = jnp.zeros((bp, 2, SSM_HEADS, SSM_STATE, SSM_HEAD_DIM), jnp.float32)
        zs_ret = jnp.zeros((bp, 2, RET_HEADS, RET_HEAD_DIM, RET_HEAD_DIM), jnp.float32)
        yp, s_ssd, s_ret, k_ctx, v_ctx = trunk_layer(yp, c_ctx[None, :], lp, zs_ssd, zs_ret, None)
        new_k.append(k_ctx)
        new_v.append(v_ctx)
        new_ssd.append(s_ssd)
        new_ret.append(s_ret)
        ys = trunk_layer(ys, c, lp, state_ssd[:, li], state_ret[:, li], (cache_k[:, li], cache_v[:, li]))[0]
    new_cache_k = jnp.stack(new_k, axis=1)
    new_cache_v = jnp.stack(new_v, axis=1)
    new_state_ssd = jnp.stack(new_ssd, axis=1)
    new_state_ret = jnp.stack(new_ret, axis=1)
    return (yp, ys, new_cache_k, new_cache_v, new_state_ssd, new_state_ret)
```

```python
import math
import numpy as np
import concourse.bass as bass
import concourse.mybir as mybir
from concourse.bass_utils import run_bass_kernel_spmd

F32 = mybir.dt.float32
BF16 = mybir.dt.bfloat16
F32R = mybir.dt.float32r
AF = mybir.ActivationFunctionType
ALU = mybir.AluOpType
AX = mybir.AxisListType

NCORE = 8
NT = 1280
NB = 5
NCH = 10
TT = [(0, 512, 0), (512, 512, 0), (1024, 256, 1)]
D = 1024
DFF = 2816
EPS = 1e-6
ENGS = ["tensor", "vector", "scalar", "gpsimd", "sync"]
SAME_ENGINE_NOSYNC = ("tensor",)


class Buf:
    __slots__ = ("name", "w", "r", "excl")

    def __init__(self, name="", excl=False):
        self.name = name
        self.w = None
        self.r = {}
        self.excl = excl


def bufs(n, name=""):
    return [Buf(name + str(i)) for i in range(n)]


class KB:
    def __init__(self, nc):
        self.nc = nc
        self.cnt = {e: 0 for e in ENGS}
        self.seen = {e: {} for e in ENGS}
        self.sems = {}
        self.dcount = {}
        self._stack = []
        self.n_inst = 0
        self.uid = 0

    def enter(self, cm):
        v = cm.__enter__()
        self._stack.append(cm)
        return v

    def mark(self):
        return len(self._stack)

    def release(self, m):
        while len(self._stack) > m:
            self._stack.pop().__exit__(None, None, None)

    def sem(self, key):
        if key not in self.sems:
            self.sems[key] = self.enter(self.nc.semaphore("s_" + key))
        return self.sems[key]

    def sb(self, name, shape, dt=F32):
        self.uid += 1
        return self.enter(self.nc.sbuf_tensor(f"{name}_{self.uid}", list(shape), dt))

    def ps(self, name, shape, dt=F32):
        return self.enter(self.nc.psum_tensor(name, list(shape), dt))

    @staticmethod
    def _flat(bl):
        out = []
        for b in bl:
            if isinstance(b, (list, tuple)):
                out.extend(KB._flat(b))
            else:
                out.append(b)
        return out

    def _need(self, eng, reads, writes):
        need = {}

        def add(k, v):
            if need.get(k, 0) < v:
                need[k] = v
        for b in reads:
            if b.w is not None:
                add(*b.w)
        for b in writes:
            if b.w is not None:
                add(*b.w)
            for k, v in b.r.items():
                add(k, v)
        out = []
        for k, v in need.items():
            if k == "p_" + eng and eng in SAME_ENGINE_NOSYNC:
                continue
            if self.seen[eng].get(k, 0) >= v:
                continue
            self.seen[eng][k] = v
            out.append((k, v))
        return out

    def _emit(self, eng, waits, fn, key, inc):
        e = getattr(self.nc, eng)
        for k, v in waits:
            e.wait_ge(self.sems[k], v)
        if fn is not None:
            fn(e).then_inc(self.sems[key], inc)

    def op(self, eng, fn, reads=(), writes=()):
        reads = self._flat(reads)
        writes = self._flat(writes)
        ex = [b for b in reads if b.excl]
        if ex:
            reads = [b for b in reads if not b.excl]
            writes = writes + [b for b in ex if b not in writes]
        waits = self._need(eng, reads, writes)
        key = "p_" + eng
        self.sem(key)
        self.cnt[eng] += 1
        val = self.cnt[eng]
        for b in reads:
            if b.r.get(key, 0) < val:
                b.r[key] = val
        for b in writes:
            b.w = (key, val)
            b.r = {}
        self._emit(eng, waits, fn, key, 1)
        self.n_inst += 1

    def dma(self, eng, fn, reads=(), writes=(), semkey=None):
        reads = self._flat(reads)
        writes = self._flat(writes)
        waits = self._need(eng, reads, writes)
        self.sem(semkey)
        self.dcount[semkey] = self.dcount.get(semkey, 0) + 16
        val = self.dcount[semkey]
        for b in reads:
            if b.r.get(semkey, 0) < val:
                b.r[semkey] = val
        for b in writes:
            b.w = (semkey, val)
            b.r = {}
        self._emit(eng, waits, fn, semkey, 16)
        self.n_inst += 1

    def barrier(self):
        tot = [("p_" + e, self.cnt[e]) for e in ENGS if self.cnt[e] > 0]
        tot += list(self.dcount.items())
        for eng in ENGS:
            waits = []
            for k, v in tot:
                if k == "p_" + eng and eng == "tensor":
                    continue
                if self.seen[eng].get(k, 0) >= v:
                    continue
                self.seen[eng][k] = v
                waits.append((k, v))
            self._emit(eng, waits, None, None, 0)

    def V(self, fn, r=(), w=()):
        self.op("vector", fn, r, w)

    def A(self, fn, r=(), w=()):
        self.op("scalar", fn, r, w)

    def T(self, fn, r=(), w=()):
        self.op("tensor", fn, r, w)

    def G(self, fn, r=(), w=()):
        self.op("gpsimd", fn, r, w)


class Rot:
    def __init__(self, K, name, shape, dt, n):
        self.t = [K.sb(f"{name}{i}", shape, dt) for i in range(n)]
        self.b = bufs(n, name)
        self.i = 0

    def next(self):
        i = self.i
        self.i = (i + 1) % len(self.t)
        return self.t[i], self.b[i]


class Cols:
    def __init__(self):
        self.m = {}
        self.n = 0

    def add(self, name, w):
        self.m[name] = (self.n, w)
        self.n += w

    def __getitem__(self, name):
        return self.m[name]


def ptab_cols():
    c = Cols()
    c.add("norm_w", 48)
    c.add("b_mod", 144)
    c.add("ssd_conv", 64)
    c.add("hy_conv", 48)
    c.add("hy_bias", 8)
    c.add("ssd_nw", 8)
    c.add("ssd_d", 8)
    c.add("ret_gn", 8)
    c.add("hyp", 6)
    return c


def ftab_cols():
    c = Cols()
    c.add("dt_bias", 160)
    c.add("a_log", 160)
    c.add("ret_logit", 16)
    c.add("qkw", 768)
    c.add("sink", 8)
    return c


def cf32_cols():
    c = Cols()
    c.add("trif", 128)
    c.add("trib", 128)
    c.add("ones", 128)
    c.add("relu_f", 128)
    c.add("relu_b", 128)
    c.add("ip1", 128)
    c.add("rmi", 128)
    c.add("tailf", 1)
    c.add("tailb", 1)
    c.add("cos", 320)
    c.add("sin", 320)
    c.add("cfb", 10)
    c.add("hl", 5)
    c.add("hr", 5)
    c.add("flag", 1)
    c.add("negpi", 1)
    c.add("nbf", 128)
    c.add("nbb", 128)
    return c


def cbf_cols():
    c = Cols()
    c.add("ident", 128)
    c.add("ones", 128)
    c.add("am", 2560)
    c.add("dft", 4096)
    c.add("idft", 1024)
    return c


PT = ptab_cols()
FT = ftab_cols()
CF = cf32_cols()
CB = cbf_cols()

FWD_E0 = [-256, -128, 0, 128]
BWD_E0 = [256, 128, 0, -128]


def dft_type(src, e0):
    return (FWD_E0.index(e0) if src == 0 else 4 + BWD_E0.index(e0))


def spectrum_entries(delta, nchunks):
    out = []
    for k in range(nchunks):
        e0 = 128 * k - 256 * delta
        if e0 in FWD_E0:
            out.append((0, k, dft_type(0, e0)))
        e0b = -128 * k - 256 * delta
        if e0b in BWD_E0:
            out.append((1, k, dft_type(1, e0b)))
    return out


def build_program(nlayers=2):
    nc = bass.Bass("TRN2", target_bir_lowering=False)

    def din(name, shape):
        return nc.dram_tensor(name, list(shape), F32, kind="ExternalInput").ap()

    def dout(name, shape):
        return nc.dram_tensor(name, list(shape), F32, kind="ExternalOutput").ap()

    xT = din("xT", [128, 8, NT])
    condT = din("condT", [128, 8, 2])
    s0ssd = din("s0ssd", [128, 80, 64])
    s0ret = din("s0ret", [64, 80, 64])
    ckT = din("ckT", [64, 4, 512])
    cvt = din("cvt", [128, 8, 128])
    ptab_d = din("ptab", [128, PT.n])
    ftab_d = din("ftab", [128, FT.n])
    cf32_d = din("cf32", [128, CF.n])
    cbf_d = din("cbf", [128, CB.n])
    featsA_d = din("featsA", [33, 1024])
    featsB_d = din("featsB", [33, 256])
    dec_d = din("dec", [128, 20, 256])
    w_mod = din("w_mod", [2, 18, 128, 4096])
    ffn_w_in = din("ffn_w_in", [2, 2, 11, 128, 4096])
    ffn_w_out = din("ffn_w_out", [2, 2, 11, 128, 2048])
    mix_w_in = din("mix_w_in", [2, D, 3336])
    mix_w_out = din("mix_w_out", [2, D, D])
    hy_w1 = din("hy_w1", [2, 33, 64])
    hy_w2 = din("hy_w2", [2, 64, 64])
    hy_w3 = din("hy_w3", [2, 64, 1024])

    yT = dout("yT", [128, 8, NT])
    nk_o = dout("nk", [2, NT, 128])
    nv_o = dout("nv", [2, NT, 128])
    nssd_o = dout("nssd", [128, 80, 64])
    nret_o = dout("nret", [64, 80, 64])

    K = KB(nc)
    V, A, T, G = K.V, K.A, K.T, K.G

    x = K.sb("x", [128, 8, NT])
    XB = [[Buf(f"x{m}_{t}") for t in range(3)] for m in range(8)]
    modt = K.sb("modt", [128, 2, 2, 72])
    MODB = Buf("mod")
    ptab = K.sb("ptab", [128, PT.n])
    ftab = K.sb("ftab", [128, FT.n])
    cf = K.sb("cf", [128, CF.n])
    cb = K.sb("cb", [128, CB.n], BF16)
    CONSTB = Buf("const")
    CONSTB2 = Buf("constb")
    WA = [K.sb(f"WA{i}", [128, 8, 512], BF16) for i in range(2)]
    WAB = bufs(2, "WA")
    WB = [K.sb(f"WB{i}", [128, 4096], BF16) for i in range(2)]
    WBB = bufs(2, "WB")
    wctr = {"a": 0, "b": 0}
    PS = [K.ps(f"ps{i}", [128, 512]) for i in range(8)]
    PSBK = [Buf(f"psb{i}", excl=True) for i in range(8)]
    PSR = [PSBK[i // 4] for i in range(32)]
    PSB = [[PSBK[i]] for i in range(8)]
    psctr = [0, 0, 8]
    prc = [0]

    def bank():
        lo, hi = psctr[1], psctr[2]
        i = psctr[0]
        if i < lo or i >= hi:
            i = lo
        psctr[0] = i + 1 if i + 1 < hi else lo
        return PS[i], PSB[i]

    def set_banks(lo, hi):
        psctr[1], psctr[2] = lo, hi

    def pr(ncols):
        n = (ncols + 127) // 128
        i = prc[0]
        if (i % 4) + n > 4:
            i = (i // 4 + 1) * 4
        if i + n > 32:
            i = 0
        prc[0] = (i + n) % 32
        b, r = divmod(i, 4)
        return PS[b][:, r * 128:r * 128 + n * 128], [PSBK[b]]

    def pipeline(gen_list, width, stagger):
        pending = list(gen_list)
        active = []
        rounds = 0
        since = stagger
        while pending or active:
            if pending and len(active) < width and (since >= stagger or not active):
                active.append(pending.pop(0))
                since = 0
            nxt = []
            for g in active:
                try:
                    next(g)
                    nxt.append(g)
                except StopIteration:
                    pass
            active = nxt
            rounds += 1
            since += 1

    def interleave(gens):
        gens = list(gens)
        while gens:
            nxt = []
            for g in gens:
                try:
                    next(g)
                    nxt.append(g)
                except StopIteration:
                    pass
            gens = nxt

    def nextA():
        i = wctr["a"] % 2
        wctr["a"] += 1
        return WA[i], WAB[i], f"wa{i}"

    def nextB():
        i = wctr["b"] % 2
        wctr["b"] += 1
        return WB[i], WBB[i], f"wb{i}"

    def pcol(name, off=0, w=1, rows=128):
        c0 = PT[name][0] + off
        return ptab[0:rows, c0:c0 + w]

    def fcol(name, off=0, w=1, rows=128):
        c0 = FT[name][0] + off
        return ftab[0:rows, c0:c0 + w]

    def ccol(name, off=0, w=None, rows=128):
        c0, ww = CF[name]
        if w is None:
            w = ww
        return cf[0:rows, c0 + off:c0 + off + w]

    def bcol(name, off=0, w=None, rows=128):
        c0, ww = CB[name]
        if w is None:
            w = ww
        return cb[0:rows, c0 + off:c0 + off + w]

    K.dma("sync", lambda e: e.dma_start(out=x[:], in_=xT), writes=[b for row in XB for b in row], semkey="ldx")
    K.dma("sync", lambda e: e.dma_start(out=ptab[:], in_=ptab_d), writes=[CONSTB], semkey="ldc")
    K.dma("sync", lambda e: e.dma_start(out=ftab[:], in_=ftab_d), writes=[CONSTB], semkey="ldc")
    K.dma("sync", lambda e: e.dma_start(out=cf[:], in_=cf32_d), writes=[CONSTB], semkey="ldc")
    K.dma("gpsimd", lambda e: e.dma_start(out=cb[:], in_=cbf_d, max_dma_last_dim=2048), writes=[CONSTB2], semkey="ldcb")

    K.barrier()

    flag = ccol("flag")
    ident_b = bcol("ident")
    ones_b = bcol("ones")
    ones_f = ccol("ones")
    trif = ccol("trif")
    trib = ccol("trib")

    condf = K.sb("condf", [128, 8, 2])
    condb = K.sb("condb", [128, 8, 2], BF16)
    CB_ = Buf("cond")
    MODL = [Buf("mod0"), Buf("mod1")]
    K.dma("sync", lambda e: e.dma_start(out=condf[:], in_=condT), writes=[CB_], semkey="ldcond")
    A(lambda e: e.activation(out=condb[:], in_=condf[:], func=AF.Silu), [CB_], [CB_])

    def mod_dma(l, ci, wa, wab, sk):
        K.dma("gpsimd", lambda e: e.dma_start(out=wa[:].rearrange("p k n -> p (k n)"), in_=w_mod[l, ci], max_dma_last_dim=8192),
              writes=[wab], semkey=sk)

    def mod_chunk(l, ci, wa, wab, sk, dma=True):
        if dma:
            mod_dma(l, ci, wa, wab, sk)
        pb, pbb = bank()
        for mb in range(4):
            for k in range(8):
                T(lambda e: e.matmul(pb[:, mb * 2:mb * 2 + 2], lhsT=wa[:, k, mb * 128:(mb + 1) * 128], rhs=condb[:, k, :],
                                     start=(k == 0), stop=(k == 7)), [wab, CB_], [pbb])
        for cnd in range(2):
            V(lambda e: e.tensor_tensor(out=modt[:, l, cnd, ci * 4:ci * 4 + 4], in0=pb[:, 0:8].rearrange("p (m c) -> p m c", c=2)[:, :, cnd],
                                        in1=pcol("b_mod", l * 72 + ci * 4, 4), op=ALU.add), [pbb, CONSTB], [MODL[l]])

    def mod_finish_j(l, j):
        for cnd in range(2):
            V(lambda e: e.scalar_tensor_tensor(
                out=modt[:, l, cnd, (3 * j + 1) * 8:(3 * j + 2) * 8], in0=modt[:, l, cnd, (3 * j + 1) * 8:(3 * j + 2) * 8],
                scalar=1.0, op0=ALU.add, in1=pcol("norm_w", (l * 3 + j) * 8, 8), op1=ALU.mult), [MODL[l], CONSTB], [MODL[l]])
            if j in (0, 2):
                V(lambda e: e.tensor_scalar(
                    out=modt[:, l, cnd, (3 * j + 2) * 8:(3 * j + 3) * 8], in0=modt[:, l, cnd, (3 * j + 2) * 8:(3 * j + 3) * 8],
                    scalar1=0.5, scalar2=None, op0=ALU.mult), [MODL[l]], [MODL[l]])

    mod_done = {}

    def mod_mark(l, ci):
        j = ci // 6
        mod_done[(l, j)] = mod_done.get((l, j), 0) + 1
        if mod_done[(l, j)] == 6:
            mod_finish_j(l, j)

    for ci in range(6):
        wa, wab, sk = nextA()
        mod_chunk(0, ci, wa, wab, sk)
        mod_mark(0, ci)
    K.barrier()
    pending_mod = [(0, ci) for ci in range(6, 18)] + ([(1, ci) for ci in range(18)] if nlayers > 1 else [])
    inflight_mod = []
    ffn_gi = [0]

    def modc(l, cnd, j, m):
        return modt[:, l, cnd, j * 8 + m:j * 8 + m + 1]

    def make_h(l, j, hbuf, HB, tiles=(0, 1, 2), rs_pool=None):
        for tt in tiles:
            t0, w, cnd = TT[tt]
            pb, pbb = bank()
            for c in range(8):
                sq, sqb = rs_pool["sq"].next()
                A(lambda e, sq=sq, c=c, t0=t0, w=w: e.activation(out=sq[:, 0:w], in_=x[:, c, t0:t0 + w], func=AF.Square),
                  [XB[c][tt]], [sqb])
                T(lambda e, pb=pb, sq=sq, c=c, w=w: e.matmul(pb[:, 0:w], lhsT=ones_b, rhs=sq[:, 0:w],
                                                               start=(c == 0), stop=(c == 7)), [sqb, CONSTB], [pbb])
            rs, rsb = rs_pool["rs"].next()
            A(lambda e, rs=rs, pb=pb, w=w: e.activation(out=rs[:, 0:w], in_=pb[:, 0:w], func=AF.Sqrt, bias=EPS, scale=1.0 / D),
              [pbb], [rsb])
            V(lambda e, rs=rs, w=w: e.reciprocal(rs[:, 0:w], rs[:, 0:w]), [rsb], [rsb])
            for c in range(8):
                tm, tmb = rs_pool["tm"].next()
                V(lambda e, tm=tm, rs=rs, c=c, t0=t0, w=w: e.tensor_tensor(out=tm[:, 0:w], in0=x[:, c, t0:t0 + w], in1=rs[:, 0:w], op=ALU.mult),
                  [XB[c][tt], rsb], [tmb])
                A(lambda e, tm=tm, c=c, t0=t0, w=w, cnd=cnd: e.activation(
                    out=hbuf[:, c, t0:t0 + w], in_=tm[:, 0:w], func=AF.Identity,
                    scale=modc(l, cnd, 3 * j + 1, c), bias=modc(l, cnd, 3 * j, c)), [tmb, MODL[l]], [HB[tt]])

    def resid_update(pb, pbb, m, tt, gate_ap, extra_reads=()):
        t0, w, cnd = TT[tt]
        V(lambda e: e.scalar_tensor_tensor(out=x[:, m, t0:t0 + w], in0=pb[:, 0:w], scalar=gate_ap, op0=ALU.mult,
                                           in1=x[:, m, t0:t0 + w], op1=ALU.add), [pbb, MODL[0], MODL[1]] + list(extra_reads), [XB[m][tt]])

    def ffn(l, f):
        j = 0 if f == 0 else 2
        mk = K.mark()
        hbuf = K.sb("h", [128, 8, NT], BF16)
        HB = bufs(3, "h")
        hid = [K.sb(f"hid{i}", [128, 2, NT], BF16) for i in range(2)]
        HIDB = [bufs(3, "hid0_"), bufs(3, "hid1_")]
        pool = {"sq": Rot(K, "sq", [128, 512], BF16, 3), "rs": Rot(K, "rs", [128, 512], F32, 2),
                "tm": Rot(K, "tm", [128, 512], F32, 3)}
        sgp = Rot(K, "sg", [128, 512], F32, 3)
        make_h(l, j, hbuf, HB, rs_pool=pool)
        stream_mod = (l == 0 and (len(pending_mod) > 0 or len(inflight_mod) > 0))
        if stream_mod:
            WM = [K.sb(f"WM{i}", [128, 8, 512], BF16) for i in range(4)]
            WMB = bufs(4, "WM")
            assert not inflight_mod
        for g in range(11):
            if stream_mod:
                while inflight_mod:
                    (ml, ci, slot) = inflight_mod.pop(0)
                    mod_chunk(ml, ci, WM[slot], WMB[slot], f"wm{slot}", dma=False)
                    mod_mark(ml, ci)
                ffn_gi[0] += 1
            wa, wab, ska = nextA()
            wb, wbb, skb = nextB()
            K.dma("gpsimd", lambda e, wa=wa, g=g: e.dma_start(out=wa[:].rearrange("p k n -> p (k n)"), in_=ffn_w_in[l, f, g], max_dma_last_dim=8192),
                  writes=[wab], semkey=ska)
            K.dma("gpsimd", lambda e, wb=wb, g=g: e.dma_start(out=wb[:, 0:2048], in_=ffn_w_out[l, f, g], max_dma_last_dim=8192),
                  writes=[wbb], semkey=skb)
            if stream_mod and g < 10:
                nper = 2 if ffn_gi[0] <= 10 else 1
                for i in range(nper):
                    if pending_mod:
                        slot = 2 * (g % 2) + i
                        (ml, ci) = pending_mod.pop(0)
                        mod_dma(ml, ci, WM[slot], WMB[slot], f"wm{slot}")
                        inflight_mod.append((ml, ci, slot))
            hd = hid[g % 2]
            hdb = HIDB[g % 2]
            for jj in range(2):
                for tt in range(3):
                    t0, w, cnd = TT[tt]
                    pg, pgb = bank()
                    pu, pub = bank()
                    for k in range(8):
                        T(lambda e, pg=pg, wa=wa, k=k, jj=jj, t0=t0, w=w: e.matmul(
                            pg[:, 0:w], lhsT=wa[:, k, jj * 128:(jj + 1) * 128], rhs=hbuf[:, k, t0:t0 + w], start=(k == 0), stop=(k == 7)),
                            [wab, HB[tt]], [pgb])
                    for k in range(8):
                        T(lambda e, pu=pu, wa=wa, k=k, jj=jj, t0=t0, w=w: e.matmul(
                            pu[:, 0:w], lhsT=wa[:, k, 256 + jj * 128:256 + (jj + 1) * 128], rhs=hbuf[:, k, t0:t0 + w], start=(k == 0), stop=(k == 7)),
                            [wab, HB[tt]], [pub])
                    sg, sgb = sgp.next()
                    A(lambda e, sg=sg, pg=pg, w=w: e.activation(out=sg[:, 0:w], in_=pg[:, 0:w], func=AF.Silu), [pgb], [sgb])
                    V(lambda e, sg=sg, pu=pu, hd=hd, jj=jj, t0=t0, w=w: e.tensor_tensor(
                        out=hd[:, jj, t0:t0 + w], in0=sg[:, 0:w], in1=pu[:, 0:w], op=ALU.mult), [sgb, pub], [hdb[tt]])
            wbv = wb[:, 0:2048].rearrange("p (j n) -> p j n", j=2)
            for m in range(8):
                for tt in range(3):
                    t0, w, cnd = TT[tt]
                    po, pob = bank()
                    for jj in range(2):
                        T(lambda e, po=po, wbv=wbv, jj=jj, m=m, hd=hd, t0=t0, w=w: e.matmul(
                            po[:, 0:w], lhsT=wbv[:, jj, m * 128:(m + 1) * 128], rhs=hd[:, jj, t0:t0 + w], start=(jj == 0), stop=(jj == 1)),
                            [wbb, hdb[tt]], [pob])
                    resid_update(po, pob, m, tt, modc(l, cnd, 3 * j + 2, m))
        if stream_mod:
            while inflight_mod:
                (ml, ci, slot) = inflight_mod.pop(0)
                mod_chunk(ml, ci, WM[slot], WMB[slot], f"wm{slot}", dma=False)
                mod_mark(ml, ci)
        K.barrier()
        K.release(mk)

    def mixer(l):
        mk_all = K.mark()
        wmi = mix_w_in[l]
        wmo = mix_w_out[l]
        cvx = {}

        def load_wa(col0, ncols):
            wa, wab, sk = nextA()
            K.dma("gpsimd", lambda e: e.dma_start(out=wa[:, :, 0:ncols],
                                                  in_=wmi[:, col0:col0 + ncols].rearrange("(k p) n -> p k n", p=128)),
                  writes=[wab], semkey=sk)
            return wa, wab

        def proj_fm(pb, pbb, wa, wab, c0, M, hbuf, HB, tt):
            t0, w, cnd = TT[tt]
            for k in range(8):
                T(lambda e, k=k: e.matmul(pb[0:M, 0:w], lhsT=wa[:, k, c0:c0 + M], rhs=hbuf[:, k, t0:t0 + w],
                                          start=(k == 0), stop=(k == 7)), [wab, HB[tt]], [pbb])

        yreg = []

        def out_proj(ychunks, YB, wrow0, kp):
            yreg.append((ychunks, YB, wrow0, kp))

        def out_proj_all():
            slots = []
            for (ychunks, YB, wrow0, kp) in yreg:
                nk_ = len(ychunks)
                if len(slots) % 2 == 0:
                    wt_, wtb_, sk = nextB()
                    flat = wt_[:, :]
                else:
                    wt_, wtb_, sk = nextA()
                    flat = wt_[:].rearrange("p k n -> p (k n)")
                wv = flat[0:kp, 0:nk_ * 1024].rearrange("p (j n) -> p j n", j=nk_)
                K.dma("gpsimd", lambda e, wv=wv, wrow0=wrow0, nk_=nk_, kp=kp: e.dma_start(
                    out=wv, in_=wmo[wrow0:wrow0 + nk_ * kp, :].rearrange("(j p) n -> p j n", p=kp)), writes=[wtb_], semkey=sk)
                slots.append((wv, wtb_))
            total = sum(len(y[0]) for y in yreg)
            for m in range(8):
                for tt in range(3):
                    t0, w, cnd = TT[tt]
                    po, pob = bank()
                    i = 0
                    for (ychunks, YB, wrow0, kp), (wv, wtb_) in zip(yreg, slots):
                        for jj, yc in enumerate(ychunks):
                            T(lambda e, wv=wv, jj=jj, yc=yc, i=i: e.matmul(po[:, 0:w], lhsT=wv[:, jj, m * 128:(m + 1) * 128],
                                                                        rhs=yc[:, t0:t0 + w], start=(i == 0), stop=(i == total - 1)),
                              [wtb_] + YB, [pob])
                            i += 1
                    resid_update(po, pob, m, tt, modc(l, cnd, 5, m))

        def conv_chunk(raw, rawb, P, pc0, out_ap, outb, silu):
            hl = ccol("hl")
            hr = ccol("hr")
            V(lambda e: e.tensor_tensor(out=raw[0:P, 1:5, 0:1], in0=raw[0:P, 0:4, 256:257], in1=hl[0:P, 1:5].unsqueeze(2), op=ALU.mult),
              [rawb, CONSTB], [rawb])
            V(lambda e: e.tensor_tensor(out=raw[0:P, 0:4, 257:258], in0=raw[0:P, 1:5, 1:2], in1=hr[0:P, 0:4].unsqueeze(2), op=ALU.mult),
              [rawb, CONSTB], [rawb])
            acc, accb = cvx["convacc"].next()
            accv = acc[0:P, :].rearrange("p (b t) -> p b t", t=256)
            w0 = ptab[0:P, pc0:pc0 + 1]
            w1 = ptab[0:P, pc0 + 1:pc0 + 2]
            w2 = ptab[0:P, pc0 + 2:pc0 + 3]
            bb = ptab[0:P, pc0 + 3:pc0 + 4]
            V(lambda e: e.tensor_scalar(out=accv, in0=raw[0:P, :, 1:257], scalar1=w1, scalar2=None, op0=ALU.mult), [rawb, CONSTB], [accb])
            V(lambda e: e.scalar_tensor_tensor(out=accv, in0=raw[0:P, :, 0:256], scalar=w0, op0=ALU.mult, in1=accv, op1=ALU.add),
              [rawb, CONSTB, accb], [accb])
            V(lambda e: e.scalar_tensor_tensor(out=accv, in0=raw[0:P, :, 2:258], scalar=w2, op0=ALU.mult, in1=accv, op1=ALU.add),
              [rawb, CONSTB, accb], [accb])
            A(lambda e: e.activation(out=out_ap, in_=acc[0:P, :], func=(AF.Silu if silu else AF.Identity), bias=bb), [accb, CONSTB], [outb])

        def raw_fill(raw, rawb, P, pb, pbb, tt):
            t0, w, cnd = TT[tt]
            b0 = t0 // 256
            nb = w // 256
            A(lambda e: e.activation(out=raw[0:P, b0:b0 + nb, 1:257], in_=pb[0:P, 0:w].rearrange("p (b t) -> p b t", t=256), func=AF.Copy),
              [pbb], [rawb])

        hbuf_m = K.sb("hmix", [128, 8, NT], BF16)
        HB_m = bufs(3, "hmix")
        mkp = K.mark()
        pool_m = {"sq": Rot(K, "sq", [128, 512], BF16, 3), "rs": Rot(K, "rs", [128, 512], F32, 2),
                  "tm": Rot(K, "tm", [128, 512], F32, 2)}
        make_h(l, 1, hbuf_m, HB_m, rs_pool=pool_m)
        K.barrier()
        K.release(mkp)

        def with_h(fn, conv=False):
            mk = K.mark()
            if conv:
                cvx["convacc"] = Rot(K, "cacc", [128, NT], F32, 1)
                raws = Rot(K, "raw", [128, 5, 258], F32, 2)
                cvx["raws"] = raws
                for i in range(2):
                    V(lambda e, i=i: e.memset(raws.t[i][:], 0.0), [], [raws.b[i]])
            fn(hbuf_m, HB_m)
            K.barrier()
            K.release(mk)

        def ssd(szb, SZB):
            mk = K.mark()
            xsf = K.sb("xsf", [64, 4, NT], BF16)
            XSF = Buf("xsf")
            bcf = K.sb("bcf", [128, 4, NT], BF16)
            BCF = Buf("bcf")
            dtr = K.sb("dtr", [128, 10, 8])
            dtt = K.sb("dtt", [128, 10, 8])
            lat = K.sb("lat", [128, 10, 8])
            DTB = Buf("dt")

            def inproj(hbuf, HB):
                wa0, wab0 = load_wa(0, 512)
                wa1, wab1 = load_wa(512, 512)
                for hh in range(4):
                    for tt in range(3):
                        t0, w, cnd = TT[tt]
                        pb, pbb = bank()
                        proj_fm(pb, pbb, wa0, wab0, hh * 64, 64, hbuf, HB, tt)
                        A(lambda e, pb=pb, hh=hh, t0=t0, w=w: e.activation(out=szb[:, hh, t0:t0 + w], in_=pb[0:64, 0:w], func=AF.Silu), [pbb], [SZB])
                for q in range(8):
                    raw, rawb = cvx["raws"].next()
                    P = 64 if q < 4 else 128
                    for tt in range(3):
                        pb, pbb = bank()
                        if q < 4:
                            proj_fm(pb, pbb, wa0, wab0, 256 + q * 64, 64, hbuf, HB, tt)
                        else:
                            proj_fm(pb, pbb, wa1, wab1, (q - 4) * 128, 128, hbuf, HB, tt)
                        raw_fill(raw, rawb, P, pb, pbb, tt)
                    pc0 = PT["ssd_conv"][0] + (l * 8 + q) * 4
                    if q < 4:
                        conv_chunk(raw, rawb, 64, pc0, xsf[:, q, :], XSF, True)
                    else:
                        conv_chunk(raw, rawb, 128, pc0, bcf[:, q - 4, :], BCF, True)
                wdt = K.sb("wdt", [128, 8, 8], BF16)
                WDT = Buf("wdt")
                K.dma("gpsimd", lambda e: e.dma_start(out=wdt[:], in_=wmi[:, 1024:1032].rearrange("(k p) n -> p k n", p=128)),
                      writes=[WDT], semkey="wdt")
                pb, pbb = bank()
                for tc in range(NCH):
                    tt = 0 if tc < 4 else (1 if tc < 8 else 2)
                    for k in range(8):
                        T(lambda e, tc=tc, k=k: e.matmul(pb[:, tc * 8:tc * 8 + 8], lhsT=hbuf[:, k, tc * 128:(tc + 1) * 128], rhs=wdt[:, k, :],
                                                         start=(k == 0), stop=(k == 7)), [HB[tt], WDT], [pbb])
                V(lambda e: e.tensor_tensor(out=dtr[:].rearrange("p a b -> p (a b)"), in0=pb[:, 0:80], in1=fcol("dt_bias", l * 80, 80), op=ALU.add),
                  [pbb, CONSTB], [DTB])
            with_h(inproj, conv=True)
            A(lambda e: e.activation(out=dtt[:], in_=dtr[:], func=AF.Exp), [DTB], [DTB])
            A(lambda e: e.activation(out=dtt[:], in_=dtt[:], func=AF.Ln, bias=1.0), [DTB], [DTB])
            A(lambda e: e.activation(out=dtr[:].rearrange("p a b -> p (a b)"), in_=fcol("a_log", l * 80, 80), func=AF.Exp), [DTB, CONSTB], [DTB])
            V(lambda e: e.scalar_tensor_tensor(out=lat[:], in0=dtr[:], scalar=-1.0, op0=ALU.mult, in1=dtt[:], op1=ALU.mult), [DTB], [DTB])
            xbtm = K.sb("xbtm", [128, NCH, 512], BF16)
            XBT = bufs(NCH, "xbtm")
            for tc in range(NCH):
                pb, pbb = bank()
                pbv = pb[:].bitcast(BF16)
                for hh in range(4):
                    T(lambda e, hh=hh, tc=tc: e.transpose(pbv[:, hh * 64:(hh + 1) * 64], xsf[:, hh, tc * 128:(tc + 1) * 128], ident_b[0:64, 0:64]),
                      [XSF, CONSTB], [pbb])
                for g in range(2):
                    T(lambda e, g=g, tc=tc: e.transpose(pbv[:, 256 + g * 128:256 + (g + 1) * 128], bcf[:, g, tc * 128:(tc + 1) * 128], ident_b),
                      [BCF, CONSTB], [pbb])
                A(lambda e, tc=tc, pbv=pbv: e.activation(out=xbtm[:, tc, :], in_=pbv[:, 0:512], func=AF.Copy), [pbb], [XBT[tc]])
            cst = K.sb("cst", [128, NCH, 16])
            wBt = K.sb("wBt", [128, NCH, 8])
            edt = K.sb("edt", [128, NCH, 8])
            pb, pbb = bank()
            for tc in range(NCH):
                T(lambda e, tc=tc: e.matmul(pb[:, tc * 16:tc * 16 + 4], lhsT=trif, rhs=lat[:, tc, 0:4], start=True, stop=True), [DTB, CONSTB], [pbb])
                T(lambda e, tc=tc: e.matmul(pb[:, tc * 16 + 4:tc * 16 + 8], lhsT=trib, rhs=lat[:, tc, 4:8], start=True, stop=True), [DTB, CONSTB], [pbb])
                T(lambda e, tc=tc: e.matmul(pb[:, tc * 16 + 8:tc * 16 + 16], lhsT=ones_f, rhs=lat[:, tc, 0:8], start=True, stop=True), [DTB, CONSTB], [pbb])
            V(lambda e: e.tensor_copy(cst[:].rearrange("p a b -> p (a b)"), pb[:, 0:160]), [pbb], [DTB])
            V(lambda e: e.tensor_tensor(out=wBt[:], in0=cst[:, :, 8:16], in1=cst[:, :, 0:8], op=ALU.subtract), [DTB], [DTB])
            A(lambda e: e.activation(out=wBt[:], in_=wBt[:], func=AF.Exp), [DTB], [DTB])
            V(lambda e: e.tensor_tensor(out=wBt[:], in0=wBt[:], in1=dtt[:], op=ALU.mult), [DTB], [DTB])
            A(lambda e: e.activation(out=edt[:], in_=cst[:, :, 8:16], func=AF.Exp), [DTB], [DTB])
            S32 = K.sb("S32", [128, 8, 64])
            SB_ = bufs(8, "S32")
            SP = K.sb("SP", [128, NCH, 8, 64], BF16)
            SPB = [bufs(8, f"SP{c}_") for c in range(NCH)]
            mk_st = K.mark()
            s0t = K.sb("s0t", [128, 40, 64])
            S0B = Buf("s0")
            K.dma("sync", lambda e: e.dma_start(out=s0t[:], in_=s0ssd[:, l * 40:(l + 1) * 40, :]), writes=[S0B], semkey="lds0")
            V(lambda e: e.memset(S32[:], 0.0), [], SB_)
            hl = ccol("hl")
            hr = ccol("hr")
            bsp = Rot(K, "bs", [128, 128], BF16, 12)

            def state_chain(d, hh):
                col = d * 4 + hh
                g = hh // 2
                for k in range(NCH):
                    c = k if d == 0 else NCH - 1 - k
                    blk = c // 2
                    first = (c % 2 == 0) if d == 0 else (c % 2 == 1)
                    sidx = (blk * 2 + d) * 4 + hh
                    if first:
                        fl = hl[:, blk:blk + 1] if d == 0 else hr[:, blk:blk + 1]
                        V(lambda e: e.scalar_tensor_tensor(out=S32[:, col, :], in0=S32[:, col, :], scalar=fl, op0=ALU.mult, in1=s0t[:, sidx, :], op1=ALU.add),
                          [SB_[col], S0B, CONSTB], [SB_[col]])
                    A(lambda e: e.activation(out=SP[:, c, col, :], in_=S32[:, col, :], func=AF.Copy), [SB_[col]], [SPB[c][col]])
                    bs, bsb = bsp.next()
                    V(lambda e: e.tensor_scalar(out=bs[:], in0=xbtm[:, c, 256 + g * 128:256 + (g + 1) * 128], scalar1=wBt[:, c, col:col + 1], scalar2=None, op0=ALU.mult),
                      [XBT[c], DTB], [bsb])
                    yield
                    pb, pbb = pr(64)
                    T(lambda e: e.matmul(pb[:, 0:64], lhsT=bs[:], rhs=xbtm[:, c, hh * 64:(hh + 1) * 64], start=True, stop=True), [bsb, XBT[c]], [pbb])
                    yield
                    V(lambda e: e.scalar_tensor_tensor(out=S32[:, col, :], in0=S32[:, col, :], scalar=edt[:, c, col:col + 1], op0=ALU.mult, in1=pb[:, 0:64], op1=ALU.add),
                      [SB_[col], DTB, pbb], [SB_[col]])
                    if not first:
                        oidx = l * 40 + sidx
                        sslot = sstp.i
                        stg, stgb = sstp.next()
                        A(lambda e: e.activation(out=stg[:], in_=S32[:, col, :], func=AF.Copy), [SB_[col]], [stgb])
                        K.dma("sync", lambda e: e.dma_start(out=nssd_o[:, oidx, :], in_=stg[:]), reads=[stgb], semkey=f"stS{sslot}")
                    yield
            sstp = Rot(K, "sstg", [128, 64], F32, 8)
            interleave([state_chain(d, hh) for d in range(2) for hh in range(4)])
            K.barrier()
            K.release(mk_st)

            dsk = pcol("ssd_d", l * 4, 4, rows=64)
            nws = pcol("ssd_nw", l * 4, 4, rows=64)
            ygp = Rot(K, "yg", [64, 4, 128], F32, 2)
            wtp = Rot(K, "wt", [128, 128], F32, 16)
            sgp2 = Rot(K, "sg2", [128, 128], F32, 16)
            csp = Rot(K, "cs", [128, 128], BF16, 16)
            sqp = Rot(K, "sq4", [64, 128], BF16, 8)

            def ssd_chain(c, hh, d, psc, pscb, res):
                cs = slice(c * 128, (c + 1) * 128)
                col = d * 4 + hh
                g = hh // 2
                U = trif if d == 0 else trib
                NB = ccol("nbf") if d == 0 else ccol("nbb")
                wt, wtb = wtp.next()
                G(lambda e: e.tensor_scalar(out=wt[:], in0=U, scalar1=lat[:, c, col:col + 1], scalar2=0.0, op0=ALU.mult, op1=ALU.add), [CONSTB, DTB], [wtb])
                yield
                pc, pcb = pr(128)
                T(lambda e: e.matmul(pc[:, 0:128], lhsT=ones_f, rhs=wt[:], start=True, stop=True), [wtb, CONSTB], [pcb])
                yield
                sg, sgb = sgp2.next()
                V(lambda e: e.scalar_tensor_tensor(out=sg[:], in0=pc[:, 0:128], scalar=cst[:, c, col:col + 1], op0=ALU.subtract, in1=NB, op1=ALU.add),
                  [pcb, DTB, CONSTB], [sgb])
                yield
                ec, ecb = wt, wtb
                A(lambda e: e.activation(out=ec[:], in_=pc[:, 0:128], func=AF.Exp), [pcb], [ecb])
                yield
                A(lambda e: e.activation(out=sg[:], in_=sg[:], func=AF.Exp), [sgb], [sgb])
                csb, csbb = csp.next()
                G(lambda e: e.tensor_tensor(out=csb[:], in0=bcf[:, 2 + g, cs], in1=ec[:], op=ALU.mult), [BCF, ecb], [csbb])
                yield
                yield
                st, stb = wt[:].bitcast(BF16)[:, 0:128], wtb
                V(lambda e: e.scalar_tensor_tensor(out=st, in0=psc[:, g * 128:(g + 1) * 128], scalar=dtt[:, c, col:col + 1], op0=ALU.mult, in1=sg[:], op1=ALU.mult),
                  [pscb, sgb, DTB], [stb])
                res[(hh, d)] = (st, stb, csb, csbb)
                yield

            def ssd_chunk(c):
                cs = slice(c * 128, (c + 1) * 128)
                psc, pscb = pr(256)
                for g in range(2):
                    T(lambda e: e.matmul(psc[:, g * 128:(g + 1) * 128], lhsT=bcf[:, g, cs], rhs=bcf[:, 2 + g, cs], start=True, stop=True), [BCF], [pscb])
                res = {}
                chains = [ssd_chain(c, hh, d, psc, pscb, res) for hh in range(4) for d in range(2)]
                while chains:
                    nxt = []
                    for gch in chains:
                        try:
                            next(gch)
                            nxt.append(gch)
                        except StopIteration:
                            pass
                    chains = nxt
                    yield
                yg, ygb = ygp.next()
                pys = []
                for hh in range(4):
                    py, pyb = pr(128)
                    pys.append((py, pyb))
                    for d in range(2):
                        col = d * 4 + hh
                        st, stb, csb, csbb = res[(hh, d)]
                        T(lambda e: e.matmul(py[0:64, 0:128], lhsT=xbtm[:, c, hh * 64:(hh + 1) * 64], rhs=st, start=(d == 0), stop=False), [XBT[c], stb], [pyb])
                        T(lambda e: e.matmul(py[0:64, 0:128], lhsT=SP[:, c, col, :], rhs=csb[:], start=False, stop=(d == 1)), [SPB[c][col], csbb], [pyb])
                yield
                sqs = []
                for hh in range(4):
                    py, pyb = pys[hh]
                    V(lambda e: e.scalar_tensor_tensor(out=yg[:, hh, :], in0=xsf[:, hh, cs], scalar=dsk[:, hh:hh + 1], op0=ALU.mult, in1=py[0:64, 0:128], op1=ALU.add),
                      [XSF, CONSTB, pyb], [ygb])
                    V(lambda e: e.tensor_tensor(out=yg[:, hh, :], in0=yg[:, hh, :], in1=szb[:, hh, cs], op=ALU.mult), [ygb, SZB], [ygb])
                    sq, sqb = sqp.next()
                    A(lambda e: e.activation(out=sq[:], in_=yg[:, hh, :], func=AF.Square), [ygb], [sqb])
                    sqs.append((sq, sqb))
                    yield
                pq, pqb = pr(128)
                for hh in range(4):
                    sq, sqb = sqs[hh]
                    T(lambda e: e.matmul(pq[0:64, 0:128], lhsT=ones_b[0:64, 0:64], rhs=sq[:], start=(hh == 0), stop=(hh == 3)), [sqb, CONSTB], [pqb])
                yield
                rs, rsb = rsp2.next()
                A(lambda e: e.activation(out=rs[:], in_=pq[0:64, 0:128], func=AF.Sqrt, bias=EPS, scale=1.0 / 256), [pqb], [rsb])
                yield
                V(lambda e: e.reciprocal(rs[:], rs[:]), [rsb], [rsb])
                yield
                for hh in range(4):
                    V(lambda e: e.scalar_tensor_tensor(out=szb[:, hh, cs], in0=yg[:, hh, :], scalar=nws[:, hh:hh + 1], op0=ALU.mult, in1=rs[:], op1=ALU.mult),
                      [ygb, rsb, CONSTB], [SZB])
                    yield
            rsp2 = Rot(K, "rs2", [64, 128], F32, 2)
            pipeline([ssd_chunk(c) for c in range(NCH)], 2, 9)
            out_proj([szb[:, hh, :] for hh in range(4)], [SZB], 0, 64)
            K.barrier()
            K.release(mk)

        def ret(sgf, SGF):
            mk = K.mark()
            qf = K.sb("qf", [64, 4, NT], BF16)
            kf = K.sb("kf", [64, 4, NT], BF16)
            kvt = K.sb("kvt", [128, NCH, 256], BF16)
            QF, KF = Buf("qf"), Buf("kf")
            KVT = bufs(NCH, "kvt")
            KST = Buf("kst")
            lg = K.sb("lg", [128, 8])
            LG = Buf("lg")
            A(lambda e: e.activation(out=lg[:], in_=fcol("ret_logit", l * 8, 8), func=AF.Exp, scale=-1.0), [CONSTB], [LG])
            A(lambda e: e.activation(out=lg[:], in_=lg[:], func=AF.Ln, bias=1.0), [LG], [LG])
            V(lambda e: e.tensor_scalar(out=lg[:], in0=lg[:], scalar1=-1.0, scalar2=None, op0=ALU.mult), [LG], [LG])
            Tl = K.sb("Tl", [128, 8])
            G128 = K.sb("G128", [128, 8])
            RC = Buf("retc")
            for hh in range(4):
                A(lambda e, hh=hh: e.activation(out=Tl[:, hh:hh + 1], in_=ccol("tailf"), func=AF.Exp, scale=lg[:, hh:hh + 1]), [LG, CONSTB], [RC])
                A(lambda e, hh=hh: e.activation(out=Tl[:, 4 + hh:5 + hh], in_=ccol("tailb"), func=AF.Exp, scale=lg[:, 4 + hh:5 + hh]), [LG, CONSTB], [RC])
            V(lambda e: e.tensor_scalar(out=Tl[:], in0=Tl[:], scalar1=0.125, scalar2=None, op0=ALU.mult), [RC], [RC])
            A(lambda e: e.activation(out=G128[:], in_=lg[:], func=AF.Exp, scale=128.0), [LG], [RC])
            S32 = K.sb("R32", [64, 8, 64])
            SB_ = bufs(8, "R32")
            SP = K.sb("RSP", [64, NCH, 8, 64], BF16)
            SPB = [bufs(8, f"RSP{c}_") for c in range(NCH)]
            mk_k = K.mark()
            kst = K.sb("kst", [128, 2, NCH, 256], BF16)

            def inproj(hbuf, HB):
                wa0, wab0 = load_wa(1800, 512)
                wa1, wab1 = load_wa(2312, 512)
                for hh in range(4):
                    for tt in range(3):
                        t0, w, cnd = TT[tt]
                        pb, pbb = bank()
                        proj_fm(pb, pbb, wa0, wab0, hh * 64, 64, hbuf, HB, tt)
                        A(lambda e, pb=pb, hh=hh, t0=t0, w=w: e.activation(out=qf[:, hh, t0:t0 + w], in_=pb[0:64, 0:w], func=AF.Copy), [pbb], [QF])
                        pb, pbb = bank()
                        proj_fm(pb, pbb, wa0, wab0, 256 + hh * 64, 64, hbuf, HB, tt)
                        A(lambda e, pb=pb, hh=hh, t0=t0, w=w: e.activation(out=kf[:, hh, t0:t0 + w], in_=pb[0:64, 0:w], func=AF.Copy, scale=0.125), [pbb], [KF])
                        pb, pbb = bank()
                        proj_fm(pb, pbb, wa1, wab1, 256 + hh * 64, 64, hbuf, HB, tt)
                        A(lambda e, pb=pb, hh=hh, t0=t0, w=w: e.activation(out=sgf[:, hh, t0:t0 + w], in_=pb[0:64, 0:w], func=AF.Silu), [pbb], [SGF])
                for tc in range(NCH):
                    tt = 0 if tc < 4 else (1 if tc < 8 else 2)
                    pb, pbb = bank()
                    for k in range(8):
                        T(lambda e, pb=pb, tc=tc, k=k: e.matmul(pb[:, 0:256], lhsT=hbuf[:, k, tc * 128:(tc + 1) * 128], rhs=wa0[:, k, 256:512],
                                                                 start=(k == 0), stop=(k == 7)), [HB[tt], wab0], [pbb])
                    for k in range(8):
                        T(lambda e, pb=pb, tc=tc, k=k: e.matmul(pb[:, 256:512], lhsT=hbuf[:, k, tc * 128:(tc + 1) * 128], rhs=wa1[:, k, 0:256],
                                                                 start=(k == 0), stop=(k == 7)), [HB[tt], wab1], [pbb])
                    for d in range(2):
                        V(lambda e, pb=pb, tc=tc, d=d: e.tensor_tensor(out=kst[:, d, tc, :].rearrange("p (h n) -> p h n", h=4),
                                                                       in0=pb[:, 0:256].rearrange("p (h n) -> p h n", h=4),
                                                                       in1=Tl[:, d * 4:(d + 1) * 4].unsqueeze(2).broadcast_to([128, 4, 64]), op=ALU.mult),
                          [pbb, RC], [KST])
                    A(lambda e, pb=pb, tc=tc: e.activation(out=kvt[:, tc, :], in_=pb[:, 256:512], func=AF.Copy), [pbb], [KVT[tc]])
            with_h(inproj)
            mk_st = K.mark()
            s0t = K.sb("rs0t", [64, 40, 64])
            S0B = Buf("rs0")
            K.dma("sync", lambda e: e.dma_start(out=s0t[:], in_=s0ret[:, l * 40:(l + 1) * 40, :]), writes=[S0B], semkey="lds0")
            V(lambda e: e.memset(S32[:], 0.0), [], SB_)
            hl = ccol("hl")
            hr = ccol("hr")
            def rstate_chain(d, hh):
                col = d * 4 + hh
                for k in range(NCH):
                    c = k if d == 0 else NCH - 1 - k
                    blk = c // 2
                    first = (c % 2 == 0) if d == 0 else (c % 2 == 1)
                    sidx = (blk * 2 + d) * 4 + hh
                    if first:
                        fl = hl[0:64, blk:blk + 1] if d == 0 else hr[0:64, blk:blk + 1]
                        V(lambda e: e.scalar_tensor_tensor(out=S32[:, col, :], in0=S32[:, col, :], scalar=fl, op0=ALU.mult, in1=s0t[:, sidx, :], op1=ALU.add),
                          [SB_[col], S0B, CONSTB], [SB_[col]])
                    A(lambda e: e.activation(out=SP[:, c, col, :], in_=S32[:, col, :], func=AF.Copy), [SB_[col]], [SPB[c][col]])
                    yield
                    pb, pbb = pr(64)
                    T(lambda e: e.matmul(pb[0:64, 0:64], lhsT=kst[:, d, c, hh * 64:(hh + 1) * 64], rhs=kvt[:, c, hh * 64:(hh + 1) * 64],
                                         start=True, stop=True), [KST, KVT[c]], [pbb])
                    yield
                    V(lambda e: e.scalar_tensor_tensor(out=S32[:, col, :], in0=S32[:, col, :], scalar=G128[0:64, col:col + 1], op0=ALU.mult, in1=pb[0:64, 0:64], op1=ALU.add),
                      [SB_[col], RC, pbb], [SB_[col]])
                    if not first:
                        oidx = l * 40 + sidx
                        sslot = rstp.i
                        stg, stgb = rstp.next()
                        A(lambda e: e.activation(out=stg[:], in_=S32[:, col, :], func=AF.Copy), [SB_[col]], [stgb])
                        K.dma("sync", lambda e: e.dma_start(out=nret_o[:, oidx, :], in_=stg[:]), reads=[stgb], semkey=f"stR{sslot}")
                    yield
            rstp = Rot(K, "rstg", [64, 64], F32, 8)
            interleave([rstate_chain(d, hh) for d in range(2) for hh in range(4)])
            K.barrier()
            K.release(mk_k)

            gnw = pcol("ret_gn", l * 4, 4, rows=64)
            Dm = K.sb("Dm", [128, 4, 128])
            Ef = K.sb("Ef", [64, 8, 128])
            tmpf = Rot(K, "tmpf", [128, 128], F32, 2)
            for hh in range(4):
                t1, t1b = tmpf.next()
                A(lambda e, t1=t1, hh=hh: e.activation(out=t1[:], in_=ccol("relu_f"), func=AF.Exp, scale=lg[:, hh:hh + 1]), [LG, CONSTB], [t1b])
                V(lambda e, t1=t1, hh=hh: e.tensor_tensor(out=Dm[:, hh, :], in0=t1[:], in1=trif, op=ALU.mult), [t1b, CONSTB], [RC])
                t2, t2b = tmpf.next()
                A(lambda e, t2=t2, hh=hh: e.activation(out=t2[:], in_=ccol("relu_b"), func=AF.Exp, scale=lg[:, 4 + hh:5 + hh]), [LG, CONSTB], [t2b])
                V(lambda e, t2=t2: e.tensor_tensor(out=t2[:], in0=t2[:], in1=trib, op=ALU.mult), [t2b, CONSTB], [t2b])
                V(lambda e, t2=t2, hh=hh: e.tensor_tensor(out=Dm[:, hh, :], in0=Dm[:, hh, :], in1=t2[:], op=ALU.add), [t2b, RC], [RC])
                A(lambda e, hh=hh: e.activation(out=Ef[:, hh, :], in_=ccol("ip1", rows=64), func=AF.Exp, scale=lg[0:64, hh:hh + 1]), [LG, CONSTB], [RC])
                A(lambda e, hh=hh: e.activation(out=Ef[:, 4 + hh, :], in_=ccol("rmi", rows=64), func=AF.Exp, scale=lg[0:64, 4 + hh:5 + hh]), [LG, CONSTB], [RC])
            qsp = Rot(K, "qs4", [64, 2, 4, 128], BF16, 2)
            st4p = Rot(K, "st4", [128, 4, 128], BF16, 2)
            yvp = Rot(K, "yv4", [64, 4, 128], F32, 2)
            sq4p = Rot(K, "sq4", [64, 4, 128], F32, 2)

            def ret_chunk(c):
                cs = slice(c * 128, (c + 1) * 128)
                ps_, psb = bank()
                for hh in range(4):
                    T(lambda e: e.matmul(ps_[:, hh * 128:(hh + 1) * 128], lhsT=kf[:, hh, cs], rhs=qf[:, hh, cs], start=True, stop=True), [KF, QF], [psb])
                yield
                st, stb = st4p.next()
                V(lambda e: e.tensor_tensor(out=st[:], in0=ps_[:, 0:512].rearrange("p (h t) -> p h t", h=4), in1=Dm[:], op=ALU.mult), [psb, RC], [stb])
                qs, qsb = qsp.next()
                V(lambda e: e.tensor_tensor(out=qs[:], in0=qf[:, :, cs].unsqueeze(1).broadcast_to([64, 2, 4, 128]),
                                            in1=Ef[:].rearrange("p (d h) t -> p d h t", d=2), op=ALU.mult), [QF, RC], [qsb])
                yield
                py, pyb = bank()
                for hh in range(4):
                    T(lambda e: e.matmul(py[0:64, hh * 128:(hh + 1) * 128], lhsT=kvt[:, c, hh * 64:(hh + 1) * 64], rhs=st[:, hh, :],
                                         start=True, stop=False), [KVT[c], stb], [pyb])
                    T(lambda e: e.matmul(py[0:64, hh * 128:(hh + 1) * 128], lhsT=SP[:, c, hh, :], rhs=qs[:, 0, hh, :], start=False, stop=False),
                      [SPB[c][hh], qsb], [pyb])
                    T(lambda e: e.matmul(py[0:64, hh * 128:(hh + 1) * 128], lhsT=SP[:, c, 4 + hh, :], rhs=qs[:, 1, hh, :], start=False, stop=True),
                      [SPB[c][4 + hh], qsb], [pyb])
                yield
                yv, yvb = yvp.next()
                yvf = yv[:].rearrange("p h t -> p (h t)")
                V(lambda e: e.tensor_copy(yvf, py[0:64, 0:512]), [pyb], [yvb])
                yield
                pm, pmb = ps_, psb
                T(lambda e: e.matmul(pm[0:64, 0:512], lhsT=ones_f[0:64, 0:64], rhs=yvf, start=True, stop=True), [yvb, CONSTB], [pmb])
                yield
                V(lambda e: e.scalar_tensor_tensor(out=yvf, in0=pm[0:64, 0:512], scalar=-1.0 / 64, op0=ALU.mult, in1=yvf, op1=ALU.add), [pmb, yvb], [yvb])
                yield
                sq, sqb = sq4p.next()
                sqf = sq[:].rearrange("p h t -> p (h t)")
                A(lambda e: e.activation(out=sqf, in_=yvf, func=AF.Square), [yvb], [sqb])
                yield
                pv_, pvb = py, pyb
                T(lambda e: e.matmul(pv_[0:64, 0:512], lhsT=ones_f[0:64, 0:64], rhs=sqf, start=True, stop=True), [sqb, CONSTB], [pvb])
                yield
                A(lambda e: e.activation(out=sqf, in_=pv_[0:64, 0:512], func=AF.Sqrt, bias=EPS, scale=1.0 / 64), [pvb], [sqb])
                yield
                V(lambda e: e.reciprocal(sqf, sqf), [sqb], [sqb])
                yield
                V(lambda e: e.tensor_tensor(out=yvf, in0=yvf, in1=sqf, op=ALU.mult), [yvb, sqb], [yvb])
                V(lambda e: e.tensor_tensor(out=yv[:], in0=yv[:], in1=gnw.unsqueeze(2).broadcast_to([64, 4, 128]), op=ALU.mult), [yvb, CONSTB], [yvb])
                yield
                V(lambda e: e.tensor_tensor(out=sgf[:, :, cs], in0=yv[:], in1=sgf[:, :, cs], op=ALU.mult), [yvb, SGF], [SGF])
                yield
            pipeline([ret_chunk(c) for c in range(NCH)], 2, 6)
            out_proj([sgf[:, hh, :] for hh in range(4)], [SGF], 512, 64)
            K.barrier()
            K.release(mk)

        def att(yat, YA):
            mk = K.mark()
            qfm = K.sb("aq", [64, 4, NT], BF16)
            kfm = K.sb("ak", [64, 2, NT], BF16)
            vtm = K.sb("av", [128, NCH, 128], BF16)
            QF, KF = Buf("aq"), Buf("ak")
            VT = bufs(NCH, "av")
            ckf = K.sb("ckf", [64, 2, 512], BF16)
            cvs = K.sb("cvs", [128, 4, 128], BF16)
            CK = Buf("ck")
            K.dma("gpsimd", lambda e: e.dma_start(out=ckf[:], in_=ckT[:, l * 2:l * 2 + 2, :]), writes=[CK], semkey="ldck")
            K.dma("gpsimd", lambda e: e.dma_start(out=cvs[:], in_=cvt[:, l * 4:l * 4 + 4, :]), writes=[CK], semkey="ldck")
            es = K.sb("es", [64, 4])
            A(lambda e: e.activation(out=es[:], in_=fcol("sink", l * 4, 4, rows=64), func=AF.Exp), [CONSTB], [CK])
            mk_in = K.mark()
            qkp = Rot(K, "qk", [128, 512], F32, 3)
            qnp = Rot(K, "qn", [128, 384], F32, 3)
            qbp = Rot(K, "qb", [128, 384], BF16, 3)
            smp = Rot(K, "sm", [128, 8], F32, 3)
            rtp = Rot(K, "rt", [128, 6, 2, 16], F32, 8)
            kvo = Rot(K, "kvo", [128, 256], F32, 3)

            def inproj(hbuf, HB):
                wa, wab = load_wa(2824, 512)

                def in_chunk(tc):
                    tt = 0 if tc < 4 else (1 if tc < 8 else 2)
                    pb, pbb = bank()
                    for k in range(8):
                        T(lambda e: e.matmul(pb[:, 0:512], lhsT=hbuf[:, k, tc * 128:(tc + 1) * 128], rhs=wa[:, k, :], start=(k == 0), stop=(k == 7)),
                          [HB[tt], wab], [pbb])
                    yield
                    qk, qkb = qkp.next()
                    A(lambda e: e.activation(out=qk[:], in_=pb[:, 0:512], func=AF.Copy), [pbb], [qkb])
                    yield
                    qn, qnb = qnp.next()
                    sm, smb = smp.next()
                    A(lambda e: e.activation(out=qn[:], in_=qk[:, 0:384], func=AF.Square), [qkb], [qnb])
                    A(lambda e: e.activation(out=vtm[:, tc, :], in_=qk[:, 384:512], func=AF.Copy), [qkb], [VT[tc]])
                    yield
                    V(lambda e: e.tensor_reduce(out=sm[:, 0:6], in_=qn[:].rearrange("p (h d) -> p h d", d=64), axis=AX.X, op=ALU.add), [qnb], [smb])
                    yield
                    A(lambda e: e.activation(out=sm[:, 0:6], in_=sm[:, 0:6], func=AF.Sqrt, bias=EPS, scale=1.0 / 64), [smb], [smb])
                    yield
                    V(lambda e: e.reciprocal(sm[:, 0:6], sm[:, 0:6]), [smb], [smb])
                    yield
                    V(lambda e: e.tensor_tensor(out=qn[:].rearrange("p (h d) -> p h d", d=64), in0=qk[:, 0:384].rearrange("p (h d) -> p h d", d=64),
                                                in1=sm[:, 0:6].unsqueeze(2).broadcast_to([128, 6, 64]), op=ALU.mult), [qkb, smb], [qnb])
                    yield
                    V(lambda e: e.tensor_tensor(out=qn[:], in0=qn[:], in1=fcol("qkw", l * 384, 384), op=ALU.mult), [qnb, CONSTB], [qnb])
                    yield
                    qv = qn[:].rearrange("p (h a b f) -> p h a b f", h=6, a=2, b=2)
                    cosv = ccol("cos", tc * 32, 32).rearrange("p (a f) -> p a f", a=2).unsqueeze(1).broadcast_to([128, 6, 2, 16])
                    sinv = ccol("sin", tc * 32, 32).rearrange("p (a f) -> p a f", a=2).unsqueeze(1).broadcast_to([128, 6, 2, 16])
                    x1 = qv[:, :, :, 0, :]
                    x2 = qv[:, :, :, 1, :]
                    t1, t1b = rtp.next()
                    t2, t2b = rtp.next()
                    t3, t3b = rtp.next()
                    t4, t4b = rtp.next()
                    V(lambda e: e.tensor_tensor(out=t1[:], in0=x1, in1=cosv, op=ALU.mult), [qnb, CONSTB], [t1b])
                    V(lambda e: e.tensor_tensor(out=t3[:], in0=x1, in1=sinv, op=ALU.mult), [qnb, CONSTB], [t3b])
                    yield
                    V(lambda e: e.tensor_tensor(out=t2[:], in0=x2, in1=sinv, op=ALU.mult), [qnb, CONSTB], [t2b])
                    V(lambda e: e.tensor_tensor(out=t4[:], in0=x2, in1=cosv, op=ALU.mult), [qnb, CONSTB], [t4b])
                    yield
                    V(lambda e: e.tensor_tensor(out=x1, in0=t1[:], in1=t2[:], op=ALU.subtract), [t1b, t2b], [qnb])
                    yield
                    V(lambda e: e.tensor_tensor(out=x2, in0=t3[:], in1=t4[:], op=ALU.add), [t3b, t4b], [qnb])
                    yield
                    kslot = kvo.i
                    ko, kob = kvo.next()
                    V(lambda e: e.tensor_copy(ko[:, 0:128], qn[:, 256:384]), [qnb], [kob])
                    V(lambda e: e.tensor_copy(ko[:, 128:256], qk[:, 384:512]), [qkb], [kob])
                    qb, qbb = qbp.next()
                    A(lambda e: e.activation(out=qb[:], in_=qn[:], func=AF.Copy), [qnb], [qbb])
                    yield
                    K.dma("sync", lambda e: e.dma_start(out=nk_o[l, tc * 128:(tc + 1) * 128, :], in_=ko[:, 0:128]), reads=[kob], semkey=f"stk{kslot}")
                    K.dma("sync", lambda e: e.dma_start(out=nv_o[l, tc * 128:(tc + 1) * 128, :], in_=ko[:, 128:256]), reads=[kob], semkey=f"stk{kslot}")
                    pt, ptb = bank()
                    ptv = pt[:].bitcast(BF16)
                    for hh in range(6):
                        T(lambda e: e.transpose(ptv[0:64, hh * 128:(hh + 1) * 128], qb[:, hh * 64:(hh + 1) * 64], ident_b), [qbb, CONSTB], [ptb])
                    yield
                    V(lambda e: e.tensor_copy(qfm[:, :, tc * 128:(tc + 1) * 128], ptv[0:64, 0:512].rearrange("p (h t) -> p h t", h=4)), [ptb], [QF])
                    V(lambda e: e.tensor_copy(kfm[:, :, tc * 128:(tc + 1) * 128], ptv[0:64, 512:768].rearrange("p (h t) -> p h t", h=2)), [ptb], [KF])
                    yield
                for c0 in range(0, NCH, 2):
                    interleave([in_chunk(c0), in_chunk(c0 + 1)])
            with_h(inproj)
            K.release(mk_in)
            ptp = Rot(K, "pt", [128, 7, 2, 128], BF16, 3)
            rcp = Rot(K, "rc", [64, 2, 128], F32, 3)
            am = bcol("am").rearrange("p (c a t) -> p c a t", c=NCH, a=2)
            cfb = ccol("cfb")

            def att_chain(qc, kv):
                qs = slice(qc * 128, (qc + 1) * 128)
                pc_ = max(qc - 1, 0)
                nc_ = min(qc + 1, NCH - 1)
                qrhs = qfm[:, 2 * kv:2 * kv + 2, qs]
                pa, pab = bank()
                pb_, pbb_ = bank()
                for sc in range(4):
                    dst, dstb = (pa, pab) if sc < 2 else (pb_, pbb_)
                    T(lambda e: e.matmul(dst[:, (sc % 2) * 256:(sc % 2) * 256 + 256], lhsT=ckf[:, kv, sc * 128:(sc + 1) * 128], rhs=qrhs, start=True, stop=True),
                      [CK, QF], [dstb])
                yield
                pt_, ptb_ = ptp.next()
                A(lambda e: e.activation(out=pt_[:, 0:2, :, :], in_=pa[:, 0:512].rearrange("p (c h t) -> p c h t", c=2, h=2), func=AF.Exp,
                                         scale=0.125, bias=cfb[:, qc:qc + 1]), [pab, CONSTB], [ptb_])
                A(lambda e: e.activation(out=pt_[:, 2:4, :, :], in_=pb_[:, 0:512].rearrange("p (c h t) -> p c h t", c=2, h=2), func=AF.Exp,
                                         scale=0.125, bias=cfb[:, qc:qc + 1]), [pbb_, CONSTB], [ptb_])
                pc2, pc2b = bank()
                pd2, pd2b = bank()
                for i, kc in enumerate((pc_, nc_, qc)):
                    dst, dstb = (pc2, pc2b) if i < 2 else (pd2, pd2b)
                    T(lambda e: e.matmul(dst[:, (i % 2) * 256:(i % 2) * 256 + 256], lhsT=kfm[:, kv, kc * 128:(kc + 1) * 128], rhs=qrhs, start=True, stop=True),
                      [KF, QF], [dstb])
                yield
                A(lambda e: e.activation(out=pt_[:, 4:6, :, :], in_=pc2[:, 0:512].rearrange("p (c h t) -> p c h t", c=2, h=2), func=AF.Exp, scale=0.125),
                  [pc2b], [ptb_])
                A(lambda e: e.activation(out=pt_[:, 6, :, :], in_=pd2[:, 0:256].rearrange("p (h t) -> p h t", h=2), func=AF.Exp, scale=0.125),
                  [pd2b], [ptb_])
                yield
                V(lambda e: e.tensor_tensor(out=pt_[:, 4:6, :, :], in0=pt_[:, 4:6, :, :], in1=am[:, qc, :, :].unsqueeze(2).broadcast_to([128, 2, 2, 128]), op=ALU.mult),
                  [ptb_, CONSTB], [ptb_])
                yield
                po, pob = bank()
                vlist = [(cvs[:, sc, kv * 64:(kv + 1) * 64], CK) for sc in range(4)] + \
                        [(vtm[:, kc, kv * 64:(kv + 1) * 64], VT[kc]) for kc in (pc_, nc_, qc)]
                for i, (vap, vb) in enumerate(vlist):
                    T(lambda e: e.matmul(po[0:64, 0:256], lhsT=vap, rhs=pt_[:, i, :, :], start=(i == 0), stop=(i == 6)), [vb, ptb_], [pob])
                for i in range(7):
                    T(lambda e: e.matmul(po[0:64, 256:512], lhsT=ones_b[:, 0:64], rhs=pt_[:, i, :, :], start=(i == 0), stop=(i == 6)), [CONSTB, ptb_], [pob])
                yield
                rc, rcb = rcp.next()
                V(lambda e: e.tensor_tensor(out=rc[:], in0=po[0:64, 256:512].rearrange("p (h t) -> p h t", h=2),
                                            in1=es[:, 2 * kv:2 * kv + 2].unsqueeze(2).broadcast_to([64, 2, 128]), op=ALU.add), [pob, CK], [rcb])
                V(lambda e: e.reciprocal(rc[:], rc[:]), [rcb], [rcb])
                V(lambda e: e.tensor_tensor(out=yat[:, 2 * kv:2 * kv + 2, qs], in0=po[0:64, 0:256].rearrange("p (h t) -> p h t", h=2), in1=rc[:], op=ALU.mult),
                  [pob, rcb], [YA])
                yield
            pipeline([att_chain(qc, kv) for qc in range(NCH) for kv in range(2)], 2, 3)
            out_proj([yat[:, hh, :] for hh in range(4)], [YA], 768, 64)
            K.barrier()
            K.release(mk)

        def hyena(yhy, YHB):
            mk = K.mark()
            vb_ = K.sb("hv", [128, 2, NT], BF16)
            x1f = K.sb("hx1", [128, 2, NT], BF16)
            x2f = K.sb("hx2", [128, 2, NT], BF16)
            HVB, HX1, HX2 = Buf("hv"), Buf("hx1"), Buf("hx2")

            def inproj(hbuf, HB):
                wa0, wab0 = load_wa(1032, 512)
                wa1, wab1 = load_wa(1544, 256)
                dests = [(vb_, HVB), (vb_, HVB), (x1f, HX1), (x1f, HX1), (x2f, HX2), (x2f, HX2)]
                for q in range(6):
                    raw, rawb = cvx["raws"].next()
                    for tt in range(3):
                        pb, pbb = bank()
                        if q < 4:
                            proj_fm(pb, pbb, wa0, wab0, q * 128, 128, hbuf, HB, tt)
                        else:
                            proj_fm(pb, pbb, wa1, wab1, (q - 4) * 128, 128, hbuf, HB, tt)
                        raw_fill(raw, rawb, 128, pb, pbb, tt)
                    pc0 = PT["hy_conv"][0] + (l * 6 + q) * 4
                    dt_, db_ = dests[q]
                    conv_chunk(raw, rawb, 128, pc0, dt_[:, q % 2, :], db_, False)
            with_h(inproj, conv=True)
            FEB = Buf("feats")
            w3s = K.sb("hw3", [64, 1024])
            K.dma("sync", lambda e: e.dma_start(out=w3s[:], in_=hy_w3[l]), writes=[FEB], semkey="ldf")
            h2 = K.sb("hh2", [64, NT])
            H2B = Buf("h2")
            mk2 = K.mark()
            feats = K.sb("feats", [33, NT])
            K.dma("sync", lambda e: e.dma_start(out=feats[:, 0:1024], in_=featsA_d), writes=[FEB], semkey="ldf")
            K.dma("sync", lambda e: e.dma_start(out=feats[:, 1024:1280], in_=featsB_d), writes=[FEB], semkey="ldf")
            w1s = K.sb("hw1", [33, 64])
            w2s = K.sb("hw2", [64, 64])
            K.dma("sync", lambda e: e.dma_start(out=w1s[:], in_=hy_w1[l]), writes=[FEB], semkey="ldf")
            K.dma("sync", lambda e: e.dma_start(out=w2s[:], in_=hy_w2[l]), writes=[FEB], semkey="ldf")
            hp = pcol("hyp", l * 3, 3, rows=64)
            fb = K.sb("fb", [64, 2])
            V(lambda e: e.tensor_tensor(out=fb[:, 0:1], in0=hp[:, 0:1], in1=hp[:, 1:2], op=ALU.mult), [CONSTB], [FEB])
            V(lambda e: e.tensor_tensor(out=fb[:, 1:2], in0=hp[:, 2:3], in1=hp[:, 1:2], op=ALU.mult), [CONSTB], [FEB])
            h1 = K.sb("hh1", [64, NT])
            H1B = Buf("h1")
            MAGIC = 12582912.0
            argp = Rot(K, "arg", [64, 512], F32, 2)
            nrp = Rot(K, "nr", [64, 512], F32, 2)

            def sin_layer(lhsT, src, srcb, KK, dst, dstb, fbcol):
                for tt in range(3):
                    t0, w, cnd = TT[tt]
                    pb, pbb = bank()
                    T(lambda e, pb=pb, t0=t0, w=w: e.matmul(pb[0:64, 0:w], lhsT=lhsT, rhs=src[0:KK, t0:t0 + w], start=True, stop=True), [FEB, srcb], [pbb])
                    ar, arb = argp.next()
                    nr, nrb = nrp.next()
                    V(lambda e, ar=ar, pb=pb, w=w: e.tensor_scalar(out=ar[:, 0:w], in0=pb[0:64, 0:w], scalar1=hp[:, 1:2], scalar2=fb[:, fbcol:fbcol + 1],
                                                                   op0=ALU.mult, op1=ALU.add), [pbb, CONSTB, FEB], [arb])
                    V(lambda e, ar=ar, nr=nr, w=w: e.tensor_scalar(out=nr[:, 0:w], in0=ar[:, 0:w], scalar1=float(1 / (2 * math.pi)), scalar2=MAGIC,
                                                                   op0=ALU.mult, op1=ALU.add), [arb], [nrb])
                    V(lambda e, nr=nr, w=w: e.tensor_scalar(out=nr[:, 0:w], in0=nr[:, 0:w], scalar1=MAGIC, scalar2=None, op0=ALU.subtract), [nrb], [nrb])
                    V(lambda e, ar=ar, nr=nr, w=w: e.scalar_tensor_tensor(out=ar[:, 0:w], in0=nr[:, 0:w], scalar=float(-2 * math.pi), op0=ALU.mult,
                                                                          in1=ar[:, 0:w], op1=ALU.add), [arb, nrb], [arb])
                    V(lambda e, ar=ar, w=w: e.tensor_scalar(out=ar[:, 0:w], in0=ar[:, 0:w], scalar1=3.1415925, scalar2=-3.1415925, op0=ALU.min, op1=ALU.max), [arb], [arb])
                    A(lambda e, ar=ar, t0=t0, w=w: e.activation(out=dst[:, t0:t0 + w], in_=ar[:, 0:w], func=AF.Sin), [arb], [dstb])
            sin_layer(w1s[:], feats, FEB, 33, h1, H1B, 0)
            sin_layer(w2s[:], h1, H1B, 64, h2, H2B, 1)
            K.barrier()
            K.release(mk2)

            gA = K.sb("gA", [128, 2, 2, 7, 256], BF16)
            gB = K.sb("gB", [128, 2, 2, 256], BF16)
            GB_ = Buf("g")
            hfa = K.sb("hfa", [128, 10, 2, 256], BF16)
            HFB = Buf("hf")
            ztm = K.sb("ztm", [128, NCH, 128], BF16)
            ZTB = bufs(NCH, "ztm")
            Yb = K.sb("Yb", [128, 2, 2, 5, 128], BF16)
            YBB = Buf("Yb")
            ytp = Rot(K, "yt", [128, 4, 128], F32, 2)
            tqp = Rot(K, "tqr", [128, 4, 128], F32, 4)
            identr = K.sb("identr", [128, 2, 128])
            IDR = Buf("identr")
            V(lambda e: e.tensor_copy(identr[:, 0, :].bitcast(F32R), ident_b), [CONSTB], [IDR])
            V(lambda e: e.tensor_scalar(out=identr[:, 1, :].bitcast(F32R), in0=ident_b, scalar1=-1.0, scalar2=None, op0=ALU.mult), [CONSTB], [IDR])
            dftb = bcol("dft").rearrange("p (t r f) -> p t r f", t=8, r=2)
            idft = bcol("idft").rearrange("p (a r t) -> p a r t", a=2, r=2)

            decs = K.sb("decs", [128, 20, 128])
            DCB = Buf("decs")
            for o in range(2):
                zin, zinb = (vb_, HVB) if o == 0 else (x1f, HX1)
                gate, gateb = (x1f, HX1) if o == 0 else (x2f, HX2)
                zout, zoutb = (x1f, HX1) if o == 0 else (yhy, YHB)
                for cc in range(2):
                    K.dma("sync", lambda e, cc=cc: e.dma_start(out=decs[:], in_=dec_d[:, :, cc * 128:(cc + 1) * 128]), writes=[DCB], semkey="lddec")
                    for pk in range(10):
                        pb, pbb = bank()
                        pos0 = pk * 128
                        for sd in range(2):
                            wc0 = sd * 512 + o * 256 + cc * 128
                            T(lambda e, pb=pb, sd=sd, wc0=wc0, pos0=pos0: e.matmul(pb[:, sd * 128:(sd + 1) * 128], lhsT=h2[:, pos0:pos0 + 128],
                                                                                   rhs=w3s[:, wc0:wc0 + 128], start=True, stop=True), [H2B, FEB], [pbb])
                        if pk < 8:
                            di = [pk, 8 + pk]
                        else:
                            di = [16 + pk - 8, 18 + pk - 8]
                        for sd in range(2):
                            V(lambda e, pb=pb, sd=sd, pk=pk, di=di, cc=cc: e.tensor_tensor(out=hfa[:, pk, sd, cc * 128:(cc + 1) * 128], in0=pb[:, sd * 128:(sd + 1) * 128],
                                                                                          in1=decs[:, di[sd], :], op=ALU.mult), [pbb, DCB], [HFB])
                for delta in range(-3, 4):
                    ents = spectrum_entries(delta, 8)
                    for fch in range(2):
                        pb, pbb = bank()
                        for r in range(2):
                            for i, (src, k, ty) in enumerate(ents):
                                T(lambda e, pb=pb, r=r, i=i, src=src, k=k, ty=ty, fch=fch: e.matmul(
                                    pb[:, r * 256:(r + 1) * 256], lhsT=dftb[:, ty, r, fch * 128:(fch + 1) * 128], rhs=hfa[:, k, src, :],
                                    start=(i == 0), stop=(i == len(ents) - 1)), [CONSTB, HFB], [pbb])
                        if delta == 0:
                            A(lambda e, pb=pb, fch=fch, delta=delta: e.activation(out=gA[:, fch, :, delta + 3, :], in_=pb[:, 0:512].rearrange("p (r c) -> p r c", r=2),
                                                                                  func=AF.Copy), [pbb], [GB_])
                        else:
                            A(lambda e, pb=pb, fch=fch, delta=delta: e.activation(out=gA[:, fch, :, delta + 3, :], in_=pb[:, 0:512].rearrange("p (r c) -> p r c", r=2),
                                                                                  func=AF.Copy, scale=flag), [pbb, CONSTB], [GB_])
                entsB = spectrum_entries(0, 2)
                for fch in range(2):
                    pb, pbb = bank()
                    for r in range(2):
                        for i, (src, k, ty) in enumerate(entsB):
                            T(lambda e, pb=pb, r=r, i=i, src=src, k=k, ty=ty, fch=fch: e.matmul(
                                pb[:, r * 256:(r + 1) * 256], lhsT=dftb[:, ty, r, fch * 128:(fch + 1) * 128], rhs=hfa[:, 8 + k, src, :],
                                start=(i == 0), stop=(i == len(entsB) - 1)), [CONSTB, HFB], [pbb])
                    A(lambda e, pb=pb, fch=fch: e.activation(out=gB[:, fch, :, :], in_=pb[:, 0:512].rearrange("p (r c) -> p r c", r=2), func=AF.Copy), [pbb], [GB_])
                for cc in range(2):
                    gcs = slice(cc * 128, (cc + 1) * 128)
                    for tc in range(NCH):
                        pb, pbb = bank()
                        pbv = pb[:].bitcast(BF16)
                        T(lambda e, pbv=pbv, tc=tc: e.transpose(pbv[:, 0:128], zin[:, cc, tc * 128:(tc + 1) * 128], ident_b), [zinb, CONSTB], [pbb])
                        A(lambda e, pbv=pbv, tc=tc: e.activation(out=ztm[:, tc, :], in_=pbv[:, 0:128], func=AF.Copy), [pbb], [ZTB[tc]])
                    ty_f = [dft_type(0, 0), dft_type(0, 128)]
                    set_banks(6, 8)
                    for fch in range(2):
                        for r in range(2):
                            for blk in range(NB):
                                if blk < 4:
                                    dst, dstb = PS[r][:, blk * 128:(blk + 1) * 128], PSB[r]
                                else:
                                    dst, dstb = PS[2][:, r * 128:(r + 1) * 128], PSB[2]
                                for tk in range(2):
                                    T(lambda e: e.matmul(dst, lhsT=dftb[:, ty_f[tk], r, fch * 128:(fch + 1) * 128], rhs=ztm[:, 2 * blk + tk, :],
                                                         start=(tk == 0), stop=(tk == 1)), [CONSTB, ZTB[2 * blk + tk]], [dstb])
                        terms = []
                        for delta in [0, 1, -1, 2, -2, 3, -3]:
                            nbk = 4 - abs(delta)
                            tb0 = max(delta, 0)
                            sb0 = tb0 - delta
                            for (acc, gr, zr, sgn) in ((3, 0, 0, 0), (3, 1, 1, 1), (4, 0, 1, 0), (4, 1, 0, 0)):
                                g_ap = gA[:, fch, gr, delta + 3, gcs].unsqueeze(1).broadcast_to([128, nbk, 128])
                                z_ap = PS[zr][:, sb0 * 128:(sb0 + nbk) * 128].rearrange("p (b c) -> p b c", b=nbk)
                                terms.append((acc, tb0 * 128, nbk * 128, sgn, z_ap, PSB[zr], g_ap, nbk))
                        cnt = {3: 0, 4: 0}
                        tot = {3: 14, 4: 14}
                        for (acc, c0, ncol, sgn, z_ap, zb, g_ap, nbk) in terms:
                            tq, tqb = tqp.next()
                            V(lambda e: e.tensor_tensor(out=tq[:, 0:nbk, :].bitcast(F32R), in0=z_ap, in1=g_ap, op=ALU.mult), [zb, GB_], [tqb])
                            T(lambda e: e.matmul(PS[acc][:, c0:c0 + ncol], lhsT=identr[:, sgn, :].bitcast(F32R),
                                                 rhs=tq[:, 0:nbk, :].rearrange("p b c -> p (b c)").bitcast(F32R),
                                                 start=(cnt[acc] == 0), stop=(cnt[acc] == tot[acc] - 1)), [tqb, IDR], [PSB[acc]])
                            cnt[acc] += 1
                        tqs = []
                        for (gr, zr, sgn) in ((0, 0, 0), (1, 1, 1), (0, 1, 0), (1, 0, 0)):
                            tq, tqb = tqp.next()
                            V(lambda e: e.tensor_tensor(out=tq[:, 0, :].bitcast(F32R), in0=PS[2][:, zr * 128:(zr + 1) * 128], in1=gB[:, fch, gr, gcs], op=ALU.mult),
                              [PSB[2], GB_], [tqb])
                            tqs.append((tq, tqb, sgn))
                        for i, (tq, tqb, sgn) in enumerate(tqs):
                            ro = i // 2
                            T(lambda e: e.matmul(PS[5][:, ro * 128:(ro + 1) * 128], lhsT=identr[:, sgn, :].bitcast(F32R), rhs=tq[:, 0, :].bitcast(F32R),
                                                 start=(i % 2 == 0), stop=(i % 2 == 1)), [tqb, IDR], [PSB[5]])
                        for r in range(2):
                            A(lambda e: e.activation(out=Yb[:, fch, r, 0:4, :], in_=PS[3 + r][:, 0:512].rearrange("p (b c) -> p b c", b=4), func=AF.Copy),
                              [PSB[3 + r]], [YBB])
                        A(lambda e: e.activation(out=Yb[:, fch, :, 4, :], in_=PS[5][:, 0:256].rearrange("p (r c) -> p r c", r=2), func=AF.Copy), [PSB[5]], [YBB])
                    set_banks(0, 8)
                    hbcol = pcol("hy_bias", (l * 2 + o) * 2 + cc, 1)
                    for blk in range(NB):
                        pb, pbb = bank()
                        i = 0
                        for fch in range(2):
                            for r in range(2):
                                T(lambda e, pb=pb, fch=fch, r=r, blk=blk, i=i: e.matmul(pb[:, 0:256], lhsT=Yb[:, fch, r, blk, :], rhs=idft[:, fch, r, :],
                                                                                       start=(i == 0), stop=(i == 3)), [YBB, CONSTB], [pbb])
                                i += 1
                        ts = slice(blk * 256, (blk + 1) * 256)
                        tq, tqb = ytp.next()
                        tqv = tq[:].rearrange("p a b -> p (a b)")[:, 0:256]
                        V(lambda e, tqv=tqv, pb=pb, ts=ts: e.scalar_tensor_tensor(out=tqv, in0=zin[:, cc, ts], scalar=hbcol, op0=ALU.mult, in1=pb[:, 0:256], op1=ALU.add),
                          [zinb, CONSTB, pbb], [tqb])
                        V(lambda e, tqv=tqv, ts=ts: e.tensor_tensor(out=zout[:, cc, ts], in0=tqv, in1=gate[:, cc, ts], op=ALU.mult), [tqb, gateb], [zoutb])
            out_proj([yhy[:, 0, :], yhy[:, 1, :]], [YHB], 256, 128)
            K.barrier()
            K.release(mk)

        yhy = K.sb("yhy", [128, 2, NT], BF16)
        YHB = Buf("yhy")
        hyena(yhy, YHB)
        yssd = K.sb("yssd", [64, 4, NT], BF16)
        YSB = Buf("yssd")
        ssd(yssd, YSB)
        yret = K.sb("yret", [64, 4, NT], BF16)
        YRB = Buf("yret")
        ret(yret, YRB)
        yatt = K.sb("yatt", [64, 4, NT], BF16)
        YAB_ = Buf("yatt")
        att(yatt, YAB_)
        out_proj_all()
        K.barrier()
        K.release(mk_all)

    for l in range(nlayers):
        if l == 1:
            assert not pending_mod and not inflight_mod, (pending_mod, inflight_mod)
        ffn(l, 0)
        mixer(l)
        ffn(l, 1)

    K.dma("sync", lambda e: e.dma_start(out=yT, in_=x[:]), reads=[b for row in XB for b in row], semkey="sty")
    K.barrier()
    K.release(0)
    return nc, K


def _consts(is_s):
    f32 = np.float32
    cfv = np.zeros((128, CF.n), f32)

    def put(name, arr):
        c0, w = CF[name]
        cfv[:, c0:c0 + w] = np.asarray(arr, f32).reshape(128, w) if np.asarray(arr).ndim > 1 or w == 1 else np.broadcast_to(np.asarray(arr, f32), (128, w))
    j = np.arange(128)[:, None]
    i = np.arange(128)[None, :]
    put("trif", (i >= j).astype(f32))
    put("trib", (j >= i).astype(f32))
    put("ones", np.ones((128, 128), f32))
    put("relu_f", np.maximum(i - j, 0).astype(f32))
    put("relu_b", np.maximum(j - i, 0).astype(f32))
    put("ip1", np.broadcast_to((i + 1).astype(f32), (128, 128)))
    put("rmi", np.broadcast_to((128 - i).astype(f32), (128, 128)))
    put("tailf", (127 - j).astype(f32))
    put("tailb", j.astype(f32))
    nf = 16
    inv = (10000.0 ** (-np.arange(nf, dtype=f32) / nf)).astype(f32)
    cos = np.ones((NT, 32), f32)
    sin = np.zeros((NT, 32), f32)
    if is_s:
        t = np.arange(1024)
        rows = (t // 64).astype(f32)
        cols = (t % 64).astype(f32)
        angr = rows[:, None] * inv[None, :]
        angc = cols[:, None] * inv[None, :]
        cos[:1024, 0:16] = np.cos(angr)
        cos[:1024, 16:32] = np.cos(angc)
        sin[:1024, 0:16] = np.sin(angr)
        sin[:1024, 16:32] = np.sin(angc)
    put("cos", cos.reshape(NCH, 128, 32).transpose(1, 0, 2).reshape(128, 320))
    put("sin", sin.reshape(NCH, 128, 32).transpose(1, 0, 2).reshape(128, 320))
    cfb = np.full((NCH,), -30000.0, f32)
    hl = np.zeros((5,), f32)
    hr = np.zeros((5,), f32)
    if is_s:
        cfb[0:8] = 0.0
        hl[1:4] = 1.0
        hr[0:3] = 1.0
    put("cfb", cfb)
    put("hl", hl)
    put("hr", hr)
    put("flag", np.full((128, 1), 1.0 if is_s else 0.0, f32))
    put("negpi", np.full((128, 1), -math.pi, f32))
    put("nbf", ((i >= j).astype(f32) - 1.0) * 30000.0)
    put("nbb", ((j >= i).astype(f32) - 1.0) * 30000.0)

    cbv = np.zeros((128, CB.n), f32)

    def putb(name, arr):
        c0, w = CB[name]
        cbv[:, c0:c0 + w] = np.asarray(arr, f32).reshape(128, w)
    putb("ident", np.eye(128, dtype=f32))
    putb("ones", np.ones((128, 128), f32))
    am = np.zeros((128, NCH, 2, 128), f32)
    band_prev = (j >= i).astype(f32)
    band_next = (j <= i).astype(f32)
    for qc in range(NCH):
        if is_s and qc < 8:
            if qc >= 1:
                am[:, qc, 0, :] = band_prev
            if qc <= 6:
                am[:, qc, 1, :] = band_next
        else:
            if qc % 2 == 1:
                am[:, qc, 0, :] = 1.0
            else:
                am[:, qc, 1, :] = 1.0
    putb("am", am)
    om = 2 * np.pi * (np.arange(256) + 0.5) / 512.0
    row = np.arange(128)
    dft = np.zeros((128, 8, 2, 256), np.float64)
    for ty in range(8):
        if ty < 4:
            e = FWD_E0[ty] + row
        else:
            e = BWD_E0[ty - 4] - row
        valid = (np.abs(e) <= 255).astype(np.float64)
        ang = e[:, None] * om[None, :]
        dft[:, ty, 0, :] = np.cos(ang) * valid[:, None]
        dft[:, ty, 1, :] = -np.sin(ang) * valid[:, None]
    putb("dft", dft)
    tt = np.arange(256)
    idft = np.zeros((128, 2, 2, 256), np.float64)
    for fch in range(2):
        omf = om[fch * 128:(fch + 1) * 128]
        ang = omf[:, None] * tt[None, :]
        idft[:, fch, 0, :] = (2.0 / 512) * np.cos(ang)
        idft[:, fch, 1, :] = -(2.0 / 512) * np.sin(ang)
    putb("idft", idft)
    return cfv, cbv


def _hy_consts(LA):
    f32 = np.float32
    l = LA
    pos = np.arange(l, dtype=f32)
    t = pos / f32(l - 1)
    bands = np.linspace(1e-4, 15, 16, dtype=f32)
    ang = (f32(2.0 * math.pi / l)) * pos[:, None] * bands[None, :]
    feats = np.concatenate([t[:, None], np.cos(ang), -np.sin(ang)], axis=-1).astype(f32)
    max_decay = math.log(1e-2) / 0.3
    min_decay = math.log(1e-2) / 1.5
    deltas = np.abs(np.linspace(min_decay, max_decay, 256, dtype=f32))
    dec = np.exp(-t[:, None] * deltas[None, :]).astype(f32)
    return feats, dec


def _prepare(inp):
    f32 = np.float32
    g = lambda k: np.asarray(inp[k], dtype=f32)
    x_prompt, x_sample = g("x_prompt"), g("x_sample")
    cache_k, cache_v = g("cache_k"), g("cache_v")
    state_ssd, state_ret = g("state_ssd"), g("state_ret")
    c, c_ctx = g("c"), g("c_ctx")
    pt = np.zeros((128, PT.n), f32)

    def putp(name, off, arr):
        arr = np.asarray(arr, f32)
        c0 = PT[name][0] + off
        pt[0:arr.shape[0], c0:c0 + arr.shape[1]] = arr
    nw = g("norm_w")
    for l in range(2):
        for j in range(3):
            putp("norm_w", (l * 3 + j) * 8, nw[l, j].reshape(8, 128).T)
        putp("b_mod", l * 72, g("b_mod")[l].reshape(72, 128).T)
        cw, cbias = g("ssd_conv_w")[l], g("ssd_conv_b")[l]
        for q in range(8):
            if q < 4:
                f0, fs = q * 64, 64
            else:
                f0, fs = 256 + (q - 4) * 128, 128
            arr = np.stack([cw[0, f0:f0 + fs], cw[1, f0:f0 + fs], cw[2, f0:f0 + fs], cbias[f0:f0 + fs]], axis=1)
            putp("ssd_conv", (l * 8 + q) * 4, arr)
        hw, hb = g("hy_conv_w")[l], g("hy_conv_b")[l]
        for q in range(6):
            f0 = q * 128
            arr = np.stack([hw[0, f0:f0 + 128], hw[1, f0:f0 + 128], hw[2, f0:f0 + 128], hb[f0:f0 + 128]], axis=1)
            putp("hy_conv", (l * 6 + q) * 4, arr)
        hbias = g("hy_bias")[l]
        for o in range(2):
            putp("hy_bias", (l * 2 + o) * 2, hbias[o].reshape(2, 128).T)
        putp("ssd_nw", l * 4, g("ssd_norm_w")[l].reshape(4, 64).T)
        putp("ssd_d", l * 4, np.broadcast_to(g("ssd_d")[l][None, :], (64, 4)))
        putp("ret_gn", l * 4, g("ret_gn_w")[l].reshape(4, 64).T)
        putp("hyp", l * 3, np.stack([g("hy_b1")[l], g("hy_freq")[l], g("hy_b2")[l]], axis=1))
    ft = np.zeros((128, FT.n), f32)

    def putf(name, off, vec):
        vec = np.asarray(vec, f32).reshape(-1)
        c0 = FT[name][0] + off
        ft[:, c0:c0 + vec.size] = vec[None, :]
    for l in range(2):
        putf("dt_bias", l * 80, np.tile(g("ssd_dt_bias")[l].reshape(8), NCH))
        putf("a_log", l * 80, np.tile(g("ssd_a_log")[l].reshape(8), NCH))
        putf("ret_logit", l * 8, g("ret_decay_logit")[l].reshape(8))
        putf("qkw", l * 384, np.concatenate([np.tile(g("attn_q_norm")[l], 4), np.tile(g("attn_k_norm")[l], 2)]))
        putf("sink", l * 4, g("attn_sink")[l])
    featsB, decB = _hy_consts(256)
    featsA_s, decA_s = _hy_consts(1024)
    wm = g("w_mod").reshape(2, 8, 128, 18, 512).transpose(0, 3, 2, 1, 4).reshape(2, 18, 128, 4096)
    wi = g("ffn_w_in").reshape(2, 2, 8, 128, 2, 11, 256)
    wi = wi.transpose(0, 1, 5, 3, 2, 4, 6).reshape(2, 2, 11, 128, 4096)
    wo = g("ffn_w_out").reshape(2, 2, 11, 2, 128, 1024).transpose(0, 1, 2, 4, 3, 5).reshape(2, 2, 11, 128, 2048)
    shared = dict(ptab=pt, ftab=ft, w_mod=np.ascontiguousarray(wm), ffn_w_in=np.ascontiguousarray(wi), ffn_w_out=np.ascontiguousarray(wo), mix_w_in=g("mix_w_in"),
                  mix_w_out=g("mix_w_out"), hy_w1=g("hy_w1"), hy_w2=g("hy_w2"), hy_w3=g("hy_w3"),
                  featsB=np.ascontiguousarray(featsB.T))
    consts = {True: _consts(True), False: _consts(False)}
    in_maps = []
    plan = []
    for cid in range(NCORE):
        is_s = cid < 2
        if is_s:
            xs = np.concatenate([x_sample[cid], x_prompt[30 + cid]], axis=0)
            seqs = [30 + cid]
            condA = c[cid]
        else:
            seqs = list(range(5 * (cid - 2), 5 * (cid - 2) + 5))
            xs = x_prompt[seqs].reshape(NT, D)
            condA = c_ctx
        plan.append((is_s, seqs))
        xTm = np.ascontiguousarray(xs.T.reshape(8, 128, NT).transpose(1, 0, 2))
        cond = np.stack([condA, c_ctx], axis=-1)
        condTm = np.ascontiguousarray(cond.reshape(8, 128, 2).transpose(1, 0, 2))
        s0s = np.zeros((2, 5, 2, 4, 128, 64), f32)
        s0r = np.zeros((2, 5, 2, 4, 64, 64), f32)
        ck = np.zeros((2, 2, 64, 512), f32)
        cv = np.zeros((2, 512, 128), f32)
        if is_s:
            for l in range(2):
                s0s[l, 0, 0] = state_ssd[cid, l, 0]
                s0s[l, 3, 1] = state_ssd[cid, l, 1]
                s0r[l, 0, 0] = state_ret[cid, l, 0]
                s0r[l, 3, 1] = state_ret[cid, l, 1]
                ck[l] = cache_k[cid, l].transpose(1, 2, 0)
                cv[l] = cache_v[cid, l].reshape(512, 128)
            featsA = featsA_s
            decFA = decA_s.copy()
        else:
            featsA = np.zeros((1024, 33), f32)
            featsA[:256] = featsB
            decFA = np.zeros((1024, 256), f32)
            decFA[:256] = decB
        decBA = decFA.copy()
        decBA[0] = 0.0
        decFB = decB.copy()
        decBB = decB.copy()
        decBB[0] = 0.0
        dec = np.concatenate([decFA.reshape(8, 128, 256), decBA.reshape(8, 128, 256), decFB.reshape(2, 128, 256), decBB.reshape(2, 128, 256)], axis=0)
        cfv, cbv = consts[is_s]
        m = dict(shared)
        m.update(xT=xTm, condT=condTm,
                 s0ssd=np.ascontiguousarray(s0s.reshape(80, 128, 64).transpose(1, 0, 2)),
                 s0ret=np.ascontiguousarray(s0r.reshape(80, 64, 64).transpose(1, 0, 2)),
                 ckT=np.ascontiguousarray(ck.reshape(4, 64, 512).transpose(1, 0, 2)),
                 cvt=np.ascontiguousarray(cv.reshape(8, 128, 128).transpose(1, 0, 2)),
                 cf32=cfv, cbf=cbv, featsA=np.ascontiguousarray(featsA.T), dec=np.ascontiguousarray(dec.transpose(1, 0, 2)))
        in_maps.append(m)
    return in_maps, plan


_CACHE = {}


def kernel(**inputs):
    in_maps, plan = _prepare(inputs)
    if "nc" not in _CACHE:
        _CACHE["nc"] = build_program()[0]
    nc = _CACHE["nc"]
    res = run_bass_kernel_spmd(nc, in_maps, core_ids=list(range(NCORE)))
    f32 = np.float32
    y_prompt = np.zeros((32, 256, D), f32)
    y_sample = np.zeros((2, 1024, D), f32)
    nck = np.zeros((32, 2, 256, 2, 64), f32)
    ncv = np.zeros((32, 2, 256, 2, 64), f32)
    nssd = np.zeros((32, 2, 2, 4, 128, 64), f32)
    nret = np.zeros((32, 2, 2, 4, 64, 64), f32)
    for cid, (is_s, seqs) in enumerate(plan):
        r = res.results[cid]
        y = np.asarray(r["yT"]).transpose(1, 0, 2).reshape(D, NT).T
        nk = np.asarray(r["nk"])
        nv = np.asarray(r["nv"])
        ss = np.asarray(r["nssd"]).transpose(1, 0, 2).reshape(2, 5, 2, 4, 128, 64)
        sr = np.asarray(r["nret"]).transpose(1, 0, 2).reshape(2, 5, 2, 4, 64, 64)
        if is_s:
            y_sample[cid] = y[:1024]
            blks = [(4, seqs[0])]
        else:
            blks = list(enumerate(seqs))
        for blk, b in blks:
            ts = slice(blk * 256, (blk + 1) * 256)
            y_prompt[b] = y[ts]
            for l in range(2):
                nck[b, l] = nk[l, ts].reshape(256, 2, 64)
                ncv[b, l] = nv[l, ts].reshape(256, 2, 64)
                nssd[b, l] = ss[l, blk]
                nret[b, l] = sr[l, blk]
    return (y_prompt, y_sample, nck, ncv, nssd, nret)
```

```python
import math
import numpy as np
import concourse.bass as bass
import concourse.mybir as mybir
from concourse.bass_utils import run_bass_kernel_spmd

F32 = mybir.dt.float32
BF16 = mybir.dt.bfloat16
F32R = mybir.dt.float32r
AF = mybir.ActivationFunctionType
ALU = mybir.AluOpType
AX = mybir.AxisListType

NCORE = 8
NT = 1280
NB = 5
NCH = 10
TT = [(0, 512, 0), (512, 512, 0), (1024, 256, 1)]
D = 1024
DFF = 2816
EPS = 1e-6
ENGS = ["tensor", "vector", "scalar", "gpsimd", "sync"]
SAME_ENGINE_NOSYNC = ("tensor",)


class Buf:
    __slots__ = ("name", "w", "r", "excl")

    def __init__(self, name="", excl=False):
        self.name = name
        self.w = None
        self.r = {}
        self.excl = excl


def bufs(n, name=""):
    return [Buf(name + str(i)) for i in range(n)]


class KB:
    def __init__(self, nc):
        self.nc = nc
        self.cnt = {e: 0 for e in ENGS}
        self.seen = {e: {} for e in ENGS}
        self.sems = {}
        self.dcount = {}
        self._stack = []
        self.n_inst = 0
        self.uid = 0

    def enter(self, cm):
        v = cm.__enter__()
        self._stack.append(cm)
        return v

    def mark(self):
        return len(self._stack)

    def release(self, m):
        while len(self._stack) > m:
            self._stack.pop().__exit__(None, None, None)

    def sem(self, key):
        if key not in self.sems:
            self.sems[key] = self.enter(self.nc.semaphore("s_" + key))
        return self.sems[key]

    def sb(self, name, shape, dt=F32):
        self.uid += 1
        return self.enter(self.nc.sbuf_tensor(f"{name}_{self.uid}", list(shape), dt))

    def ps(self, name, shape, dt=F32):
        return self.enter(self.nc.psum_tensor(name, list(shape), dt))

    @staticmethod
    def _flat(bl):
        out = []
        for b in bl:
            if isinstance(b, (list, tuple)):
                out.extend(KB._flat(b))
            else:
                out.append(b)
        return out

    def _need(self, eng, reads, writes):
        need = {}

        def add(k, v):
            if need.get(k, 0) < v:
                need[k] = v
        for b in reads:
            if b.w is not None:
                add(*b.w)
        for b in writes:
            if b.w is not None:
                add(*b.w)
            for k, v in b.r.items():
                add(k, v)
        out = []
        for k, v in need.items():
            if k == "p_" + eng and eng in SAME_ENGINE_NOSYNC:
                continue
            if self.seen[eng].get(k, 0) >= v:
                continue
            self.seen[eng][k] = v
            out.append((k, v))
        return out

    def _emit(self, eng, waits, fn, key, inc):
        e = getattr(self.nc, eng)
        for k, v in waits:
            e.wait_ge(self.sems[k], v)
        if fn is not None:
            fn(e).then_inc(self.sems[key], inc)

    def op(self, eng, fn, reads=(), writes=()):
        reads = self._flat(reads)
        writes = self._flat(writes)
        ex = [b for b in reads if b.excl]
        if ex:
            reads = [b for b in reads if not b.excl]
            writes = writes + [b for b in ex if b not in writes]
        waits = self._need(eng, reads, writes)
        key = "p_" + eng
        self.sem(key)
        self.cnt[eng] += 1
        val = self.cnt[eng]
        for b in reads:
            if b.r.get(key, 0) < val:
                b.r[key] = val
        for b in writes:
            b.w = (key, val)
            b.r = {}
        self._emit(eng, waits, fn, key, 1)
        self.n_inst += 1

    def dma(self, eng, fn, reads=(), writes=(), semkey=None):
        reads = self._flat(reads)
        writes = self._flat(writes)
        waits = self._need(eng, reads, writes)
        self.sem(semkey)
        self.dcount[semkey] = self.dcount.get(semkey, 0) + 16
        val = self.dcount[semkey]
        for b in reads:
            if b.r.get(semkey, 0) < val:
                b.r[semkey] = val
        for b in writes:
            b.w = (semkey, val)
            b.r = {}
        self._emit(eng, waits, fn, semkey, 16)
        self.n_inst += 1

    def barrier(self):
        tot = [("p_" + e, self.cnt[e]) for e in ENGS if self.cnt[e] > 0]
        tot += list(self.dcount.items())
        for eng in ENGS:
            waits = []
            for k, v in tot:
                if k == "p_" + eng and eng == "tensor":
                    continue
                if self.seen[eng].get(k, 0) >= v:
                    continue
                self.seen[eng][k] = v
                waits.append((k, v))
            self._emit(eng, waits, None, None, 0)

    def V(self, fn, r=(), w=()):
        self.op("vector", fn, r, w)

    def A(self, fn, r=(), w=()):
        self.op("scalar", fn, r, w)

    def T(self, fn, r=(), w=()):
        self.op("tensor", fn, r, w)

    def G(self, fn, r=(), w=()):
        self.op("gpsimd", fn, r, w)


class Rot:
    def __init__(self, K, name, shape, dt, n):
        self.t = [K.sb(f"{name}{i}", shape, dt) for i in range(n)]
        self.b = bufs(n, name)
        self.i = 0

    def next(self):
        i = self.i
        self.i = (i + 1) % len(self.t)
        return self.t[i], self.b[i]


class Cols:
    def __init__(self):
        self.m = {}
        self.n = 0

    def add(self, name, w):
        self.m[name] = (self.n, w)
        self.n += w

    def __getitem__(self, name):
        return self.m[name]


def ptab_cols():
    c = Cols()
    c.add("norm_w", 48)
    c.add("b_mod", 144)
    c.add("ssd_conv", 64)
    c.add("hy_conv", 48)
    c.add("hy_bias", 8)
    c.add("ssd_nw", 8)
    c.add("ssd_d", 8)
    c.add("ret_gn", 8)
    c.add("hyp", 6)
    return c


def ftab_cols():
    c = Cols()
    c.add("dt_bias", 160)
    c.add("a_log", 160)
    c.add("ret_logit", 16)
    c.add("qkw", 768)
    c.add("sink", 8)
    return c


def cf32_cols():
    c = Cols()
    c.add("trif", 128)
    c.add("trib", 128)
    c.add("ones", 128)
    c.add("relu_f", 128)
    c.add("relu_b", 128)
    c.add("ip1", 128)
    c.add("rmi", 128)
    c.add("tailf", 1)
    c.add("tailb", 1)
    c.add("cos", 320)
    c.add("sin", 320)
    c.add("cfb", 10)
    c.add("hl", 5)
    c.add("hr", 5)
    c.add("flag", 1)
    c.add("negpi", 1)
    c.add("nbf", 128)
    c.add("nbb", 128)
    return c


def cbf_cols():
    c = Cols()
    c.add("ident", 128)
    c.add("ones", 128)
    c.add("am", 2560)
    c.add("dft", 4096)
    c.add("idft", 1024)
    return c


PT = ptab_cols()
FT = ftab_cols()
CF = cf32_cols()
CB = cbf_cols()

FWD_E0 = [-256, -128, 0, 128]
BWD_E0 = [256, 128, 0, -128]


def dft_type(src, e0):
    return (FWD_E0.index(e0) if src == 0 else 4 + BWD_E0.index(e0))


def spectrum_entries(delta, nchunks):
    out = []
    for k in range(nchunks):
        e0 = 128 * k - 256 * delta
        if e0 in FWD_E0:
            out.append((0, k, dft_type(0, e0)))
        e0b = -128 * k - 256 * delta
        if e0b in BWD_E0:
            out.append((1, k, dft_type(1, e0b)))
    return out


def build_program(nlayers=2):
    nc = bass.Bass("TRN2", target_bir_lowering=False)

    def din(name, shape):
        return nc.dram_tensor(name, list(shape), F32, kind="ExternalInput").ap()

    def dout(name, shape):
        return nc.dram_tensor(name, list(shape), F32, kind="ExternalOutput").ap()

    xT = din("xT", [128, 8, NT])
    condT = din("condT", [128, 8, 2])
    s0ssd = din("s0ssd", [128, 80, 64])
    s0ret = din("s0ret", [64, 80, 64])
    ckT = din("ckT", [64, 4, 512])
    cvt = din("cvt", [128, 8, 128])
    ptab_d = din("ptab", [128, PT.n])
    ftab_d = din("ftab", [128, FT.n])
    cf32_d = din("cf32", [128, CF.n])
    cbf_d = din("cbf", [128, CB.n])
    featsA_d = din("featsA", [33, 1024])
    featsB_d = din("featsB", [33, 256])
    dec_d = din("dec", [128, 20, 256])
    w_mod = din("w_mod", [2, 18, 128, 4096])
    ffn_w_in = din("ffn_w_in", [2, 2, 11, 128, 4096])
    ffn_w_out = din("ffn_w_out", [2, 2, 11, 128, 2048])
    mix_w_in = din("mix_w_in", [2, D, 3336])
    mix_w_out = din("mix_w_out", [2, D, D])
    hy_w1 = din("hy_w1", [2, 33, 64])
    hy_w2 = din("hy_w2", [2, 64, 64])
    hy_w3 = din("hy_w3", [2, 64, 1024])

    yT = dout("yT", [128, 8, NT])
    nk_o = dout("nk", [2, NT, 128])
    nv_o = dout("nv", [2, NT, 128])
    nssd_o = dout("nssd", [128, 80, 64])
    nret_o = dout("nret", [64, 80, 64])

    K = KB(nc)
    V, A, T, G = K.V, K.A, K.T, K.G

    x = K.sb("x", [128, 8, NT])
    XB = [[Buf(f"x{m}_{t}") for t in range(3)] for m in range(8)]
    modt = K.sb("modt", [128, 2, 2, 72])
    MODB = Buf("mod")
    ptab = K.sb("ptab", [128, PT.n])
    ftab = K.sb("ftab", [128, FT.n])
    cf = K.sb("cf", [128, CF.n])
    cb = K.sb("cb", [128, CB.n], BF16)
    CONSTB = Buf("const")
    CONSTB2 = Buf("constb")
    WA = [K.sb(f"WA{i}", [128, 8, 512], BF16) for i in range(2)]
    WAB = bufs(2, "WA")
    WB = [K.sb(f"WB{i}", [128, 4096], BF16) for i in range(2)]
    WBB = bufs(2, "WB")
    wctr = {"a": 0, "b": 0}
    PS = [K.ps(f"ps{i}", [128, 512]) for i in range(8)]
    PSBK = [Buf(f"psb{i}", excl=True) for i in range(8)]
    PSR = [PSBK[i // 4] for i in range(32)]
    PSB = [[PSBK[i]] for i in range(8)]
    psctr = [0, 0, 8]
    prc = [0]

    def bank():
        lo, hi = psctr[1], psctr[2]
        i = psctr[0]
        if i < lo or i >= hi:
            i = lo
        psctr[0] = i + 1 if i + 1 < hi else lo
        return PS[i], PSB[i]

    def set_banks(lo, hi):
        psctr[1], psctr[2] = lo, hi

    def pr(ncols):
        n = (ncols + 127) // 128
        i = prc[0]
        if (i % 4) + n > 4:
            i = (i // 4 + 1) * 4
        if i + n > 32:
            i = 0
        prc[0] = (i + n) % 32
        b, r = divmod(i, 4)
        return PS[b][:, r * 128:r * 128 + n * 128], [PSBK[b]]

    def pipeline(gen_list, width, stagger):
        pending = list(gen_list)
        active = []
        rounds = 0
        since = stagger
        while pending or active:
            if pending and len(active) < width and (since >= stagger or not active):
                active.append(pending.pop(0))
                since = 0
            nxt = []
            for g in active:
                try:
                    next(g)
                    nxt.append(g)
                except StopIteration:
                    pass
            active = nxt
            rounds += 1
            since += 1

    def interleave(gens):
        gens = list(gens)
        while gens:
            nxt = []
            for g in gens:
                try:
                    next(g)
                    nxt.append(g)
                except StopIteration:
                    pass
            gens = nxt

    def nextA():
        i = wctr["a"] % 2
        wctr["a"] += 1
        return WA[i], WAB[i], f"wa{i}"

    def nextB():
        i = wctr["b"] % 2
        wctr["b"] += 1
        return WB[i], WBB[i], f"wb{i}"

    def pcol(name, off=0, w=1, rows=128):
        c0 = PT[name][0] + off
        return ptab[0:rows, c0:c0 + w]

    def fcol(name, off=0, w=1, rows=128):
        c0 = FT[name][0] + off
        return ftab[0:rows, c0:c0 + w]

    def ccol(name, off=0, w=None, rows=128):
        c0, ww = CF[name]
        if w is None:
            w = ww
        return cf[0:rows, c0 + off:c0 + off + w]

    def bcol(name, off=0, w=None, rows=128):
        c0, ww = CB[name]
        if w is None:
            w = ww
        return cb[0:rows, c0 + off:c0 + off + w]

    K.dma("sync", lambda e: e.dma_start(out=x[:], in_=xT), writes=[b for row in XB for b in row], semkey="ldx")
    K.dma("sync", lambda e: e.dma_start(out=ptab[:], in_=ptab_d), writes=[CONSTB], semkey="ldc")
    K.dma("sync", lambda e: e.dma_start(out=ftab[:], in_=ftab_d), writes=[CONSTB], semkey="ldc")
    K.dma("sync", lambda e: e.dma_start(out=cf[:], in_=cf32_d), writes=[CONSTB], semkey="ldc")
    K.dma("gpsimd", lambda e: e.dma_start(out=cb[:], in_=cbf_d, max_dma_last_dim=2048), writes=[CONSTB2], semkey="ldcb")

    K.barrier()

    flag = ccol("flag")
    ident_b = bcol("ident")
    ones_b = bcol("ones")
    ones_f = ccol("ones")
    trif = ccol("trif")
    trib = ccol("trib")

    condf = K.sb("condf", [128, 8, 2])
    condb = K.sb("condb", [128, 8, 2], BF16)
    CB_ = Buf("cond")
    MODL = [Buf("mod0"), Buf("mod1")]
    K.dma("sync", lambda e: e.dma_start(out=condf[:], in_=condT), writes=[CB_], semkey="ldcond")
    A(lambda e: e.activation(out=condb[:], in_=condf[:], func=AF.Silu), [CB_], [CB_])

    def mod_dma(l, ci, wa, wab, sk):
        K.dma("gpsimd", lambda e: e.dma_start(out=wa[:].rearrange("p k n -> p (k n)"), in_=w_mod[l, ci], max_dma_last_dim=8192),
              writes=[wab], semkey=sk)

    def mod_chunk(l, ci, wa, wab, sk, dma=True):
        if dma:
            mod_dma(l, ci, wa, wab, sk)
        pb, pbb = bank()
        for mb in range(4):
            for k in range(8):
                T(lambda e: e.matmul(pb[:, mb * 2:mb * 2 + 2], lhsT=wa[:, k, mb * 128:(mb + 1) * 128], rhs=condb[:, k, :],
                                     start=(k == 0), stop=(k == 7)), [wab, CB_], [pbb])
        for cnd in range(2):
            V(lambda e: e.tensor_tensor(out=modt[:, l, cnd, ci * 4:ci * 4 + 4], in0=pb[:, 0:8].rearrange("p (m c) -> p m c", c=2)[:, :, cnd],
                                        in1=pcol("b_mod", l * 72 + ci * 4, 4), op=ALU.add), [pbb, CONSTB], [MODL[l]])

    def mod_finish_j(l, j):
        for cnd in range(2):
            V(lambda e: e.scalar_tensor_tensor(
                out=modt[:, l, cnd, (3 * j + 1) * 8:(3 * j + 2) * 8], in0=modt[:, l, cnd, (3 * j + 1) * 8:(3 * j + 2) * 8],
                scalar=1.0, op0=ALU.add, in1=pcol("norm_w", (l * 3 + j) * 8, 8), op1=ALU.mult), [MODL[l], CONSTB], [MODL[l]])
            if j in (0, 2):
                V(lambda e: e.tensor_scalar(
                    out=modt[:, l, cnd, (3 * j + 2) * 8:(3 * j + 3) * 8], in0=modt[:, l, cnd, (3 * j + 2) * 8:(3 * j + 3) * 8],
                    scalar1=0.5, scalar2=None, op0=ALU.mult), [MODL[l]], [MODL[l]])

    mod_done = {}

    def mod_mark(l, ci):
        j = ci // 6
        mod_done[(l, j)] = mod_done.get((l, j), 0) + 1
        if mod_done[(l, j)] == 6:
            mod_finish_j(l, j)

    for ci in range(6):
        wa, wab, sk = nextA()
        mod_chunk(0, ci, wa, wab, sk)
        mod_mark(0, ci)
    K.barrier()
    pending_mod = [(0, ci) for ci in range(6, 18)] + ([(1, ci) for ci in range(18)] if nlayers > 1 else [])
    inflight_mod = []
    ffn_gi = [0]

    def modc(l, cnd, j, m):
        return modt[:, l, cnd, j * 8 + m:j * 8 + m + 1]

    def make_h(l, j, hbuf, HB, tiles=(0, 1, 2), rs_pool=None):
        for tt in tiles:
            t0, w, cnd = TT[tt]
            pb, pbb = bank()
            for c in range(8):
                sq, sqb = rs_pool["sq"].next()
                A(lambda e, sq=sq, c=c, t0=t0, w=w: e.activation(out=sq[:, 0:w], in_=x[:, c, t0:t0 + w], func=AF.Square),
                  [XB[c][tt]], [sqb])
                T(lambda e, pb=pb, sq=sq, c=c, w=w: e.matmul(pb[:, 0:w], lhsT=ones_b, rhs=sq[:, 0:w],
                                                               start=(c == 0), stop=(c == 7)), [sqb, CONSTB], [pbb])
            rs, rsb = rs_pool["rs"].next()
            A(lambda e, rs=rs, pb=pb, w=w: e.activation(out=rs[:, 0:w], in_=pb[:, 0:w], func=AF.Sqrt, bias=EPS, scale=1.0 / D),
              [pbb], [rsb])
            V(lambda e, rs=rs, w=w: e.reciprocal(rs[:, 0:w], rs[:, 0:w]), [rsb], [rsb])
            for c in range(8):
                tm, tmb = rs_pool["tm"].next()
                V(lambda e, tm=tm, rs=rs, c=c, t0=t0, w=w: e.tensor_tensor(out=tm[:, 0:w], in0=x[:, c, t0:t0 + w], in1=rs[:, 0:w], op=ALU.mult),
                  [XB[c][tt], rsb], [tmb])
                A(lambda e, tm=tm, c=c, t0=t0, w=w, cnd=cnd: e.activation(
                    out=hbuf[:, c, t0:t0 + w], in_=tm[:, 0:w], func=AF.Identity,
                    scale=modc(l, cnd, 3 * j + 1, c), bias=modc(l, cnd, 3 * j, c)), [tmb, MODL[l]], [HB[tt]])

    def resid_update(pb, pbb, m, tt, gate_ap, extra_reads=()):
        t0, w, cnd = TT[tt]
        V(lambda e: e.scalar_tensor_tensor(out=x[:, m, t0:t0 + w], in0=pb[:, 0:w], scalar=gate_ap, op0=ALU.mult,
                                           in1=x[:, m, t0:t0 + w], op1=ALU.add), [pbb, MODL[0], MODL[1]] + list(extra_reads), [XB[m][tt]])

    def ffn(l, f):
        j = 0 if f == 0 else 2
        mk = K.mark()
        hbuf = K.sb("h", [128, 8, NT], BF16)
        HB = bufs(3, "h")
        hid = [K.sb(f"hid{i}", [128, 2, NT], BF16) for i in range(2)]
        HIDB = [bufs(3, "hid0_"), bufs(3, "hid1_")]
        pool = {"sq": Rot(K, "sq", [128, 512], BF16, 3), "rs": Rot(K, "rs", [128, 512], F32, 2),
                "tm": Rot(K, "tm", [128, 512], F32, 3)}
        sgp = Rot(K, "sg", [128, 512], F32, 3)
        make_h(l, j, hbuf, HB, rs_pool=pool)
        stream_mod = (l == 0 and (len(pending_mod) > 0 or len(inflight_mod) > 0))
        if stream_mod:
            WM = [K.sb(f"WM{i}", [128, 8, 512], BF16) for i in range(4)]
            WMB = bufs(4, "WM")
            assert not inflight_mod
        for g in range(11):
            if stream_mod:
                while inflight_mod:
                    (ml, ci, slot) = inflight_mod.pop(0)
                    mod_chunk(ml, ci, WM[slot], WMB[slot], f"wm{slot}", dma=False)
                    mod_mark(ml, ci)
                ffn_gi[0] += 1
            wa, wab, ska = nextA()
            wb, wbb, skb = nextB()
            K.dma("gpsimd", lambda e, wa=wa, g=g: e.dma_start(out=wa[:].rearrange("p k n -> p (k n)"), in_=ffn_w_in[l, f, g], max_dma_last_dim=8192),
                  writes=[wab], semkey=ska)
            K.dma("gpsimd", lambda e, wb=wb, g=g: e.dma_start(out=wb[:, 0:2048], in_=ffn_w_out[l, f, g], max_dma_last_dim=8192),
                  writes=[wbb], semkey=skb)
            if stream_mod and g < 10:
                nper = 2 if ffn_gi[0] <= 10 else 1
                for i in range(nper):
                    if pending_mod:
                        slot = 2 * (g % 2) + i
                        (ml, ci) = pending_mod.pop(0)
                        mod_dma(ml, ci, WM[slot], WMB[slot], f"wm{slot}")
                        inflight_mod.append((ml, ci, slot))
            hd = hid[g % 2]
            hdb = HIDB[g % 2]
            for jj in range(2):
                for tt in range(3):
                    t0, w, cnd = TT[tt]
                    pg, pgb = bank()
                    pu, pub = bank()
                    for k in range(8):
                        T(lambda e, pg=pg, wa=wa, k=k, jj=jj, t0=t0, w=w: e.matmul(
                            pg[:, 0:w], lhsT=wa[:, k, jj * 128:(jj + 1) * 128], rhs=hbuf[:, k, t0:t0 + w], start=(k == 0), stop=(k == 7)),
                            [wab, HB[tt]], [pgb])
                    for k in range(8):
                        T(lambda e, pu=pu, wa=wa, k=k, jj=jj, t0=t0, w=w: e.matmul(
                            pu[:, 0:w], lhsT=wa[:, k, 256 + jj * 128:256 + (jj + 1) * 128], rhs=hbuf[:, k, t0:t0 + w], start=(k == 0), stop=(k == 7)),
                            [wab, HB[tt]], [pub])
                    sg, sgb = sgp.next()
                    A(lambda e, sg=sg, pg=pg, w=w: e.activation(out=sg[:, 0:w], in_=pg[:, 0:w], func=AF.Silu), [pgb], [sgb])
                    V(lambda e, sg=sg, pu=pu, hd=hd, jj=jj, t0=t0, w=w: e.tensor_tensor(
                        out=hd[:, jj, t0:t0 + w], in0=sg[:, 0:w], in1=pu[:, 0:w], op=ALU.mult), [sgb, pub], [hdb[tt]])
            wbv = wb[:, 0:2048].rearrange("p (j n) -> p j n", j=2)
            for m in range(8):
                for tt in range(3):
                    t0, w, cnd = TT[tt]
                    po, pob = bank()
                    for jj in range(2):
                        T(lambda e, po=po, wbv=wbv, jj=jj, m=m, hd=hd, t0=t0, w=w: e.matmul(
                            po[:, 0:w], lhsT=wbv[:, jj, m * 128:(m + 1) * 128], rhs=hd[:, jj, t0:t0 + w], start=(jj == 0), stop=(jj == 1)),
                            [wbb, hdb[tt]], [pob])
                    resid_update(po, pob, m, tt, modc(l, cnd, 3 * j + 2, m))
        if stream_mod:
            while inflight_mod:
                (ml, ci, slot) = inflight_mod.pop(0)
                mod_chunk(ml, ci, WM[slot], WMB[slot], f"wm{slot}", dma=False)
                mod_mark(ml, ci)
        K.barrier()
        K.release(mk)

    def mixer(l):
        mk_all = K.mark()
        wmi = mix_w_in[l]
        wmo = mix_w_out[l]
        cvx = {}

        def load_wa(col0, ncols):
            wa, wab, sk = nextA()
            K.dma("gpsimd", lambda e: e.dma_start(out=wa[:, :, 0:ncols],
                                                  in_=wmi[:, col0:col0 + ncols].rearrange("(k p) n -> p k n", p=128)),
                  writes=[wab], semkey=sk)
            return wa, wab

        def proj_fm(pb, pbb, wa, wab, c0, M, hbuf, HB, tt):
            t0, w, cnd = TT[tt]
            for k in range(8):
                T(lambda e, k=k: e.matmul(pb[0:M, 0:w], lhsT=wa[:, k, c0:c0 + M], rhs=hbuf[:, k, t0:t0 + w],
                                          start=(k == 0), stop=(k == 7)), [wab, HB[tt]], [pbb])

        yreg = []

        def out_proj(ychunks, YB, wrow0, kp):
            yreg.append((ychunks, YB, wrow0, kp))

        def out_proj_all():
            slots = []
            for (ychunks, YB, wrow0, kp) in yreg:
                nk_ = len(ychunks)
                if len(slots) % 2 == 0:
                    wt_, wtb_, sk = nextB()
                    flat = wt_[:, :]
                else:
                    wt_, wtb_, sk = nextA()
                    flat = wt_[:].rearrange("p k n -> p (k n)")
                wv = flat[0:kp, 0:nk_ * 1024].rearrange("p (j n) -> p j n", j=nk_)
                K.dma("gpsimd", lambda e, wv=wv, wrow0=wrow0, nk_=nk_, kp=kp: e.dma_start(
                    out=wv, in_=wmo[wrow0:wrow0 + nk_ * kp, :].rearrange("(j p) n -> p j n", p=kp)), writes=[wtb_], semkey=sk)
                slots.append((wv, wtb_))
            total = sum(len(y[0]) for y in yreg)
            for m in range(8):
                for tt in range(3):
                    t0, w, cnd = TT[tt]
                    po, pob = bank()
                    i = 0
                    for (ychunks, YB, wrow0, kp), (wv, wtb_) in zip(yreg, slots):
                        for jj, yc in enumerate(ychunks):
                            T(lambda e, wv=wv, jj=jj, yc=yc, i=i: e.matmul(po[:, 0:w], lhsT=wv[:, jj, m * 128:(m + 1) * 128],
                                                                        rhs=yc[:, t0:t0 + w], start=(i == 0), stop=(i == total - 1)),
                              [wtb_] + YB, [pob])
                            i += 1
                    resid_update(po, pob, m, tt, modc(l, cnd, 5, m))

        def conv_chunk(raw, rawb, P, pc0, out_ap, outb, silu):
            hl = ccol("hl")
            hr = ccol("hr")
            V(lambda e: e.tensor_tensor(out=raw[0:P, 1:5, 0:1], in0=raw[0:P, 0:4, 256:257], in1=hl[0:P, 1:5].unsqueeze(2), op=ALU.mult),
              [rawb, CONSTB], [rawb])
            V(lambda e: e.tensor_tensor(out=raw[0:P, 0:4, 257:258], in0=raw[0:P, 1:5, 1:2], in1=hr[0:P, 0:4].unsqueeze(2), op=ALU.mult),
              [rawb, CONSTB], [rawb])
            acc, accb = cvx["convacc"].next()
            accv = acc[0:P, :].rearrange("p (b t) -> p b t", t=256)
            w0 = ptab[0:P, pc0:pc0 + 1]
            w1 = ptab[0:P, pc0 + 1:pc0 + 2]
            w2 = ptab[0:P, pc0 + 2:pc0 + 3]
            bb = ptab[0:P, pc0 + 3:pc0 + 4]
            V(lambda e: e.tensor_scalar(out=accv, in0=raw[0:P, :, 1:257], scalar1=w1, scalar2=None, op0=ALU.mult), [rawb, CONSTB], [accb])
            V(lambda e: e.scalar_tensor_tensor(out=accv, in0=raw[0:P, :, 0:256], scalar=w0, op0=ALU.mult, in1=accv, op1=ALU.add),
              [rawb, CONSTB, accb], [accb])
            V(lambda e: e.scalar_tensor_tensor(out=accv, in0=raw[0:P, :, 2:258], scalar=w2, op0=ALU.mult, in1=accv, op1=ALU.add),
              [rawb, CONSTB, accb], [accb])
            A(lambda e: e.activation(out=out_ap, in_=acc[0:P, :], func=(AF.Silu if silu else AF.Identity), bias=bb), [accb, CONSTB], [outb])

        def raw_fill(raw, rawb, P, pb, pbb, tt):
            t0, w, cnd = TT[tt]
            b0 = t0 // 256
            nb = w // 256
            A(lambda e: e.activation(out=raw[0:P, b0:b0 + nb, 1:257], in_=pb[0:P, 0:w].rearrange("p (b t) -> p b t", t=256), func=AF.Copy),
              [pbb], [rawb])

        hbuf_m = K.sb("hmix", [128, 8, NT], BF16)
        HB_m = bufs(3, "hmix")
        mkp = K.mark()
        pool_m = {"sq": Rot(K, "sq", [128, 512], BF16, 3), "rs": Rot(K, "rs", [128, 512], F32, 2),
                  "tm": Rot(K, "tm", [128, 512], F32, 2)}
        make_h(l, 1, hbuf_m, HB_m, rs_pool=pool_m)
        K.barrier()
        K.release(mkp)

        def with_h(fn, conv=False):
            mk = K.mark()
            if conv:
                cvx["convacc"] = Rot(K, "cacc", [128, NT], F32, 1)
                raws = Rot(K, "raw", [128, 5, 258], F32, 2)
                cvx["raws"] = raws
                for i in range(2):
                    V(lambda e, i=i: e.memset(raws.t[i][:], 0.0), [], [raws.b[i]])
            fn(hbuf_m, HB_m)
            K.barrier()
            K.release(mk)

        def ssd(szb, SZB):
            mk = K.mark()
            xsf = K.sb("xsf", [64, 4, NT], BF16)
            XSF = Buf("xsf")
            bcf = K.sb("bcf", [128, 4, NT], BF16)
            BCF = Buf("bcf")
            dtr = K.sb("dtr", [128, 10, 8])
            dtt = K.sb("dtt", [128, 10, 8])
            lat = K.sb("lat", [128, 10, 8])
            DTB = Buf("dt")

            def inproj(hbuf, HB):
                wa0, wab0 = load_wa(0, 512)
                wa1, wab1 = load_wa(512, 512)
                for hh in range(4):
                    for tt in range(3):
                        t0, w, cnd = TT[tt]
                        pb, pbb = bank()
                        proj_fm(pb, pbb, wa0, wab0, hh * 64, 64, hbuf, HB, tt)
                        A(lambda e, pb=pb, hh=hh, t0=t0, w=w: e.activation(out=szb[:, hh, t0:t0 + w], in_=pb[0:64, 0:w], func=AF.Silu), [pbb], [SZB])
                for q in range(8):
                    raw, rawb = cvx["raws"].next()
                    P = 64 if q < 4 else 128
                    for tt in range(3):
                        pb, pbb = bank()
                        if q < 4:
                            proj_fm(pb, pbb, wa0, wab0, 256 + q * 64, 64, hbuf, HB, tt)
                        else:
                            proj_fm(pb, pbb, wa1, wab1, (q - 4) * 128, 128, hbuf, HB, tt)
                        raw_fill(raw, rawb, P, pb, pbb, tt)
                    pc0 = PT["ssd_conv"][0] + (l * 8 + q) * 4
                    if q < 4:
                        conv_chunk(raw, rawb, 64, pc0, xsf[:, q, :], XSF, True)
                    else:
                        conv_chunk(raw, rawb, 128, pc0, bcf[:, q - 4, :], BCF, True)
                wdt = K.sb("wdt", [128, 8, 8], BF16)
                WDT = Buf("wdt")
                K.dma("gpsimd", lambda e: e.dma_start(out=wdt[:], in_=wmi[:, 1024:1032].rearrange("(k p) n -> p k n", p=128)),
                      writes=[WDT], semkey="wdt")
                pb, pbb = bank()
                for tc in range(NCH):
                    tt = 0 if tc < 4 else (1 if tc < 8 else 2)
                    for k in range(8):
                        T(lambda e, tc=tc, k=k: e.matmul(pb[:, tc * 8:tc * 8 + 8], lhsT=hbuf[:, k, tc * 128:(tc + 1) * 128], rhs=wdt[:, k, :],
                                                         start=(k == 0), stop=(k == 7)), [HB[tt], WDT], [pbb])
                V(lambda e: e.tensor_tensor(out=dtr[:].rearrange("p a b -> p (a b)"), in0=pb[:, 0:80], in1=fcol("dt_bias", l * 80, 80), op=ALU.add),
                  [pbb, CONSTB], [DTB])
            with_h(inproj, conv=True)
            A(lambda e: e.activation(out=dtt[:], in_=dtr[:], func=AF.Exp), [DTB], [DTB])
            A(lambda e: e.activation(out=dtt[:], in_=dtt[:], func=AF.Ln, bias=1.0), [DTB], [DTB])
            A(lambda e: e.activation(out=dtr[:].rearrange("p a b -> p (a b)"), in_=fcol("a_log", l * 80, 80), func=AF.Exp), [DTB, CONSTB], [DTB])
            V(lambda e: e.scalar_tensor_tensor(out=lat[:], in0=dtr[:], scalar=-1.0, op0=ALU.mult, in1=dtt[:], op1=ALU.mult), [DTB], [DTB])
            xbtm = K.sb("xbtm", [128, NCH, 512], BF16)
            XBT = bufs(NCH, "xbtm")
            for tc in range(NCH):
                pb, pbb = bank()
                pbv = pb[:].bitcast(BF16)
                for hh in range(4):
                    T(lambda e, hh=hh, tc=tc: e.transpose(pbv[:, hh * 64:(hh + 1) * 64], xsf[:, hh, tc * 128:(tc + 1) * 128], ident_b[0:64, 0:64]),
                      [XSF, CONSTB], [pbb])
                for g in range(2):
                    T(lambda e, g=g, tc=tc: e.transpose(pbv[:, 256 + g * 128:256 + (g + 1) * 128], bcf[:, g, tc * 128:(tc + 1) * 128], ident_b),
                      [BCF, CONSTB], [pbb])
                A(lambda e, tc=tc, pbv=pbv: e.activation(out=xbtm[:, tc, :], in_=pbv[:, 0:512], func=AF.Copy), [pbb], [XBT[tc]])
            cst = K.sb("cst", [128, NCH, 16])
            wBt = K.sb("wBt", [128, NCH, 8])
            edt = K.sb("edt", [128, NCH, 8])
            pb, pbb = bank()
            for tc in range(NCH):
                T(lambda e, tc=tc: e.matmul(pb[:, tc * 16:tc * 16 + 4], lhsT=trif, rhs=lat[:, tc, 0:4], start=True, stop=True), [DTB, CONSTB], [pbb])
                T(lambda e, tc=tc: e.matmul(pb[:, tc * 16 + 4:tc * 16 + 8], lhsT=trib, rhs=lat[:, tc, 4:8], start=True, stop=True), [DTB, CONSTB], [pbb])
                T(lambda e, tc=tc: e.matmul(pb[:, tc * 16 + 8:tc * 16 + 16], lhsT=ones_f, rhs=lat[:, tc, 0:8], start=True, stop=True), [DTB, CONSTB], [pbb])
            V(lambda e: e.tensor_copy(cst[:].rearrange("p a b -> p (a b)"), pb[:, 0:160]), [pbb], [DTB])
            V(lambda e: e.tensor_tensor(out=wBt[:], in0=cst[:, :, 8:16], in1=cst[:, :, 0:8], op=ALU.subtract), [DTB], [DTB])
            A(lambda e: e.activation(out=wBt[:], in_=wBt[:], func=AF.Exp), [DTB], [DTB])
            V(lambda e: e.tensor_tensor(out=wBt[:], in0=wBt[:], in1=dtt[:], op=ALU.mult), [DTB], [DTB])
            A(lambda e: e.activation(out=edt[:], in_=cst[:, :, 8:16], func=AF.Exp), [DTB], [DTB])
            S32 = K.sb("S32", [128, 8, 64])
            SB_ = bufs(8, "S32")
            SP = K.sb("SP", [128, NCH, 8, 64], BF16)
            SPB = [bufs(8, f"SP{c}_") for c in range(NCH)]
            mk_st = K.mark()
            s0t = K.sb("s0t", [128, 40, 64])
            S0B = Buf("s0")
            K.dma("sync", lambda e: e.dma_start(out=s0t[:], in_=s0ssd[:, l * 40:(l + 1) * 40, :]), writes=[S0B], semkey="lds0")
            V(lambda e: e.memset(S32[:], 0.0), [], SB_)
            hl = ccol("hl")
            hr = ccol("hr")
            bsp = Rot(K, "bs", [128, 128], BF16, 12)

            def state_chain(d, hh):
                col = d * 4 + hh
                g = hh // 2
                for k in range(NCH):
                    c = k if d == 0 else NCH - 1 - k
                    blk = c // 2
                    first = (c % 2 == 0) if d == 0 else (c % 2 == 1)
                    sidx = (blk * 2 + d) * 4 + hh
                    if first:
                        fl = hl[:, blk:blk + 1] if d == 0 else hr[:, blk:blk + 1]
                        V(lambda e: e.scalar_tensor_tensor(out=S32[:, col, :], in0=S32[:, col, :], scalar=fl, op0=ALU.mult, in1=s0t[:, sidx, :], op1=ALU.add),
                          [SB_[col], S0B, CONSTB], [SB_[col]])
                    A(lambda e: e.activation(out=SP[:, c, col, :], in_=S32[:, col, :], func=AF.Copy), [SB_[col]], [SPB[c][col]])
                    bs, bsb = bsp.next()
                    V(lambda e: e.tensor_scalar(out=bs[:], in0=xbtm[:, c, 256 + g * 128:256 + (g + 1) * 128], scalar1=wBt[:, c, col:col + 1], scalar2=None, op0=ALU.mult),
                      [XBT[c], DTB], [bsb])
                    yield
                    pb, pbb = pr(64)
                    T(lambda e: e.matmul(pb[:, 0:64], lhsT=bs[:], rhs=xbtm[:, c, hh * 64:(hh + 1) * 64], start=True, stop=True), [bsb, XBT[c]], [pbb])
                    yield
                    V(lambda e: e.scalar_tensor_tensor(out=S32[:, col, :], in0=S32[:, col, :], scalar=edt[:, c, col:col + 1], op0=ALU.mult, in1=pb[:, 0:64], op1=ALU.add),
                      [SB_[col], DTB, pbb], [SB_[col]])
                    if not first:
                        oidx = l * 40 + sidx
                        sslot = sstp.i
                        stg, stgb = sstp.next()
                        A(lambda e: e.activation(out=stg[:], in_=S32[:, col, :], func=AF.Copy), [SB_[col]], [stgb])
                        K.dma("sync", lambda e: e.dma_start(out=nssd_o[:, oidx, :], in_=stg[:]), reads=[stgb], semkey=f"stS{sslot}")
                    yield
            sstp = Rot(K, "sstg", [128, 64], F32, 8)
            interleave([state_chain(d, hh) for d in range(2) for hh in range(4)])
            K.barrier()
            K.release(mk_st)

            dsk = pcol("ssd_d", l * 4, 4, rows=64)
            nws = pcol("ssd_nw", l * 4, 4, rows=64)
            ygp = Rot(K, "yg", [64, 4, 128], F32, 2)
            wtp = Rot(K, "wt", [128, 128], F32, 16)
            sgp2 = Rot(K, "sg2", [128, 128], F32, 16)
            csp = Rot(K, "cs", [128, 128], BF16, 16)
            sqp = Rot(K, "sq4", [64, 128], BF16, 8)

            def ssd_chain(c, hh, d, psc, pscb, res):
                cs = slice(c * 128, (c + 1) * 128)
                col = d * 4 + hh
                g = hh // 2
                U = trif if d == 0 else trib
                NB = ccol("nbf") if d == 0 else ccol("nbb")
                wt, wtb = wtp.next()
                G(lambda e: e.tensor_scalar(out=wt[:], in0=U, scalar1=lat[:, c, col:col + 1], scalar2=0.0, op0=ALU.mult, op1=ALU.add), [CONSTB, DTB], [wtb])
                yield
                pc, pcb = pr(128)
                T(lambda e: e.matmul(pc[:, 0:128], lhsT=ones_f, rhs=wt[:], start=True, stop=True), [wtb, CONSTB], [pcb])
                yield
                sg, sgb = sgp2.next()
                V(lambda e: e.scalar_tensor_tensor(out=sg[:], in0=pc[:, 0:128], scalar=cst[:, c, col:col + 1], op0=ALU.subtract, in1=NB, op1=ALU.add),
                  [pcb, DTB, CONSTB], [sgb])
                yield
                ec, ecb = wt, wtb
                A(lambda e: e.activation(out=ec[:], in_=pc[:, 0:128], func=AF.Exp), [pcb], [ecb])
                yield
                A(lambda e: e.activation(out=sg[:], in_=sg[:], func=AF.Exp), [sgb], [sgb])
                csb, csbb = csp.next()
                G(lambda e: e.tensor_tensor(out=csb[:], in0=bcf[:, 2 + g, cs], in1=ec[:], op=ALU.mult), [BCF, ecb], [csbb])
                yield
                yield
                st, stb = wt[:].bitcast(BF16)[:, 0:128], wtb
                V(lambda e: e.scalar_tensor_tensor(out=st, in0=psc[:, g * 128:(g + 1) * 128], scalar=dtt[:, c, col:col + 1], op0=ALU.mult, in1=sg[:], op1=ALU.mult),
                  [pscb, sgb, DTB], [stb])
                res[(hh, d)] = (st, stb, csb, csbb)
                yield

            def ssd_chunk(c):
                cs = slice(c * 128, (c + 1) * 128)
                psc, pscb = pr(256)
                for g in range(2):
                    T(lambda e: e.matmul(psc[:, g * 128:(g + 1) * 128], lhsT=bcf[:, g, cs], rhs=bcf[:, 2 + g, cs], start=True, stop=True), [BCF], [pscb])
                res = {}
                chains = [ssd_chain(c, hh, d, psc, pscb, res) for hh in range(4) for d in range(2)]
                while chains:
                    nxt = []
                    for gch in chains:
                        try:
                            next(gch)
                            nxt.append(gch)
                        except StopIteration:
                            pass
                    chains = nxt
                    yield
                yg, ygb = ygp.next()
                pys = []
                for hh in range(4):
                    py, pyb = pr(128)
                    pys.append((py, pyb))
                    for d in range(2):
                        col = d * 4 + hh
                        st, stb, csb, csbb = res[(hh, d)]
                        T(lambda e: e.matmul(py[0:64, 0:128], lhsT=xbtm[:, c, hh * 64:(hh + 1) * 64], rhs=st, start=(d == 0), stop=False), [XBT[c], stb], [pyb])
                        T(lambda e: e.matmul(py[0:64, 0:128], lhsT=SP[:, c, col, :], rhs=csb[:], start=False, stop=(d == 1)), [SPB[c][col], csbb], [pyb])
                yield
                sqs = []
                for hh in range(4):
                    py, pyb = pys[hh]
                    V(lambda e: e.scalar_tensor_tensor(out=yg[:, hh, :], in0=xsf[:, hh, cs], scalar=dsk[:, hh:hh + 1], op0=ALU.mult, in1=py[0:64, 0:128], op1=ALU.add),
                      [XSF, CONSTB, pyb], [ygb])
                    V(lambda e: e.tensor_tensor(out=yg[:, hh, :], in0=yg[:, hh, :], in1=szb[:, hh, cs], op=ALU.mult), [ygb, SZB], [ygb])
                    sq, sqb = sqp.next()
                    A(lambda e: e.activation(out=sq[:], in_=yg[:, hh, :], func=AF.Square), [ygb], [sqb])
                    sqs.append((sq, sqb))
                    yield
                pq, pqb = pr(128)
                for hh in range(4):
                    sq, sqb = sqs[hh]
                    T(lambda e: e.matmul(pq[0:64, 0:128], lhsT=ones_b[0:64, 0:64], rhs=sq[:], start=(hh == 0), stop=(hh == 3)), [sqb, CONSTB], [pqb])
                yield
                rs, rsb = rsp2.next()
                A(lambda e: e.activation(out=rs[:], in_=pq[0:64, 0:128], func=AF.Sqrt, bias=EPS, scale=1.0 / 256), [pqb], [rsb])
                yield
                V(lambda e: e.reciprocal(rs[:], rs[:]), [rsb], [rsb])
                yield
                for hh in range(4):
                    V(lambda e: e.scalar_tensor_tensor(out=szb[:, hh, cs], in0=yg[:, hh, :], scalar=nws[:, hh:hh + 1], op0=ALU.mult, in1=rs[:], op1=ALU.mult),
                      [ygb, rsb, CONSTB], [SZB])
                    yield
            rsp2 = Rot(K, "rs2", [64, 128], F32, 2)
            for c0 in range(0, NCH, 2):
                interleave([ssd_chunk(c0), ssd_chunk(c0 + 1)])
            out_proj([szb[:, hh, :] for hh in range(4)], [SZB], 0, 64)
            K.barrier()
            K.release(mk)

        def ret(sgf, SGF):
            mk = K.mark()
            qf = K.sb("qf", [64, 4, NT], BF16)
            kf = K.sb("kf", [64, 4, NT], BF16)
            kvt = K.sb("kvt", [128, NCH, 256], BF16)
            QF, KF = Buf("qf"), Buf("kf")
            KVT = bufs(NCH, "kvt")
            KST = Buf("kst")
            lg = K.sb("lg", [128, 8])
            LG = Buf("lg")
            A(lambda e: e.activation(out=lg[:], in_=fcol("ret_logit", l * 8, 8), func=AF.Exp, scale=-1.0), [CONSTB], [LG])
            A(lambda e: e.activation(out=lg[:], in_=lg[:], func=AF.Ln, bias=1.0), [LG], [LG])
            V(lambda e: e.tensor_scalar(out=lg[:], in0=lg[:], scalar1=-1.0, scalar2=None, op0=ALU.mult), [LG], [LG])
            Tl = K.sb("Tl", [128, 8])
            G128 = K.sb("G128", [128, 8])
            RC = Buf("retc")
            for hh in range(4):
                A(lambda e, hh=hh: e.activation(out=Tl[:, hh:hh + 1], in_=ccol("tailf"), func=AF.Exp, scale=lg[:, hh:hh + 1]), [LG, CONSTB], [RC])
                A(lambda e, hh=hh: e.activation(out=Tl[:, 4 + hh:5 + hh], in_=ccol("tailb"), func=AF.Exp, scale=lg[:, 4 + hh:5 + hh]), [LG, CONSTB], [RC])
            V(lambda e: e.tensor_scalar(out=Tl[:], in0=Tl[:], scalar1=0.125, scalar2=None, op0=ALU.mult), [RC], [RC])
            A(lambda e: e.activation(out=G128[:], in_=lg[:], func=AF.Exp, scale=128.0), [LG], [RC])
            S32 = K.sb("R32", [64, 8, 64])
            SB_ = bufs(8, "R32")
            SP = K.sb("RSP", [64, NCH, 8, 64], BF16)
            SPB = [bufs(8, f"RSP{c}_") for c in range(NCH)]
            mk_k = K.mark()
            kst = K.sb("kst", [128, 2, NCH, 256], BF16)

            def inproj(hbuf, HB):
                wa0, wab0 = load_wa(1800, 512)
                wa1, wab1 = load_wa(2312, 512)
                for hh in range(4):
                    for tt in range(3):
                        t0, w, cnd = TT[tt]
                        pb, pbb = bank()
                        proj_fm(pb, pbb, wa0, wab0, hh * 64, 64, hbuf, HB, tt)
                        A(lambda e, pb=pb, hh=hh, t0=t0, w=w: e.activation(out=qf[:, hh, t0:t0 + w], in_=pb[0:64, 0:w], func=AF.Copy), [pbb], [QF])
                        pb, pbb = bank()
                        proj_fm(pb, pbb, wa0, wab0, 256 + hh * 64, 64, hbuf, HB, tt)
                        A(lambda e, pb=pb, hh=hh, t0=t0, w=w: e.activation(out=kf[:, hh, t0:t0 + w], in_=pb[0:64, 0:w], func=AF.Copy, scale=0.125), [pbb], [KF])
                        pb, pbb = bank()
                        proj_fm(pb, pbb, wa1, wab1, 256 + hh * 64, 64, hbuf, HB, tt)
                        A(lambda e, pb=pb, hh=hh, t0=t0, w=w: e.activation(out=sgf[:, hh, t0:t0 + w], in_=pb[0:64, 0:w], func=AF.Silu), [pbb], [SGF])
                for tc in range(NCH):
                    tt = 0 if tc < 4 else (1 if tc < 8 else 2)
                    pb, pbb = bank()
                    for k in range(8):
                        T(lambda e, pb=pb, tc=tc, k=k: e.matmul(pb[:, 0:256], lhsT=hbuf[:, k, tc * 128:(tc + 1) * 128], rhs=wa0[:, k, 256:512],
                                                                 start=(k == 0), stop=(k == 7)), [HB[tt], wab0], [pbb])
                    for k in range(8):
                        T(lambda e, pb=pb, tc=tc, k=k: e.matmul(pb[:, 256:512], lhsT=hbuf[:, k, tc * 128:(tc + 1) * 128], rhs=wa1[:, k, 0:256],
                                                                 start=(k == 0), stop=(k == 7)), [HB[tt], wab1], [pbb])
                    for d in range(2):
                        V(lambda e, pb=pb, tc=tc, d=d: e.tensor_tensor(out=kst[:, d, tc, :].rearrange("p (h n) -> p h n", h=4),
                                                                       in0=pb[:, 0:256].rearrange("p (h n) -> p h n", h=4),
                                                                       in1=Tl[:, d * 4:(d + 1) * 4].unsqueeze(2).broadcast_to([128, 4, 64]), op=ALU.mult),
                          [pbb, RC], [KST])
                    A(lambda e, pb=pb, tc=tc: e.activation(out=kvt[:, tc, :], in_=pb[:, 256:512], func=AF.Copy), [pbb], [KVT[tc]])
            with_h(inproj)
            mk_st = K.mark()
            s0t = K.sb("rs0t", [64, 40, 64])
            S0B = Buf("rs0")
            K.dma("sync", lambda e: e.dma_start(out=s0t[:], in_=s0ret[:, l * 40:(l + 1) * 40, :]), writes=[S0B], semkey="lds0")
            V(lambda e: e.memset(S32[:], 0.0), [], SB_)
            hl = ccol("hl")
            hr = ccol("hr")
            def rstate_chain(d, hh):
                col = d * 4 + hh
                for k in range(NCH):
                    c = k if d == 0 else NCH - 1 - k
                    blk = c // 2
                    first = (c % 2 == 0) if d == 0 else (c % 2 == 1)
                    sidx = (blk * 2 + d) * 4 + hh
                    if first:
                        fl = hl[0:64, blk:blk + 1] if d == 0 else hr[0:64, blk:blk + 1]
                        V(lambda e: e.scalar_tensor_tensor(out=S32[:, col, :], in0=S32[:, col, :], scalar=fl, op0=ALU.mult, in1=s0t[:, sidx, :], op1=ALU.add),
                          [SB_[col], S0B, CONSTB], [SB_[col]])
                    A(lambda e: e.activation(out=SP[:, c, col, :], in_=S32[:, col, :], func=AF.Copy), [SB_[col]], [SPB[c][col]])
                    yield
                    pb, pbb = pr(64)
                    T(lambda e: e.matmul(pb[0:64, 0:64], lhsT=kst[:, d, c, hh * 64:(hh + 1) * 64], rhs=kvt[:, c, hh * 64:(hh + 1) * 64],
                                         start=True, stop=True), [KST, KVT[c]], [pbb])
                    yield
                    V(lambda e: e.scalar_tensor_tensor(out=S32[:, col, :], in0=S32[:, col, :], scalar=G128[0:64, col:col + 1], op0=ALU.mult, in1=pb[0:64, 0:64], op1=ALU.add),
                      [SB_[col], RC, pbb], [SB_[col]])
                    if not first:
                        oidx = l * 40 + sidx
                        sslot = rstp.i
                        stg, stgb = rstp.next()
                        A(lambda e: e.activation(out=stg[:], in_=S32[:, col, :], func=AF.Copy), [SB_[col]], [stgb])
                        K.dma("sync", lambda e: e.dma_start(out=nret_o[:, oidx, :], in_=stg[:]), reads=[stgb], semkey=f"stR{sslot}")
                    yield
            rstp = Rot(K, "rstg", [64, 64], F32, 8)
            interleave([rstate_chain(d, hh) for d in range(2) for hh in range(4)])
            K.barrier()
            K.release(mk_k)

            gnw = pcol("ret_gn", l * 4, 4, rows=64)
            Dm = K.sb("Dm", [128, 4, 128])
            Ef = K.sb("Ef", [64, 8, 128])
            tmpf = Rot(K, "tmpf", [128, 128], F32, 2)
            for hh in range(4):
                t1, t1b = tmpf.next()
                A(lambda e, t1=t1, hh=hh: e.activation(out=t1[:], in_=ccol("relu_f"), func=AF.Exp, scale=lg[:, hh:hh + 1]), [LG, CONSTB], [t1b])
                V(lambda e, t1=t1, hh=hh: e.tensor_tensor(out=Dm[:, hh, :], in0=t1[:], in1=trif, op=ALU.mult), [t1b, CONSTB], [RC])
                t2, t2b = tmpf.next()
                A(lambda e, t2=t2, hh=hh: e.activation(out=t2[:], in_=ccol("relu_b"), func=AF.Exp, scale=lg[:, 4 + hh:5 + hh]), [LG, CONSTB], [t2b])
                V(lambda e, t2=t2: e.tensor_tensor(out=t2[:], in0=t2[:], in1=trib, op=ALU.mult), [t2b, CONSTB], [t2b])
                V(lambda e, t2=t2, hh=hh: e.tensor_tensor(out=Dm[:, hh, :], in0=Dm[:, hh, :], in1=t2[:], op=ALU.add), [t2b, RC], [RC])
                A(lambda e, hh=hh: e.activation(out=Ef[:, hh, :], in_=ccol("ip1", rows=64), func=AF.Exp, scale=lg[0:64, hh:hh + 1]), [LG, CONSTB], [RC])
                A(lambda e, hh=hh: e.activation(out=Ef[:, 4 + hh, :], in_=ccol("rmi", rows=64), func=AF.Exp, scale=lg[0:64, 4 + hh:5 + hh]), [LG, CONSTB], [RC])
            qsp = Rot(K, "qs4", [64, 2, 4, 128], BF16, 2)
            st4p = Rot(K, "st4", [128, 4, 128], BF16, 2)
            yvp = Rot(K, "yv4", [64, 4, 128], F32, 2)
            sq4p = Rot(K, "sq4", [64, 4, 128], F32, 2)

            def ret_chunk(c):
                cs = slice(c * 128, (c + 1) * 128)
                ps_, psb = bank()
                for hh in range(4):
                    T(lambda e: e.matmul(ps_[:, hh * 128:(hh + 1) * 128], lhsT=kf[:, hh, cs], rhs=qf[:, hh, cs], start=True, stop=True), [KF, QF], [psb])
                yield
                st, stb = st4p.next()
                V(lambda e: e.tensor_tensor(out=st[:], in0=ps_[:, 0:512].rearrange("p (h t) -> p h t", h=4), in1=Dm[:], op=ALU.mult), [psb, RC], [stb])
                qs, qsb = qsp.next()
                V(lambda e: e.tensor_tensor(out=qs[:], in0=qf[:, :, cs].unsqueeze(1).broadcast_to([64, 2, 4, 128]),
                                            in1=Ef[:].rearrange("p (d h) t -> p d h t", d=2), op=ALU.mult), [QF, RC], [qsb])
                yield
                py, pyb = bank()
                for hh in range(4):
                    T(lambda e: e.matmul(py[0:64, hh * 128:(hh + 1) * 128], lhsT=kvt[:, c, hh * 64:(hh + 1) * 64], rhs=st[:, hh, :],
                                         start=True, stop=False), [KVT[c], stb], [pyb])
                    T(lambda e: e.matmul(py[0:64, hh * 128:(hh + 1) * 128], lhsT=SP[:, c, hh, :], rhs=qs[:, 0, hh, :], start=False, stop=False),
                      [SPB[c][hh], qsb], [pyb])
                    T(lambda e: e.matmul(py[0:64, hh * 128:(hh + 1) * 128], lhsT=SP[:, c, 4 + hh, :], rhs=qs[:, 1, hh, :], start=False, stop=True),
                      [SPB[c][4 + hh], qsb], [pyb])
                yield
                yv, yvb = yvp.next()
                yvf = yv[:].rearrange("p h t -> p (h t)")
                V(lambda e: e.tensor_copy(yvf, py[0:64, 0:512]), [pyb], [yvb])
                yield
                pm, pmb = ps_, psb
                T(lambda e: e.matmul(pm[0:64, 0:512], lhsT=ones_f[0:64, 0:64], rhs=yvf, start=True, stop=True), [yvb, CONSTB], [pmb])
                yield
                V(lambda e: e.scalar_tensor_tensor(out=yvf, in0=pm[0:64, 0:512], scalar=-1.0 / 64, op0=ALU.mult, in1=yvf, op1=ALU.add), [pmb, yvb], [yvb])
                yield
                sq, sqb = sq4p.next()
                sqf = sq[:].rearrange("p h t -> p (h t)")
                A(lambda e: e.activation(out=sqf, in_=yvf, func=AF.Square), [yvb], [sqb])
                yield
                pv_, pvb = py, pyb
                T(lambda e: e.matmul(pv_[0:64, 0:512], lhsT=ones_f[0:64, 0:64], rhs=sqf, start=True, stop=True), [sqb, CONSTB], [pvb])
                yield
                A(lambda e: e.activation(out=sqf, in_=pv_[0:64, 0:512], func=AF.Sqrt, bias=EPS, scale=1.0 / 64), [pvb], [sqb])
                yield
                V(lambda e: e.reciprocal(sqf, sqf), [sqb], [sqb])
                yield
                V(lambda e: e.tensor_tensor(out=yvf, in0=yvf, in1=sqf, op=ALU.mult), [yvb, sqb], [yvb])
                V(lambda e: e.tensor_tensor(out=yv[:], in0=yv[:], in1=gnw.unsqueeze(2).broadcast_to([64, 4, 128]), op=ALU.mult), [yvb, CONSTB], [yvb])
                yield
                V(lambda e: e.tensor_tensor(out=sgf[:, :, cs], in0=yv[:], in1=sgf[:, :, cs], op=ALU.mult), [yvb, SGF], [SGF])
                yield
            pipeline([ret_chunk(c) for c in range(NCH)], 2, 6)
            out_proj([sgf[:, hh, :] for hh in range(4)], [SGF], 512, 64)
            K.barrier()
            K.release(mk)

        def att(yat, YA):
            mk = K.mark()
            qfm = K.sb("aq", [64, 4, NT], BF16)
            kfm = K.sb("ak", [64, 2, NT], BF16)
            vtm = K.sb("av", [128, NCH, 128], BF16)
            QF, KF = Buf("aq"), Buf("ak")
            VT = bufs(NCH, "av")
            ckf = K.sb("ckf", [64, 2, 512], BF16)
            cvs = K.sb("cvs", [128, 4, 128], BF16)
            CK = Buf("ck")
            K.dma("gpsimd", lambda e: e.dma_start(out=ckf[:], in_=ckT[:, l * 2:l * 2 + 2, :]), writes=[CK], semkey="ldck")
            K.dma("gpsimd", lambda e: e.dma_start(out=cvs[:], in_=cvt[:, l * 4:l * 4 + 4, :]), writes=[CK], semkey="ldck")
            es = K.sb("es", [64, 4])
            A(lambda e: e.activation(out=es[:], in_=fcol("sink", l * 4, 4, rows=64), func=AF.Exp), [CONSTB], [CK])
            mk_in = K.mark()
            qkp = Rot(K, "qk", [128, 512], F32, 3)
            qnp = Rot(K, "qn", [128, 384], F32, 3)
            qbp = Rot(K, "qb", [128, 384], BF16, 3)
            smp = Rot(K, "sm", [128, 8], F32, 3)
            rtp = Rot(K, "rt", [128, 6, 2, 16], F32, 8)
            kvo = Rot(K, "kvo", [128, 256], F32, 3)

            def inproj(hbuf, HB):
                wa, wab = load_wa(2824, 512)

                def in_chunk(tc):
                    tt = 0 if tc < 4 else (1 if tc < 8 else 2)
                    pb, pbb = bank()
                    for k in range(8):
                        T(lambda e: e.matmul(pb[:, 0:512], lhsT=hbuf[:, k, tc * 128:(tc + 1) * 128], rhs=wa[:, k, :], start=(k == 0), stop=(k == 7)),
                          [HB[tt], wab], [pbb])
                    yield
                    qk, qkb = qkp.next()
                    A(lambda e: e.activation(out=qk[:], in_=pb[:, 0:512], func=AF.Copy), [pbb], [qkb])
                    yield
                    qn, qnb = qnp.next()
                    sm, smb = smp.next()
                    A(lambda e: e.activation(out=qn[:], in_=qk[:, 0:384], func=AF.Square), [qkb], [qnb])
                    A(lambda e: e.activation(out=vtm[:, tc, :], in_=qk[:, 384:512], func=AF.Copy), [qkb], [VT[tc]])
                    yield
                    V(lambda e: e.tensor_reduce(out=sm[:, 0:6], in_=qn[:].rearrange("p (h d) -> p h d", d=64), axis=AX.X, op=ALU.add), [qnb], [smb])
                    yield
                    A(lambda e: e.activation(out=sm[:, 0:6], in_=sm[:, 0:6], func=AF.Sqrt, bias=EPS, scale=1.0 / 64), [smb], [smb])
                    yield
                    V(lambda e: e.reciprocal(sm[:, 0:6], sm[:, 0:6]), [smb], [smb])
                    yield
                    V(lambda e: e.tensor_tensor(out=qn[:].rearrange("p (h d) -> p h d", d=64), in0=qk[:, 0:384].rearrange("p (h d) -> p h d", d=64),
                                                in1=sm[:, 0:6].unsqueeze(2).broadcast_to([128, 6, 64]), op=ALU.mult), [qkb, smb], [qnb])
                    yield
                    V(lambda e: e.tensor_tensor(out=qn[:], in0=qn[:], in1=fcol("qkw", l * 384, 384), op=ALU.mult), [qnb, CONSTB], [qnb])
                    yield
                    qv = qn[:].rearrange("p (h a b f) -> p h a b f", h=6, a=2, b=2)
                    cosv = ccol("cos", tc * 32, 32).rearrange("p (a f) -> p a f", a=2).unsqueeze(1).broadcast_to([128, 6, 2, 16])
                    sinv = ccol("sin", tc * 32, 32).rearrange("p (a f) -> p a f", a=2).unsqueeze(1).broadcast_to([128, 6, 2, 16])
                    x1 = qv[:, :, :, 0, :]
                    x2 = qv[:, :, :, 1, :]
                    t1, t1b = rtp.next()
                    t2, t2b = rtp.next()
                    t3, t3b = rtp.next()
                    t4, t4b = rtp.next()
                    V(lambda e: e.tensor_tensor(out=t1[:], in0=x1, in1=cosv, op=ALU.mult), [qnb, CONSTB], [t1b])
                    V(lambda e: e.tensor_tensor(out=t3[:], in0=x1, in1=sinv, op=ALU.mult), [qnb, CONSTB], [t3b])
                    yield
                    V(lambda e: e.tensor_tensor(out=t2[:], in0=x2, in1=sinv, op=ALU.mult), [qnb, CONSTB], [t2b])
                    V(lambda e: e.tensor_tensor(out=t4[:], in0=x2, in1=cosv, op=ALU.mult), [qnb, CONSTB], [t4b])
                    yield
                    V(lambda e: e.tensor_tensor(out=x1, in0=t1[:], in1=t2[:], op=ALU.subtract), [t1b, t2b], [qnb])
                    yield
                    V(lambda e: e.tensor_tensor(out=x2, in0=t3[:], in1=t4[:], op=ALU.add), [t3b, t4b], [qnb])
                    yield
                    kslot = kvo.i
                    ko, kob = kvo.next()
                    V(lambda e: e.tensor_copy(ko[:, 0:128], qn[:, 256:384]), [qnb], [kob])
                    V(lambda e: e.tensor_copy(ko[:, 128:256], qk[:, 384:512]), [qkb], [kob])
                    qb, qbb = qbp.next()
                    A(lambda e: e.activation(out=qb[:], in_=qn[:], func=AF.Copy), [qnb], [qbb])
                    yield
                    K.dma("sync", lambda e: e.dma_start(out=nk_o[l, tc * 128:(tc + 1) * 128, :], in_=ko[:, 0:128]), reads=[kob], semkey=f"stk{kslot}")
                    K.dma("sync", lambda e: e.dma_start(out=nv_o[l, tc * 128:(tc + 1) * 128, :], in_=ko[:, 128:256]), reads=[kob], semkey=f"stk{kslot}")
                    pt, ptb = bank()
                    ptv = pt[:].bitcast(BF16)
                    for hh in range(6):
                        T(lambda e: e.transpose(ptv[0:64, hh * 128:(hh + 1) * 128], qb[:, hh * 64:(hh + 1) * 64], ident_b), [qbb, CONSTB], [ptb])
                    yield
                    V(lambda e: e.tensor_copy(qfm[:, :, tc * 128:(tc + 1) * 128], ptv[0:64, 0:512].rearrange("p (h t) -> p h t", h=4)), [ptb], [QF])
                    V(lambda e: e.tensor_copy(kfm[:, :, tc * 128:(tc + 1) * 128], ptv[0:64, 512:768].rearrange("p (h t) -> p h t", h=2)), [ptb], [KF])
                    yield
                for c0 in range(0, NCH, 2):
                    interleave([in_chunk(c0), in_chunk(c0 + 1)])
            with_h(inproj)
            K.release(mk_in)
            ptp = Rot(K, "pt", [128, 7, 2, 128], BF16, 3)
            rcp = Rot(K, "rc", [64, 2, 128], F32, 3)
            am = bcol("am").rearrange("p (c a t) -> p c a t", c=NCH, a=2)
            cfb = ccol("cfb")

            def att_chain(qc, kv):
                qs = slice(qc * 128, (qc + 1) * 128)
                pc_ = max(qc - 1, 0)
                nc_ = min(qc + 1, NCH - 1)
                qrhs = qfm[:, 2 * kv:2 * kv + 2, qs]
                pa, pab = bank()
                pb_, pbb_ = bank()
                for sc in range(4):
                    dst, dstb = (pa, pab) if sc < 2 else (pb_, pbb_)
                    T(lambda e: e.matmul(dst[:, (sc % 2) * 256:(sc % 2) * 256 + 256], lhsT=ckf[:, kv, sc * 128:(sc + 1) * 128], rhs=qrhs, start=True, stop=True),
                      [CK, QF], [dstb])
                yield
                pt_, ptb_ = ptp.next()
                A(lambda e: e.activation(out=pt_[:, 0:2, :, :], in_=pa[:, 0:512].rearrange("p (c h t) -> p c h t", c=2, h=2), func=AF.Exp,
                                         scale=0.125, bias=cfb[:, qc:qc + 1]), [pab, CONSTB], [ptb_])
                A(lambda e: e.activation(out=pt_[:, 2:4, :, :], in_=pb_[:, 0:512].rearrange("p (c h t) -> p c h t", c=2, h=2), func=AF.Exp,
                                         scale=0.125, bias=cfb[:, qc:qc + 1]), [pbb_, CONSTB], [ptb_])
                pc2, pc2b = bank()
                pd2, pd2b = bank()
                for i, kc in enumerate((pc_, nc_, qc)):
                    dst, dstb = (pc2, pc2b) if i < 2 else (pd2, pd2b)
                    T(lambda e: e.matmul(dst[:, (i % 2) * 256:(i % 2) * 256 + 256], lhsT=kfm[:, kv, kc * 128:(kc + 1) * 128], rhs=qrhs, start=True, stop=True),
                      [KF, QF], [dstb])
                yield
                A(lambda e: e.activation(out=pt_[:, 4:6, :, :], in_=pc2[:, 0:512].rearrange("p (c h t) -> p c h t", c=2, h=2), func=AF.Exp, scale=0.125),
                  [pc2b], [ptb_])
                A(lambda e: e.activation(out=pt_[:, 6, :, :], in_=pd2[:, 0:256].rearrange("p (h t) -> p h t", h=2), func=AF.Exp, scale=0.125),
                  [pd2b], [ptb_])
                yield
                V(lambda e: e.tensor_tensor(out=pt_[:, 4:6, :, :], in0=pt_[:, 4:6, :, :], in1=am[:, qc, :, :].unsqueeze(2).broadcast_to([128, 2, 2, 128]), op=ALU.mult),
                  [ptb_, CONSTB], [ptb_])
                yield
                po, pob = bank()
                vlist = [(cvs[:, sc, kv * 64:(kv + 1) * 64], CK) for sc in range(4)] + \
                        [(vtm[:, kc, kv * 64:(kv + 1) * 64], VT[kc]) for kc in (pc_, nc_, qc)]
                for i, (vap, vb) in enumerate(vlist):
                    T(lambda e: e.matmul(po[0:64, 0:256], lhsT=vap, rhs=pt_[:, i, :, :], start=(i == 0), stop=(i == 6)), [vb, ptb_], [pob])
                for i in range(7):
                    T(lambda e: e.matmul(po[0:64, 256:512], lhsT=ones_b[:, 0:64], rhs=pt_[:, i, :, :], start=(i == 0), stop=(i == 6)), [CONSTB, ptb_], [pob])
                yield
                rc, rcb = rcp.next()
                V(lambda e: e.tensor_tensor(out=rc[:], in0=po[0:64, 256:512].rearrange("p (h t) -> p h t", h=2),
                                            in1=es[:, 2 * kv:2 * kv + 2].unsqueeze(2).broadcast_to([64, 2, 128]), op=ALU.add), [pob, CK], [rcb])
                V(lambda e: e.reciprocal(rc[:], rc[:]), [rcb], [rcb])
                V(lambda e: e.tensor_tensor(out=yat[:, 2 * kv:2 * kv + 2, qs], in0=po[0:64, 0:256].rearrange("p (h t) -> p h t", h=2), in1=rc[:], op=ALU.mult),
                  [pob, rcb], [YA])
                yield
            for qc in range(NCH):
                interleave([att_chain(qc, 0), att_chain(qc, 1)])
            out_proj([yat[:, hh, :] for hh in range(4)], [YA], 768, 64)
            K.barrier()
            K.release(mk)

        def hyena(yhy, YHB):
            mk = K.mark()
            vb_ = K.sb("hv", [128, 2, NT], BF16)
            x1f = K.sb("hx1", [128, 2, NT], BF16)
            x2f = K.sb("hx2", [128, 2, NT], BF16)
            HVB, HX1, HX2 = Buf("hv"), Buf("hx1"), Buf("hx2")

            def inproj(hbuf, HB):
                wa0, wab0 = load_wa(1032, 512)
                wa1, wab1 = load_wa(1544, 256)
                dests = [(vb_, HVB), (vb_, HVB), (x1f, HX1), (x1f, HX1), (x2f, HX2), (x2f, HX2)]
                for q in range(6):
                    raw, rawb = cvx["raws"].next()
                    for tt in range(3):
                        pb, pbb = bank()
                        if q < 4:
                            proj_fm(pb, pbb, wa0, wab0, q * 128, 128, hbuf, HB, tt)
                        else:
                            proj_fm(pb, pbb, wa1, wab1, (q - 4) * 128, 128, hbuf, HB, tt)
                        raw_fill(raw, rawb, 128, pb, pbb, tt)
                    pc0 = PT["hy_conv"][0] + (l * 6 + q) * 4
                    dt_, db_ = dests[q]
                    conv_chunk(raw, rawb, 128, pc0, dt_[:, q % 2, :], db_, False)
            with_h(inproj, conv=True)
            FEB = Buf("feats")
            w3s = K.sb("hw3", [64, 1024])
            K.dma("sync", lambda e: e.dma_start(out=w3s[:], in_=hy_w3[l]), writes=[FEB], semkey="ldf")
            h2 = K.sb("hh2", [64, NT])
            H2B = Buf("h2")
            mk2 = K.mark()
            feats = K.sb("feats", [33, NT])
            K.dma("sync", lambda e: e.dma_start(out=feats[:, 0:1024], in_=featsA_d), writes=[FEB], semkey="ldf")
            K.dma("sync", lambda e: e.dma_start(out=feats[:, 1024:1280], in_=featsB_d), writes=[FEB], semkey="ldf")
            w1s = K.sb("hw1", [33, 64])
            w2s = K.sb("hw2", [64, 64])
            K.dma("sync", lambda e: e.dma_start(out=w1s[:], in_=hy_w1[l]), writes=[FEB], semkey="ldf")
            K.dma("sync", lambda e: e.dma_start(out=w2s[:], in_=hy_w2[l]), writes=[FEB], semkey="ldf")
            hp = pcol("hyp", l * 3, 3, rows=64)
            fb = K.sb("fb", [64, 2])
            V(lambda e: e.tensor_tensor(out=fb[:, 0:1], in0=hp[:, 0:1], in1=hp[:, 1:2], op=ALU.mult), [CONSTB], [FEB])
            V(lambda e: e.tensor_tensor(out=fb[:, 1:2], in0=hp[:, 2:3], in1=hp[:, 1:2], op=ALU.mult), [CONSTB], [FEB])
            h1 = K.sb("hh1", [64, NT])
            H1B = Buf("h1")
            MAGIC = 12582912.0
            argp = Rot(K, "arg", [64, 512], F32, 2)
            nrp = Rot(K, "nr", [64, 512], F32, 2)

            def sin_layer(lhsT, src, srcb, KK, dst, dstb, fbcol):
                for tt in range(3):
                    t0, w, cnd = TT[tt]
                    pb, pbb = bank()
                    T(lambda e, pb=pb, t0=t0, w=w: e.matmul(pb[0:64, 0:w], lhsT=lhsT, rhs=src[0:KK, t0:t0 + w], start=True, stop=True), [FEB, srcb], [pbb])
                    ar, arb = argp.next()
                    nr, nrb = nrp.next()
                    V(lambda e, ar=ar, pb=pb, w=w: e.tensor_scalar(out=ar[:, 0:w], in0=pb[0:64, 0:w], scalar1=hp[:, 1:2], scalar2=fb[:, fbcol:fbcol + 1],
                                                                   op0=ALU.mult, op1=ALU.add), [pbb, CONSTB, FEB], [arb])
                    V(lambda e, ar=ar, nr=nr, w=w: e.tensor_scalar(out=nr[:, 0:w], in0=ar[:, 0:w], scalar1=float(1 / (2 * math.pi)), scalar2=MAGIC,
                                                                   op0=ALU.mult, op1=ALU.add), [arb], [nrb])
                    V(lambda e, nr=nr, w=w: e.tensor_scalar(out=nr[:, 0:w], in0=nr[:, 0:w], scalar1=MAGIC, scalar2=None, op0=ALU.subtract), [nrb], [nrb])
                    V(lambda e, ar=ar, nr=nr, w=w: e.scalar_tensor_tensor(out=ar[:, 0:w], in0=nr[:, 0:w], scalar=float(-2 * math.pi), op0=ALU.mult,
                                                                          in1=ar[:, 0:w], op1=ALU.add), [arb, nrb], [arb])
                    V(lambda e, ar=ar, w=w: e.tensor_scalar(out=ar[:, 0:w], in0=ar[:, 0:w], scalar1=3.1415925, scalar2=-3.1415925, op0=ALU.min, op1=ALU.max), [arb], [arb])
                    A(lambda e, ar=ar, t0=t0, w=w: e.activation(out=dst[:, t0:t0 + w], in_=ar[:, 0:w], func=AF.Sin), [arb], [dstb])
            sin_layer(w1s[:], feats, FEB, 33, h1, H1B, 0)
            sin_layer(w2s[:], h1, H1B, 64, h2, H2B, 1)
            K.barrier()
            K.release(mk2)

            gA = K.sb("gA", [128, 2, 2, 7, 256], BF16)
            gB = K.sb("gB", [128, 2, 2, 256], BF16)
            GB_ = Buf("g")
            hfa = K.sb("hfa", [128, 10, 2, 256], BF16)
            HFB = Buf("hf")
            ztm = K.sb("ztm", [128, NCH, 128], BF16)
            ZTB = bufs(NCH, "ztm")
            Yb = K.sb("Yb", [128, 2, 2, 5, 128], BF16)
            YBB = Buf("Yb")
            ytp = Rot(K, "yt", [128, 4, 128], F32, 2)
            tqp = Rot(K, "tqr", [128, 4, 128], F32, 4)
            identr = K.sb("identr", [128, 2, 128])
            IDR = Buf("identr")
            V(lambda e: e.tensor_copy(identr[:, 0, :].bitcast(F32R), ident_b), [CONSTB], [IDR])
            V(lambda e: e.tensor_scalar(out=identr[:, 1, :].bitcast(F32R), in0=ident_b, scalar1=-1.0, scalar2=None, op0=ALU.mult), [CONSTB], [IDR])
            dftb = bcol("dft").rearrange("p (t r f) -> p t r f", t=8, r=2)
            idft = bcol("idft").rearrange("p (a r t) -> p a r t", a=2, r=2)

            decs = K.sb("decs", [128, 20, 128])
            DCB = Buf("decs")
            for o in range(2):
                zin, zinb = (vb_, HVB) if o == 0 else (x1f, HX1)
                gate, gateb = (x1f, HX1) if o == 0 else (x2f, HX2)
                zout, zoutb = (x1f, HX1) if o == 0 else (yhy, YHB)
                for cc in range(2):
                    K.dma("sync", lambda e, cc=cc: e.dma_start(out=decs[:], in_=dec_d[:, :, cc * 128:(cc + 1) * 128]), writes=[DCB], semkey="lddec")
                    for pk in range(10):
                        pb, pbb = bank()
                        pos0 = pk * 128
                        for sd in range(2):
                            wc0 = sd * 512 + o * 256 + cc * 128
                            T(lambda e, pb=pb, sd=sd, wc0=wc0, pos0=pos0: e.matmul(pb[:, sd * 128:(sd + 1) * 128], lhsT=h2[:, pos0:pos0 + 128],
                                                                                   rhs=w3s[:, wc0:wc0 + 128], start=True, stop=True), [H2B, FEB], [pbb])
                        if pk < 8:
                            di = [pk, 8 + pk]
                        else:
                            di = [16 + pk - 8, 18 + pk - 8]
                        for sd in range(2):
                            V(lambda e, pb=pb, sd=sd, pk=pk, di=di, cc=cc: e.tensor_tensor(out=hfa[:, pk, sd, cc * 128:(cc + 1) * 128], in0=pb[:, sd * 128:(sd + 1) * 128],
                                                                                          in1=decs[:, di[sd], :], op=ALU.mult), [pbb, DCB], [HFB])
                for delta in range(-3, 4):
                    ents = spectrum_entries(delta, 8)
                    for fch in range(2):
                        pb, pbb = bank()
                        for r in range(2):
                            for i, (src, k, ty) in enumerate(ents):
                                T(lambda e, pb=pb, r=r, i=i, src=src, k=k, ty=ty, fch=fch: e.matmul(
                                    pb[:, r * 256:(r + 1) * 256], lhsT=dftb[:, ty, r, fch * 128:(fch + 1) * 128], rhs=hfa[:, k, src, :],
                                    start=(i == 0), stop=(i == len(ents) - 1)), [CONSTB, HFB], [pbb])
                        if delta == 0:
                            A(lambda e, pb=pb, fch=fch, delta=delta: e.activation(out=gA[:, fch, :, delta + 3, :], in_=pb[:, 0:512].rearrange("p (r c) -> p r c", r=2),
                                                                                  func=AF.Copy), [pbb], [GB_])
                        else:
                            A(lambda e, pb=pb, fch=fch, delta=delta: e.activation(out=gA[:, fch, :, delta + 3, :], in_=pb[:, 0:512].rearrange("p (r c) -> p r c", r=2),
                                                                                  func=AF.Copy, scale=flag), [pbb, CONSTB], [GB_])
                entsB = spectrum_entries(0, 2)
                for fch in range(2):
                    pb, pbb = bank()
                    for r in range(2):
                        for i, (src, k, ty) in enumerate(entsB):
                            T(lambda e, pb=pb, r=r, i=i, src=src, k=k, ty=ty, fch=fch: e.matmul(
                                pb[:, r * 256:(r + 1) * 256], lhsT=dftb[:, ty, r, fch * 128:(fch + 1) * 128], rhs=hfa[:, 8 + k, src, :],
                                start=(i == 0), stop=(i == len(entsB) - 1)), [CONSTB, HFB], [pbb])
                    A(lambda e, pb=pb, fch=fch: e.activation(out=gB[:, fch, :, :], in_=pb[:, 0:512].rearrange("p (r c) -> p r c", r=2), func=AF.Copy), [pbb], [GB_])
                for cc in range(2):
                    gcs = slice(cc * 128, (cc + 1) * 128)
                    for tc in range(NCH):
                        pb, pbb = bank()
                        pbv = pb[:].bitcast(BF16)
                        T(lambda e, pbv=pbv, tc=tc: e.transpose(pbv[:, 0:128], zin[:, cc, tc * 128:(tc + 1) * 128], ident_b), [zinb, CONSTB], [pbb])
                        A(lambda e, pbv=pbv, tc=tc: e.activation(out=ztm[:, tc, :], in_=pbv[:, 0:128], func=AF.Copy), [pbb], [ZTB[tc]])
                    ty_f = [dft_type(0, 0), dft_type(0, 128)]
                    set_banks(6, 8)
                    for fch in range(2):
                        for r in range(2):
                            for blk in range(NB):
                                if blk < 4:
                                    dst, dstb = PS[r][:, blk * 128:(blk + 1) * 128], PSB[r]
                                else:
                                    dst, dstb = PS[2][:, r * 128:(r + 1) * 128], PSB[2]
                                for tk in range(2):
                                    T(lambda e: e.matmul(dst, lhsT=dftb[:, ty_f[tk], r, fch * 128:(fch + 1) * 128], rhs=ztm[:, 2 * blk + tk, :],
                                                         start=(tk == 0), stop=(tk == 1)), [CONSTB, ZTB[2 * blk + tk]], [dstb])
                        terms = []
                        for delta in [0, 1, -1, 2, -2, 3, -3]:
                            nbk = 4 - abs(delta)
                            tb0 = max(delta, 0)
                            sb0 = tb0 - delta
                            for (acc, gr, zr, sgn) in ((3, 0, 0, 0), (3, 1, 1, 1), (4, 0, 1, 0), (4, 1, 0, 0)):
                                g_ap = gA[:, fch, gr, delta + 3, gcs].unsqueeze(1).broadcast_to([128, nbk, 128])
                                z_ap = PS[zr][:, sb0 * 128:(sb0 + nbk) * 128].rearrange("p (b c) -> p b c", b=nbk)
                                terms.append((acc, tb0 * 128, nbk * 128, sgn, z_ap, PSB[zr], g_ap, nbk))
                        cnt = {3: 0, 4: 0}
                        tot = {3: 14, 4: 14}
                        for (acc, c0, ncol, sgn, z_ap, zb, g_ap, nbk) in terms:
                            tq, tqb = tqp.next()
                            V(lambda e: e.tensor_tensor(out=tq[:, 0:nbk, :].bitcast(F32R), in0=z_ap, in1=g_ap, op=ALU.mult), [zb, GB_], [tqb])
                            T(lambda e: e.matmul(PS[acc][:, c0:c0 + ncol], lhsT=identr[:, sgn, :].bitcast(F32R),
                                                 rhs=tq[:, 0:nbk, :].rearrange("p b c -> p (b c)").bitcast(F32R),
                                                 start=(cnt[acc] == 0), stop=(cnt[acc] == tot[acc] - 1)), [tqb, IDR], [PSB[acc]])
                            cnt[acc] += 1
                        tqs = []
                        for (gr, zr, sgn) in ((0, 0, 0), (1, 1, 1), (0, 1, 0), (1, 0, 0)):
                            tq, tqb = tqp.next()
                            V(lambda e: e.tensor_tensor(out=tq[:, 0, :].bitcast(F32R), in0=PS[2][:, zr * 128:(zr + 1) * 128], in1=gB[:, fch, gr, gcs], op=ALU.mult),
                              [PSB[2], GB_], [tqb])
                            tqs.append((tq, tqb, sgn))
                        for i, (tq, tqb, sgn) in enumerate(tqs):
                            ro = i // 2
                            T(lambda e: e.matmul(PS[5][:, ro * 128:(ro + 1) * 128], lhsT=identr[:, sgn, :].bitcast(F32R), rhs=tq[:, 0, :].bitcast(F32R),
                                                 start=(i % 2 == 0), stop=(i % 2 == 1)), [tqb, IDR], [PSB[5]])
                        for r in range(2):
                            A(lambda e: e.activation(out=Yb[:, fch, r, 0:4, :], in_=PS[3 + r][:, 0:512].rearrange("p (b c) -> p b c", b=4), func=AF.Copy),
                              [PSB[3 + r]], [YBB])
                        A(lambda e: e.activation(out=Yb[:, fch, :, 4, :], in_=PS[5][:, 0:256].rearrange("p (r c) -> p r c", r=2), func=AF.Copy), [PSB[5]], [YBB])
                    set_banks(0, 8)
                    hbcol = pcol("hy_bias", (l * 2 + o) * 2 + cc, 1)
                    for blk in range(NB):
                        pb, pbb = bank()
                        i = 0
                        for fch in range(2):
                            for r in range(2):
                                T(lambda e, pb=pb, fch=fch, r=r, blk=blk, i=i: e.matmul(pb[:, 0:256], lhsT=Yb[:, fch, r, blk, :], rhs=idft[:, fch, r, :],
                                                                                       start=(i == 0), stop=(i == 3)), [YBB, CONSTB], [pbb])
                                i += 1
                        ts = slice(blk * 256, (blk + 1) * 256)
                        tq, tqb = ytp.next()
                        tqv = tq[:].rearrange("p a b -> p (a b)")[:, 0:256]
                        V(lambda e, tqv=tqv, pb=pb, ts=ts: e.scalar_tensor_tensor(out=tqv, in0=zin[:, cc, ts], scalar=hbcol, op0=ALU.mult, in1=pb[:, 0:256], op1=ALU.add),
                          [zinb, CONSTB, pbb], [tqb])
                        V(lambda e, tqv=tqv, ts=ts: e.tensor_tensor(out=zout[:, cc, ts], in0=tqv, in1=gate[:, cc, ts], op=ALU.mult), [tqb, gateb], [zoutb])
            out_proj([yhy[:, 0, :], yhy[:, 1, :]], [YHB], 256, 128)
            K.barrier()
            K.release(mk)

        yhy = K.sb("yhy", [128, 2, NT], BF16)
        YHB = Buf("yhy")
        hyena(yhy, YHB)
        yssd = K.sb("yssd", [64, 4, NT], BF16)
        YSB = Buf("yssd")
        ssd(yssd, YSB)
        yret = K.sb("yret", [64, 4, NT], BF16)
        YRB = Buf("yret")
        ret(yret, YRB)
        yatt = K.sb("yatt", [64, 4, NT], BF16)
        YAB_ = Buf("yatt")
        att(yatt, YAB_)
        out_proj_all()
        K.barrier()
        K.release(mk_all)

    for l in range(nlayers):
        if l == 1:
            assert not pending_mod and not inflight_mod, (pending_mod, inflight_mod)
        ffn(l, 0)
        mixer(l)
        ffn(l, 1)

    K.dma("sync", lambda e: e.dma_start(out=yT, in_=x[:]), reads=[b for row in XB for b in row], semkey="sty")
    K.barrier()
    K.release(0)
    return nc, K


def _consts(is_s):
    f32 = np.float32
    cfv = np.zeros((128, CF.n), f32)

    def put(name, arr):
        c0, w = CF[name]
        cfv[:, c0:c0 + w] = np.asarray(arr, f32).reshape(128, w) if np.asarray(arr).ndim > 1 or w == 1 else np.broadcast_to(np.asarray(arr, f32), (128, w))
    j = np.arange(128)[:, None]
    i = np.arange(128)[None, :]
    put("trif", (i >= j).astype(f32))
    put("trib", (j >= i).astype(f32))
    put("ones", np.ones((128, 128), f32))
    put("relu_f", np.maximum(i - j, 0).astype(f32))
    put("relu_b", np.maximum(j - i, 0).astype(f32))
    put("ip1", np.broadcast_to((i + 1).astype(f32), (128, 128)))
    put("rmi", np.broadcast_to((128 - i).astype(f32), (128, 128)))
    put("tailf", (127 - j).astype(f32))
    put("tailb", j.astype(f32))
    nf = 16
    inv = (10000.0 ** (-np.arange(nf, dtype=f32) / nf)).astype(f32)
    cos = np.ones((NT, 32), f32)
    sin = np.zeros((NT, 32), f32)
    if is_s:
        t = np.arange(1024)
        rows = (t // 64).astype(f32)
        cols = (t % 64).astype(f32)
        angr = rows[:, None] * inv[None, :]
        angc = cols[:, None] * inv[None, :]
        cos[:1024, 0:16] = np.cos(angr)
        cos[:1024, 16:32] = np.cos(angc)
        sin[:1024, 0:16] = np.sin(angr)
        sin[:1024, 16:32] = np.sin(angc)
    put("cos", cos.reshape(NCH, 128, 32).transpose(1, 0, 2).reshape(128, 320))
    put("sin", sin.reshape(NCH, 128, 32).transpose(1, 0, 2).reshape(128, 320))
    cfb = np.full((NCH,), -30000.0, f32)
    hl = np.zeros((5,), f32)
    hr = np.zeros((5,), f32)
    if is_s:
        cfb[0:8] = 0.0
        hl[1:4] = 1.0
        hr[0:3] = 1.0
    put("cfb", cfb)
    put("hl", hl)
    put("hr", hr)
    put("flag", np.full((128, 1), 1.0 if is_s else 0.0, f32))
    put("negpi", np.full((128, 1), -math.pi, f32))
    put("nbf", ((i >= j).astype(f32) - 1.0) * 30000.0)
    put("nbb", ((j >= i).astype(f32) - 1.0) * 30000.0)

    cbv = np.zeros((128, CB.n), f32)

    def putb(name, arr):
        c0, w = CB[name]
        cbv[:, c0:c0 + w] = np.asarray(arr, f32).reshape(128, w)
    putb("ident", np.eye(128, dtype=f32))
    putb("ones", np.ones((128, 128), f32))
    am = np.zeros((128, NCH, 2, 128), f32)
    band_prev = (j >= i).astype(f32)
    band_next = (j <= i).astype(f32)
    for qc in range(NCH):
        if is_s and qc < 8:
            if qc >= 1:
                am[:, qc, 0, :] = band_prev
            if qc <= 6:
                am[:, qc, 1, :] = band_next
        else:
            if qc % 2 == 1:
                am[:, qc, 0, :] = 1.0
            else:
                am[:, qc, 1, :] = 1.0
    putb("am", am)
    om = 2 * np.pi * (np.arange(256) + 0.5) / 512.0
    row = np.arange(128)
    dft = np.zeros((128, 8, 2, 256), np.float64)
    for ty in range(8):
        if ty < 4:
            e = FWD_E0[ty] + row
        else:
            e = BWD_E0[ty - 4] - row
        valid = (np.abs(e) <= 255).astype(np.float64)
        ang = e[:, None] * om[None, :]
        dft[:, ty, 0, :] = np.cos(ang) * valid[:, None]
        dft[:, ty, 1, :] = -np.sin(ang) * valid[:, None]
    putb("dft", dft)
    tt = np.arange(256)
    idft = np.zeros((128, 2, 2, 256), np.float64)
    for fch in range(2):
        omf = om[fch * 128:(fch + 1) * 128]
        ang = omf[:, None] * tt[None, :]
        idft[:, fch, 0, :] = (2.0 / 512) * np.cos(ang)
        idft[:, fch, 1, :] = -(2.0 / 512) * np.sin(ang)
    putb("idft", idft)
    return cfv, cbv


def _hy_consts(LA):
    f32 = np.float32
    l = LA
    pos = np.arange(l, dtype=f32)
    t = pos / f32(l - 1)
    bands = np.linspace(1e-4, 15, 16, dtype=f32)
    ang = (f32(2.0 * math.pi / l)) * pos[:, None] * bands[None, :]
    feats = np.concatenate([t[:, None], np.cos(ang), -np.sin(ang)], axis=-1).astype(f32)
    max_decay = math.log(1e-2) / 0.3
    min_decay = math.log(1e-2) / 1.5
    deltas = np.abs(np.linspace(min_decay, max_decay, 256, dtype=f32))
    dec = np.exp(-t[:, None] * deltas[None, :]).astype(f32)
    return feats, dec


def _prepare(inp):
    f32 = np.float32
    g = lambda k: np.asarray(inp[k], dtype=f32)
    x_prompt, x_sample = g("x_prompt"), g("x_sample")
    cache_k, cache_v = g("cache_k"), g("cache_v")
    state_ssd, state_ret = g("state_ssd"), g("state_ret")
    c, c_ctx = g("c"), g("c_ctx")
    pt = np.zeros((128, PT.n), f32)

    def putp(name, off, arr):
        arr = np.asarray(arr, f32)
        c0 = PT[name][0] + off
        pt[0:arr.shape[0], c0:c0 + arr.shape[1]] = arr
    nw = g("norm_w")
    for l in range(2):
        for j in range(3):
            putp("norm_w", (l * 3 + j) * 8, nw[l, j].reshape(8, 128).T)
        putp("b_mod", l * 72, g("b_mod")[l].reshape(72, 128).T)
        cw, cbias = g("ssd_conv_w")[l], g("ssd_conv_b")[l]
        for q in range(8):
            if q < 4:
                f0, fs = q * 64, 64
            else:
                f0, fs = 256 + (q - 4) * 128, 128
            arr = np.stack([cw[0, f0:f0 + fs], cw[1, f0:f0 + fs], cw[2, f0:f0 + fs], cbias[f0:f0 + fs]], axis=1)
            putp("ssd_conv", (l * 8 + q) * 4, arr)
        hw, hb = g("hy_conv_w")[l], g("hy_conv_b")[l]
        for q in range(6):
            f0 = q * 128
            arr = np.stack([hw[0, f0:f0 + 128], hw[1, f0:f0 + 128], hw[2, f0:f0 + 128], hb[f0:f0 + 128]], axis=1)
            putp("hy_conv", (l * 6 + q) * 4, arr)
        hbias = g("hy_bias")[l]
        for o in range(2):
            putp("hy_bias", (l * 2 + o) * 2, hbias[o].reshape(2, 128).T)
        putp("ssd_nw", l * 4, g("ssd_norm_w")[l].reshape(4, 64).T)
        putp("ssd_d", l * 4, np.broadcast_to(g("ssd_d")[l][None, :], (64, 4)))
        putp("ret_gn", l * 4, g("ret_gn_w")[l].reshape(4, 64).T)
        putp("hyp", l * 3, np.stack([g("hy_b1")[l], g("hy_freq")[l], g("hy_b2")[l]], axis=1))
    ft = np.zeros((128, FT.n), f32)

    def putf(name, off, vec):
        vec = np.asarray(vec, f32).reshape(-1)
        c0 = FT[name][0] + off
        ft[:, c0:c0 + vec.size] = vec[None, :]
    for l in range(2):
        putf("dt_bias", l * 80, np.tile(g("ssd_dt_bias")[l].reshape(8), NCH))
        putf("a_log", l * 80, np.tile(g("ssd_a_log")[l].reshape(8), NCH))
        putf("ret_logit", l * 8, g("ret_decay_logit")[l].reshape(8))
        putf("qkw", l * 384, np.concatenate([np.tile(g("attn_q_norm")[l], 4), np.tile(g("attn_k_norm")[l], 2)]))
        putf("sink", l * 4, g("attn_sink")[l])
    featsB, decB = _hy_consts(256)
    featsA_s, decA_s = _hy_consts(1024)
    wm = g("w_mod").reshape(2, 8, 128, 18, 512).transpose(0, 3, 2, 1, 4).reshape(2, 18, 128, 4096)
    wi = g("ffn_w_in").reshape(2, 2, 8, 128, 2, 11, 256)
    wi = wi.transpose(0, 1, 5, 3, 2, 4, 6).reshape(2, 2, 11, 128, 4096)
    wo = g("ffn_w_out").reshape(2, 2, 11, 2, 128, 1024).transpose(0, 1, 2, 4, 3, 5).reshape(2, 2, 11, 128, 2048)
    shared = dict(ptab=pt, ftab=ft, w_mod=np.ascontiguousarray(wm), ffn_w_in=np.ascontiguousarray(wi), ffn_w_out=np.ascontiguousarray(wo), mix_w_in=g("mix_w_in"),
                  mix_w_out=g("mix_w_out"), hy_w1=g("hy_w1"), hy_w2=g("hy_w2"), hy_w3=g("hy_w3"),
                  featsB=np.ascontiguousarray(featsB.T))
    consts = {True: _consts(True), False: _consts(False)}
    in_maps = []
    plan = []
    for cid in range(NCORE):
        is_s = cid < 2
        if is_s:
            xs = np.concatenate([x_sample[cid], x_prompt[30 + cid]], axis=0)
            seqs = [30 + cid]
            condA = c[cid]
        else:
            seqs = list(range(5 * (cid - 2), 5 * (cid - 2) + 5))
            xs = x_prompt[seqs].reshape(NT, D)
            condA = c_ctx
        plan.append((is_s, seqs))
        xTm = np.ascontiguousarray(xs.T.reshape(8, 128, NT).transpose(1, 0, 2))
        cond = np.stack([condA, c_ctx], axis=-1)
        condTm = np.ascontiguousarray(cond.reshape(8, 128, 2).transpose(1, 0, 2))
        s0s = np.zeros((2, 5, 2, 4, 128, 64), f32)
        s0r = np.zeros((2, 5, 2, 4, 64, 64), f32)
        ck = np.zeros((2, 2, 64, 512), f32)
        cv = np.zeros((2, 512, 128), f32)
        if is_s:
            for l in range(2):
                s0s[l, 0, 0] = state_ssd[cid, l, 0]
                s0s[l, 3, 1] = state_ssd[cid, l, 1]
                s0r[l, 0, 0] = state_ret[cid, l, 0]
                s0r[l, 3, 1] = state_ret[cid, l, 1]
                ck[l] = cache_k[cid, l].transpose(1, 2, 0)
                cv[l] = cache_v[cid, l].reshape(512, 128)
            featsA = featsA_s
            decFA = decA_s.copy()
        else:
            featsA = np.zeros((1024, 33), f32)
            featsA[:256] = featsB
            decFA = np.zeros((1024, 256), f32)
            decFA[:256] = decB
        decBA = decFA.copy()
        decBA[0] = 0.0
        decFB = decB.copy()
        decBB = decB.copy()
        decBB[0] = 0.0
        dec = np.concatenate([decFA.reshape(8, 128, 256), decBA.reshape(8, 128, 256), decFB.reshape(2, 128, 256), decBB.reshape(2, 128, 256)], axis=0)
        cfv, cbv = consts[is_s]
        m = dict(shared)
        m.update(xT=xTm, condT=condTm,
                 s0ssd=np.ascontiguousarray(s0s.reshape(80, 128, 64).transpose(1, 0, 2)),
                 s0ret=np.ascontiguousarray(s0r.reshape(80, 64, 64).transpose(1, 0, 2)),
                 ckT=np.ascontiguousarray(ck.reshape(4, 64, 512).transpose(1, 0, 2)),
                 cvt=np.ascontiguousarray(cv.reshape(8, 128, 128).transpose(1, 0, 2)),
                 cf32=cfv, cbf=cbv, featsA=np.ascontiguousarray(featsA.T), dec=np.ascontiguousarray(dec.transpose(1, 0, 2)))
        in_maps.append(m)
    return in_maps, plan


_CACHE = {}


def kernel(**inputs):
    in_maps, plan = _prepare(inputs)
    if "nc" not in _CACHE:
        _CACHE["nc"] = build_program()[0]
    nc = _CACHE["nc"]
    res = run_bass_kernel_spmd(nc, in_maps, core_ids=list(range(NCORE)))
    f32 = np.float32
    y_prompt = np.zeros((32, 256, D), f32)
    y_sample = np.zeros((2, 1024, D), f32)
    nck = np.zeros((32, 2, 256, 2, 64), f32)
    ncv = np.zeros((32, 2, 256, 2, 64), f32)
    nssd = np.zeros((32, 2, 2, 4, 128, 64), f32)
    nret = np.zeros((32, 2, 2, 4, 64, 64), f32)
    for cid, (is_s, seqs) in enumerate(plan):
        r = res.results[cid]
        y = np.asarray(r["yT"]).transpose(1, 0, 2).reshape(D, NT).T
        nk = np.asarray(r["nk"])
        nv = np.asarray(r["nv"])
        ss = np.asarray(r["nssd"]).transpose(1, 0, 2).reshape(2, 5, 2, 4, 128, 64)
        sr = np.asarray(r["nret"]).transpose(1, 0, 2).reshape(2, 5, 2, 4, 64, 64)
        if is_s:
            y_sample[cid] = y[:1024]
            blks = [(4, seqs[0])]
        else:
            blks = list(enumerate(seqs))
        for blk, b in blks:
            ts = slice(blk * 256, (blk + 1) * 256)
            y_prompt[b] = y[ts]
            for l in range(2):
                nck[b, l] = nk[l, ts].reshape(256, 2, 64)
                ncv[b, l] = nv[l, ts].reshape(256, 2, 64)
                nssd[b, l] = ss[l, blk]
                nret[b, l] = sr[l, blk]
    return (y_prompt, y_sample, nck, ncv, nssd, nret)
```

```python
import math
import numpy as np
import concourse.bass as bass
import concourse.mybir as mybir
from concourse.bass_utils import run_bass_kernel_spmd

F32 = mybir.dt.float32
BF16 = mybir.dt.bfloat16
F32R = mybir.dt.float32r
AF = mybir.ActivationFunctionType
ALU = mybir.AluOpType
AX = mybir.AxisListType

NCORE = 8
NT = 1280
NB = 5
NCH = 10
TT = [(0, 512, 0), (512, 512, 0), (1024, 256, 1)]
D = 1024
DFF = 2816
EPS = 1e-6
ENGS = ["tensor", "vector", "scalar", "gpsimd", "sync"]
SAME_ENGINE_NOSYNC = ("tensor",)


class Buf:
    __slots__ = ("name", "w", "r", "excl")

    def __init__(self, name="", excl=False):
        self.name = name
        self.w = None
        self.r = {}
        self.excl = excl


def bufs(n, name=""):
    return [Buf(name + str(i)) for i in range(n)]


class KB:
    def __init__(self, nc):
        self.nc = nc
        self.cnt = {e: 0 for e in ENGS}
        self.seen = {e: {} for e in ENGS}
        self.sems = {}
        self.dcount = {}
        self._stack = []
        self.n_inst = 0
        self.uid = 0

    def enter(self, cm):
        v = cm.__enter__()
        self._stack.append(cm)
        return v

    def mark(self):
        return len(self._stack)

    def release(self, m):
        while len(self._stack) > m:
            self._stack.pop().__exit__(None, None, None)

    def sem(self, key):
        if key not in self.sems:
            self.sems[key] = self.enter(self.nc.semaphore("s_" + key))
        return self.sems[key]

    def sb(self, name, shape, dt=F32):
        self.uid += 1
        return self.enter(self.nc.sbuf_tensor(f"{name}_{self.uid}", list(shape), dt))

    def ps(self, name, shape, dt=F32):
        return self.enter(self.nc.psum_tensor(name, list(shape), dt))

    @staticmethod
    def _flat(bl):
        out = []
        for b in bl:
            if isinstance(b, (list, tuple)):
                out.extend(KB._flat(b))
            else:
                out.append(b)
        return out

    def _need(self, eng, reads, writes):
        need = {}

        def add(k, v):
            if need.get(k, 0) < v:
                need[k] = v
        for b in reads:
            if b.w is not None:
                add(*b.w)
        for b in writes:
            if b.w is not None:
                add(*b.w)
            for k, v in b.r.items():
                add(k, v)
        out = []
        for k, v in need.items():
            if k == "p_" + eng and eng in SAME_ENGINE_NOSYNC:
                continue
            if self.seen[eng].get(k, 0) >= v:
                continue
            self.seen[eng][k] = v
            out.append((k, v))
        return out

    def _emit(self, eng, waits, fn, key, inc):
        e = getattr(self.nc, eng)
        for k, v in waits:
            e.wait_ge(self.sems[k], v)
        if fn is not None:
            fn(e).then_inc(self.sems[key], inc)

    def op(self, eng, fn, reads=(), writes=()):
        reads = self._flat(reads)
        writes = self._flat(writes)
        ex = [b for b in reads if b.excl]
        if ex:
            reads = [b for b in reads if not b.excl]
            writes = writes + [b for b in ex if b not in writes]
        waits = self._need(eng, reads, writes)
        key = "p_" + eng
        self.sem(key)
        self.cnt[eng] += 1
        val = self.cnt[eng]
        for b in reads:
            if b.r.get(key, 0) < val:
                b.r[key] = val
        for b in writes:
            b.w = (key, val)
            b.r = {}
        self._emit(eng, waits, fn, key, 1)
        self.n_inst += 1

    def dma(self, eng, fn, reads=(), writes=(), semkey=None):
        reads = self._flat(reads)
        writes = self._flat(writes)
        waits = self._need(eng, reads, writes)
        self.sem(semkey)
        self.dcount[semkey] = self.dcount.get(semkey, 0) + 16
        val = self.dcount[semkey]
        for b in reads:
            if b.r.get(semkey, 0) < val:
                b.r[semkey] = val
        for b in writes:
            b.w = (semkey, val)
            b.r = {}
        self._emit(eng, waits, fn, semkey, 16)
        self.n_inst += 1

    def barrier(self):
        tot = [("p_" + e, self.cnt[e]) for e in ENGS if self.cnt[e] > 0]
        tot += list(self.dcount.items())
        for eng in ENGS:
            waits = []
            for k, v in tot:
                if k == "p_" + eng and eng == "tensor":
                    continue
                if self.seen[eng].get(k, 0) >= v:
                    continue
                self.seen[eng][k] = v
                waits.append((k, v))
            self._emit(eng, waits, None, None, 0)

    def V(self, fn, r=(), w=()):
        self.op("vector", fn, r, w)

    def A(self, fn, r=(), w=()):
        self.op("scalar", fn, r, w)

    def T(self, fn, r=(), w=()):
        self.op("tensor", fn, r, w)

    def G(self, fn, r=(), w=()):
        self.op("gpsimd", fn, r, w)


class Rot:
    def __init__(self, K, name, shape, dt, n):
        self.t = [K.sb(f"{name}{i}", shape, dt) for i in range(n)]
        self.b = bufs(n, name)
        self.i = 0

    def next(self):
        i = self.i
        self.i = (i + 1) % len(self.t)
        return self.t[i], self.b[i]


class Cols:
    def __init__(self):
        self.m = {}
        self.n = 0

    def add(self, name, w):
        self.m[name] = (self.n, w)
        self.n += w

    def __getitem__(self, name):
        return self.m[name]


def ptab_cols():
    c = Cols()
    c.add("norm_w", 48)
    c.add("b_mod", 144)
    c.add("ssd_conv", 64)
    c.add("hy_conv", 48)
    c.add("hy_bias", 8)
    c.add("ssd_nw", 8)
    c.add("ssd_d", 8)
    c.add("ret_gn", 8)
    c.add("hyp", 6)
    return c


def ftab_cols():
    c = Cols()
    c.add("dt_bias", 160)
    c.add("a_log", 160)
    c.add("ret_logit", 16)
    c.add("qkw", 768)
    c.add("sink", 8)
    return c


def cf32_cols():
    c = Cols()
    c.add("trif", 128)
    c.add("trib", 128)
    c.add("ones", 128)
    c.add("relu_f", 128)
    c.add("relu_b", 128)
    c.add("ip1", 128)
    c.add("rmi", 128)
    c.add("tailf", 1)
    c.add("tailb", 1)
    c.add("cos", 320)
    c.add("sin", 320)
    c.add("cfb", 10)
    c.add("hl", 5)
    c.add("hr", 5)
    c.add("flag", 1)
    c.add("negpi", 1)
    c.add("nbf", 128)
    c.add("nbb", 128)
    return c


def cbf_cols():
    c = Cols()
    c.add("ident", 128)
    c.add("ones", 128)
    c.add("am", 2560)
    c.add("dft", 4096)
    c.add("idft", 1024)
    return c


PT = ptab_cols()
FT = ftab_cols()
CF = cf32_cols()
CB = cbf_cols()

FWD_E0 = [-256, -128, 0, 128]
BWD_E0 = [256, 128, 0, -128]


def dft_type(src, e0):
    return (FWD_E0.index(e0) if src == 0 else 4 + BWD_E0.index(e0))


def spectrum_entries(delta, nchunks):
    out = []
    for k in range(nchunks):
        e0 = 128 * k - 256 * delta
        if e0 in FWD_E0:
            out.append((0, k, dft_type(0, e0)))
        e0b = -128 * k - 256 * delta
        if e0b in BWD_E0:
            out.append((1, k, dft_type(1, e0b)))
    return out


def build_program(nlayers=2):
    nc = bass.Bass("TRN2", target_bir_lowering=False)

    def din(name, shape):
        return nc.dram_tensor(name, list(shape), F32, kind="ExternalInput").ap()

    def dout(name, shape):
        return nc.dram_tensor(name, list(shape), F32, kind="ExternalOutput").ap()

    xT = din("xT", [128, 8, NT])
    condT = din("condT", [128, 8, 2])
    s0ssd = din("s0ssd", [128, 80, 64])
    s0ret = din("s0ret", [64, 80, 64])
    ckT = din("ckT", [64, 4, 512])
    cvt = din("cvt", [128, 8, 128])
    ptab_d = din("ptab", [128, PT.n])
    ftab_d = din("ftab", [128, FT.n])
    cf32_d = din("cf32", [128, CF.n])
    cbf_d = din("cbf", [128, CB.n])
    featsA_d = din("featsA", [33, 1024])
    featsB_d = din("featsB", [33, 256])
    dec_d = din("dec", [128, 20, 256])
    w_mod = din("w_mod", [2, 18, 128, 4096])
    ffn_w_in = din("ffn_w_in", [2, 2, 11, 128, 4096])
    ffn_w_out = din("ffn_w_out", [2, 2, 11, 128, 2048])
    mix_w_in = din("mix_w_in", [2, D, 3336])
    mix_w_out = din("mix_w_out", [2, D, D])
    hy_w1 = din("hy_w1", [2, 33, 64])
    hy_w2 = din("hy_w2", [2, 64, 64])
    hy_w3 = din("hy_w3", [2, 64, 1024])

    yT = dout("yT", [128, 8, NT])
    nk_o = dout("nk", [2, NT, 128])
    nv_o = dout("nv", [2, NT, 128])
    nssd_o = dout("nssd", [128, 80, 64])
    nret_o = dout("nret", [64, 80, 64])

    K = KB(nc)
    V, A, T, G = K.V, K.A, K.T, K.G

    x = K.sb("x", [128, 8, NT])
    XB = [[Buf(f"x{m}_{t}") for t in range(3)] for m in range(8)]
    modt = K.sb("modt", [128, 2, 2, 72])
    MODB = Buf("mod")
    ptab = K.sb("ptab", [128, PT.n])
    ftab = K.sb("ftab", [128, FT.n])
    cf = K.sb("cf", [128, CF.n])
    cb = K.sb("cb", [128, CB.n], BF16)
    CONSTB = Buf("const")
    CONSTB2 = Buf("constb")
    WA = [K.sb(f"WA{i}", [128, 8, 512], BF16) for i in range(2)]
    WAB = bufs(2, "WA")
    WB = [K.sb(f"WB{i}", [128, 4096], BF16) for i in range(2)]
    WBB = bufs(2, "WB")
    wctr = {"a": 0, "b": 0}
    PS = [K.ps(f"ps{i}", [128, 512]) for i in range(8)]
    PSBK = [Buf(f"psb{i}", excl=True) for i in range(8)]
    PSR = [PSBK[i // 4] for i in range(32)]
    PSB = [[PSBK[i]] for i in range(8)]
    psctr = [0, 0, 8]
    prc = [0]

    def bank():
        lo, hi = psctr[1], psctr[2]
        i = psctr[0]
        if i < lo or i >= hi:
            i = lo
        psctr[0] = i + 1 if i + 1 < hi else lo
        return PS[i], PSB[i]

    def set_banks(lo, hi):
        psctr[1], psctr[2] = lo, hi

    def pr(ncols):
        n = (ncols + 127) // 128
        i = prc[0]
        if (i % 4) + n > 4:
            i = (i // 4 + 1) * 4
        if i + n > 32:
            i = 0
        prc[0] = (i + n) % 32
        b, r = divmod(i, 4)
        return PS[b][:, r * 128:r * 128 + n * 128], [PSBK[b]]

    def interleave(gens):
        gens = list(gens)
        while gens:
            nxt = []
            for g in gens:
                try:
                    next(g)
                    nxt.append(g)
                except StopIteration:
                    pass
            gens = nxt

    def nextA():
        i = wctr["a"] % 2
        wctr["a"] += 1
        return WA[i], WAB[i], f"wa{i}"

    def nextB():
        i = wctr["b"] % 2
        wctr["b"] += 1
        return WB[i], WBB[i], f"wb{i}"

    def pcol(name, off=0, w=1, rows=128):
        c0 = PT[name][0] + off
        return ptab[0:rows, c0:c0 + w]

    def fcol(name, off=0, w=1, rows=128):
        c0 = FT[name][0] + off
        return ftab[0:rows, c0:c0 + w]

    def ccol(name, off=0, w=None, rows=128):
        c0, ww = CF[name]
        if w is None:
            w = ww
        return cf[0:rows, c0 + off:c0 + off + w]

    def bcol(name, off=0, w=None, rows=128):
        c0, ww = CB[name]
        if w is None:
            w = ww
        return cb[0:rows, c0 + off:c0 + off + w]

    K.dma("sync", lambda e: e.dma_start(out=x[:], in_=xT), writes=[b for row in XB for b in row], semkey="ldx")
    K.dma("sync", lambda e: e.dma_start(out=ptab[:], in_=ptab_d), writes=[CONSTB], semkey="ldc")
    K.dma("sync", lambda e: e.dma_start(out=ftab[:], in_=ftab_d), writes=[CONSTB], semkey="ldc")
    K.dma("sync", lambda e: e.dma_start(out=cf[:], in_=cf32_d), writes=[CONSTB], semkey="ldc")
    K.dma("gpsimd", lambda e: e.dma_start(out=cb[:], in_=cbf_d, max_dma_last_dim=2048), writes=[CONSTB2], semkey="ldcb")

    K.barrier()

    flag = ccol("flag")
    ident_b = bcol("ident")
    ones_b = bcol("ones")
    ones_f = ccol("ones")
    trif = ccol("trif")
    trib = ccol("trib")

    condf = K.sb("condf", [128, 8, 2])
    condb = K.sb("condb", [128, 8, 2], BF16)
    CB_ = Buf("cond")
    MODL = [Buf("mod0"), Buf("mod1")]
    K.dma("sync", lambda e: e.dma_start(out=condf[:], in_=condT), writes=[CB_], semkey="ldcond")
    A(lambda e: e.activation(out=condb[:], in_=condf[:], func=AF.Silu), [CB_], [CB_])

    def mod_dma(l, ci, wa, wab, sk):
        K.dma("gpsimd", lambda e: e.dma_start(out=wa[:].rearrange("p k n -> p (k n)"), in_=w_mod[l, ci], max_dma_last_dim=8192),
              writes=[wab], semkey=sk)

    def mod_chunk(l, ci, wa, wab, sk, dma=True):
        if dma:
            mod_dma(l, ci, wa, wab, sk)
        pb, pbb = bank()
        for mb in range(4):
            for k in range(8):
                T(lambda e: e.matmul(pb[:, mb * 2:mb * 2 + 2], lhsT=wa[:, k, mb * 128:(mb + 1) * 128], rhs=condb[:, k, :],
                                     start=(k == 0), stop=(k == 7)), [wab, CB_], [pbb])
        for cnd in range(2):
            V(lambda e: e.tensor_tensor(out=modt[:, l, cnd, ci * 4:ci * 4 + 4], in0=pb[:, 0:8].rearrange("p (m c) -> p m c", c=2)[:, :, cnd],
                                        in1=pcol("b_mod", l * 72 + ci * 4, 4), op=ALU.add), [pbb, CONSTB], [MODL[l]])

    def mod_finish_j(l, j):
        for cnd in range(2):
            V(lambda e: e.scalar_tensor_tensor(
                out=modt[:, l, cnd, (3 * j + 1) * 8:(3 * j + 2) * 8], in0=modt[:, l, cnd, (3 * j + 1) * 8:(3 * j + 2) * 8],
                scalar=1.0, op0=ALU.add, in1=pcol("norm_w", (l * 3 + j) * 8, 8), op1=ALU.mult), [MODL[l], CONSTB], [MODL[l]])
            if j in (0, 2):
                V(lambda e: e.tensor_scalar(
                    out=modt[:, l, cnd, (3 * j + 2) * 8:(3 * j + 3) * 8], in0=modt[:, l, cnd, (3 * j + 2) * 8:(3 * j + 3) * 8],
                    scalar1=0.5, scalar2=None, op0=ALU.mult), [MODL[l]], [MODL[l]])

    mod_done = {}

    def mod_mark(l, ci):
        j = ci // 6
        mod_done[(l, j)] = mod_done.get((l, j), 0) + 1
        if mod_done[(l, j)] == 6:
            mod_finish_j(l, j)

    for ci in range(6):
        wa, wab, sk = nextA()
        mod_chunk(0, ci, wa, wab, sk)
        mod_mark(0, ci)
    K.barrier()
    pending_mod = [(0, ci) for ci in range(6, 18)] + ([(1, ci) for ci in range(18)] if nlayers > 1 else [])
    inflight_mod = []
    ffn_gi = [0]

    def modc(l, cnd, j, m):
        return modt[:, l, cnd, j * 8 + m:j * 8 + m + 1]

    def make_h(l, j, hbuf, HB, tiles=(0, 1, 2), rs_pool=None):
        for tt in tiles:
            t0, w, cnd = TT[tt]
            pb, pbb = bank()
            for c in range(8):
                sq, sqb = rs_pool["sq"].next()
                if c % 2 == 0:
                    A(lambda e, sq=sq, c=c, t0=t0, w=w: e.activation(out=sq[:, 0:w], in_=x[:, c, t0:t0 + w], func=AF.Square),
                      [XB[c][tt]], [sqb])
                else:
                    V(lambda e, sq=sq, c=c, t0=t0, w=w: e.tensor_tensor(out=sq[:, 0:w], in0=x[:, c, t0:t0 + w], in1=x[:, c, t0:t0 + w], op=ALU.mult),
                      [XB[c][tt]], [sqb])
                T(lambda e, pb=pb, sq=sq, c=c, w=w: e.matmul(pb[:, 0:w], lhsT=ones_b, rhs=sq[:, 0:w],
                                                               start=(c == 0), stop=(c == 7)), [sqb, CONSTB], [pbb])
            rs, rsb = rs_pool["rs"].next()
            A(lambda e, rs=rs, pb=pb, w=w: e.activation(out=rs[:, 0:w], in_=pb[:, 0:w], func=AF.Sqrt, bias=EPS, scale=1.0 / D),
              [pbb], [rsb])
            V(lambda e, rs=rs, w=w: e.reciprocal(rs[:, 0:w], rs[:, 0:w]), [rsb], [rsb])
            for c in range(8):
                tm, tmb = rs_pool["tm"].next()
                V(lambda e, tm=tm, rs=rs, c=c, t0=t0, w=w: e.tensor_tensor(out=tm[:, 0:w], in0=x[:, c, t0:t0 + w], in1=rs[:, 0:w], op=ALU.mult),
                  [XB[c][tt], rsb], [tmb])
                A(lambda e, tm=tm, c=c, t0=t0, w=w, cnd=cnd: e.activation(
                    out=hbuf[:, c, t0:t0 + w], in_=tm[:, 0:w], func=AF.Identity,
                    scale=modc(l, cnd, 3 * j + 1, c), bias=modc(l, cnd, 3 * j, c)), [tmb, MODL[l]], [HB[tt]])

    def resid_update(pb, pbb, m, tt, gate_ap, extra_reads=()):
        t0, w, cnd = TT[tt]
        V(lambda e: e.scalar_tensor_tensor(out=x[:, m, t0:t0 + w], in0=pb[:, 0:w], scalar=gate_ap, op0=ALU.mult,
                                           in1=x[:, m, t0:t0 + w], op1=ALU.add), [pbb, MODL[0], MODL[1]] + list(extra_reads), [XB[m][tt]])

    def ffn(l, f):
        j = 0 if f == 0 else 2
        mk = K.mark()
        hbuf = K.sb("h", [128, 8, NT], BF16)
        HB = bufs(3, "h")
        hid = [K.sb(f"hid{i}", [128, 2, NT], BF16) for i in range(2)]
        HIDB = [bufs(3, "hid0_"), bufs(3, "hid1_")]
        pool = {"sq": Rot(K, "sq", [128, 512], BF16, 3), "rs": Rot(K, "rs", [128, 512], F32, 2),
                "tm": Rot(K, "tm", [128, 512], F32, 3)}
        sgp = Rot(K, "sg", [128, 512], F32, 3)
        make_h(l, j, hbuf, HB, rs_pool=pool)
        stream_mod = (l == 0 and (len(pending_mod) > 0 or len(inflight_mod) > 0))
        if stream_mod:
            WM = [K.sb(f"WM{i}", [128, 8, 512], BF16) for i in range(4)]
            WMB = bufs(4, "WM")
            assert not inflight_mod
        for g in range(11):
            if stream_mod:
                while inflight_mod:
                    (ml, ci, slot) = inflight_mod.pop(0)
                    mod_chunk(ml, ci, WM[slot], WMB[slot], f"wm{slot}", dma=False)
                    mod_mark(ml, ci)
                ffn_gi[0] += 1
            wa, wab, ska = nextA()
            wb, wbb, skb = nextB()
            K.dma("gpsimd", lambda e, wa=wa, g=g: e.dma_start(out=wa[:].rearrange("p k n -> p (k n)"), in_=ffn_w_in[l, f, g], max_dma_last_dim=8192),
                  writes=[wab], semkey=ska)
            K.dma("gpsimd", lambda e, wb=wb, g=g: e.dma_start(out=wb[:, 0:2048], in_=ffn_w_out[l, f, g], max_dma_last_dim=8192),
                  writes=[wbb], semkey=skb)
            if stream_mod and g < 10:
                nper = 2 if ffn_gi[0] <= 10 else 1
                for i in range(nper):
                    if pending_mod:
                        slot = 2 * (g % 2) + i
                        (ml, ci) = pending_mod.pop(0)
                        mod_dma(ml, ci, WM[slot], WMB[slot], f"wm{slot}")
                        inflight_mod.append((ml, ci, slot))
            hd = hid[g % 2]
            hdb = HIDB[g % 2]
            for jj in range(2):
                for tt in range(3):
                    t0, w, cnd = TT[tt]
                    pg, pgb = bank()
                    pu, pub = bank()
                    for k in range(8):
                        T(lambda e, pg=pg, wa=wa, k=k, jj=jj, t0=t0, w=w: e.matmul(
                            pg[:, 0:w], lhsT=wa[:, k, jj * 128:(jj + 1) * 128], rhs=hbuf[:, k, t0:t0 + w], start=(k == 0), stop=(k == 7)),
                            [wab, HB[tt]], [pgb])
                    for k in range(8):
                        T(lambda e, pu=pu, wa=wa, k=k, jj=jj, t0=t0, w=w: e.matmul(
                            pu[:, 0:w], lhsT=wa[:, k, 256 + jj * 128:256 + (jj + 1) * 128], rhs=hbuf[:, k, t0:t0 + w], start=(k == 0), stop=(k == 7)),
                            [wab, HB[tt]], [pub])
                    sg, sgb = sgp.next()
                    A(lambda e, sg=sg, pg=pg, w=w: e.activation(out=sg[:, 0:w], in_=pg[:, 0:w], func=AF.Silu), [pgb], [sgb])
                    V(lambda e, sg=sg, pu=pu, hd=hd, jj=jj, t0=t0, w=w: e.tensor_tensor(
                        out=hd[:, jj, t0:t0 + w], in0=sg[:, 0:w], in1=pu[:, 0:w], op=ALU.mult), [sgb, pub], [hdb[tt]])
            wbv = wb[:, 0:2048].rearrange("p (j n) -> p j n", j=2)
            for m in range(8):
                for tt in range(3):
                    t0, w, cnd = TT[tt]
                    po, pob = bank()
                    for jj in range(2):
                        T(lambda e, po=po, wbv=wbv, jj=jj, m=m, hd=hd, t0=t0, w=w: e.matmul(
                            po[:, 0:w], lhsT=wbv[:, jj, m * 128:(m + 1) * 128], rhs=hd[:, jj, t0:t0 + w], start=(jj == 0), stop=(jj == 1)),
                            [wbb, hdb[tt]], [pob])
                    resid_update(po, pob, m, tt, modc(l, cnd, 3 * j + 2, m))
        if stream_mod:
            while inflight_mod:
                (ml, ci, slot) = inflight_mod.pop(0)
                mod_chunk(ml, ci, WM[slot], WMB[slot], f"wm{slot}", dma=False)
                mod_mark(ml, ci)
        K.barrier()
        K.release(mk)

    def mixer(l):
        mk_all = K.mark()
        wmi = mix_w_in[l]
        wmo = mix_w_out[l]
        cvx = {}

        def load_wa(col0, ncols):
            wa, wab, sk = nextA()
            K.dma("gpsimd", lambda e: e.dma_start(out=wa[:, :, 0:ncols],
                                                  in_=wmi[:, col0:col0 + ncols].rearrange("(k p) n -> p k n", p=128)),
                  writes=[wab], semkey=sk)
            return wa, wab

        def proj_fm(pb, pbb, wa, wab, c0, M, hbuf, HB, tt):
            t0, w, cnd = TT[tt]
            for k in range(8):
                T(lambda e, k=k: e.matmul(pb[0:M, 0:w], lhsT=wa[:, k, c0:c0 + M], rhs=hbuf[:, k, t0:t0 + w],
                                          start=(k == 0), stop=(k == 7)), [wab, HB[tt]], [pbb])

        yreg = []

        def out_proj(ychunks, YB, wrow0, kp):
            yreg.append((ychunks, YB, wrow0, kp))

        def out_proj_all():
            slots = []
            for (ychunks, YB, wrow0, kp) in yreg:
                nk_ = len(ychunks)
                if len(slots) % 2 == 0:
                    wt_, wtb_, sk = nextB()
                    flat = wt_[:, :]
                else:
                    wt_, wtb_, sk = nextA()
                    flat = wt_[:].rearrange("p k n -> p (k n)")
                wv = flat[0:kp, 0:nk_ * 1024].rearrange("p (j n) -> p j n", j=nk_)
                K.dma("gpsimd", lambda e, wv=wv, wrow0=wrow0, nk_=nk_, kp=kp: e.dma_start(
                    out=wv, in_=wmo[wrow0:wrow0 + nk_ * kp, :].rearrange("(j p) n -> p j n", p=kp)), writes=[wtb_], semkey=sk)
                slots.append((wv, wtb_))
            total = sum(len(y[0]) for y in yreg)
            for m in range(8):
                for tt in range(3):
                    t0, w, cnd = TT[tt]
                    po, pob = bank()
                    i = 0
                    for (ychunks, YB, wrow0, kp), (wv, wtb_) in zip(yreg, slots):
                        for jj, yc in enumerate(ychunks):
                            T(lambda e, wv=wv, jj=jj, yc=yc, i=i: e.matmul(po[:, 0:w], lhsT=wv[:, jj, m * 128:(m + 1) * 128],
                                                                        rhs=yc[:, t0:t0 + w], start=(i == 0), stop=(i == total - 1)),
                              [wtb_] + YB, [pob])
                            i += 1
                    resid_update(po, pob, m, tt, modc(l, cnd, 5, m))

        def conv_chunk(raw, rawb, P, pc0, out_ap, outb, silu):
            hl = ccol("hl")
            hr = ccol("hr")
            V(lambda e: e.tensor_tensor(out=raw[0:P, 1:5, 0:1], in0=raw[0:P, 0:4, 256:257], in1=hl[0:P, 1:5].unsqueeze(2), op=ALU.mult),
              [rawb, CONSTB], [rawb])
            V(lambda e: e.tensor_tensor(out=raw[0:P, 0:4, 257:258], in0=raw[0:P, 1:5, 1:2], in1=hr[0:P, 0:4].unsqueeze(2), op=ALU.mult),
              [rawb, CONSTB], [rawb])
            acc, accb = cvx["convacc"].next()
            accv = acc[0:P, :].rearrange("p (b t) -> p b t", t=256)
            w0 = ptab[0:P, pc0:pc0 + 1]
            w1 = ptab[0:P, pc0 + 1:pc0 + 2]
            w2 = ptab[0:P, pc0 + 2:pc0 + 3]
            bb = ptab[0:P, pc0 + 3:pc0 + 4]
            V(lambda e: e.tensor_scalar(out=accv, in0=raw[0:P, :, 1:257], scalar1=w1, scalar2=None, op0=ALU.mult), [rawb, CONSTB], [accb])
            V(lambda e: e.scalar_tensor_tensor(out=accv, in0=raw[0:P, :, 0:256], scalar=w0, op0=ALU.mult, in1=accv, op1=ALU.add),
              [rawb, CONSTB, accb], [accb])
            V(lambda e: e.scalar_tensor_tensor(out=accv, in0=raw[0:P, :, 2:258], scalar=w2, op0=ALU.mult, in1=accv, op1=ALU.add),
              [rawb, CONSTB, accb], [accb])
            A(lambda e: e.activation(out=out_ap, in_=acc[0:P, :], func=(AF.Silu if silu else AF.Identity), bias=bb), [accb, CONSTB], [outb])

        def raw_fill(raw, rawb, P, pb, pbb, tt):
            t0, w, cnd = TT[tt]
            b0 = t0 // 256
            nb = w // 256
            A(lambda e: e.activation(out=raw[0:P, b0:b0 + nb, 1:257], in_=pb[0:P, 0:w].rearrange("p (b t) -> p b t", t=256), func=AF.Copy),
              [pbb], [rawb])

        hbuf_m = K.sb("hmix", [128, 8, NT], BF16)
        HB_m = bufs(3, "hmix")
        mkp = K.mark()
        pool_m = {"sq": Rot(K, "sq", [128, 512], BF16, 3), "rs": Rot(K, "rs", [128, 512], F32, 2),
                  "tm": Rot(K, "tm", [128, 512], F32, 2)}
        make_h(l, 1, hbuf_m, HB_m, rs_pool=pool_m)
        K.barrier()
        K.release(mkp)

        def with_h(fn, conv=False):
            mk = K.mark()
            if conv:
                cvx["convacc"] = Rot(K, "cacc", [128, NT], F32, 1)
                raws = Rot(K, "raw", [128, 5, 258], F32, 2)
                cvx["raws"] = raws
                for i in range(2):
                    V(lambda e, i=i: e.memset(raws.t[i][:], 0.0), [], [raws.b[i]])
            fn(hbuf_m, HB_m)
            K.barrier()
            K.release(mk)

        def ssd(szb, SZB):
            mk = K.mark()
            xsf = K.sb("xsf", [64, 4, NT], BF16)
            XSF = Buf("xsf")
            bcf = K.sb("bcf", [128, 4, NT], BF16)
            BCF = Buf("bcf")
            dtr = K.sb("dtr", [128, 10, 8])
            dtt = K.sb("dtt", [128, 10, 8])
            lat = K.sb("lat", [128, 10, 8])
            DTB = Buf("dt")

            def inproj(hbuf, HB):
                wa0, wab0 = load_wa(0, 512)
                wa1, wab1 = load_wa(512, 512)
                for hh in range(4):
                    for tt in range(3):
                        t0, w, cnd = TT[tt]
                        pb, pbb = bank()
                        proj_fm(pb, pbb, wa0, wab0, hh * 64, 64, hbuf, HB, tt)
                        A(lambda e, pb=pb, hh=hh, t0=t0, w=w: e.activation(out=szb[:, hh, t0:t0 + w], in_=pb[0:64, 0:w], func=AF.Silu), [pbb], [SZB])
                for q in range(8):
                    raw, rawb = cvx["raws"].next()
                    P = 64 if q < 4 else 128
                    for tt in range(3):
                        pb, pbb = bank()
                        if q < 4:
                            proj_fm(pb, pbb, wa0, wab0, 256 + q * 64, 64, hbuf, HB, tt)
                        else:
                            proj_fm(pb, pbb, wa1, wab1, (q - 4) * 128, 128, hbuf, HB, tt)
                        raw_fill(raw, rawb, P, pb, pbb, tt)
                    pc0 = PT["ssd_conv"][0] + (l * 8 + q) * 4
                    if q < 4:
                        conv_chunk(raw, rawb, 64, pc0, xsf[:, q, :], XSF, True)
                    else:
                        conv_chunk(raw, rawb, 128, pc0, bcf[:, q - 4, :], BCF, True)
                wdt = K.sb("wdt", [128, 8, 8], BF16)
                WDT = Buf("wdt")
                K.dma("gpsimd", lambda e: e.dma_start(out=wdt[:], in_=wmi[:, 1024:1032].rearrange("(k p) n -> p k n", p=128)),
                      writes=[WDT], semkey="wdt")
                pb, pbb = bank()
                for tc in range(NCH):
                    tt = 0 if tc < 4 else (1 if tc < 8 else 2)
                    for k in range(8):
                        T(lambda e, tc=tc, k=k: e.matmul(pb[:, tc * 8:tc * 8 + 8], lhsT=hbuf[:, k, tc * 128:(tc + 1) * 128], rhs=wdt[:, k, :],
                                                         start=(k == 0), stop=(k == 7)), [HB[tt], WDT], [pbb])
                V(lambda e: e.tensor_tensor(out=dtr[:].rearrange("p a b -> p (a b)"), in0=pb[:, 0:80], in1=fcol("dt_bias", l * 80, 80), op=ALU.add),
                  [pbb, CONSTB], [DTB])
            with_h(inproj, conv=True)
            A(lambda e: e.activation(out=dtt[:], in_=dtr[:], func=AF.Exp), [DTB], [DTB])
            A(lambda e: e.activation(out=dtt[:], in_=dtt[:], func=AF.Ln, bias=1.0), [DTB], [DTB])
            A(lambda e: e.activation(out=dtr[:].rearrange("p a b -> p (a b)"), in_=fcol("a_log", l * 80, 80), func=AF.Exp), [DTB, CONSTB], [DTB])
            V(lambda e: e.scalar_tensor_tensor(out=lat[:], in0=dtr[:], scalar=-1.0, op0=ALU.mult, in1=dtt[:], op1=ALU.mult), [DTB], [DTB])
            xbtm = K.sb("xbtm", [128, NCH, 512], BF16)
            XBT = bufs(NCH, "xbtm")
            for tc in range(NCH):
                pb, pbb = bank()
                pbv = pb[:].bitcast(BF16)
                for hh in range(4):
                    T(lambda e, hh=hh, tc=tc: e.transpose(pbv[:, hh * 64:(hh + 1) * 64], xsf[:, hh, tc * 128:(tc + 1) * 128], ident_b[0:64, 0:64]),
                      [XSF, CONSTB], [pbb])
                for g in range(2):
                    T(lambda e, g=g, tc=tc: e.transpose(pbv[:, 256 + g * 128:256 + (g + 1) * 128], bcf[:, g, tc * 128:(tc + 1) * 128], ident_b),
                      [BCF, CONSTB], [pbb])
                A(lambda e, tc=tc, pbv=pbv: e.activation(out=xbtm[:, tc, :], in_=pbv[:, 0:512], func=AF.Copy), [pbb], [XBT[tc]])
            cst = K.sb("cst", [128, NCH, 16])
            wBt = K.sb("wBt", [128, NCH, 8])
            edt = K.sb("edt", [128, NCH, 8])
            pb, pbb = bank()
            for tc in range(NCH):
                T(lambda e, tc=tc: e.matmul(pb[:, tc * 16:tc * 16 + 4], lhsT=trif, rhs=lat[:, tc, 0:4], start=True, stop=True), [DTB, CONSTB], [pbb])
                T(lambda e, tc=tc: e.matmul(pb[:, tc * 16 + 4:tc * 16 + 8], lhsT=trib, rhs=lat[:, tc, 4:8], start=True, stop=True), [DTB, CONSTB], [pbb])
                T(lambda e, tc=tc: e.matmul(pb[:, tc * 16 + 8:tc * 16 + 16], lhsT=ones_f, rhs=lat[:, tc, 0:8], start=True, stop=True), [DTB, CONSTB], [pbb])
            V(lambda e: e.tensor_copy(cst[:].rearrange("p a b -> p (a b)"), pb[:, 0:160]), [pbb], [DTB])
            V(lambda e: e.tensor_tensor(out=wBt[:], in0=cst[:, :, 8:16], in1=cst[:, :, 0:8], op=ALU.subtract), [DTB], [DTB])
            A(lambda e: e.activation(out=wBt[:], in_=wBt[:], func=AF.Exp), [DTB], [DTB])
            V(lambda e: e.tensor_tensor(out=wBt[:], in0=wBt[:], in1=dtt[:], op=ALU.mult), [DTB], [DTB])
            A(lambda e: e.activation(out=edt[:], in_=cst[:, :, 8:16], func=AF.Exp), [DTB], [DTB])
            S32 = K.sb("S32", [128, 8, 64])
            SB_ = bufs(8, "S32")
            SP = K.sb("SP", [128, NCH, 8, 64], BF16)
            SPB = [bufs(8, f"SP{c}_") for c in range(NCH)]
            mk_st = K.mark()
            s0t = K.sb("s0t", [128, 40, 64])
            S0B = Buf("s0")
            K.dma("sync", lambda e: e.dma_start(out=s0t[:], in_=s0ssd[:, l * 40:(l + 1) * 40, :]), writes=[S0B], semkey="lds0")
            V(lambda e: e.memset(S32[:], 0.0), [], SB_)
            hl = ccol("hl")
            hr = ccol("hr")
            bsp = Rot(K, "bs", [128, 128], BF16, 12)

            def state_chain(d, hh):
                col = d * 4 + hh
                g = hh // 2
                for k in range(NCH):
                    c = k if d == 0 else NCH - 1 - k
                    blk = c // 2
                    first = (c % 2 == 0) if d == 0 else (c % 2 == 1)
                    sidx = (blk * 2 + d) * 4 + hh
                    if first:
                        fl = hl[:, blk:blk + 1] if d == 0 else hr[:, blk:blk + 1]
                        V(lambda e: e.scalar_tensor_tensor(out=S32[:, col, :], in0=S32[:, col, :], scalar=fl, op0=ALU.mult, in1=s0t[:, sidx, :], op1=ALU.add),
                          [SB_[col], S0B, CONSTB], [SB_[col]])
                    A(lambda e: e.activation(out=SP[:, c, col, :], in_=S32[:, col, :], func=AF.Copy), [SB_[col]], [SPB[c][col]])
                    bs, bsb = bsp.next()
                    V(lambda e: e.tensor_scalar(out=bs[:], in0=xbtm[:, c, 256 + g * 128:256 + (g + 1) * 128], scalar1=wBt[:, c, col:col + 1], scalar2=None, op0=ALU.mult),
                      [XBT[c], DTB], [bsb])
                    yield
                    pb, pbb = pr(64)
                    T(lambda e: e.matmul(pb[:, 0:64], lhsT=bs[:], rhs=xbtm[:, c, hh * 64:(hh + 1) * 64], start=True, stop=True), [bsb, XBT[c]], [pbb])
                    yield
                    V(lambda e: e.scalar_tensor_tensor(out=S32[:, col, :], in0=S32[:, col, :], scalar=edt[:, c, col:col + 1], op0=ALU.mult, in1=pb[:, 0:64], op1=ALU.add),
                      [SB_[col], DTB, pbb], [SB_[col]])
                    if not first:
                        oidx = l * 40 + sidx
                        sslot = sstp.i
                        stg, stgb = sstp.next()
                        A(lambda e: e.activation(out=stg[:], in_=S32[:, col, :], func=AF.Copy), [SB_[col]], [stgb])
                        K.dma("sync", lambda e: e.dma_start(out=nssd_o[:, oidx, :], in_=stg[:]), reads=[stgb], semkey=f"stS{sslot}")
                    yield
            sstp = Rot(K, "sstg", [128, 64], F32, 8)
            interleave([state_chain(d, hh) for d in range(2) for hh in range(4)])
            K.barrier()
            K.release(mk_st)

            dsk = pcol("ssd_d", l * 4, 4, rows=64)
            nws = pcol("ssd_nw", l * 4, 4, rows=64)
            ygp = Rot(K, "yg", [64, 4, 128], F32, 2)
            wtp = Rot(K, "wt", [128, 128], F32, 16)
            sgp2 = Rot(K, "sg2", [128, 128], F32, 16)
            csp = Rot(K, "cs", [128, 128], BF16, 16)
            sqp = Rot(K, "sq4", [64, 128], BF16, 8)

            def ssd_chain(c, hh, d, psc, pscb, res):
                cs = slice(c * 128, (c + 1) * 128)
                col = d * 4 + hh
                g = hh // 2
                U = trif if d == 0 else trib
                NB = ccol("nbf") if d == 0 else ccol("nbb")
                wt, wtb = wtp.next()
                G(lambda e: e.tensor_scalar(out=wt[:], in0=U, scalar1=lat[:, c, col:col + 1], scalar2=0.0, op0=ALU.mult, op1=ALU.add), [CONSTB, DTB], [wtb])
                yield
                pc, pcb = pr(128)
                T(lambda e: e.matmul(pc[:, 0:128], lhsT=ones_f, rhs=wt[:], start=True, stop=True), [wtb, CONSTB], [pcb])
                yield
                sg, sgb = sgp2.next()
                V(lambda e: e.scalar_tensor_tensor(out=sg[:], in0=pc[:, 0:128], scalar=cst[:, c, col:col + 1], op0=ALU.subtract, in1=NB, op1=ALU.add),
                  [pcb, DTB, CONSTB], [sgb])
                yield
                ec, ecb = wt, wtb
                A(lambda e: e.activation(out=ec[:], in_=pc[:, 0:128], func=AF.Exp), [pcb], [ecb])
                yield
                A(lambda e: e.activation(out=sg[:], in_=sg[:], func=AF.Exp), [sgb], [sgb])
                csb, csbb = csp.next()
                G(lambda e: e.tensor_tensor(out=csb[:], in0=bcf[:, 2 + g, cs], in1=ec[:], op=ALU.mult), [BCF, ecb], [csbb])
                yield
                yield
                st, stb = wt[:].bitcast(BF16)[:, 0:128], wtb
                V(lambda e: e.scalar_tensor_tensor(out=st, in0=psc[:, g * 128:(g + 1) * 128], scalar=dtt[:, c, col:col + 1], op0=ALU.mult, in1=sg[:], op1=ALU.mult),
                  [pscb, sgb, DTB], [stb])
                res[(hh, d)] = (st, stb, csb, csbb)
                yield

            def ssd_chunk(c):
                cs = slice(c * 128, (c + 1) * 128)
                psc, pscb = pr(256)
                for g in range(2):
                    T(lambda e: e.matmul(psc[:, g * 128:(g + 1) * 128], lhsT=bcf[:, g, cs], rhs=bcf[:, 2 + g, cs], start=True, stop=True), [BCF], [pscb])
                res = {}
                chains = [ssd_chain(c, hh, d, psc, pscb, res) for hh in range(4) for d in range(2)]
                while chains:
                    nxt = []
                    for gch in chains:
                        try:
                            next(gch)
                            nxt.append(gch)
                        except StopIteration:
                            pass
                    chains = nxt
                    yield
                yg, ygb = ygp.next()
                pys = []
                for hh in range(4):
                    py, pyb = pr(128)
                    pys.append((py, pyb))
                    for d in range(2):
                        col = d * 4 + hh
                        st, stb, csb, csbb = res[(hh, d)]
                        T(lambda e: e.matmul(py[0:64, 0:128], lhsT=xbtm[:, c, hh * 64:(hh + 1) * 64], rhs=st, start=(d == 0), stop=False), [XBT[c], stb], [pyb])
                        T(lambda e: e.matmul(py[0:64, 0:128], lhsT=SP[:, c, col, :], rhs=csb[:], start=False, stop=(d == 1)), [SPB[c][col], csbb], [pyb])
                yield
                sqs = []
                for hh in range(4):
                    py, pyb = pys[hh]
                    V(lambda e: e.scalar_tensor_tensor(out=yg[:, hh, :], in0=xsf[:, hh, cs], scalar=dsk[:, hh:hh + 1], op0=ALU.mult, in1=py[0:64, 0:128], op1=ALU.add),
                      [XSF, CONSTB, pyb], [ygb])
                    V(lambda e: e.tensor_tensor(out=yg[:, hh, :], in0=yg[:, hh, :], in1=szb[:, hh, cs], op=ALU.mult), [ygb, SZB], [ygb])
                    sq, sqb = sqp.next()
                    A(lambda e: e.activation(out=sq[:], in_=yg[:, hh, :], func=AF.Square), [ygb], [sqb])
                    sqs.append((sq, sqb))
                    yield
                pq, pqb = pr(128)
                for hh in range(4):
                    sq, sqb = sqs[hh]
                    T(lambda e: e.matmul(pq[0:64, 0:128], lhsT=ones_b[0:64, 0:64], rhs=sq[:], start=(hh == 0), stop=(hh == 3)), [sqb, CONSTB], [pqb])
                yield
                rs, rsb = rsp2.next()
                A(lambda e: e.activation(out=rs[:], in_=pq[0:64, 0:128], func=AF.Sqrt, bias=EPS, scale=1.0 / 256), [pqb], [rsb])
                yield
                V(lambda e: e.reciprocal(rs[:], rs[:]), [rsb], [rsb])
                yield
                for hh in range(4):
                    V(lambda e: e.scalar_tensor_tensor(out=szb[:, hh, cs], in0=yg[:, hh, :], scalar=nws[:, hh:hh + 1], op0=ALU.mult, in1=rs[:], op1=ALU.mult),
                      [ygb, rsb, CONSTB], [SZB])
                    yield
            rsp2 = Rot(K, "rs2", [64, 128], F32, 2)
            for c0 in range(0, NCH, 2):
                interleave([ssd_chunk(c0), ssd_chunk(c0 + 1)])
            out_proj([szb[:, hh, :] for hh in range(4)], [SZB], 0, 64)
            K.barrier()
            K.release(mk)

        def ret(sgf, SGF):
            mk = K.mark()
            qf = K.sb("qf", [64, 4, NT], BF16)
            kf = K.sb("kf", [64, 4, NT], BF16)
            kvt = K.sb("kvt", [128, NCH, 256], BF16)
            QF, KF = Buf("qf"), Buf("kf")
            KVT = bufs(NCH, "kvt")
            KST = Buf("kst")
            lg = K.sb("lg", [128, 8])
            LG = Buf("lg")
            A(lambda e: e.activation(out=lg[:], in_=fcol("ret_logit", l * 8, 8), func=AF.Exp, scale=-1.0), [CONSTB], [LG])
            A(lambda e: e.activation(out=lg[:], in_=lg[:], func=AF.Ln, bias=1.0), [LG], [LG])
            V(lambda e: e.tensor_scalar(out=lg[:], in0=lg[:], scalar1=-1.0, scalar2=None, op0=ALU.mult), [LG], [LG])
            Tl = K.sb("Tl", [128, 8])
            G128 = K.sb("G128", [128, 8])
            RC = Buf("retc")
            for hh in range(4):
                A(lambda e, hh=hh: e.activation(out=Tl[:, hh:hh + 1], in_=ccol("tailf"), func=AF.Exp, scale=lg[:, hh:hh + 1]), [LG, CONSTB], [RC])
                A(lambda e, hh=hh: e.activation(out=Tl[:, 4 + hh:5 + hh], in_=ccol("tailb"), func=AF.Exp, scale=lg[:, 4 + hh:5 + hh]), [LG, CONSTB], [RC])
            V(lambda e: e.tensor_scalar(out=Tl[:], in0=Tl[:], scalar1=0.125, scalar2=None, op0=ALU.mult), [RC], [RC])
            A(lambda e: e.activation(out=G128[:], in_=lg[:], func=AF.Exp, scale=128.0), [LG], [RC])
            S32 = K.sb("R32", [64, 8, 64])
            SB_ = bufs(8, "R32")
            SP = K.sb("RSP", [64, NCH, 8, 64], BF16)
            SPB = [bufs(8, f"RSP{c}_") for c in range(NCH)]
            mk_k = K.mark()
            kst = K.sb("kst", [128, 2, NCH, 256], BF16)

            def inproj(hbuf, HB):
                wa0, wab0 = load_wa(1800, 512)
                wa1, wab1 = load_wa(2312, 512)
                for hh in range(4):
                    for tt in range(3):
                        t0, w, cnd = TT[tt]
                        pb, pbb = bank()
                        proj_fm(pb, pbb, wa0, wab0, hh * 64, 64, hbuf, HB, tt)
                        A(lambda e, pb=pb, hh=hh, t0=t0, w=w: e.activation(out=qf[:, hh, t0:t0 + w], in_=pb[0:64, 0:w], func=AF.Copy), [pbb], [QF])
                        pb, pbb = bank()
                        proj_fm(pb, pbb, wa0, wab0, 256 + hh * 64, 64, hbuf, HB, tt)
                        A(lambda e, pb=pb, hh=hh, t0=t0, w=w: e.activation(out=kf[:, hh, t0:t0 + w], in_=pb[0:64, 0:w], func=AF.Copy, scale=0.125), [pbb], [KF])
                        pb, pbb = bank()
                        proj_fm(pb, pbb, wa1, wab1, 256 + hh * 64, 64, hbuf, HB, tt)
                        A(lambda e, pb=pb, hh=hh, t0=t0, w=w: e.activation(out=sgf[:, hh, t0:t0 + w], in_=pb[0:64, 0:w], func=AF.Silu), [pbb], [SGF])
                for tc in range(NCH):
                    tt = 0 if tc < 4 else (1 if tc < 8 else 2)
                    pb, pbb = bank()
                    for k in range(8):
                        T(lambda e, pb=pb, tc=tc, k=k: e.matmul(pb[:, 0:256], lhsT=hbuf[:, k, tc * 128:(tc + 1) * 128], rhs=wa0[:, k, 256:512],
                                                                 start=(k == 0), stop=(k == 7)), [HB[tt], wab0], [pbb])
                    for k in range(8):
                        T(lambda e, pb=pb, tc=tc, k=k: e.matmul(pb[:, 256:512], lhsT=hbuf[:, k, tc * 128:(tc + 1) * 128], rhs=wa1[:, k, 0:256],
                                                                 start=(k == 0), stop=(k == 7)), [HB[tt], wab1], [pbb])
                    for d in range(2):
                        V(lambda e, pb=pb, tc=tc, d=d: e.tensor_tensor(out=kst[:, d, tc, :].rearrange("p (h n) -> p h n", h=4),
                                                                       in0=pb[:, 0:256].rearrange("p (h n) -> p h n", h=4),
                                                                       in1=Tl[:, d * 4:(d + 1) * 4].unsqueeze(2).broadcast_to([128, 4, 64]), op=ALU.mult),
                          [pbb, RC], [KST])
                    A(lambda e, pb=pb, tc=tc: e.activation(out=kvt[:, tc, :], in_=pb[:, 256:512], func=AF.Copy), [pbb], [KVT[tc]])
            with_h(inproj)
            mk_st = K.mark()
            s0t = K.sb("rs0t", [64, 40, 64])
            S0B = Buf("rs0")
            K.dma("sync", lambda e: e.dma_start(out=s0t[:], in_=s0ret[:, l * 40:(l + 1) * 40, :]), writes=[S0B], semkey="lds0")
            V(lambda e: e.memset(S32[:], 0.0), [], SB_)
            hl = ccol("hl")
            hr = ccol("hr")
            def rstate_chain(d, hh):
                col = d * 4 + hh
                for k in range(NCH):
                    c = k if d == 0 else NCH - 1 - k
                    blk = c // 2
                    first = (c % 2 == 0) if d == 0 else (c % 2 == 1)
                    sidx = (blk * 2 + d) * 4 + hh
                    if first:
                        fl = hl[0:64, blk:blk + 1] if d == 0 else hr[0:64, blk:blk + 1]
                        V(lambda e: e.scalar_tensor_tensor(out=S32[:, col, :], in0=S32[:, col, :], scalar=fl, op0=ALU.mult, in1=s0t[:, sidx, :], op1=ALU.add),
                          [SB_[col], S0B, CONSTB], [SB_[col]])
                    A(lambda e: e.activation(out=SP[:, c, col, :], in_=S32[:, col, :], func=AF.Copy), [SB_[col]], [SPB[c][col]])
                    yield
                    pb, pbb = pr(64)
                    T(lambda e: e.matmul(pb[0:64, 0:64], lhsT=kst[:, d, c, hh * 64:(hh + 1) * 64], rhs=kvt[:, c, hh * 64:(hh + 1) * 64],
                                         start=True, stop=True), [KST, KVT[c]], [pbb])
                    yield
                    V(lambda e: e.scalar_tensor_tensor(out=S32[:, col, :], in0=S32[:, col, :], scalar=G128[0:64, col:col + 1], op0=ALU.mult, in1=pb[0:64, 0:64], op1=ALU.add),
                      [SB_[col], RC, pbb], [SB_[col]])
                    if not first:
                        oidx = l * 40 + sidx
                        sslot = rstp.i
                        stg, stgb = rstp.next()
                        A(lambda e: e.activation(out=stg[:], in_=S32[:, col, :], func=AF.Copy), [SB_[col]], [stgb])
                        K.dma("sync", lambda e: e.dma_start(out=nret_o[:, oidx, :], in_=stg[:]), reads=[stgb], semkey=f"stR{sslot}")
                    yield
            rstp = Rot(K, "rstg", [64, 64], F32, 8)
            interleave([rstate_chain(d, hh) for d in range(2) for hh in range(4)])
            K.barrier()
            K.release(mk_k)

            gnw = pcol("ret_gn", l * 4, 4, rows=64)
            Dm = K.sb("Dm", [128, 4, 128])
            Ef = K.sb("Ef", [64, 8, 128])
            tmpf = Rot(K, "tmpf", [128, 128], F32, 2)
            for hh in range(4):
                t1, t1b = tmpf.next()
                A(lambda e, t1=t1, hh=hh: e.activation(out=t1[:], in_=ccol("relu_f"), func=AF.Exp, scale=lg[:, hh:hh + 1]), [LG, CONSTB], [t1b])
                V(lambda e, t1=t1, hh=hh: e.tensor_tensor(out=Dm[:, hh, :], in0=t1[:], in1=trif, op=ALU.mult), [t1b, CONSTB], [RC])
                t2, t2b = tmpf.next()
                A(lambda e, t2=t2, hh=hh: e.activation(out=t2[:], in_=ccol("relu_b"), func=AF.Exp, scale=lg[:, 4 + hh:5 + hh]), [LG, CONSTB], [t2b])
                V(lambda e, t2=t2: e.tensor_tensor(out=t2[:], in0=t2[:], in1=trib, op=ALU.mult), [t2b, CONSTB], [t2b])
                V(lambda e, t2=t2, hh=hh: e.tensor_tensor(out=Dm[:, hh, :], in0=Dm[:, hh, :], in1=t2[:], op=ALU.add), [t2b, RC], [RC])
                A(lambda e, hh=hh: e.activation(out=Ef[:, hh, :], in_=ccol("ip1", rows=64), func=AF.Exp, scale=lg[0:64, hh:hh + 1]), [LG, CONSTB], [RC])
                A(lambda e, hh=hh: e.activation(out=Ef[:, 4 + hh, :], in_=ccol("rmi", rows=64), func=AF.Exp, scale=lg[0:64, 4 + hh:5 + hh]), [LG, CONSTB], [RC])
            qsp = Rot(K, "qs4", [64, 2, 4, 128], BF16, 2)
            st4p = Rot(K, "st4", [128, 4, 128], BF16, 2)
            yvp = Rot(K, "yv4", [64, 4, 128], F32, 2)
            sq4p = Rot(K, "sq4", [64, 4, 128], F32, 2)

            def ret_chunk(c):
                cs = slice(c * 128, (c + 1) * 128)
                ps_, psb = bank()
                for hh in range(4):
                    T(lambda e: e.matmul(ps_[:, hh * 128:(hh + 1) * 128], lhsT=kf[:, hh, cs], rhs=qf[:, hh, cs], start=True, stop=True), [KF, QF], [psb])
                yield
                st, stb = st4p.next()
                V(lambda e: e.tensor_tensor(out=st[:], in0=ps_[:, 0:512].rearrange("p (h t) -> p h t", h=4), in1=Dm[:], op=ALU.mult), [psb, RC], [stb])
                qs, qsb = qsp.next()
                V(lambda e: e.tensor_tensor(out=qs[:], in0=qf[:, :, cs].unsqueeze(1).broadcast_to([64, 2, 4, 128]),
                                            in1=Ef[:].rearrange("p (d h) t -> p d h t", d=2), op=ALU.mult), [QF, RC], [qsb])
                yield
                py, pyb = bank()
                for hh in range(4):
                    T(lambda e: e.matmul(py[0:64, hh * 128:(hh + 1) * 128], lhsT=kvt[:, c, hh * 64:(hh + 1) * 64], rhs=st[:, hh, :],
                                         start=True, stop=False), [KVT[c], stb], [pyb])
                    T(lambda e: e.matmul(py[0:64, hh * 128:(hh + 1) * 128], lhsT=SP[:, c, hh, :], rhs=qs[:, 0, hh, :], start=False, stop=False),
                      [SPB[c][hh], qsb], [pyb])
                    T(lambda e: e.matmul(py[0:64, hh * 128:(hh + 1) * 128], lhsT=SP[:, c, 4 + hh, :], rhs=qs[:, 1, hh, :], start=False, stop=True),
                      [SPB[c][4 + hh], qsb], [pyb])
                yield
                yv, yvb = yvp.next()
                yvf = yv[:].rearrange("p h t -> p (h t)")
                V(lambda e: e.tensor_copy(yvf, py[0:64, 0:512]), [pyb], [yvb])
                yield
                pm, pmb = bank()
                T(lambda e: e.matmul(pm[0:64, 0:512], lhsT=ones_f[0:64, 0:64], rhs=yvf, start=True, stop=True), [yvb, CONSTB], [pmb])
                yield
                V(lambda e: e.scalar_tensor_tensor(out=yvf, in0=pm[0:64, 0:512], scalar=-1.0 / 64, op0=ALU.mult, in1=yvf, op1=ALU.add), [pmb, yvb], [yvb])
                yield
                sq, sqb = sq4p.next()
                sqf = sq[:].rearrange("p h t -> p (h t)")
                A(lambda e: e.activation(out=sqf, in_=yvf, func=AF.Square), [yvb], [sqb])
                yield
                pv_, pvb = bank()
                T(lambda e: e.matmul(pv_[0:64, 0:512], lhsT=ones_f[0:64, 0:64], rhs=sqf, start=True, stop=True), [sqb, CONSTB], [pvb])
                yield
                A(lambda e: e.activation(out=sqf, in_=pv_[0:64, 0:512], func=AF.Sqrt, bias=EPS, scale=1.0 / 64), [pvb], [sqb])
                yield
                V(lambda e: e.reciprocal(sqf, sqf), [sqb], [sqb])
                yield
                V(lambda e: e.tensor_tensor(out=yvf, in0=yvf, in1=sqf, op=ALU.mult), [yvb, sqb], [yvb])
                V(lambda e: e.tensor_tensor(out=yv[:], in0=yv[:], in1=gnw.unsqueeze(2).broadcast_to([64, 4, 128]), op=ALU.mult), [yvb, CONSTB], [yvb])
                yield
                V(lambda e: e.tensor_tensor(out=sgf[:, :, cs], in0=yv[:], in1=sgf[:, :, cs], op=ALU.mult), [yvb, SGF], [SGF])
                yield
            for c0 in range(0, NCH, 2):
                interleave([ret_chunk(c0), ret_chunk(c0 + 1)])
            out_proj([sgf[:, hh, :] for hh in range(4)], [SGF], 512, 64)
            K.barrier()
            K.release(mk)

        def att(yat, YA):
            mk = K.mark()
            qfm = K.sb("aq", [64, 4, NT], BF16)
            kfm = K.sb("ak", [64, 2, NT], BF16)
            vtm = K.sb("av", [128, NCH, 128], BF16)
            QF, KF = Buf("aq"), Buf("ak")
            VT = bufs(NCH, "av")
            ckf = K.sb("ckf", [64, 2, 512], BF16)
            cvs = K.sb("cvs", [128, 4, 128], BF16)
            CK = Buf("ck")
            K.dma("gpsimd", lambda e: e.dma_start(out=ckf[:], in_=ckT[:, l * 2:l * 2 + 2, :]), writes=[CK], semkey="ldck")
            K.dma("gpsimd", lambda e: e.dma_start(out=cvs[:], in_=cvt[:, l * 4:l * 4 + 4, :]), writes=[CK], semkey="ldck")
            es = K.sb("es", [64, 4])
            A(lambda e: e.activation(out=es[:], in_=fcol("sink", l * 4, 4, rows=64), func=AF.Exp), [CONSTB], [CK])
            mk_in = K.mark()
            qkp = Rot(K, "qk", [128, 512], F32, 3)
            qnp = Rot(K, "qn", [128, 384], F32, 3)
            qbp = Rot(K, "qb", [128, 384], BF16, 3)
            smp = Rot(K, "sm", [128, 8], F32, 3)
            rtp = Rot(K, "rt", [128, 6, 2, 16], F32, 8)
            kvo = Rot(K, "kvo", [128, 256], F32, 3)

            def inproj(hbuf, HB):
                wa, wab = load_wa(2824, 512)

                def in_chunk(tc):
                    tt = 0 if tc < 4 else (1 if tc < 8 else 2)
                    pb, pbb = bank()
                    for k in range(8):
                        T(lambda e: e.matmul(pb[:, 0:512], lhsT=hbuf[:, k, tc * 128:(tc + 1) * 128], rhs=wa[:, k, :], start=(k == 0), stop=(k == 7)),
                          [HB[tt], wab], [pbb])
                    yield
                    qk, qkb = qkp.next()
                    A(lambda e: e.activation(out=qk[:], in_=pb[:, 0:512], func=AF.Copy), [pbb], [qkb])
                    yield
                    qn, qnb = qnp.next()
                    sm, smb = smp.next()
                    A(lambda e: e.activation(out=qn[:], in_=qk[:, 0:384], func=AF.Square), [qkb], [qnb])
                    A(lambda e: e.activation(out=vtm[:, tc, :], in_=qk[:, 384:512], func=AF.Copy), [qkb], [VT[tc]])
                    yield
                    V(lambda e: e.tensor_reduce(out=sm[:, 0:6], in_=qn[:].rearrange("p (h d) -> p h d", d=64), axis=AX.X, op=ALU.add), [qnb], [smb])
                    yield
                    A(lambda e: e.activation(out=sm[:, 0:6], in_=sm[:, 0:6], func=AF.Sqrt, bias=EPS, scale=1.0 / 64), [smb], [smb])
                    yield
                    V(lambda e: e.reciprocal(sm[:, 0:6], sm[:, 0:6]), [smb], [smb])
                    yield
                    V(lambda e: e.tensor_tensor(out=qn[:].rearrange("p (h d) -> p h d", d=64), in0=qk[:, 0:384].rearrange("p (h d) -> p h d", d=64),
                                                in1=sm[:, 0:6].unsqueeze(2).broadcast_to([128, 6, 64]), op=ALU.mult), [qkb, smb], [qnb])
                    yield
                    V(lambda e: e.tensor_tensor(out=qn[:], in0=qn[:], in1=fcol("qkw", l * 384, 384), op=ALU.mult), [qnb, CONSTB], [qnb])
                    yield
                    qv = qn[:].rearrange("p (h a b f) -> p h a b f", h=6, a=2, b=2)
                    cosv = ccol("cos", tc * 32, 32).rearrange("p (a f) -> p a f", a=2).unsqueeze(1).broadcast_to([128, 6, 2, 16])
                    sinv = ccol("sin", tc * 32, 32).rearrange("p (a f) -> p a f", a=2).unsqueeze(1).broadcast_to([128, 6, 2, 16])
                    x1 = qv[:, :, :, 0, :]
                    x2 = qv[:, :, :, 1, :]
                    t1, t1b = rtp.next()
                    t2, t2b = rtp.next()
                    t3, t3b = rtp.next()
                    t4, t4b = rtp.next()
                    V(lambda e: e.tensor_tensor(out=t1[:], in0=x1, in1=cosv, op=ALU.mult), [qnb, CONSTB], [t1b])
                    V(lambda e: e.tensor_tensor(out=t3[:], in0=x1, in1=sinv, op=ALU.mult), [qnb, CONSTB], [t3b])
                    yield
                    V(lambda e: e.tensor_tensor(out=t2[:], in0=x2, in1=sinv, op=ALU.mult), [qnb, CONSTB], [t2b])
                    V(lambda e: e.tensor_tensor(out=t4[:], in0=x2, in1=cosv, op=ALU.mult), [qnb, CONSTB], [t4b])
                    yield
                    V(lambda e: e.tensor_tensor(out=x1, in0=t1[:], in1=t2[:], op=ALU.subtract), [t1b, t2b], [qnb])
                    yield
                    V(lambda e: e.tensor_tensor(out=x2, in0=t3[:], in1=t4[:], op=ALU.add), [t3b, t4b], [qnb])
                    yield
                    kslot = kvo.i
                    ko, kob = kvo.next()
                    V(lambda e: e.tensor_copy(ko[:, 0:128], qn[:, 256:384]), [qnb], [kob])
                    V(lambda e: e.tensor_copy(ko[:, 128:256], qk[:, 384:512]), [qkb], [kob])
                    qb, qbb = qbp.next()
                    A(lambda e: e.activation(out=qb[:], in_=qn[:], func=AF.Copy), [qnb], [qbb])
                    yield
                    K.dma("sync", lambda e: e.dma_start(out=nk_o[l, tc * 128:(tc + 1) * 128, :], in_=ko[:, 0:128]), reads=[kob], semkey=f"stk{kslot}")
                    K.dma("sync", lambda e: e.dma_start(out=nv_o[l, tc * 128:(tc + 1) * 128, :], in_=ko[:, 128:256]), reads=[kob], semkey=f"stk{kslot}")
                    pt, ptb = bank()
                    ptv = pt[:].bitcast(BF16)
                    for hh in range(6):
                        T(lambda e: e.transpose(ptv[0:64, hh * 128:(hh + 1) * 128], qb[:, hh * 64:(hh + 1) * 64], ident_b), [qbb, CONSTB], [ptb])
                    yield
                    V(lambda e: e.tensor_copy(qfm[:, :, tc * 128:(tc + 1) * 128], ptv[0:64, 0:512].rearrange("p (h t) -> p h t", h=4)), [ptb], [QF])
                    V(lambda e: e.tensor_copy(kfm[:, :, tc * 128:(tc + 1) * 128], ptv[0:64, 512:768].rearrange("p (h t) -> p h t", h=2)), [ptb], [KF])
                    yield
                for c0 in range(0, NCH, 2):
                    interleave([in_chunk(c0), in_chunk(c0 + 1)])
            with_h(inproj)
            K.release(mk_in)
            ptp = Rot(K, "pt", [128, 7, 2, 128], BF16, 3)
            rcp = Rot(K, "rc", [64, 2, 128], F32, 3)
            am = bcol("am").rearrange("p (c a t) -> p c a t", c=NCH, a=2)
            cfb = ccol("cfb")

            def att_chain(qc, kv):
                qs = slice(qc * 128, (qc + 1) * 128)
                pc_ = max(qc - 1, 0)
                nc_ = min(qc + 1, NCH - 1)
                qrhs = qfm[:, 2 * kv:2 * kv + 2, qs]
                pa, pab = bank()
                pb_, pbb_ = bank()
                for sc in range(4):
                    dst, dstb = (pa, pab) if sc < 2 else (pb_, pbb_)
                    T(lambda e: e.matmul(dst[:, (sc % 2) * 256:(sc % 2) * 256 + 256], lhsT=ckf[:, kv, sc * 128:(sc + 1) * 128], rhs=qrhs, start=True, stop=True),
                      [CK, QF], [dstb])
                yield
                pt_, ptb_ = ptp.next()
                A(lambda e: e.activation(out=pt_[:, 0:2, :, :], in_=pa[:, 0:512].rearrange("p (c h t) -> p c h t", c=2, h=2), func=AF.Exp,
                                         scale=0.125, bias=cfb[:, qc:qc + 1]), [pab, CONSTB], [ptb_])
                A(lambda e: e.activation(out=pt_[:, 2:4, :, :], in_=pb_[:, 0:512].rearrange("p (c h t) -> p c h t", c=2, h=2), func=AF.Exp,
                                         scale=0.125, bias=cfb[:, qc:qc + 1]), [pbb_, CONSTB], [ptb_])
                pc2, pc2b = bank()
                pd2, pd2b = bank()
                for i, kc in enumerate((pc_, nc_, qc)):
                    dst, dstb = (pc2, pc2b) if i < 2 else (pd2, pd2b)
                    T(lambda e: e.matmul(dst[:, (i % 2) * 256:(i % 2) * 256 + 256], lhsT=kfm[:, kv, kc * 128:(kc + 1) * 128], rhs=qrhs, start=True, stop=True),
                      [KF, QF], [dstb])
                yield
                A(lambda e: e.activation(out=pt_[:, 4:6, :, :], in_=pc2[:, 0:512].rearrange("p (c h t) -> p c h t", c=2, h=2), func=AF.Exp, scale=0.125),
                  [pc2b], [ptb_])
                A(lambda e: e.activation(out=pt_[:, 6, :, :], in_=pd2[:, 0:256].rearrange("p (h t) -> p h t", h=2), func=AF.Exp, scale=0.125),
                  [pd2b], [ptb_])
                yield
                V(lambda e: e.tensor_tensor(out=pt_[:, 4:6, :, :], in0=pt_[:, 4:6, :, :], in1=am[:, qc, :, :].unsqueeze(2).broadcast_to([128, 2, 2, 128]), op=ALU.mult),
                  [ptb_, CONSTB], [ptb_])
                yield
                po, pob = bank()
                vlist = [(cvs[:, sc, kv * 64:(kv + 1) * 64], CK) for sc in range(4)] + \
                        [(vtm[:, kc, kv * 64:(kv + 1) * 64], VT[kc]) for kc in (pc_, nc_, qc)]
                for i, (vap, vb) in enumerate(vlist):
                    T(lambda e: e.matmul(po[0:64, 0:256], lhsT=vap, rhs=pt_[:, i, :, :], start=(i == 0), stop=(i == 6)), [vb, ptb_], [pob])
                for i in range(7):
                    T(lambda e: e.matmul(po[0:64, 256:512], lhsT=ones_b[:, 0:64], rhs=pt_[:, i, :, :], start=(i == 0), stop=(i == 6)), [CONSTB, ptb_], [pob])
                yield
                rc, rcb = rcp.next()
                V(lambda e: e.tensor_tensor(out=rc[:], in0=po[0:64, 256:512].rearrange("p (h t) -> p h t", h=2),
                                            in1=es[:, 2 * kv:2 * kv + 2].unsqueeze(2).broadcast_to([64, 2, 128]), op=ALU.add), [pob, CK], [rcb])
                V(lambda e: e.reciprocal(rc[:], rc[:]), [rcb], [rcb])
                V(lambda e: e.tensor_tensor(out=yat[:, 2 * kv:2 * kv + 2, qs], in0=po[0:64, 0:256].rearrange("p (h t) -> p h t", h=2), in1=rc[:], op=ALU.mult),
                  [pob, rcb], [YA])
                yield
            for qc in range(NCH):
                interleave([att_chain(qc, 0), att_chain(qc, 1)])
            out_proj([yat[:, hh, :] for hh in range(4)], [YA], 768, 64)
            K.barrier()
            K.release(mk)

        def hyena(yhy, YHB):
            mk = K.mark()
            vb_ = K.sb("hv", [128, 2, NT], BF16)
            x1f = K.sb("hx1", [128, 2, NT], BF16)
            x2f = K.sb("hx2", [128, 2, NT], BF16)
            HVB, HX1, HX2 = Buf("hv"), Buf("hx1"), Buf("hx2")

            def inproj(hbuf, HB):
                wa0, wab0 = load_wa(1032, 512)
                wa1, wab1 = load_wa(1544, 256)
                dests = [(vb_, HVB), (vb_, HVB), (x1f, HX1), (x1f, HX1), (x2f, HX2), (x2f, HX2)]
                for q in range(6):
                    raw, rawb = cvx["raws"].next()
                    for tt in range(3):
                        pb, pbb = bank()
                        if q < 4:
                            proj_fm(pb, pbb, wa0, wab0, q * 128, 128, hbuf, HB, tt)
                        else:
                            proj_fm(pb, pbb, wa1, wab1, (q - 4) * 128, 128, hbuf, HB, tt)
                        raw_fill(raw, rawb, 128, pb, pbb, tt)
                    pc0 = PT["hy_conv"][0] + (l * 6 + q) * 4
                    dt_, db_ = dests[q]
                    conv_chunk(raw, rawb, 128, pc0, dt_[:, q % 2, :], db_, False)
            with_h(inproj, conv=True)
            FEB = Buf("feats")
            w3s = K.sb("hw3", [64, 1024])
            K.dma("sync", lambda e: e.dma_start(out=w3s[:], in_=hy_w3[l]), writes=[FEB], semkey="ldf")
            h2 = K.sb("hh2", [64, NT])
            H2B = Buf("h2")
            mk2 = K.mark()
            feats = K.sb("feats", [33, NT])
            K.dma("sync", lambda e: e.dma_start(out=feats[:, 0:1024], in_=featsA_d), writes=[FEB], semkey="ldf")
            K.dma("sync", lambda e: e.dma_start(out=feats[:, 1024:1280], in_=featsB_d), writes=[FEB], semkey="ldf")
            w1s = K.sb("hw1", [33, 64])
            w2s = K.sb("hw2", [64, 64])
            K.dma("sync", lambda e: e.dma_start(out=w1s[:], in_=hy_w1[l]), writes=[FEB], semkey="ldf")
            K.dma("sync", lambda e: e.dma_start(out=w2s[:], in_=hy_w2[l]), writes=[FEB], semkey="ldf")
            hp = pcol("hyp", l * 3, 3, rows=64)
            fb = K.sb("fb", [64, 2])
            V(lambda e: e.tensor_tensor(out=fb[:, 0:1], in0=hp[:, 0:1], in1=hp[:, 1:2], op=ALU.mult), [CONSTB], [FEB])
            V(lambda e: e.tensor_tensor(out=fb[:, 1:2], in0=hp[:, 2:3], in1=hp[:, 1:2], op=ALU.mult), [CONSTB], [FEB])
            h1 = K.sb("hh1", [64, NT])
            H1B = Buf("h1")
            MAGIC = 12582912.0
            argp = Rot(K, "arg", [64, 512], F32, 2)
            nrp = Rot(K, "nr", [64, 512], F32, 2)

            def sin_layer(lhsT, src, srcb, KK, dst, dstb, fbcol):
                for tt in range(3):
                    t0, w, cnd = TT[tt]
                    pb, pbb = bank()
                    T(lambda e, pb=pb, t0=t0, w=w: e.matmul(pb[0:64, 0:w], lhsT=lhsT, rhs=src[0:KK, t0:t0 + w], start=True, stop=True), [FEB, srcb], [pbb])
                    ar, arb = argp.next()
                    nr, nrb = nrp.next()
                    V(lambda e, ar=ar, pb=pb, w=w: e.tensor_scalar(out=ar[:, 0:w], in0=pb[0:64, 0:w], scalar1=hp[:, 1:2], scalar2=fb[:, fbcol:fbcol + 1],
                                                                   op0=ALU.mult, op1=ALU.add), [pbb, CONSTB, FEB], [arb])
                    V(lambda e, ar=ar, nr=nr, w=w: e.tensor_scalar(out=nr[:, 0:w], in0=ar[:, 0:w], scalar1=float(1 / (2 * math.pi)), scalar2=MAGIC,
                                                                   op0=ALU.mult, op1=ALU.add), [arb], [nrb])
                    V(lambda e, nr=nr, w=w: e.tensor_scalar(out=nr[:, 0:w], in0=nr[:, 0:w], scalar1=MAGIC, scalar2=None, op0=ALU.subtract), [nrb], [nrb])
                    V(lambda e, ar=ar, nr=nr, w=w: e.scalar_tensor_tensor(out=ar[:, 0:w], in0=nr[:, 0:w], scalar=float(-2 * math.pi), op0=ALU.mult,
                                                                          in1=ar[:, 0:w], op1=ALU.add), [arb, nrb], [arb])
                    V(lambda e, ar=ar, w=w: e.tensor_scalar(out=ar[:, 0:w], in0=ar[:, 0:w], scalar1=3.1415925, scalar2=-3.1415925, op0=ALU.min, op1=ALU.max), [arb], [arb])
                    A(lambda e, ar=ar, t0=t0, w=w: e.activation(out=dst[:, t0:t0 + w], in_=ar[:, 0:w], func=AF.Sin), [arb], [dstb])
            sin_layer(w1s[:], feats, FEB, 33, h1, H1B, 0)
            sin_layer(w2s[:], h1, H1B, 64, h2, H2B, 1)
            K.barrier()
            K.release(mk2)

            gA = K.sb("gA", [128, 2, 2, 7, 256], BF16)
            gB = K.sb("gB", [128, 2, 2, 256], BF16)
            GB_ = Buf("g")
            hfa = K.sb("hfa", [128, 10, 2, 256], BF16)
            HFB = Buf("hf")
            ztm = K.sb("ztm", [128, NCH, 128], BF16)
            ZTB = bufs(NCH, "ztm")
            Yb = K.sb("Yb", [128, 2, 2, 5, 128], BF16)
            YBB = Buf("Yb")
            ytp = Rot(K, "yt", [128, 4, 128], F32, 2)
            tqp = Rot(K, "tqr", [128, 4, 128], F32, 4)
            identr = K.sb("identr", [128, 2, 128])
            IDR = Buf("identr")
            V(lambda e: e.tensor_copy(identr[:, 0, :].bitcast(F32R), ident_b), [CONSTB], [IDR])
            V(lambda e: e.tensor_scalar(out=identr[:, 1, :].bitcast(F32R), in0=ident_b, scalar1=-1.0, scalar2=None, op0=ALU.mult), [CONSTB], [IDR])
            dftb = bcol("dft").rearrange("p (t r f) -> p t r f", t=8, r=2)
            idft = bcol("idft").rearrange("p (a r t) -> p a r t", a=2, r=2)

            decs = K.sb("decs", [128, 20, 128])
            DCB = Buf("decs")
            for o in range(2):
                zin, zinb = (vb_, HVB) if o == 0 else (x1f, HX1)
                gate, gateb = (x1f, HX1) if o == 0 else (x2f, HX2)
                zout, zoutb = (x1f, HX1) if o == 0 else (yhy, YHB)
                for cc in range(2):
                    K.dma("sync", lambda e, cc=cc: e.dma_start(out=decs[:], in_=dec_d[:, :, cc * 128:(cc + 1) * 128]), writes=[DCB], semkey="lddec")
                    for pk in range(10):
                        pb, pbb = bank()
                        pos0 = pk * 128
                        for sd in range(2):
                            wc0 = sd * 512 + o * 256 + cc * 128
                            T(lambda e, pb=pb, sd=sd, wc0=wc0, pos0=pos0: e.matmul(pb[:, sd * 128:(sd + 1) * 128], lhsT=h2[:, pos0:pos0 + 128],
                                                                                   rhs=w3s[:, wc0:wc0 + 128], start=True, stop=True), [H2B, FEB], [pbb])
                        if pk < 8:
                            di = [pk, 8 + pk]
                        else:
                            di = [16 + pk - 8, 18 + pk - 8]
                        for sd in range(2):
                            V(lambda e, pb=pb, sd=sd, pk=pk, di=di, cc=cc: e.tensor_tensor(out=hfa[:, pk, sd, cc * 128:(cc + 1) * 128], in0=pb[:, sd * 128:(sd + 1) * 128],
                                                                                          in1=decs[:, di[sd], :], op=ALU.mult), [pbb, DCB], [HFB])
                for delta in range(-3, 4):
                    ents = spectrum_entries(delta, 8)
                    for fch in range(2):
                        pb, pbb = bank()
                        for r in range(2):
                            for i, (src, k, ty) in enumerate(ents):
                                T(lambda e, pb=pb, r=r, i=i, src=src, k=k, ty=ty, fch=fch: e.matmul(
                                    pb[:, r * 256:(r + 1) * 256], lhsT=dftb[:, ty, r, fch * 128:(fch + 1) * 128], rhs=hfa[:, k, src, :],
                                    start=(i == 0), stop=(i == len(ents) - 1)), [CONSTB, HFB], [pbb])
                        if delta == 0:
                            A(lambda e, pb=pb, fch=fch, delta=delta: e.activation(out=gA[:, fch, :, delta + 3, :], in_=pb[:, 0:512].rearrange("p (r c) -> p r c", r=2),
                                                                                  func=AF.Copy), [pbb], [GB_])
                        else:
                            A(lambda e, pb=pb, fch=fch, delta=delta: e.activation(out=gA[:, fch, :, delta + 3, :], in_=pb[:, 0:512].rearrange("p (r c) -> p r c", r=2),
                                                                                  func=AF.Copy, scale=flag), [pbb, CONSTB], [GB_])
                entsB = spectrum_entries(0, 2)
                for fch in range(2):
                    pb, pbb = bank()
                    for r in range(2):
                        for i, (src, k, ty) in enumerate(entsB):
                            T(lambda e, pb=pb, r=r, i=i, src=src, k=k, ty=ty, fch=fch: e.matmul(
                                pb[:, r * 256:(r + 1) * 256], lhsT=dftb[:, ty, r, fch * 128:(fch + 1) * 128], rhs=hfa[:, 8 + k, src, :],
                                start=(i == 0), stop=(i == len(entsB) - 1)), [CONSTB, HFB], [pbb])
                    A(lambda e, pb=pb, fch=fch: e.activation(out=gB[:, fch, :, :], in_=pb[:, 0:512].rearrange("p (r c) -> p r c", r=2), func=AF.Copy), [pbb], [GB_])
                for cc in range(2):
                    gcs = slice(cc * 128, (cc + 1) * 128)
                    for tc in range(NCH):
                        pb, pbb = bank()
                        pbv = pb[:].bitcast(BF16)
                        T(lambda e, pbv=pbv, tc=tc: e.transpose(pbv[:, 0:128], zin[:, cc, tc * 128:(tc + 1) * 128], ident_b), [zinb, CONSTB], [pbb])
                        A(lambda e, pbv=pbv, tc=tc: e.activation(out=ztm[:, tc, :], in_=pbv[:, 0:128], func=AF.Copy), [pbb], [ZTB[tc]])
                    ty_f = [dft_type(0, 0), dft_type(0, 128)]
                    set_banks(6, 8)
                    for fch in range(2):
                        for r in range(2):
                            for blk in range(NB):
                                if blk < 4:
                                    dst, dstb = PS[r][:, blk * 128:(blk + 1) * 128], PSB[r]
                                else:
                                    dst, dstb = PS[2][:, r * 128:(r + 1) * 128], PSB[2]
                                for tk in range(2):
                                    T(lambda e: e.matmul(dst, lhsT=dftb[:, ty_f[tk], r, fch * 128:(fch + 1) * 128], rhs=ztm[:, 2 * blk + tk, :],
                                                         start=(tk == 0), stop=(tk == 1)), [CONSTB, ZTB[2 * blk + tk]], [dstb])
                        terms = []
                        for delta in [0, 1, -1, 2, -2, 3, -3]:
                            nbk = 4 - abs(delta)
                            tb0 = max(delta, 0)
                            sb0 = tb0 - delta
                            for (acc, gr, zr, sgn) in ((3, 0, 0, 0), (3, 1, 1, 1), (4, 0, 1, 0), (4, 1, 0, 0)):
                                g_ap = gA[:, fch, gr, delta + 3, gcs].unsqueeze(1).broadcast_to([128, nbk, 128])
                                z_ap = PS[zr][:, sb0 * 128:(sb0 + nbk) * 128].rearrange("p (b c) -> p b c", b=nbk)
                                terms.append((acc, tb0 * 128, nbk * 128, sgn, z_ap, PSB[zr], g_ap, nbk))
                        cnt = {3: 0, 4: 0}
                        tot = {3: 14, 4: 14}
                        for (acc, c0, ncol, sgn, z_ap, zb, g_ap, nbk) in terms:
                            tq, tqb = tqp.next()
                            V(lambda e: e.tensor_tensor(out=tq[:, 0:nbk, :].bitcast(F32R), in0=z_ap, in1=g_ap, op=ALU.mult), [zb, GB_], [tqb])
                            T(lambda e: e.matmul(PS[acc][:, c0:c0 + ncol], lhsT=identr[:, sgn, :].bitcast(F32R),
                                                 rhs=tq[:, 0:nbk, :].rearrange("p b c -> p (b c)").bitcast(F32R),
                                                 start=(cnt[acc] == 0), stop=(cnt[acc] == tot[acc] - 1)), [tqb, IDR], [PSB[acc]])
                            cnt[acc] += 1
                        tqs = []
                        for (gr, zr, sgn) in ((0, 0, 0), (1, 1, 1), (0, 1, 0), (1, 0, 0)):
                            tq, tqb = tqp.next()
                            V(lambda e: e.tensor_tensor(out=tq[:, 0, :].bitcast(F32R), in0=PS[2][:, zr * 128:(zr + 1) * 128], in1=gB[:, fch, gr, gcs], op=ALU.mult),
                              [PSB[2], GB_], [tqb])
                            tqs.append((tq, tqb, sgn))
                        for i, (tq, tqb, sgn) in enumerate(tqs):
                            ro = i // 2
                            T(lambda e: e.matmul(PS[5][:, ro * 128:(ro + 1) * 128], lhsT=identr[:, sgn, :].bitcast(F32R), rhs=tq[:, 0, :].bitcast(F32R),
                                                 start=(i % 2 == 0), stop=(i % 2 == 1)), [tqb, IDR], [PSB[5]])
                        for r in range(2):
                            A(lambda e: e.activation(out=Yb[:, fch, r, 0:4, :], in_=PS[3 + r][:, 0:512].rearrange("p (b c) -> p b c", b=4), func=AF.Copy),
                              [PSB[3 + r]], [YBB])
                        A(lambda e: e.activation(out=Yb[:, fch, :, 4, :], in_=PS[5][:, 0:256].rearrange("p (r c) -> p r c", r=2), func=AF.Copy), [PSB[5]], [YBB])
                    set_banks(0, 8)
                    hbcol = pcol("hy_bias", (l * 2 + o) * 2 + cc, 1)
                    for blk in range(NB):
                        pb, pbb = bank()
                        i = 0
                        for fch in range(2):
                            for r in range(2):
                                T(lambda e, pb=pb, fch=fch, r=r, blk=blk, i=i: e.matmul(pb[:, 0:256], lhsT=Yb[:, fch, r, blk, :], rhs=idft[:, fch, r, :],
                                                                                       start=(i == 0), stop=(i == 3)), [YBB, CONSTB], [pbb])
                                i += 1
                        ts = slice(blk * 256, (blk + 1) * 256)
                        tq, tqb = ytp.next()
                        tqv = tq[:].rearrange("p a b -> p (a b)")[:, 0:256]
                        V(lambda e, tqv=tqv, pb=pb, ts=ts: e.scalar_tensor_tensor(out=tqv, in0=zin[:, cc, ts], scalar=hbcol, op0=ALU.mult, in1=pb[:, 0:256], op1=ALU.add),
                          [zinb, CONSTB, pbb], [tqb])
                        V(lambda e, tqv=tqv, ts=ts: e.tensor_tensor(out=zout[:, cc, ts], in0=tqv, in1=gate[:, cc, ts], op=ALU.mult), [tqb, gateb], [zoutb])
            out_proj([yhy[:, 0, :], yhy[:, 1, :]], [YHB], 256, 128)
            K.barrier()
            K.release(mk)

        yhy = K.sb("yhy", [128, 2, NT], BF16)
        YHB = Buf("yhy")
        hyena(yhy, YHB)
        yssd = K.sb("yssd", [64, 4, NT], BF16)
        YSB = Buf("yssd")
        ssd(yssd, YSB)
        yret = K.sb("yret", [64, 4, NT], BF16)
        YRB = Buf("yret")
        ret(yret, YRB)
        yatt = K.sb("yatt", [64, 4, NT], BF16)
        YAB_ = Buf("yatt")
        att(yatt, YAB_)
        out_proj_all()
        K.barrier()
        K.release(mk_all)

    for l in range(nlayers):
        if l == 1:
            assert not pending_mod and not inflight_mod, (pending_mod, inflight_mod)
        ffn(l, 0)
        mixer(l)
        ffn(l, 1)

    K.dma("sync", lambda e: e.dma_start(out=yT, in_=x[:]), reads=[b for row in XB for b in row], semkey="sty")
    K.barrier()
    K.release(0)
    return nc, K


def _consts(is_s):
    f32 = np.float32
    cfv = np.zeros((128, CF.n), f32)

    def put(name, arr):
        c0, w = CF[name]
        cfv[:, c0:c0 + w] = np.asarray(arr, f32).reshape(128, w) if np.asarray(arr).ndim > 1 or w == 1 else np.broadcast_to(np.asarray(arr, f32), (128, w))
    j = np.arange(128)[:, None]
    i = np.arange(128)[None, :]
    put("trif", (i >= j).astype(f32))
    put("trib", (j >= i).astype(f32))
    put("ones", np.ones((128, 128), f32))
    put("relu_f", np.maximum(i - j, 0).astype(f32))
    put("relu_b", np.maximum(j - i, 0).astype(f32))
    put("ip1", np.broadcast_to((i + 1).astype(f32), (128, 128)))
    put("rmi", np.broadcast_to((128 - i).astype(f32), (128, 128)))
    put("tailf", (127 - j).astype(f32))
    put("tailb", j.astype(f32))
    nf = 16
    inv = (10000.0 ** (-np.arange(nf, dtype=f32) / nf)).astype(f32)
    cos = np.ones((NT, 32), f32)
    sin = np.zeros((NT, 32), f32)
    if is_s:
        t = np.arange(1024)
        rows = (t // 64).astype(f32)
        cols = (t % 64).astype(f32)
        angr = rows[:, None] * inv[None, :]
        angc = cols[:, None] * inv[None, :]
        cos[:1024, 0:16] = np.cos(angr)
        cos[:1024, 16:32] = np.cos(angc)
        sin[:1024, 0:16] = np.sin(angr)
        sin[:1024, 16:32] = np.sin(angc)
    put("cos", cos.reshape(NCH, 128, 32).transpose(1, 0, 2).reshape(128, 320))
    put("sin", sin.reshape(NCH, 128, 32).transpose(1, 0, 2).reshape(128, 320))
    cfb = np.full((NCH,), -30000.0, f32)
    hl = np.zeros((5,), f32)
    hr = np.zeros((5,), f32)
    if is_s:
        cfb[0:8] = 0.0
        hl[1:4] = 1.0
        hr[0:3] = 1.0
    put("cfb", cfb)
    put("hl", hl)
    put("hr", hr)
    put("flag", np.full((128, 1), 1.0 if is_s else 0.0, f32))
    put("negpi", np.full((128, 1), -math.pi, f32))
    put("nbf", ((i >= j).astype(f32) - 1.0) * 30000.0)
    put("nbb", ((j >= i).astype(f32) - 1.0) * 30000.0)

    cbv = np.zeros((128, CB.n), f32)

    def putb(name, arr):
        c0, w = CB[name]
        cbv[:, c0:c0 + w] = np.asarray(arr, f32).reshape(128, w)
    putb("ident", np.eye(128, dtype=f32))
    putb("ones", np.ones((128, 128), f32))
    am = np.zeros((128, NCH, 2, 128), f32)
    band_prev = (j >= i).astype(f32)
    band_next = (j <= i).astype(f32)
    for qc in range(NCH):
        if is_s and qc < 8:
            if qc >= 1:
                am[:, qc, 0, :] = band_prev
            if qc <= 6:
                am[:, qc, 1, :] = band_next
        else:
            if qc % 2 == 1:
                am[:, qc, 0, :] = 1.0
            else:
                am[:, qc, 1, :] = 1.0
    putb("am", am)
    om = 2 * np.pi * (np.arange(256) + 0.5) / 512.0
    row = np.arange(128)
    dft = np.zeros((128, 8, 2, 256), np.float64)
    for ty in range(8):
        if ty < 4:
            e = FWD_E0[ty] + row
        else:
            e = BWD_E0[ty - 4] - row
        valid = (np.abs(e) <= 255).astype(np.float64)
        ang = e[:, None] * om[None, :]
        dft[:, ty, 0, :] = np.cos(ang) * valid[:, None]
        dft[:, ty, 1, :] = -np.sin(ang) * valid[:, None]
    putb("dft", dft)
    tt = np.arange(256)
    idft = np.zeros((128, 2, 2, 256), np.float64)
    for fch in range(2):
        omf = om[fch * 128:(fch + 1) * 128]
        ang = omf[:, None] * tt[None, :]
        idft[:, fch, 0, :] = (2.0 / 512) * np.cos(ang)
        idft[:, fch, 1, :] = -(2.0 / 512) * np.sin(ang)
    putb("idft", idft)
    return cfv, cbv


def _hy_consts(LA):
    f32 = np.float32
    l = LA
    pos = np.arange(l, dtype=f32)
    t = pos / f32(l - 1)
    bands = np.linspace(1e-4, 15, 16, dtype=f32)
    ang = (f32(2.0 * math.pi / l)) * pos[:, None] * bands[None, :]
    feats = np.concatenate([t[:, None], np.cos(ang), -np.sin(ang)], axis=-1).astype(f32)
    max_decay = math.log(1e-2) / 0.3
    min_decay = math.log(1e-2) / 1.5
    deltas = np.abs(np.linspace(min_decay, max_decay, 256, dtype=f32))
    dec = np.exp(-t[:, None] * deltas[None, :]).astype(f32)
    return feats, dec


def _prepare(inp):
    f32 = np.float32
    g = lambda k: np.asarray(inp[k], dtype=f32)
    x_prompt, x_sample = g("x_prompt"), g("x_sample")
    cache_k, cache_v = g("cache_k"), g("cache_v")
    state_ssd, state_ret = g("state_ssd"), g("state_ret")
    c, c_ctx = g("c"), g("c_ctx")
    pt = np.zeros((128, PT.n), f32)

    def putp(name, off, arr):
        arr = np.asarray(arr, f32)
        c0 = PT[name][0] + off
        pt[0:arr.shape[0], c0:c0 + arr.shape[1]] = arr
    nw = g("norm_w")
    for l in range(2):
        for j in range(3):
            putp("norm_w", (l * 3 + j) * 8, nw[l, j].reshape(8, 128).T)
        putp("b_mod", l * 72, g("b_mod")[l].reshape(72, 128).T)
        cw, cbias = g("ssd_conv_w")[l], g("ssd_conv_b")[l]
        for q in range(8):
            if q < 4:
                f0, fs = q * 64, 64
            else:
                f0, fs = 256 + (q - 4) * 128, 128
            arr = np.stack([cw[0, f0:f0 + fs], cw[1, f0:f0 + fs], cw[2, f0:f0 + fs], cbias[f0:f0 + fs]], axis=1)
            putp("ssd_conv", (l * 8 + q) * 4, arr)
        hw, hb = g("hy_conv_w")[l], g("hy_conv_b")[l]
        for q in range(6):
            f0 = q * 128
            arr = np.stack([hw[0, f0:f0 + 128], hw[1, f0:f0 + 128], hw[2, f0:f0 + 128], hb[f0:f0 + 128]], axis=1)
            putp("hy_conv", (l * 6 + q) * 4, arr)
        hbias = g("hy_bias")[l]
        for o in range(2):
            putp("hy_bias", (l * 2 + o) * 2, hbias[o].reshape(2, 128).T)
        putp("ssd_nw", l * 4, g("ssd_norm_w")[l].reshape(4, 64).T)
        putp("ssd_d", l * 4, np.broadcast_to(g("ssd_d")[l][None, :], (64, 4)))
        putp("ret_gn", l * 4, g("ret_gn_w")[l].reshape(4, 64).T)
        putp("hyp", l * 3, np.stack([g("hy_b1")[l], g("hy_freq")[l], g("hy_b2")[l]], axis=1))
    ft = np.zeros((128, FT.n), f32)

    def putf(name, off, vec):
        vec = np.asarray(vec, f32).reshape(-1)
        c0 = FT[name][0] + off
        ft[:, c0:c0 + vec.size] = vec[None, :]
    for l in range(2):
        putf("dt_bias", l * 80, np.tile(g("ssd_dt_bias")[l].reshape(8), NCH))
        putf("a_log", l * 80, np.tile(g("ssd_a_log")[l].reshape(8), NCH))
        putf("ret_logit", l * 8, g("ret_decay_logit")[l].reshape(8))
        putf("qkw", l * 384, np.concatenate([np.tile(g("attn_q_norm")[l], 4), np.tile(g("attn_k_norm")[l], 2)]))
        putf("sink", l * 4, g("attn_sink")[l])
    featsB, decB = _hy_consts(256)
    featsA_s, decA_s = _hy_consts(1024)
    wm = g("w_mod").reshape(2, 8, 128, 18, 512).transpose(0, 3, 2, 1, 4).reshape(2, 18, 128, 4096)
    wi = g("ffn_w_in").reshape(2, 2, 8, 128, 2, 11, 256)
    wi = wi.transpose(0, 1, 5, 3, 2, 4, 6).reshape(2, 2, 11, 128, 4096)
    wo = g("ffn_w_out").reshape(2, 2, 11, 2, 128, 1024).transpose(0, 1, 2, 4, 3, 5).reshape(2, 2, 11, 128, 2048)
    shared = dict(ptab=pt, ftab=ft, w_mod=np.ascontiguousarray(wm), ffn_w_in=np.ascontiguousarray(wi), ffn_w_out=np.ascontiguousarray(wo), mix_w_in=g("mix_w_in"),
                  mix_w_out=g("mix_w_out"), hy_w1=g("hy_w1"), hy_w2=g("hy_w2"), hy_w3=g("hy_w3"),
                  featsB=np.ascontiguousarray(featsB.T))
    consts = {True: _consts(True), False: _consts(False)}
    in_maps = []
    plan = []
    for cid in range(NCORE):
        is_s = cid < 2
        if is_s:
            xs = np.concatenate([x_sample[cid], x_prompt[30 + cid]], axis=0)
            seqs = [30 + cid]
            condA = c[cid]
        else:
            seqs = list(range(5 * (cid - 2), 5 * (cid - 2) + 5))
            xs = x_prompt[seqs].reshape(NT, D)
            condA = c_ctx
        plan.append((is_s, seqs))
        xTm = np.ascontiguousarray(xs.T.reshape(8, 128, NT).transpose(1, 0, 2))
        cond = np.stack([condA, c_ctx], axis=-1)
        condTm = np.ascontiguousarray(cond.reshape(8, 128, 2).transpose(1, 0, 2))
        s0s = np.zeros((2, 5, 2, 4, 128, 64), f32)
        s0r = np.zeros((2, 5, 2, 4, 64, 64), f32)
        ck = np.zeros((2, 2, 64, 512), f32)
        cv = np.zeros((2, 512, 128), f32)
        if is_s:
            for l in range(2):
                s0s[l, 0, 0] = state_ssd[cid, l, 0]
                s0s[l, 3, 1] = state_ssd[cid, l, 1]
                s0r[l, 0, 0] = state_ret[cid, l, 0]
                s0r[l, 3, 1] = state_ret[cid, l, 1]
                ck[l] = cache_k[cid, l].transpose(1, 2, 0)
                cv[l] = cache_v[cid, l].reshape(512, 128)
            featsA = featsA_s
            decFA = decA_s.copy()
        else:
            featsA = np.zeros((1024, 33), f32)
            featsA[:256] = featsB
            decFA = np.zeros((1024, 256), f32)
            decFA[:256] = decB
        decBA = decFA.copy()
        decBA[0] = 0.0
        decFB = decB.copy()
        decBB = decB.copy()
        decBB[0] = 0.0
        dec = np.concatenate([decFA.reshape(8, 128, 256), decBA.reshape(8, 128, 256), decFB.reshape(2, 128, 256), decBB.reshape(2, 128, 256)], axis=0)
        cfv, cbv = consts[is_s]
        m = dict(shared)
        m.update(xT=xTm, condT=condTm,
                 s0ssd=np.ascontiguousarray(s0s.reshape(80, 128, 64).transpose(1, 0, 2)),
                 s0ret=np.ascontiguousarray(s0r.reshape(80, 64, 64).transpose(1, 0, 2)),
                 ckT=np.ascontiguousarray(ck.reshape(4, 64, 512).transpose(1, 0, 2)),
                 cvt=np.ascontiguousarray(cv.reshape(8, 128, 128).transpose(1, 0, 2)),
                 cf32=cfv, cbf=cbv, featsA=np.ascontiguousarray(featsA.T), dec=np.ascontiguousarray(dec.transpose(1, 0, 2)))
        in_maps.append(m)
    return in_maps, plan


_CACHE = {}


def kernel(**inputs):
    in_maps, plan = _prepare(inputs)
    if "nc" not in _CACHE:
        _CACHE["nc"] = build_program()[0]
    nc = _CACHE["nc"]
    res = run_bass_kernel_spmd(nc, in_maps, core_ids=list(range(NCORE)))
    f32 = np.float32
    y_prompt = np.zeros((32, 256, D), f32)
    y_sample = np.zeros((2, 1024, D), f32)
    nck = np.zeros((32, 2, 256, 2, 64), f32)
    ncv = np.zeros((32, 2, 256, 2, 64), f32)
    nssd = np.zeros((32, 2, 2, 4, 128, 64), f32)
    nret = np.zeros((32, 2, 2, 4, 64, 64), f32)
    for cid, (is_s, seqs) in enumerate(plan):
        r = res.results[cid]
        y = np.asarray(r["yT"]).transpose(1, 0, 2).reshape(D, NT).T
        nk = np.asarray(r["nk"])
        nv = np.asarray(r["nv"])
        ss = np.asarray(r["nssd"]).transpose(1, 0, 2).reshape(2, 5, 2, 4, 128, 64)
        sr = np.asarray(r["nret"]).transpose(1, 0, 2).reshape(2, 5, 2, 4, 64, 64)
        if is_s:
            y_sample[cid] = y[:1024]
            blks = [(4, seqs[0])]
        else:
            blks = list(enumerate(seqs))
        for blk, b in blks:
            ts = slice(blk * 256, (blk + 1) * 256)
            y_prompt[b] = y[ts]
            for l in range(2):
                nck[b, l] = nk[l, ts].reshape(256, 2, 64)
                ncv[b, l] = nv[l, ts].reshape(256, 2, 64)
                nssd[b, l] = ss[l, blk]
                nret[b, l] = sr[l, blk]
    return (y_prompt, y_sample, nck, ncv, nssd, nret)
```

```python
import math
import numpy as np
import concourse.bass as bass
import concourse.mybir as mybir
from concourse.bass_utils import run_bass_kernel_spmd

F32 = mybir.dt.float32
BF16 = mybir.dt.bfloat16
F32R = mybir.dt.float32r
AF = mybir.ActivationFunctionType
ALU = mybir.AluOpType
AX = mybir.AxisListType

NCORE = 8
NT = 1280
NB = 5
NCH = 10
TT = [(0, 512, 0), (512, 512, 0), (1024, 256, 1)]
D = 1024
DFF = 2816
EPS = 1e-6
ENGS = ["tensor", "vector", "scalar", "gpsimd", "sync"]
SAME_ENGINE_NOSYNC = ("tensor",)


class Buf:
    __slots__ = ("name", "w", "r", "excl")

    def __init__(self, name="", excl=False):
        self.name = name
        self.w = None
        self.r = {}
        self.excl = excl


def bufs(n, name=""):
    return [Buf(name + str(i)) for i in range(n)]


class KB:
    def __init__(self, nc):
        self.nc = nc
        self.cnt = {e: 0 for e in ENGS}
        self.seen = {e: {} for e in ENGS}
        self.sems = {}
        self.dcount = {}
        self._stack = []
        self.n_inst = 0
        self.uid = 0

    def enter(self, cm):
        v = cm.__enter__()
        self._stack.append(cm)
        return v

    def mark(self):
        return len(self._stack)

    def release(self, m):
        while len(self._stack) > m:
            self._stack.pop().__exit__(None, None, None)

    def sem(self, key):
        if key not in self.sems:
            self.sems[key] = self.enter(self.nc.semaphore("s_" + key))
        return self.sems[key]

    def sb(self, name, shape, dt=F32):
        self.uid += 1
        return self.enter(self.nc.sbuf_tensor(f"{name}_{self.uid}", list(shape), dt))

    def ps(self, name, shape, dt=F32):
        return self.enter(self.nc.psum_tensor(name, list(shape), dt))

    @staticmethod
    def _flat(bl):
        out = []
        for b in bl:
            if isinstance(b, (list, tuple)):
                out.extend(KB._flat(b))
            else:
                out.append(b)
        return out

    def _need(self, eng, reads, writes):
        need = {}

        def add(k, v):
            if need.get(k, 0) < v:
                need[k] = v
        for b in reads:
            if b.w is not None:
                add(*b.w)
        for b in writes:
            if b.w is not None:
                add(*b.w)
            for k, v in b.r.items():
                add(k, v)
        out = []
        for k, v in need.items():
            if k == "p_" + eng and eng in SAME_ENGINE_NOSYNC:
                continue
            if self.seen[eng].get(k, 0) >= v:
                continue
            self.seen[eng][k] = v
            out.append((k, v))
        return out

    def _emit(self, eng, waits, fn, key, inc):
        e = getattr(self.nc, eng)
        for k, v in waits:
            e.wait_ge(self.sems[k], v)
        if fn is not None:
            fn(e).then_inc(self.sems[key], inc)

    def op(self, eng, fn, reads=(), writes=()):
        reads = self._flat(reads)
        writes = self._flat(writes)
        ex = [b for b in reads if b.excl]
        if ex:
            reads = [b for b in reads if not b.excl]
            writes = writes + [b for b in ex if b not in writes]
        waits = self._need(eng, reads, writes)
        key = "p_" + eng
        self.sem(key)
        self.cnt[eng] += 1
        val = self.cnt[eng]
        for b in reads:
            if b.r.get(key, 0) < val:
                b.r[key] = val
        for b in writes:
            b.w = (key, val)
            b.r = {}
        self._emit(eng, waits, fn, key, 1)
        self.n_inst += 1

    def dma(self, eng, fn, reads=(), writes=(), semkey=None):
        reads = self._flat(reads)
        writes = self._flat(writes)
        waits = self._need(eng, reads, writes)
        self.sem(semkey)
        self.dcount[semkey] = self.dcount.get(semkey, 0) + 16
        val = self.dcount[semkey]
        for b in reads:
            if b.r.get(semkey, 0) < val:
                b.r[semkey] = val
        for b in writes:
            b.w = (semkey, val)
            b.r = {}
        self._emit(eng, waits, fn, semkey, 16)
        self.n_inst += 1

    def barrier(self):
        tot = [("p_" + e, self.cnt[e]) for e in ENGS if self.cnt[e] > 0]
        tot += list(self.dcount.items())
        for eng in ENGS:
            waits = []
            for k, v in tot:
                if k == "p_" + eng and eng == "tensor":
                    continue
                if self.seen[eng].get(k, 0) >= v:
                    continue
                self.seen[eng][k] = v
                waits.append((k, v))
            self._emit(eng, waits, None, None, 0)

    def V(self, fn, r=(), w=()):
        self.op("vector", fn, r, w)

    def A(self, fn, r=(), w=()):
        self.op("scalar", fn, r, w)

    def T(self, fn, r=(), w=()):
        self.op("tensor", fn, r, w)

    def G(self, fn, r=(), w=()):
        self.op("gpsimd", fn, r, w)


class Rot:
    def __init__(self, K, name, shape, dt, n):
        self.t = [K.sb(f"{name}{i}", shape, dt) for i in range(n)]
        self.b = bufs(n, name)
        self.i = 0

    def next(self):
        i = self.i
        self.i = (i + 1) % len(self.t)
        return self.t[i], self.b[i]


class Cols:
    def __init__(self):
        self.m = {}
        self.n = 0

    def add(self, name, w):
        self.m[name] = (self.n, w)
        self.n += w

    def __getitem__(self, name):
        return self.m[name]


def ptab_cols():
    c = Cols()
    c.add("norm_w", 48)
    c.add("b_mod", 144)
    c.add("ssd_conv", 64)
    c.add("hy_conv", 48)
    c.add("hy_bias", 8)
    c.add("ssd_nw", 8)
    c.add("ssd_d", 8)
    c.add("ret_gn", 8)
    c.add("hyp", 6)
    return c


def ftab_cols():
    c = Cols()
    c.add("dt_bias", 160)
    c.add("a_log", 160)
    c.add("ret_logit", 16)
    c.add("qkw", 768)
    c.add("sink", 8)
    return c


def cf32_cols():
    c = Cols()
    c.add("trif", 128)
    c.add("trib", 128)
    c.add("ones", 128)
    c.add("relu_f", 128)
    c.add("relu_b", 128)
    c.add("ip1", 128)
    c.add("rmi", 128)
    c.add("tailf", 1)
    c.add("tailb", 1)
    c.add("cos", 320)
    c.add("sin", 320)
    c.add("cfb", 10)
    c.add("hl", 5)
    c.add("hr", 5)
    c.add("flag", 1)
    c.add("negpi", 1)
    c.add("nbf", 128)
    c.add("nbb", 128)
    return c


def cbf_cols():
    c = Cols()
    c.add("ident", 128)
    c.add("ones", 128)
    c.add("am", 2560)
    c.add("dft", 4096)
    c.add("idft", 1024)
    return c


PT = ptab_cols()
FT = ftab_cols()
CF = cf32_cols()
CB = cbf_cols()

FWD_E0 = [-256, -128, 0, 128]
BWD_E0 = [256, 128, 0, -128]


def dft_type(src, e0):
    return (FWD_E0.index(e0) if src == 0 else 4 + BWD_E0.index(e0))


def spectrum_entries(delta, nchunks):
    out = []
    for k in range(nchunks):
        e0 = 128 * k - 256 * delta
        if e0 in FWD_E0:
            out.append((0, k, dft_type(0, e0)))
        e0b = -128 * k - 256 * delta
        if e0b in BWD_E0:
            out.append((1, k, dft_type(1, e0b)))
    return out


def build_program(nlayers=2):
    nc = bass.Bass("TRN2", target_bir_lowering=False)

    def din(name, shape):
        return nc.dram_tensor(name, list(shape), F32, kind="ExternalInput").ap()

    def dout(name, shape):
        return nc.dram_tensor(name, list(shape), F32, kind="ExternalOutput").ap()

    xT = din("xT", [128, 8, NT])
    condT = din("condT", [128, 8, 2])
    s0ssd = din("s0ssd", [128, 80, 64])
    s0ret = din("s0ret", [64, 80, 64])
    ckT = din("ckT", [64, 4, 512])
    cvt = din("cvt", [128, 8, 128])
    ptab_d = din("ptab", [128, PT.n])
    ftab_d = din("ftab", [128, FT.n])
    cf32_d = din("cf32", [128, CF.n])
    cbf_d = din("cbf", [128, CB.n])
    featsA_d = din("featsA", [33, 1024])
    featsB_d = din("featsB", [33, 256])
    dec_d = din("dec", [128, 20, 256])
    w_mod = din("w_mod", [2, 18, 128, 4096])
    ffn_w_in = din("ffn_w_in", [2, 2, 11, 128, 4096])
    ffn_w_out = din("ffn_w_out", [2, 2, 11, 128, 2048])
    mix_w_in = din("mix_w_in", [2, D, 3336])
    mix_w_out = din("mix_w_out", [2, D, D])
    hy_w1 = din("hy_w1", [2, 33, 64])
    hy_w2 = din("hy_w2", [2, 64, 64])
    hy_w3 = din("hy_w3", [2, 64, 1024])

    yT = dout("yT", [128, 8, NT])
    nk_o = dout("nk", [2, NT, 128])
    nv_o = dout("nv", [2, NT, 128])
    nssd_o = dout("nssd", [128, 80, 64])
    nret_o = dout("nret", [64, 80, 64])

    K = KB(nc)
    V, A, T, G = K.V, K.A, K.T, K.G

    x = K.sb("x", [128, 8, NT])
    XB = [[Buf(f"x{m}_{t}") for t in range(3)] for m in range(8)]
    modt = K.sb("modt", [128, 2, 2, 72])
    MODB = Buf("mod")
    ptab = K.sb("ptab", [128, PT.n])
    ftab = K.sb("ftab", [128, FT.n])
    cf = K.sb("cf", [128, CF.n])
    cb = K.sb("cb", [128, CB.n], BF16)
    CONSTB = Buf("const")
    CONSTB2 = Buf("constb")
    WA = [K.sb(f"WA{i}", [128, 8, 512], BF16) for i in range(2)]
    WAB = bufs(2, "WA")
    WB = [K.sb(f"WB{i}", [128, 4096], BF16) for i in range(2)]
    WBB = bufs(2, "WB")
    wctr = {"a": 0, "b": 0}
    PS = [K.ps(f"ps{i}", [128, 512]) for i in range(8)]
    PSBK = [Buf(f"psb{i}", excl=True) for i in range(8)]
    PSR = [PSBK[i // 4] for i in range(32)]
    PSB = [[PSBK[i]] for i in range(8)]
    psctr = [0, 0, 8]
    prc = [0]

    def bank():
        lo, hi = psctr[1], psctr[2]
        i = psctr[0]
        if i < lo or i >= hi:
            i = lo
        psctr[0] = i + 1 if i + 1 < hi else lo
        return PS[i], PSB[i]

    def set_banks(lo, hi):
        psctr[1], psctr[2] = lo, hi

    def pr(ncols):
        n = (ncols + 127) // 128
        i = prc[0]
        if (i % 4) + n > 4:
            i = (i // 4 + 1) * 4
        if i + n > 32:
            i = 0
        prc[0] = (i + n) % 32
        b, r = divmod(i, 4)
        return PS[b][:, r * 128:r * 128 + n * 128], [PSBK[b]]

    def interleave(gens):
        gens = list(gens)
        while gens:
            nxt = []
            for g in gens:
                try:
                    next(g)
                    nxt.append(g)
                except StopIteration:
                    pass
            gens = nxt

    def nextA():
        i = wctr["a"] % 2
        wctr["a"] += 1
        return WA[i], WAB[i], f"wa{i}"

    def nextB():
        i = wctr["b"] % 2
        wctr["b"] += 1
        return WB[i], WBB[i], f"wb{i}"

    def pcol(name, off=0, w=1, rows=128):
        c0 = PT[name][0] + off
        return ptab[0:rows, c0:c0 + w]

    def fcol(name, off=0, w=1, rows=128):
        c0 = FT[name][0] + off
        return ftab[0:rows, c0:c0 + w]

    def ccol(name, off=0, w=None, rows=128):
        c0, ww = CF[name]
        if w is None:
            w = ww
        return cf[0:rows, c0 + off:c0 + off + w]

    def bcol(name, off=0, w=None, rows=128):
        c0, ww = CB[name]
        if w is None:
            w = ww
        return cb[0:rows, c0 + off:c0 + off + w]

    K.dma("sync", lambda e: e.dma_start(out=x[:], in_=xT), writes=[b for row in XB for b in row], semkey="ldx")
    K.dma("sync", lambda e: e.dma_start(out=ptab[:], in_=ptab_d), writes=[CONSTB], semkey="ldc")
    K.dma("sync", lambda e: e.dma_start(out=ftab[:], in_=ftab_d), writes=[CONSTB], semkey="ldc")
    K.dma("sync", lambda e: e.dma_start(out=cf[:], in_=cf32_d), writes=[CONSTB], semkey="ldc")
    K.dma("gpsimd", lambda e: e.dma_start(out=cb[:], in_=cbf_d, max_dma_last_dim=2048), writes=[CONSTB2], semkey="ldcb")

    K.barrier()

    flag = ccol("flag")
    ident_b = bcol("ident")
    ones_b = bcol("ones")
    ones_f = ccol("ones")
    trif = ccol("trif")
    trib = ccol("trib")

    condf = K.sb("condf", [128, 8, 2])
    condb = K.sb("condb", [128, 8, 2], BF16)
    CB_ = Buf("cond")
    MODL = [Buf("mod0"), Buf("mod1")]
    K.dma("sync", lambda e: e.dma_start(out=condf[:], in_=condT), writes=[CB_], semkey="ldcond")
    A(lambda e: e.activation(out=condb[:], in_=condf[:], func=AF.Silu), [CB_], [CB_])

    def mod_dma(l, ci, wa, wab, sk):
        K.dma("gpsimd", lambda e: e.dma_start(out=wa[:].rearrange("p k n -> p (k n)"), in_=w_mod[l, ci], max_dma_last_dim=8192),
              writes=[wab], semkey=sk)

    def mod_chunk(l, ci, wa, wab, sk, dma=True):
        if dma:
            mod_dma(l, ci, wa, wab, sk)
        pb, pbb = bank()
        for mb in range(4):
            for k in range(8):
                T(lambda e: e.matmul(pb[:, mb * 2:mb * 2 + 2], lhsT=wa[:, k, mb * 128:(mb + 1) * 128], rhs=condb[:, k, :],
                                     start=(k == 0), stop=(k == 7)), [wab, CB_], [pbb])
        for cnd in range(2):
            V(lambda e: e.tensor_tensor(out=modt[:, l, cnd, ci * 4:ci * 4 + 4], in0=pb[:, 0:8].rearrange("p (m c) -> p m c", c=2)[:, :, cnd],
                                        in1=pcol("b_mod", l * 72 + ci * 4, 4), op=ALU.add), [pbb, CONSTB], [MODL[l]])

    def mod_finish_j(l, j):
        for cnd in range(2):
            V(lambda e: e.scalar_tensor_tensor(
                out=modt[:, l, cnd, (3 * j + 1) * 8:(3 * j + 2) * 8], in0=modt[:, l, cnd, (3 * j + 1) * 8:(3 * j + 2) * 8],
                scalar=1.0, op0=ALU.add, in1=pcol("norm_w", (l * 3 + j) * 8, 8), op1=ALU.mult), [MODL[l], CONSTB], [MODL[l]])
            if j in (0, 2):
                V(lambda e: e.tensor_scalar(
                    out=modt[:, l, cnd, (3 * j + 2) * 8:(3 * j + 3) * 8], in0=modt[:, l, cnd, (3 * j + 2) * 8:(3 * j + 3) * 8],
                    scalar1=0.5, scalar2=None, op0=ALU.mult), [MODL[l]], [MODL[l]])

    mod_done = {}

    def mod_mark(l, ci):
        j = ci // 6
        mod_done[(l, j)] = mod_done.get((l, j), 0) + 1
        if mod_done[(l, j)] == 6:
            mod_finish_j(l, j)

    for ci in range(6):
        wa, wab, sk = nextA()
        mod_chunk(0, ci, wa, wab, sk)
        mod_mark(0, ci)
    K.barrier()
    pending_mod = [(0, ci) for ci in range(6, 18)] + ([(1, ci) for ci in range(18)] if nlayers > 1 else [])
    inflight_mod = []
    ffn_gi = [0]

    def modc(l, cnd, j, m):
        return modt[:, l, cnd, j * 8 + m:j * 8 + m + 1]

    def make_h(l, j, hbuf, HB, tiles=(0, 1, 2), rs_pool=None):
        for tt in tiles:
            t0, w, cnd = TT[tt]
            pb, pbb = bank()
            for c in range(8):
                sq, sqb = rs_pool["sq"].next()
                A(lambda e, sq=sq, c=c, t0=t0, w=w: e.activation(out=sq[:, 0:w], in_=x[:, c, t0:t0 + w], func=AF.Square),
                  [XB[c][tt]], [sqb])
                T(lambda e, pb=pb, sq=sq, c=c, w=w: e.matmul(pb[:, 0:w], lhsT=ones_b, rhs=sq[:, 0:w],
                                                               start=(c == 0), stop=(c == 7)), [sqb, CONSTB], [pbb])
            rs, rsb = rs_pool["rs"].next()
            A(lambda e, rs=rs, pb=pb, w=w: e.activation(out=rs[:, 0:w], in_=pb[:, 0:w], func=AF.Sqrt, bias=EPS, scale=1.0 / D),
              [pbb], [rsb])
            V(lambda e, rs=rs, w=w: e.reciprocal(rs[:, 0:w], rs[:, 0:w]), [rsb], [rsb])
            for c in range(8):
                tm, tmb = rs_pool["tm"].next()
                V(lambda e, tm=tm, rs=rs, c=c, t0=t0, w=w: e.tensor_tensor(out=tm[:, 0:w], in0=x[:, c, t0:t0 + w], in1=rs[:, 0:w], op=ALU.mult),
                  [XB[c][tt], rsb], [tmb])
                A(lambda e, tm=tm, c=c, t0=t0, w=w, cnd=cnd: e.activation(
                    out=hbuf[:, c, t0:t0 + w], in_=tm[:, 0:w], func=AF.Identity,
                    scale=modc(l, cnd, 3 * j + 1, c), bias=modc(l, cnd, 3 * j, c)), [tmb, MODL[l]], [HB[tt]])

    def resid_update(pb, pbb, m, tt, gate_ap, extra_reads=()):
        t0, w, cnd = TT[tt]
        V(lambda e: e.scalar_tensor_tensor(out=x[:, m, t0:t0 + w], in0=pb[:, 0:w], scalar=gate_ap, op0=ALU.mult,
                                           in1=x[:, m, t0:t0 + w], op1=ALU.add), [pbb, MODL[0], MODL[1]] + list(extra_reads), [XB[m][tt]])

    def ffn(l, f):
        j = 0 if f == 0 else 2
        mk = K.mark()
        hbuf = K.sb("h", [128, 8, NT], BF16)
        HB = bufs(3, "h")
        hid = [K.sb(f"hid{i}", [128, 2, NT], BF16) for i in range(2)]
        HIDB = [bufs(3, "hid0_"), bufs(3, "hid1_")]
        pool = {"sq": Rot(K, "sq", [128, 512], BF16, 3), "rs": Rot(K, "rs", [128, 512], F32, 2),
                "tm": Rot(K, "tm", [128, 512], F32, 3)}
        sgp = Rot(K, "sg", [128, 512], F32, 3)
        make_h(l, j, hbuf, HB, rs_pool=pool)
        stream_mod = (l == 0 and (len(pending_mod) > 0 or len(inflight_mod) > 0))
        if stream_mod:
            WM = [K.sb(f"WM{i}", [128, 8, 512], BF16) for i in range(4)]
            WMB = bufs(4, "WM")
            assert not inflight_mod
        for g in range(11):
            if stream_mod:
                while inflight_mod:
                    (ml, ci, slot) = inflight_mod.pop(0)
                    mod_chunk(ml, ci, WM[slot], WMB[slot], f"wm{slot}", dma=False)
                    mod_mark(ml, ci)
                ffn_gi[0] += 1
            wa, wab, ska = nextA()
            wb, wbb, skb = nextB()
            K.dma("gpsimd", lambda e, wa=wa, g=g: e.dma_start(out=wa[:].rearrange("p k n -> p (k n)"), in_=ffn_w_in[l, f, g], max_dma_last_dim=8192),
                  writes=[wab], semkey=ska)
            K.dma("gpsimd", lambda e, wb=wb, g=g: e.dma_start(out=wb[:, 0:2048], in_=ffn_w_out[l, f, g], max_dma_last_dim=8192),
                  writes=[wbb], semkey=skb)
            if stream_mod and g < 10:
                nper = 2 if ffn_gi[0] <= 10 else 1
                for i in range(nper):
                    if pending_mod:
                        slot = 2 * (g % 2) + i
                        (ml, ci) = pending_mod.pop(0)
                        mod_dma(ml, ci, WM[slot], WMB[slot], f"wm{slot}")
                        inflight_mod.append((ml, ci, slot))
            hd = hid[g % 2]
            hdb = HIDB[g % 2]
            for jj in range(2):
                for tt in range(3):
                    t0, w, cnd = TT[tt]
                    pg, pgb = bank()
                    pu, pub = bank()
                    for k in range(8):
                        T(lambda e, pg=pg, wa=wa, k=k, jj=jj, t0=t0, w=w: e.matmul(
                            pg[:, 0:w], lhsT=wa[:, k, jj * 128:(jj + 1) * 128], rhs=hbuf[:, k, t0:t0 + w], start=(k == 0), stop=(k == 7)),
                            [wab, HB[tt]], [pgb])
                    for k in range(8):
                        T(lambda e, pu=pu, wa=wa, k=k, jj=jj, t0=t0, w=w: e.matmul(
                            pu[:, 0:w], lhsT=wa[:, k, 256 + jj * 128:256 + (jj + 1) * 128], rhs=hbuf[:, k, t0:t0 + w], start=(k == 0), stop=(k == 7)),
                            [wab, HB[tt]], [pub])
                    sg, sgb = sgp.next()
                    A(lambda e, sg=sg, pg=pg, w=w: e.activation(out=sg[:, 0:w], in_=pg[:, 0:w], func=AF.Silu), [pgb], [sgb])
                    V(lambda e, sg=sg, pu=pu, hd=hd, jj=jj, t0=t0, w=w: e.tensor_tensor(
                        out=hd[:, jj, t0:t0 + w], in0=sg[:, 0:w], in1=pu[:, 0:w], op=ALU.mult), [sgb, pub], [hdb[tt]])
            wbv = wb[:, 0:2048].rearrange("p (j n) -> p j n", j=2)
            for m in range(8):
                for tt in range(3):
                    t0, w, cnd = TT[tt]
                    po, pob = bank()
                    for jj in range(2):
                        T(lambda e, po=po, wbv=wbv, jj=jj, m=m, hd=hd, t0=t0, w=w: e.matmul(
                            po[:, 0:w], lhsT=wbv[:, jj, m * 128:(m + 1) * 128], rhs=hd[:, jj, t0:t0 + w], start=(jj == 0), stop=(jj == 1)),
                            [wbb, hdb[tt]], [pob])
                    resid_update(po, pob, m, tt, modc(l, cnd, 3 * j + 2, m))
        if stream_mod:
            while inflight_mod:
                (ml, ci, slot) = inflight_mod.pop(0)
                mod_chunk(ml, ci, WM[slot], WMB[slot], f"wm{slot}", dma=False)
                mod_mark(ml, ci)
        K.barrier()
        K.release(mk)

    def mixer(l):
        mk_all = K.mark()
        wmi = mix_w_in[l]
        wmo = mix_w_out[l]
        cvx = {}

        def load_wa(col0, ncols):
            wa, wab, sk = nextA()
            K.dma("gpsimd", lambda e: e.dma_start(out=wa[:, :, 0:ncols],
                                                  in_=wmi[:, col0:col0 + ncols].rearrange("(k p) n -> p k n", p=128)),
                  writes=[wab], semkey=sk)
            return wa, wab

        def proj_fm(pb, pbb, wa, wab, c0, M, hbuf, HB, tt):
            t0, w, cnd = TT[tt]
            for k in range(8):
                T(lambda e, k=k: e.matmul(pb[0:M, 0:w], lhsT=wa[:, k, c0:c0 + M], rhs=hbuf[:, k, t0:t0 + w],
                                          start=(k == 0), stop=(k == 7)), [wab, HB[tt]], [pbb])

        yreg = []

        def out_proj(ychunks, YB, wrow0, kp, tile=None):
            yreg.append((ychunks, YB, wrow0, kp, tile))

        def out_proj_all():
            packed = []
            for (ychunks, YB, wrow0, kp, tile) in yreg:
                if kp == 64 and tile is not None:
                    yp = K.sb("ypair", [128, 2, NT], BF16)
                    YPB = Buf("ypair")
                    tv = tile[:].rearrange("p (j two) t -> p j two t", two=2)
                    K.dma("sync", lambda e, yp=yp, tv=tv: e.dma_start(out=yp[0:64, :, :], in_=tv[:, :, 0, :]), reads=YB, writes=[YPB], semkey=f"ldyp{len(packed)}")
                    K.dma("sync", lambda e, yp=yp, tv=tv: e.dma_start(out=yp[64:128, :, :], in_=tv[:, :, 1, :]), reads=YB, writes=[YPB], semkey=f"ldyp{len(packed)}")
                    packed.append(([yp[:, 0, :], yp[:, 1, :]], [YPB], wrow0, 128))
                else:
                    packed.append((ychunks, YB, wrow0, kp))
            yreg[:] = packed
            slots = []
            for (ychunks, YB, wrow0, kp) in yreg:
                nk_ = len(ychunks)
                if len(slots) % 2 == 0:
                    wt_, wtb_, sk = nextB()
                    flat = wt_[:, :]
                else:
                    wt_, wtb_, sk = nextA()
                    flat = wt_[:].rearrange("p k n -> p (k n)")
                wv = flat[0:kp, 0:nk_ * 1024].rearrange("p (j n) -> p j n", j=nk_)
                K.dma("gpsimd", lambda e, wv=wv, wrow0=wrow0, nk_=nk_, kp=kp: e.dma_start(
                    out=wv, in_=wmo[wrow0:wrow0 + nk_ * kp, :].rearrange("(j p) n -> p j n", p=kp)), writes=[wtb_], semkey=sk)
                slots.append((wv, wtb_))
            total = sum(len(y[0]) for y in yreg)
            for m in range(8):
                for tt in range(3):
                    t0, w, cnd = TT[tt]
                    po, pob = bank()
                    i = 0
                    for (ychunks, YB, wrow0, kp), (wv, wtb_) in zip(yreg, slots):
                        for jj, yc in enumerate(ychunks):
                            T(lambda e, wv=wv, jj=jj, yc=yc, i=i: e.matmul(po[:, 0:w], lhsT=wv[:, jj, m * 128:(m + 1) * 128],
                                                                        rhs=yc[:, t0:t0 + w], start=(i == 0), stop=(i == total - 1)),
                              [wtb_] + YB, [pob])
                            i += 1
                    resid_update(po, pob, m, tt, modc(l, cnd, 5, m))

        def conv_chunk(raw, rawb, P, pc0, out_ap, outb, silu):
            hl = ccol("hl")
            hr = ccol("hr")
            V(lambda e: e.tensor_tensor(out=raw[0:P, 1:5, 0:1], in0=raw[0:P, 0:4, 256:257], in1=hl[0:P, 1:5].unsqueeze(2), op=ALU.mult),
              [rawb, CONSTB], [rawb])
            V(lambda e: e.tensor_tensor(out=raw[0:P, 0:4, 257:258], in0=raw[0:P, 1:5, 1:2], in1=hr[0:P, 0:4].unsqueeze(2), op=ALU.mult),
              [rawb, CONSTB], [rawb])
            acc, accb = cvx["convacc"].next()
            accv = acc[0:P, :].rearrange("p (b t) -> p b t", t=256)
            w0 = ptab[0:P, pc0:pc0 + 1]
            w1 = ptab[0:P, pc0 + 1:pc0 + 2]
            w2 = ptab[0:P, pc0 + 2:pc0 + 3]
            bb = ptab[0:P, pc0 + 3:pc0 + 4]
            V(lambda e: e.tensor_scalar(out=accv, in0=raw[0:P, :, 1:257], scalar1=w1, scalar2=None, op0=ALU.mult), [rawb, CONSTB], [accb])
            V(lambda e: e.scalar_tensor_tensor(out=accv, in0=raw[0:P, :, 0:256], scalar=w0, op0=ALU.mult, in1=accv, op1=ALU.add),
              [rawb, CONSTB, accb], [accb])
            V(lambda e: e.scalar_tensor_tensor(out=accv, in0=raw[0:P, :, 2:258], scalar=w2, op0=ALU.mult, in1=accv, op1=ALU.add),
              [rawb, CONSTB, accb], [accb])
            A(lambda e: e.activation(out=out_ap, in_=acc[0:P, :], func=(AF.Silu if silu else AF.Identity), bias=bb), [accb, CONSTB], [outb])

        def raw_fill(raw, rawb, P, pb, pbb, tt):
            t0, w, cnd = TT[tt]
            b0 = t0 // 256
            nb = w // 256
            A(lambda e: e.activation(out=raw[0:P, b0:b0 + nb, 1:257], in_=pb[0:P, 0:w].rearrange("p (b t) -> p b t", t=256), func=AF.Copy),
              [pbb], [rawb])

        hbuf_m = K.sb("hmix", [128, 8, NT], BF16)
        HB_m = bufs(3, "hmix")
        mkp = K.mark()
        pool_m = {"sq": Rot(K, "sq", [128, 512], BF16, 3), "rs": Rot(K, "rs", [128, 512], F32, 2),
                  "tm": Rot(K, "tm", [128, 512], F32, 2)}
        make_h(l, 1, hbuf_m, HB_m, rs_pool=pool_m)
        K.barrier()
        K.release(mkp)

        def with_h(fn, conv=False):
            mk = K.mark()
            if conv:
                cvx["convacc"] = Rot(K, "cacc", [128, NT], F32, 1)
                raws = Rot(K, "raw", [128, 5, 258], F32, 2)
                cvx["raws"] = raws
                for i in range(2):
                    V(lambda e, i=i: e.memset(raws.t[i][:], 0.0), [], [raws.b[i]])
            fn(hbuf_m, HB_m)
            K.barrier()
            K.release(mk)

        def ssd(szb, SZB):
            mk = K.mark()
            xsf = K.sb("xsf", [64, 4, NT], BF16)
            XSF = Buf("xsf")
            bcf = K.sb("bcf", [128, 4, NT], BF16)
            BCF = Buf("bcf")
            dtr = K.sb("dtr", [128, 10, 8])
            dtt = K.sb("dtt", [128, 10, 8])
            lat = K.sb("lat", [128, 10, 8])
            DTB = Buf("dt")

            def inproj(hbuf, HB):
                wa0, wab0 = load_wa(0, 512)
                wa1, wab1 = load_wa(512, 512)
                for hh in range(4):
                    for tt in range(3):
                        t0, w, cnd = TT[tt]
                        pb, pbb = bank()
                        proj_fm(pb, pbb, wa0, wab0, hh * 64, 64, hbuf, HB, tt)
                        A(lambda e, pb=pb, hh=hh, t0=t0, w=w: e.activation(out=szb[:, hh, t0:t0 + w], in_=pb[0:64, 0:w], func=AF.Silu), [pbb], [SZB])
                for q in range(8):
                    raw, rawb = cvx["raws"].next()
                    P = 64 if q < 4 else 128
                    for tt in range(3):
                        pb, pbb = bank()
                        if q < 4:
                            proj_fm(pb, pbb, wa0, wab0, 256 + q * 64, 64, hbuf, HB, tt)
                        else:
                            proj_fm(pb, pbb, wa1, wab1, (q - 4) * 128, 128, hbuf, HB, tt)
                        raw_fill(raw, rawb, P, pb, pbb, tt)
                    pc0 = PT["ssd_conv"][0] + (l * 8 + q) * 4
                    if q < 4:
                        conv_chunk(raw, rawb, 64, pc0, xsf[:, q, :], XSF, True)
                    else:
                        conv_chunk(raw, rawb, 128, pc0, bcf[:, q - 4, :], BCF, True)
                wdt = K.sb("wdt", [128, 8, 8], BF16)
                WDT = Buf("wdt")
                K.dma("gpsimd", lambda e: e.dma_start(out=wdt[:], in_=wmi[:, 1024:1032].rearrange("(k p) n -> p k n", p=128)),
                      writes=[WDT], semkey="wdt")
                pb, pbb = bank()
                for tc in range(NCH):
                    tt = 0 if tc < 4 else (1 if tc < 8 else 2)
                    for k in range(8):
                        T(lambda e, tc=tc, k=k: e.matmul(pb[:, tc * 8:tc * 8 + 8], lhsT=hbuf[:, k, tc * 128:(tc + 1) * 128], rhs=wdt[:, k, :],
                                                         start=(k == 0), stop=(k == 7)), [HB[tt], WDT], [pbb])
                V(lambda e: e.tensor_tensor(out=dtr[:].rearrange("p a b -> p (a b)"), in0=pb[:, 0:80], in1=fcol("dt_bias", l * 80, 80), op=ALU.add),
                  [pbb, CONSTB], [DTB])
            with_h(inproj, conv=True)
            A(lambda e: e.activation(out=dtt[:], in_=dtr[:], func=AF.Exp), [DTB], [DTB])
            A(lambda e: e.activation(out=dtt[:], in_=dtt[:], func=AF.Ln, bias=1.0), [DTB], [DTB])
            A(lambda e: e.activation(out=dtr[:].rearrange("p a b -> p (a b)"), in_=fcol("a_log", l * 80, 80), func=AF.Exp), [DTB, CONSTB], [DTB])
            V(lambda e: e.scalar_tensor_tensor(out=lat[:], in0=dtr[:], scalar=-1.0, op0=ALU.mult, in1=dtt[:], op1=ALU.mult), [DTB], [DTB])
            xbtm = K.sb("xbtm", [128, NCH, 512], BF16)
            XBT = bufs(NCH, "xbtm")
            for tc in range(NCH):
                pb, pbb = bank()
                pbv = pb[:].bitcast(BF16)
                for hh in range(4):
                    T(lambda e, hh=hh, tc=tc: e.transpose(pbv[:, hh * 64:(hh + 1) * 64], xsf[:, hh, tc * 128:(tc + 1) * 128], ident_b[0:64, 0:64]),
                      [XSF, CONSTB], [pbb])
                for g in range(2):
                    T(lambda e, g=g, tc=tc: e.transpose(pbv[:, 256 + g * 128:256 + (g + 1) * 128], bcf[:, g, tc * 128:(tc + 1) * 128], ident_b),
                      [BCF, CONSTB], [pbb])
                A(lambda e, tc=tc, pbv=pbv: e.activation(out=xbtm[:, tc, :], in_=pbv[:, 0:512], func=AF.Copy), [pbb], [XBT[tc]])
            cst = K.sb("cst", [128, NCH, 16])
            wBt = K.sb("wBt", [128, NCH, 8])
            edt = K.sb("edt", [128, NCH, 8])
            pb, pbb = bank()
            for tc in range(NCH):
                T(lambda e, tc=tc: e.matmul(pb[:, tc * 16:tc * 16 + 4], lhsT=trif, rhs=lat[:, tc, 0:4], start=True, stop=True), [DTB, CONSTB], [pbb])
                T(lambda e, tc=tc: e.matmul(pb[:, tc * 16 + 4:tc * 16 + 8], lhsT=trib, rhs=lat[:, tc, 4:8], start=True, stop=True), [DTB, CONSTB], [pbb])
                T(lambda e, tc=tc: e.matmul(pb[:, tc * 16 + 8:tc * 16 + 16], lhsT=ones_f, rhs=lat[:, tc, 0:8], start=True, stop=True), [DTB, CONSTB], [pbb])
            V(lambda e: e.tensor_copy(cst[:].rearrange("p a b -> p (a b)"), pb[:, 0:160]), [pbb], [DTB])
            V(lambda e: e.tensor_tensor(out=wBt[:], in0=cst[:, :, 8:16], in1=cst[:, :, 0:8], op=ALU.subtract), [DTB], [DTB])
            A(lambda e: e.activation(out=wBt[:], in_=wBt[:], func=AF.Exp), [DTB], [DTB])
            V(lambda e: e.tensor_tensor(out=wBt[:], in0=wBt[:], in1=dtt[:], op=ALU.mult), [DTB], [DTB])
            A(lambda e: e.activation(out=edt[:], in_=cst[:, :, 8:16], func=AF.Exp), [DTB], [DTB])
            S32 = K.sb("S32", [128, 8, 64])
            SB_ = bufs(8, "S32")
            SP = K.sb("SP", [128, NCH, 8, 64], BF16)
            SPB = [bufs(8, f"SP{c}_") for c in range(NCH)]
            mk_st = K.mark()
            s0t = K.sb("s0t", [128, 40, 64])
            S0B = Buf("s0")
            K.dma("sync", lambda e: e.dma_start(out=s0t[:], in_=s0ssd[:, l * 40:(l + 1) * 40, :]), writes=[S0B], semkey="lds0")
            V(lambda e: e.memset(S32[:], 0.0), [], SB_)
            hl = ccol("hl")
            hr = ccol("hr")
            bsp = Rot(K, "bs", [128, 128], BF16, 12)

            def state_chain(d, hh):
                col = d * 4 + hh
                g = hh // 2
                for k in range(NCH):
                    c = k if d == 0 else NCH - 1 - k
                    blk = c // 2
                    first = (c % 2 == 0) if d == 0 else (c % 2 == 1)
                    sidx = (blk * 2 + d) * 4 + hh
                    if first:
                        fl = hl[:, blk:blk + 1] if d == 0 else hr[:, blk:blk + 1]
                        V(lambda e: e.scalar_tensor_tensor(out=S32[:, col, :], in0=S32[:, col, :], scalar=fl, op0=ALU.mult, in1=s0t[:, sidx, :], op1=ALU.add),
                          [SB_[col], S0B, CONSTB], [SB_[col]])
                    A(lambda e: e.activation(out=SP[:, c, col, :], in_=S32[:, col, :], func=AF.Copy), [SB_[col]], [SPB[c][col]])
                    bs, bsb = bsp.next()
                    V(lambda e: e.tensor_scalar(out=bs[:], in0=xbtm[:, c, 256 + g * 128:256 + (g + 1) * 128], scalar1=wBt[:, c, col:col + 1], scalar2=None, op0=ALU.mult),
                      [XBT[c], DTB], [bsb])
                    yield
                    pb, pbb = pr(64)
                    T(lambda e: e.matmul(pb[:, 0:64], lhsT=bs[:], rhs=xbtm[:, c, hh * 64:(hh + 1) * 64], start=True, stop=True), [bsb, XBT[c]], [pbb])
                    yield
                    V(lambda e: e.scalar_tensor_tensor(out=S32[:, col, :], in0=S32[:, col, :], scalar=edt[:, c, col:col + 1], op0=ALU.mult, in1=pb[:, 0:64], op1=ALU.add),
                      [SB_[col], DTB, pbb], [SB_[col]])
                    if not first:
                        oidx = l * 40 + sidx
                        sslot = sstp.i
                        stg, stgb = sstp.next()
                        A(lambda e: e.activation(out=stg[:], in_=S32[:, col, :], func=AF.Copy), [SB_[col]], [stgb])
                        K.dma("sync", lambda e: e.dma_start(out=nssd_o[:, oidx, :], in_=stg[:]), reads=[stgb], semkey=f"stS{sslot}")
                    yield
            sstp = Rot(K, "sstg", [128, 64], F32, 8)
            interleave([state_chain(d, hh) for d in range(2) for hh in range(4)])
            K.barrier()
            K.release(mk_st)

            dsk = pcol("ssd_d", l * 4, 4, rows=64)
            nws = pcol("ssd_nw", l * 4, 4, rows=64)
            ygp = Rot(K, "yg", [64, 4, 128], F32, 2)
            wtp = Rot(K, "wt", [128, 128], F32, 16)
            sgp2 = Rot(K, "sg2", [128, 128], F32, 16)
            csp = Rot(K, "cs", [128, 128], BF16, 16)
            sqp = Rot(K, "sq4", [64, 128], BF16, 8)

            def ssd_chain(c, hh, d, psc, pscb, res):
                cs = slice(c * 128, (c + 1) * 128)
                col = d * 4 + hh
                g = hh // 2
                U = trif if d == 0 else trib
                NB = ccol("nbf") if d == 0 else ccol("nbb")
                wt, wtb = wtp.next()
                G(lambda e: e.tensor_scalar(out=wt[:], in0=U, scalar1=lat[:, c, col:col + 1], scalar2=0.0, op0=ALU.mult, op1=ALU.add), [CONSTB, DTB], [wtb])
                yield
                pc, pcb = pr(128)
                T(lambda e: e.matmul(pc[:, 0:128], lhsT=ones_f, rhs=wt[:], start=True, stop=True), [wtb, CONSTB], [pcb])
                yield
                sg, sgb = sgp2.next()
                V(lambda e: e.scalar_tensor_tensor(out=sg[:], in0=pc[:, 0:128], scalar=cst[:, c, col:col + 1], op0=ALU.subtract, in1=NB, op1=ALU.add),
                  [pcb, DTB, CONSTB], [sgb])
                yield
                ec, ecb = wt, wtb
                A(lambda e: e.activation(out=ec[:], in_=pc[:, 0:128], func=AF.Exp), [pcb], [ecb])
                yield
                A(lambda e: e.activation(out=sg[:], in_=sg[:], func=AF.Exp), [sgb], [sgb])
                csb, csbb = csp.next()
                G(lambda e: e.tensor_tensor(out=csb[:], in0=bcf[:, 2 + g, cs], in1=ec[:], op=ALU.mult), [BCF, ecb], [csbb])
                yield
                yield
                st, stb = wt[:].bitcast(BF16)[:, 0:128], wtb
                V(lambda e: e.scalar_tensor_tensor(out=st, in0=psc[:, g * 128:(g + 1) * 128], scalar=dtt[:, c, col:col + 1], op0=ALU.mult, in1=sg[:], op1=ALU.mult),
                  [pscb, sgb, DTB], [stb])
                res[(hh, d)] = (st, stb, csb, csbb)
                yield

            def ssd_chunk(c):
                cs = slice(c * 128, (c + 1) * 128)
                psc, pscb = pr(256)
                for g in range(2):
                    T(lambda e: e.matmul(psc[:, g * 128:(g + 1) * 128], lhsT=bcf[:, g, cs], rhs=bcf[:, 2 + g, cs], start=True, stop=True), [BCF], [pscb])
                res = {}
                chains = [ssd_chain(c, hh, d, psc, pscb, res) for hh in range(4) for d in range(2)]
                while chains:
                    nxt = []
                    for gch in chains:
                        try:
                            next(gch)
                            nxt.append(gch)
                        except StopIteration:
                            pass
                    chains = nxt
                    yield
                yg, ygb = ygp.next()
                pys = []
                for hh in range(4):
                    py, pyb = pr(128)
                    pys.append((py, pyb))
                    for d in range(2):
                        col = d * 4 + hh
                        st, stb, csb, csbb = res[(hh, d)]
                        T(lambda e: e.matmul(py[0:64, 0:128], lhsT=xbtm[:, c, hh * 64:(hh + 1) * 64], rhs=st, start=(d == 0), stop=False), [XBT[c], stb], [pyb])
                        T(lambda e: e.matmul(py[0:64, 0:128], lhsT=SP[:, c, col, :], rhs=csb[:], start=False, stop=(d == 1)), [SPB[c][col], csbb], [pyb])
                yield
                sqs = []
                for hh in range(4):
                    py, pyb = pys[hh]
                    V(lambda e: e.scalar_tensor_tensor(out=yg[:, hh, :], in0=xsf[:, hh, cs], scalar=dsk[:, hh:hh + 1], op0=ALU.mult, in1=py[0:64, 0:128], op1=ALU.add),
                      [XSF, CONSTB, pyb], [ygb])
                    V(lambda e: e.tensor_tensor(out=yg[:, hh, :], in0=yg[:, hh, :], in1=szb[:, hh, cs], op=ALU.mult), [ygb, SZB], [ygb])
                    sq, sqb = sqp.next()
                    A(lambda e: e.activation(out=sq[:], in_=yg[:, hh, :], func=AF.Square), [ygb], [sqb])
                    sqs.append((sq, sqb))
                    yield
                pq, pqb = pr(128)
                for hh in range(4):
                    sq, sqb = sqs[hh]
                    T(lambda e: e.matmul(pq[0:64, 0:128], lhsT=ones_b[0:64, 0:64], rhs=sq[:], start=(hh == 0), stop=(hh == 3)), [sqb, CONSTB], [pqb])
                yield
                rs, rsb = rsp2.next()
                A(lambda e: e.activation(out=rs[:], in_=pq[0:64, 0:128], func=AF.Sqrt, bias=EPS, scale=1.0 / 256), [pqb], [rsb])
                yield
                V(lambda e: e.reciprocal(rs[:], rs[:]), [rsb], [rsb])
                yield
                for hh in range(4):
                    V(lambda e: e.scalar_tensor_tensor(out=szb[:, hh, cs], in0=yg[:, hh, :], scalar=nws[:, hh:hh + 1], op0=ALU.mult, in1=rs[:], op1=ALU.mult),
                      [ygb, rsb, CONSTB], [SZB])
                    yield
            rsp2 = Rot(K, "rs2", [64, 128], F32, 2)
            for c0 in range(0, NCH, 2):
                interleave([ssd_chunk(c0), ssd_chunk(c0 + 1)])
            out_proj([szb[:, hh, :] for hh in range(4)], [SZB], 0, 64, tile=szb)
            K.barrier()
            K.release(mk)

        def ret(sgf, SGF):
            mk = K.mark()
            qf = K.sb("qf", [64, 4, NT], BF16)
            kf = K.sb("kf", [64, 4, NT], BF16)
            kvt = K.sb("kvt", [128, NCH, 256], BF16)
            QF, KF = Buf("qf"), Buf("kf")
            KVT = bufs(NCH, "kvt")
            KST = Buf("kst")
            lg = K.sb("lg", [128, 8])
            LG = Buf("lg")
            A(lambda e: e.activation(out=lg[:], in_=fcol("ret_logit", l * 8, 8), func=AF.Exp, scale=-1.0), [CONSTB], [LG])
            A(lambda e: e.activation(out=lg[:], in_=lg[:], func=AF.Ln, bias=1.0), [LG], [LG])
            V(lambda e: e.tensor_scalar(out=lg[:], in0=lg[:], scalar1=-1.0, scalar2=None, op0=ALU.mult), [LG], [LG])
            Tl = K.sb("Tl", [128, 8])
            G128 = K.sb("G128", [128, 8])
            RC = Buf("retc")
            for hh in range(4):
                A(lambda e, hh=hh: e.activation(out=Tl[:, hh:hh + 1], in_=ccol("tailf"), func=AF.Exp, scale=lg[:, hh:hh + 1]), [LG, CONSTB], [RC])
                A(lambda e, hh=hh: e.activation(out=Tl[:, 4 + hh:5 + hh], in_=ccol("tailb"), func=AF.Exp, scale=lg[:, 4 + hh:5 + hh]), [LG, CONSTB], [RC])
            V(lambda e: e.tensor_scalar(out=Tl[:], in0=Tl[:], scalar1=0.125, scalar2=None, op0=ALU.mult), [RC], [RC])
            A(lambda e: e.activation(out=G128[:], in_=lg[:], func=AF.Exp, scale=128.0), [LG], [RC])
            S32 = K.sb("R32", [64, 8, 64])
            SB_ = bufs(8, "R32")
            SP = K.sb("RSP", [64, NCH, 8, 64], BF16)
            SPB = [bufs(8, f"RSP{c}_") for c in range(NCH)]
            mk_k = K.mark()
            kst = K.sb("kst", [128, 2, NCH, 256], BF16)

            def inproj(hbuf, HB):
                wa0, wab0 = load_wa(1800, 512)
                wa1, wab1 = load_wa(2312, 512)
                for hh in range(4):
                    for tt in range(3):
                        t0, w, cnd = TT[tt]
                        pb, pbb = bank()
                        proj_fm(pb, pbb, wa0, wab0, hh * 64, 64, hbuf, HB, tt)
                        A(lambda e, pb=pb, hh=hh, t0=t0, w=w: e.activation(out=qf[:, hh, t0:t0 + w], in_=pb[0:64, 0:w], func=AF.Copy), [pbb], [QF])
                        pb, pbb = bank()
                        proj_fm(pb, pbb, wa0, wab0, 256 + hh * 64, 64, hbuf, HB, tt)
                        A(lambda e, pb=pb, hh=hh, t0=t0, w=w: e.activation(out=kf[:, hh, t0:t0 + w], in_=pb[0:64, 0:w], func=AF.Copy, scale=0.125), [pbb], [KF])
                        pb, pbb = bank()
                        proj_fm(pb, pbb, wa1, wab1, 256 + hh * 64, 64, hbuf, HB, tt)
                        A(lambda e, pb=pb, hh=hh, t0=t0, w=w: e.activation(out=sgf[:, hh, t0:t0 + w], in_=pb[0:64, 0:w], func=AF.Silu), [pbb], [SGF])
                for tc in range(NCH):
                    tt = 0 if tc < 4 else (1 if tc < 8 else 2)
                    pb, pbb = bank()
                    for k in range(8):
                        T(lambda e, pb=pb, tc=tc, k=k: e.matmul(pb[:, 0:256], lhsT=hbuf[:, k, tc * 128:(tc + 1) * 128], rhs=wa0[:, k, 256:512],
                                                                 start=(k == 0), stop=(k == 7)), [HB[tt], wab0], [pbb])
                    for k in range(8):
                        T(lambda e, pb=pb, tc=tc, k=k: e.matmul(pb[:, 256:512], lhsT=hbuf[:, k, tc * 128:(tc + 1) * 128], rhs=wa1[:, k, 0:256],
                                                                 start=(k == 0), stop=(k == 7)), [HB[tt], wab1], [pbb])
                    for d in range(2):
                        V(lambda e, pb=pb, tc=tc, d=d: e.tensor_tensor(out=kst[:, d, tc, :].rearrange("p (h n) -> p h n", h=4),
                                                                       in0=pb[:, 0:256].rearrange("p (h n) -> p h n", h=4),
                                                                       in1=Tl[:, d * 4:(d + 1) * 4].unsqueeze(2).broadcast_to([128, 4, 64]), op=ALU.mult),
                          [pbb, RC], [KST])
                    A(lambda e, pb=pb, tc=tc: e.activation(out=kvt[:, tc, :], in_=pb[:, 256:512], func=AF.Copy), [pbb], [KVT[tc]])
            with_h(inproj)
            mk_st = K.mark()
            s0t = K.sb("rs0t", [64, 40, 64])
            S0B = Buf("rs0")
            K.dma("sync", lambda e: e.dma_start(out=s0t[:], in_=s0ret[:, l * 40:(l + 1) * 40, :]), writes=[S0B], semkey="lds0")
            V(lambda e: e.memset(S32[:], 0.0), [], SB_)
            hl = ccol("hl")
            hr = ccol("hr")
            def rstate_chain(d, hh):
                col = d * 4 + hh
                for k in range(NCH):
                    c = k if d == 0 else NCH - 1 - k
                    blk = c // 2
                    first = (c % 2 == 0) if d == 0 else (c % 2 == 1)
                    sidx = (blk * 2 + d) * 4 + hh
                    if first:
                        fl = hl[0:64, blk:blk + 1] if d == 0 else hr[0:64, blk:blk + 1]
                        V(lambda e: e.scalar_tensor_tensor(out=S32[:, col, :], in0=S32[:, col, :], scalar=fl, op0=ALU.mult, in1=s0t[:, sidx, :], op1=ALU.add),
                          [SB_[col], S0B, CONSTB], [SB_[col]])
                    A(lambda e: e.activation(out=SP[:, c, col, :], in_=S32[:, col, :], func=AF.Copy), [SB_[col]], [SPB[c][col]])
                    yield
                    pb, pbb = pr(64)
                    T(lambda e: e.matmul(pb[0:64, 0:64], lhsT=kst[:, d, c, hh * 64:(hh + 1) * 64], rhs=kvt[:, c, hh * 64:(hh + 1) * 64],
                                         start=True, stop=True), [KST, KVT[c]], [pbb])
                    yield
                    V(lambda e: e.scalar_tensor_tensor(out=S32[:, col, :], in0=S32[:, col, :], scalar=G128[0:64, col:col + 1], op0=ALU.mult, in1=pb[0:64, 0:64], op1=ALU.add),
                      [SB_[col], RC, pbb], [SB_[col]])
                    if not first:
                        oidx = l * 40 + sidx
                        sslot = rstp.i
                        stg, stgb = rstp.next()
                        A(lambda e: e.activation(out=stg[:], in_=S32[:, col, :], func=AF.Copy), [SB_[col]], [stgb])
                        K.dma("sync", lambda e: e.dma_start(out=nret_o[:, oidx, :], in_=stg[:]), reads=[stgb], semkey=f"stR{sslot}")
                    yield
            rstp = Rot(K, "rstg", [64, 64], F32, 8)
            interleave([rstate_chain(d, hh) for d in range(2) for hh in range(4)])
            K.barrier()
            K.release(mk_k)

            gnw = pcol("ret_gn", l * 4, 4, rows=64)
            Dm = K.sb("Dm", [128, 4, 128])
            Ef = K.sb("Ef", [64, 8, 128])
            tmpf = Rot(K, "tmpf", [128, 128], F32, 2)
            for hh in range(4):
                t1, t1b = tmpf.next()
                A(lambda e, t1=t1, hh=hh: e.activation(out=t1[:], in_=ccol("relu_f"), func=AF.Exp, scale=lg[:, hh:hh + 1]), [LG, CONSTB], [t1b])
                V(lambda e, t1=t1, hh=hh: e.tensor_tensor(out=Dm[:, hh, :], in0=t1[:], in1=trif, op=ALU.mult), [t1b, CONSTB], [RC])
                t2, t2b = tmpf.next()
                A(lambda e, t2=t2, hh=hh: e.activation(out=t2[:], in_=ccol("relu_b"), func=AF.Exp, scale=lg[:, 4 + hh:5 + hh]), [LG, CONSTB], [t2b])
                V(lambda e, t2=t2: e.tensor_tensor(out=t2[:], in0=t2[:], in1=trib, op=ALU.mult), [t2b, CONSTB], [t2b])
                V(lambda e, t2=t2, hh=hh: e.tensor_tensor(out=Dm[:, hh, :], in0=Dm[:, hh, :], in1=t2[:], op=ALU.add), [t2b, RC], [RC])
                A(lambda e, hh=hh: e.activation(out=Ef[:, hh, :], in_=ccol("ip1", rows=64), func=AF.Exp, scale=lg[0:64, hh:hh + 1]), [LG, CONSTB], [RC])
                A(lambda e, hh=hh: e.activation(out=Ef[:, 4 + hh, :], in_=ccol("rmi", rows=64), func=AF.Exp, scale=lg[0:64, 4 + hh:5 + hh]), [LG, CONSTB], [RC])
            qsp = Rot(K, "qs4", [64, 2, 4, 128], BF16, 2)
            st4p = Rot(K, "st4", [128, 4, 128], BF16, 2)
            yvp = Rot(K, "yv4", [64, 4, 128], F32, 2)
            sq4p = Rot(K, "sq4", [64, 4, 128], F32, 2)

            def ret_chunk(c):
                cs = slice(c * 128, (c + 1) * 128)
                ps_, psb = bank()
                for hh in range(4):
                    T(lambda e: e.matmul(ps_[:, hh * 128:(hh + 1) * 128], lhsT=kf[:, hh, cs], rhs=qf[:, hh, cs], start=True, stop=True), [KF, QF], [psb])
                yield
                st, stb = st4p.next()
                V(lambda e: e.tensor_tensor(out=st[:], in0=ps_[:, 0:512].rearrange("p (h t) -> p h t", h=4), in1=Dm[:], op=ALU.mult), [psb, RC], [stb])
                qs, qsb = qsp.next()
                V(lambda e: e.tensor_tensor(out=qs[:], in0=qf[:, :, cs].unsqueeze(1).broadcast_to([64, 2, 4, 128]),
                                            in1=Ef[:].rearrange("p (d h) t -> p d h t", d=2), op=ALU.mult), [QF, RC], [qsb])
                yield
                py, pyb = bank()
                for hh in range(4):
                    T(lambda e: e.matmul(py[0:64, hh * 128:(hh + 1) * 128], lhsT=kvt[:, c, hh * 64:(hh + 1) * 64], rhs=st[:, hh, :],
                                         start=True, stop=False), [KVT[c], stb], [pyb])
                    T(lambda e: e.matmul(py[0:64, hh * 128:(hh + 1) * 128], lhsT=SP[:, c, hh, :], rhs=qs[:, 0, hh, :], start=False, stop=False),
                      [SPB[c][hh], qsb], [pyb])
                    T(lambda e: e.matmul(py[0:64, hh * 128:(hh + 1) * 128], lhsT=SP[:, c, 4 + hh, :], rhs=qs[:, 1, hh, :], start=False, stop=True),
                      [SPB[c][4 + hh], qsb], [pyb])
                yield
                yv, yvb = yvp.next()
                yvf = yv[:].rearrange("p h t -> p (h t)")
                V(lambda e: e.tensor_copy(yvf, py[0:64, 0:512]), [pyb], [yvb])
                yield
                pm, pmb = bank()
                T(lambda e: e.matmul(pm[0:64, 0:512], lhsT=ones_f[0:64, 0:64], rhs=yvf, start=True, stop=True), [yvb, CONSTB], [pmb])
                yield
                V(lambda e: e.scalar_tensor_tensor(out=yvf, in0=pm[0:64, 0:512], scalar=-1.0 / 64, op0=ALU.mult, in1=yvf, op1=ALU.add), [pmb, yvb], [yvb])
                yield
                sq, sqb = sq4p.next()
                sqf = sq[:].rearrange("p h t -> p (h t)")
                A(lambda e: e.activation(out=sqf, in_=yvf, func=AF.Square), [yvb], [sqb])
                yield
                pv_, pvb = bank()
                T(lambda e: e.matmul(pv_[0:64, 0:512], lhsT=ones_f[0:64, 0:64], rhs=sqf, start=True, stop=True), [sqb, CONSTB], [pvb])
                yield
                A(lambda e: e.activation(out=sqf, in_=pv_[0:64, 0:512], func=AF.Sqrt, bias=EPS, scale=1.0 / 64), [pvb], [sqb])
                yield
                V(lambda e: e.reciprocal(sqf, sqf), [sqb], [sqb])
                yield
                V(lambda e: e.tensor_tensor(out=yvf, in0=yvf, in1=sqf, op=ALU.mult), [yvb, sqb], [yvb])
                V(lambda e: e.tensor_tensor(out=yv[:], in0=yv[:], in1=gnw.unsqueeze(2).broadcast_to([64, 4, 128]), op=ALU.mult), [yvb, CONSTB], [yvb])
                yield
                V(lambda e: e.tensor_tensor(out=sgf[:, :, cs], in0=yv[:], in1=sgf[:, :, cs], op=ALU.mult), [yvb, SGF], [SGF])
                yield
            for c0 in range(0, NCH, 2):
                interleave([ret_chunk(c0), ret_chunk(c0 + 1)])
            out_proj([sgf[:, hh, :] for hh in range(4)], [SGF], 512, 64, tile=sgf)
            K.barrier()
            K.release(mk)

        def att(yat, YA):
            mk = K.mark()
            qfm = K.sb("aq", [64, 4, NT], BF16)
            kfm = K.sb("ak", [64, 2, NT], BF16)
            vtm = K.sb("av", [128, NCH, 128], BF16)
            QF, KF = Buf("aq"), Buf("ak")
            VT = bufs(NCH, "av")
            ckf = K.sb("ckf", [64, 2, 512], BF16)
            cvs = K.sb("cvs", [128, 4, 128], BF16)
            CK = Buf("ck")
            K.dma("gpsimd", lambda e: e.dma_start(out=ckf[:], in_=ckT[:, l * 2:l * 2 + 2, :]), writes=[CK], semkey="ldck")
            K.dma("gpsimd", lambda e: e.dma_start(out=cvs[:], in_=cvt[:, l * 4:l * 4 + 4, :]), writes=[CK], semkey="ldck")
            es = K.sb("es", [64, 4])
            A(lambda e: e.activation(out=es[:], in_=fcol("sink", l * 4, 4, rows=64), func=AF.Exp), [CONSTB], [CK])
            mk_in = K.mark()
            qkp = Rot(K, "qk", [128, 512], F32, 3)
            qnp = Rot(K, "qn", [128, 384], F32, 3)
            qbp = Rot(K, "qb", [128, 384], BF16, 3)
            smp = Rot(K, "sm", [128, 8], F32, 3)
            rtp = Rot(K, "rt", [128, 6, 2, 16], F32, 8)
            kvo = Rot(K, "kvo", [128, 256], F32, 3)

            def inproj(hbuf, HB):
                wa, wab = load_wa(2824, 512)

                def in_chunk(tc):
                    tt = 0 if tc < 4 else (1 if tc < 8 else 2)
                    pb, pbb = bank()
                    for k in range(8):
                        T(lambda e: e.matmul(pb[:, 0:512], lhsT=hbuf[:, k, tc * 128:(tc + 1) * 128], rhs=wa[:, k, :], start=(k == 0), stop=(k == 7)),
                          [HB[tt], wab], [pbb])
                    yield
                    qk, qkb = qkp.next()
                    A(lambda e: e.activation(out=qk[:], in_=pb[:, 0:512], func=AF.Copy), [pbb], [qkb])
                    yield
                    qn, qnb = qnp.next()
                    sm, smb = smp.next()
                    A(lambda e: e.activation(out=qn[:], in_=qk[:, 0:384], func=AF.Square), [qkb], [qnb])
                    A(lambda e: e.activation(out=vtm[:, tc, :], in_=qk[:, 384:512], func=AF.Copy), [qkb], [VT[tc]])
                    yield
                    V(lambda e: e.tensor_reduce(out=sm[:, 0:6], in_=qn[:].rearrange("p (h d) -> p h d", d=64), axis=AX.X, op=ALU.add), [qnb], [smb])
                    yield
                    A(lambda e: e.activation(out=sm[:, 0:6], in_=sm[:, 0:6], func=AF.Sqrt, bias=EPS, scale=1.0 / 64), [smb], [smb])
                    yield
                    V(lambda e: e.reciprocal(sm[:, 0:6], sm[:, 0:6]), [smb], [smb])
                    yield
                    V(lambda e: e.tensor_tensor(out=qn[:].rearrange("p (h d) -> p h d", d=64), in0=qk[:, 0:384].rearrange("p (h d) -> p h d", d=64),
                                                in1=sm[:, 0:6].unsqueeze(2).broadcast_to([128, 6, 64]), op=ALU.mult), [qkb, smb], [qnb])
                    yield
                    V(lambda e: e.tensor_tensor(out=qn[:], in0=qn[:], in1=fcol("qkw", l * 384, 384), op=ALU.mult), [qnb, CONSTB], [qnb])
                    yield
                    qv = qn[:].rearrange("p (h a b f) -> p h a b f", h=6, a=2, b=2)
                    cosv = ccol("cos", tc * 32, 32).rearrange("p (a f) -> p a f", a=2).unsqueeze(1).broadcast_to([128, 6, 2, 16])
                    sinv = ccol("sin", tc * 32, 32).rearrange("p (a f) -> p a f", a=2).unsqueeze(1).broadcast_to([128, 6, 2, 16])
                    x1 = qv[:, :, :, 0, :]
                    x2 = qv[:, :, :, 1, :]
                    t1, t1b = rtp.next()
                    t2, t2b = rtp.next()
                    t3, t3b = rtp.next()
                    t4, t4b = rtp.next()
                    V(lambda e: e.tensor_tensor(out=t1[:], in0=x1, in1=cosv, op=ALU.mult), [qnb, CONSTB], [t1b])
                    V(lambda e: e.tensor_tensor(out=t3[:], in0=x1, in1=sinv, op=ALU.mult), [qnb, CONSTB], [t3b])
                    yield
                    V(lambda e: e.tensor_tensor(out=t2[:], in0=x2, in1=sinv, op=ALU.mult), [qnb, CONSTB], [t2b])
                    V(lambda e: e.tensor_tensor(out=t4[:], in0=x2, in1=cosv, op=ALU.mult), [qnb, CONSTB], [t4b])
                    yield
                    V(lambda e: e.tensor_tensor(out=x1, in0=t1[:], in1=t2[:], op=ALU.subtract), [t1b, t2b], [qnb])
                    yield
                    V(lambda e: e.tensor_tensor(out=x2, in0=t3[:], in1=t4[:], op=ALU.add), [t3b, t4b], [qnb])
                    yield
                    kslot = kvo.i
                    ko, kob = kvo.next()
                    V(lambda e: e.tensor_copy(ko[:, 0:128], qn[:, 256:384]), [qnb], [kob])
                    V(lambda e: e.tensor_copy(ko[:, 128:256], qk[:, 384:512]), [qkb], [kob])
                    qb, qbb = qbp.next()
                    A(lambda e: e.activation(out=qb[:], in_=qn[:], func=AF.Copy), [qnb], [qbb])
                    yield
                    K.dma("sync", lambda e: e.dma_start(out=nk_o[l, tc * 128:(tc + 1) * 128, :], in_=ko[:, 0:128]), reads=[kob], semkey=f"stk{kslot}")
                    K.dma("sync", lambda e: e.dma_start(out=nv_o[l, tc * 128:(tc + 1) * 128, :], in_=ko[:, 128:256]), reads=[kob], semkey=f"stk{kslot}")
                    pt, ptb = bank()
                    ptv = pt[:].bitcast(BF16)
                    for hh in range(6):
                        T(lambda e: e.transpose(ptv[0:64, hh * 128:(hh + 1) * 128], qb[:, hh * 64:(hh + 1) * 64], ident_b), [qbb, CONSTB], [ptb])
                    yield
                    V(lambda e: e.tensor_copy(qfm[:, :, tc * 128:(tc + 1) * 128], ptv[0:64, 0:512].rearrange("p (h t) -> p h t", h=4)), [ptb], [QF])
                    V(lambda e: e.tensor_copy(kfm[:, :, tc * 128:(tc + 1) * 128], ptv[0:64, 512:768].rearrange("p (h t) -> p h t", h=2)), [ptb], [KF])
                    yield
                for c0 in range(0, NCH, 2):
                    interleave([in_chunk(c0), in_chunk(c0 + 1)])
            with_h(inproj)
            K.release(mk_in)
            ptp = Rot(K, "pt", [128, 7, 2, 128], BF16, 3)
            rcp = Rot(K, "rc", [64, 2, 128], F32, 3)
            am = bcol("am").rearrange("p (c a t) -> p c a t", c=NCH, a=2)
            cfb = ccol("cfb")

            def att_chain(qc, kv):
                qs = slice(qc * 128, (qc + 1) * 128)
                pc_ = max(qc - 1, 0)
                nc_ = min(qc + 1, NCH - 1)
                qrhs = qfm[:, 2 * kv:2 * kv + 2, qs]
                pa, pab = bank()
                pb_, pbb_ = bank()
                for sc in range(4):
                    dst, dstb = (pa, pab) if sc < 2 else (pb_, pbb_)
                    T(lambda e: e.matmul(dst[:, (sc % 2) * 256:(sc % 2) * 256 + 256], lhsT=ckf[:, kv, sc * 128:(sc + 1) * 128], rhs=qrhs, start=True, stop=True),
                      [CK, QF], [dstb])
                yield
                pt_, ptb_ = ptp.next()
                A(lambda e: e.activation(out=pt_[:, 0:2, :, :], in_=pa[:, 0:512].rearrange("p (c h t) -> p c h t", c=2, h=2), func=AF.Exp,
                                         scale=0.125, bias=cfb[:, qc:qc + 1]), [pab, CONSTB], [ptb_])
                A(lambda e: e.activation(out=pt_[:, 2:4, :, :], in_=pb_[:, 0:512].rearrange("p (c h t) -> p c h t", c=2, h=2), func=AF.Exp,
                                         scale=0.125, bias=cfb[:, qc:qc + 1]), [pbb_, CONSTB], [ptb_])
                pc2, pc2b = bank()
                pd2, pd2b = bank()
                for i, kc in enumerate((pc_, nc_, qc)):
                    dst, dstb = (pc2, pc2b) if i < 2 else (pd2, pd2b)
                    T(lambda e: e.matmul(dst[:, (i % 2) * 256:(i % 2) * 256 + 256], lhsT=kfm[:, kv, kc * 128:(kc + 1) * 128], rhs=qrhs, start=True, stop=True),
                      [KF, QF], [dstb])
                yield
                A(lambda e: e.activation(out=pt_[:, 4:6, :, :], in_=pc2[:, 0:512].rearrange("p (c h t) -> p c h t", c=2, h=2), func=AF.Exp, scale=0.125),
                  [pc2b], [ptb_])
                A(lambda e: e.activation(out=pt_[:, 6, :, :], in_=pd2[:, 0:256].rearrange("p (h t) -> p h t", h=2), func=AF.Exp, scale=0.125),
                  [pd2b], [ptb_])
                yield
                V(lambda e: e.tensor_tensor(out=pt_[:, 4:6, :, :], in0=pt_[:, 4:6, :, :], in1=am[:, qc, :, :].unsqueeze(2).broadcast_to([128, 2, 2, 128]), op=ALU.mult),
                  [ptb_, CONSTB], [ptb_])
                yield
                po, pob = bank()
                vlist = [(cvs[:, sc, kv * 64:(kv + 1) * 64], CK) for sc in range(4)] + \
                        [(vtm[:, kc, kv * 64:(kv + 1) * 64], VT[kc]) for kc in (pc_, nc_, qc)]
                for i, (vap, vb) in enumerate(vlist):
                    T(lambda e: e.matmul(po[0:64, 0:256], lhsT=vap, rhs=pt_[:, i, :, :], start=(i == 0), stop=(i == 6)), [vb, ptb_], [pob])
                for i in range(7):
                    T(lambda e: e.matmul(po[0:64, 256:512], lhsT=ones_b[:, 0:64], rhs=pt_[:, i, :, :], start=(i == 0), stop=(i == 6)), [CONSTB, ptb_], [pob])
                yield
                rc, rcb = rcp.next()
                V(lambda e: e.tensor_tensor(out=rc[:], in0=po[0:64, 256:512].rearrange("p (h t) -> p h t", h=2),
                                            in1=es[:, 2 * kv:2 * kv + 2].unsqueeze(2).broadcast_to([64, 2, 128]), op=ALU.add), [pob, CK], [rcb])
                V(lambda e: e.reciprocal(rc[:], rc[:]), [rcb], [rcb])
                V(lambda e: e.tensor_tensor(out=yat[:, 2 * kv:2 * kv + 2, qs], in0=po[0:64, 0:256].rearrange("p (h t) -> p h t", h=2), in1=rc[:], op=ALU.mult),
                  [pob, rcb], [YA])
                yield
            for qc in range(NCH):
                interleave([att_chain(qc, 0), att_chain(qc, 1)])
            out_proj([yat[:, hh, :] for hh in range(4)], [YA], 768, 64, tile=yat)
            K.barrier()
            K.release(mk)

        def hyena(yhy, YHB):
            mk = K.mark()
            vb_ = K.sb("hv", [128, 2, NT], BF16)
            x1f = K.sb("hx1", [128, 2, NT], BF16)
            x2f = K.sb("hx2", [128, 2, NT], BF16)
            HVB, HX1, HX2 = Buf("hv"), Buf("hx1"), Buf("hx2")

            def inproj(hbuf, HB):
                wa0, wab0 = load_wa(1032, 512)
                wa1, wab1 = load_wa(1544, 256)
                dests = [(vb_, HVB), (vb_, HVB), (x1f, HX1), (x1f, HX1), (x2f, HX2), (x2f, HX2)]
                for q in range(6):
                    raw, rawb = cvx["raws"].next()
                    for tt in range(3):
                        pb, pbb = bank()
                        if q < 4:
                            proj_fm(pb, pbb, wa0, wab0, q * 128, 128, hbuf, HB, tt)
                        else:
                            proj_fm(pb, pbb, wa1, wab1, (q - 4) * 128, 128, hbuf, HB, tt)
                        raw_fill(raw, rawb, 128, pb, pbb, tt)
                    pc0 = PT["hy_conv"][0] + (l * 6 + q) * 4
                    dt_, db_ = dests[q]
                    conv_chunk(raw, rawb, 128, pc0, dt_[:, q % 2, :], db_, False)
            with_h(inproj, conv=True)
            FEB = Buf("feats")
            w3s = K.sb("hw3", [64, 1024])
            K.dma("sync", lambda e: e.dma_start(out=w3s[:], in_=hy_w3[l]), writes=[FEB], semkey="ldf")
            h2 = K.sb("hh2", [64, NT])
            H2B = Buf("h2")
            mk2 = K.mark()
            feats = K.sb("feats", [33, NT])
            K.dma("sync", lambda e: e.dma_start(out=feats[:, 0:1024], in_=featsA_d), writes=[FEB], semkey="ldf")
            K.dma("sync", lambda e: e.dma_start(out=feats[:, 1024:1280], in_=featsB_d), writes=[FEB], semkey="ldf")
            w1s = K.sb("hw1", [33, 64])
            w2s = K.sb("hw2", [64, 64])
            K.dma("sync", lambda e: e.dma_start(out=w1s[:], in_=hy_w1[l]), writes=[FEB], semkey="ldf")
            K.dma("sync", lambda e: e.dma_start(out=w2s[:], in_=hy_w2[l]), writes=[FEB], semkey="ldf")
            hp = pcol("hyp", l * 3, 3, rows=64)
            fb = K.sb("fb", [64, 2])
            V(lambda e: e.tensor_tensor(out=fb[:, 0:1], in0=hp[:, 0:1], in1=hp[:, 1:2], op=ALU.mult), [CONSTB], [FEB])
            V(lambda e: e.tensor_tensor(out=fb[:, 1:2], in0=hp[:, 2:3], in1=hp[:, 1:2], op=ALU.mult), [CONSTB], [FEB])
            h1 = K.sb("hh1", [64, NT])
            H1B = Buf("h1")
            MAGIC = 12582912.0
            argp = Rot(K, "arg", [64, 512], F32, 2)
            nrp = Rot(K, "nr", [64, 512], F32, 2)

            def sin_layer(lhsT, src, srcb, KK, dst, dstb, fbcol):
                for tt in range(3):
                    t0, w, cnd = TT[tt]
                    pb, pbb = bank()
                    T(lambda e, pb=pb, t0=t0, w=w: e.matmul(pb[0:64, 0:w], lhsT=lhsT, rhs=src[0:KK, t0:t0 + w], start=True, stop=True), [FEB, srcb], [pbb])
                    ar, arb = argp.next()
                    nr, nrb = nrp.next()
                    V(lambda e, ar=ar, pb=pb, w=w: e.tensor_scalar(out=ar[:, 0:w], in0=pb[0:64, 0:w], scalar1=hp[:, 1:2], scalar2=fb[:, fbcol:fbcol + 1],
                                                                   op0=ALU.mult, op1=ALU.add), [pbb, CONSTB, FEB], [arb])
                    V(lambda e, ar=ar, nr=nr, w=w: e.tensor_scalar(out=nr[:, 0:w], in0=ar[:, 0:w], scalar1=float(1 / (2 * math.pi)), scalar2=MAGIC,
                                                                   op0=ALU.mult, op1=ALU.add), [arb], [nrb])
                    V(lambda e, nr=nr, w=w: e.tensor_scalar(out=nr[:, 0:w], in0=nr[:, 0:w], scalar1=MAGIC, scalar2=None, op0=ALU.subtract), [nrb], [nrb])
                    V(lambda e, ar=ar, nr=nr, w=w: e.scalar_tensor_tensor(out=ar[:, 0:w], in0=nr[:, 0:w], scalar=float(-2 * math.pi), op0=ALU.mult,
                                                                          in1=ar[:, 0:w], op1=ALU.add), [arb, nrb], [arb])
                    V(lambda e, ar=ar, w=w: e.tensor_scalar(out=ar[:, 0:w], in0=ar[:, 0:w], scalar1=3.1415925, scalar2=-3.1415925, op0=ALU.min, op1=ALU.max), [arb], [arb])
                    A(lambda e, ar=ar, t0=t0, w=w: e.activation(out=dst[:, t0:t0 + w], in_=ar[:, 0:w], func=AF.Sin), [arb], [dstb])
            sin_layer(w1s[:], feats, FEB, 33, h1, H1B, 0)
            sin_layer(w2s[:], h1, H1B, 64, h2, H2B, 1)
            K.barrier()
            K.release(mk2)

            gA = K.sb("gA", [128, 2, 2, 7, 256], BF16)
            gB = K.sb("gB", [128, 2, 2, 256], BF16)
            GB_ = Buf("g")
            hfa = K.sb("hfa", [128, 10, 2, 256], BF16)
            HFB = Buf("hf")
            ztm = K.sb("ztm", [128, NCH, 128], BF16)
            ZTB = bufs(NCH, "ztm")
            Yb = K.sb("Yb", [128, 2, 2, 5, 128], BF16)
            YBB = Buf("Yb")
            ytp = Rot(K, "yt", [128, 4, 128], F32, 2)
            tqp = Rot(K, "tqr", [128, 4, 128], F32, 4)
            identr = K.sb("identr", [128, 2, 128])
            IDR = Buf("identr")
            V(lambda e: e.tensor_copy(identr[:, 0, :].bitcast(F32R), ident_b), [CONSTB], [IDR])
            V(lambda e: e.tensor_scalar(out=identr[:, 1, :].bitcast(F32R), in0=ident_b, scalar1=-1.0, scalar2=None, op0=ALU.mult), [CONSTB], [IDR])
            dftb = bcol("dft").rearrange("p (t r f) -> p t r f", t=8, r=2)
            idft = bcol("idft").rearrange("p (a r t) -> p a r t", a=2, r=2)

            decs = K.sb("decs", [128, 20, 128])
            DCB = Buf("decs")
            for o in range(2):
                zin, zinb = (vb_, HVB) if o == 0 else (x1f, HX1)
                gate, gateb = (x1f, HX1) if o == 0 else (x2f, HX2)
                zout, zoutb = (x1f, HX1) if o == 0 else (yhy, YHB)
                for cc in range(2):
                    K.dma("sync", lambda e, cc=cc: e.dma_start(out=decs[:], in_=dec_d[:, :, cc * 128:(cc + 1) * 128]), writes=[DCB], semkey="lddec")
                    for pk in range(10):
                        pb, pbb = bank()
                        pos0 = pk * 128
                        for sd in range(2):
                            wc0 = sd * 512 + o * 256 + cc * 128
                            T(lambda e, pb=pb, sd=sd, wc0=wc0, pos0=pos0: e.matmul(pb[:, sd * 128:(sd + 1) * 128], lhsT=h2[:, pos0:pos0 + 128],
                                                                                   rhs=w3s[:, wc0:wc0 + 128], start=True, stop=True), [H2B, FEB], [pbb])
                        if pk < 8:
                            di = [pk, 8 + pk]
                        else:
                            di = [16 + pk - 8, 18 + pk - 8]
                        for sd in range(2):
                            V(lambda e, pb=pb, sd=sd, pk=pk, di=di, cc=cc: e.tensor_tensor(out=hfa[:, pk, sd, cc * 128:(cc + 1) * 128], in0=pb[:, sd * 128:(sd + 1) * 128],
                                                                                          in1=decs[:, di[sd], :], op=ALU.mult), [pbb, DCB], [HFB])
                for delta in range(-3, 4):
                    ents = spectrum_entries(delta, 8)
                    for fch in range(2):
                        pb, pbb = bank()
                        for r in range(2):
                            for i, (src, k, ty) in enumerate(ents):
                                T(lambda e, pb=pb, r=r, i=i, src=src, k=k, ty=ty, fch=fch: e.matmul(
                                    pb[:, r * 256:(r + 1) * 256], lhsT=dftb[:, ty, r, fch * 128:(fch + 1) * 128], rhs=hfa[:, k, src, :],
                                    start=(i == 0), stop=(i == len(ents) - 1)), [CONSTB, HFB], [pbb])
                        if delta == 0:
                            A(lambda e, pb=pb, fch=fch, delta=delta: e.activation(out=gA[:, fch, :, delta + 3, :], in_=pb[:, 0:512].rearrange("p (r c) -> p r c", r=2),
                                                                                  func=AF.Copy), [pbb], [GB_])
                        else:
                            A(lambda e, pb=pb, fch=fch, delta=delta: e.activation(out=gA[:, fch, :, delta + 3, :], in_=pb[:, 0:512].rearrange("p (r c) -> p r c", r=2),
                                                                                  func=AF.Copy, scale=flag), [pbb, CONSTB], [GB_])
                entsB = spectrum_entries(0, 2)
                for fch in range(2):
                    pb, pbb = bank()
                    for r in range(2):
                        for i, (src, k, ty) in enumerate(entsB):
                            T(lambda e, pb=pb, r=r, i=i, src=src, k=k, ty=ty, fch=fch: e.matmul(
                                pb[:, r * 256:(r + 1) * 256], lhsT=dftb[:, ty, r, fch * 128:(fch + 1) * 128], rhs=hfa[:, 8 + k, src, :],
                                start=(i == 0), stop=(i == len(entsB) - 1)), [CONSTB, HFB], [pbb])
                    A(lambda e, pb=pb, fch=fch: e.activation(out=gB[:, fch, :, :], in_=pb[:, 0:512].rearrange("p (r c) -> p r c", r=2), func=AF.Copy), [pbb], [GB_])
                for cc in range(2):
                    gcs = slice(cc * 128, (cc + 1) * 128)
                    for tc in range(NCH):
                        pb, pbb = bank()
                        pbv = pb[:].bitcast(BF16)
                        T(lambda e, pbv=pbv, tc=tc: e.transpose(pbv[:, 0:128], zin[:, cc, tc * 128:(tc + 1) * 128], ident_b), [zinb, CONSTB], [pbb])
                        A(lambda e, pbv=pbv, tc=tc: e.activation(out=ztm[:, tc, :], in_=pbv[:, 0:128], func=AF.Copy), [pbb], [ZTB[tc]])
                    ty_f = [dft_type(0, 0), dft_type(0, 128)]
                    set_banks(6, 8)
                    for fch in range(2):
                        for r in range(2):
                            for blk in range(NB):
                                if blk < 4:
                                    dst, dstb = PS[r][:, blk * 128:(blk + 1) * 128], PSB[r]
                                else:
                                    dst, dstb = PS[2][:, r * 128:(r + 1) * 128], PSB[2]
                                for tk in range(2):
                                    T(lambda e: e.matmul(dst, lhsT=dftb[:, ty_f[tk], r, fch * 128:(fch + 1) * 128], rhs=ztm[:, 2 * blk + tk, :],
                                                         start=(tk == 0), stop=(tk == 1)), [CONSTB, ZTB[2 * blk + tk]], [dstb])
                        terms = []
                        for delta in [0, 1, -1, 2, -2, 3, -3]:
                            nbk = 4 - abs(delta)
                            tb0 = max(delta, 0)
                            sb0 = tb0 - delta
                            for (acc, gr, zr, sgn) in ((3, 0, 0, 0), (3, 1, 1, 1), (4, 0, 1, 0), (4, 1, 0, 0)):
                                g_ap = gA[:, fch, gr, delta + 3, gcs].unsqueeze(1).broadcast_to([128, nbk, 128])
                                z_ap = PS[zr][:, sb0 * 128:(sb0 + nbk) * 128].rearrange("p (b c) -> p b c", b=nbk)
                                terms.append((acc, tb0 * 128, nbk * 128, sgn, z_ap, PSB[zr], g_ap, nbk))
                        cnt = {3: 0, 4: 0}
                        tot = {3: 14, 4: 14}
                        for (acc, c0, ncol, sgn, z_ap, zb, g_ap, nbk) in terms:
                            tq, tqb = tqp.next()
                            V(lambda e: e.tensor_tensor(out=tq[:, 0:nbk, :].bitcast(F32R), in0=z_ap, in1=g_ap, op=ALU.mult), [zb, GB_], [tqb])
                            T(lambda e: e.matmul(PS[acc][:, c0:c0 + ncol], lhsT=identr[:, sgn, :].bitcast(F32R),
                                                 rhs=tq[:, 0:nbk, :].rearrange("p b c -> p (b c)").bitcast(F32R),
                                                 start=(cnt[acc] == 0), stop=(cnt[acc] == tot[acc] - 1)), [tqb, IDR], [PSB[acc]])
                            cnt[acc] += 1
                        tqs = []
                        for (gr, zr, sgn) in ((0, 0, 0), (1, 1, 1), (0, 1, 0), (1, 0, 0)):
                            tq, tqb = tqp.next()
                            V(lambda e: e.tensor_tensor(out=tq[:, 0, :].bitcast(F32R), in0=PS[2][:, zr * 128:(zr + 1) * 128], in1=gB[:, fch, gr, gcs], op=ALU.mult),
                              [PSB[2], GB_], [tqb])
                            tqs.append((tq, tqb, sgn))
                        for i, (tq, tqb, sgn) in enumerate(tqs):
                            ro = i // 2
                            T(lambda e: e.matmul(PS[5][:, ro * 128:(ro + 1) * 128], lhsT=identr[:, sgn, :].bitcast(F32R), rhs=tq[:, 0, :].bitcast(F32R),
                                                 start=(i % 2 == 0), stop=(i % 2 == 1)), [tqb, IDR], [PSB[5]])
                        for r in range(2):
                            A(lambda e: e.activation(out=Yb[:, fch, r, 0:4, :], in_=PS[3 + r][:, 0:512].rearrange("p (b c) -> p b c", b=4), func=AF.Copy),
                              [PSB[3 + r]], [YBB])
                        A(lambda e: e.activation(out=Yb[:, fch, :, 4, :], in_=PS[5][:, 0:256].rearrange("p (r c) -> p r c", r=2), func=AF.Copy), [PSB[5]], [YBB])
                    set_banks(0, 8)
                    hbcol = pcol("hy_bias", (l * 2 + o) * 2 + cc, 1)
                    for blk in range(NB):
                        pb, pbb = bank()
                        i = 0
                        for fch in range(2):
                            for r in range(2):
                                T(lambda e, pb=pb, fch=fch, r=r, blk=blk, i=i: e.matmul(pb[:, 0:256], lhsT=Yb[:, fch, r, blk, :], rhs=idft[:, fch, r, :],
                                                                                       start=(i == 0), stop=(i == 3)), [YBB, CONSTB], [pbb])
                                i += 1
                        ts = slice(blk * 256, (blk + 1) * 256)
                        tq, tqb = ytp.next()
                        tqv = tq[:].rearrange("p a b -> p (a b)")[:, 0:256]
                        V(lambda e, tqv=tqv, pb=pb, ts=ts: e.scalar_tensor_tensor(out=tqv, in0=zin[:, cc, ts], scalar=hbcol, op0=ALU.mult, in1=pb[:, 0:256], op1=ALU.add),
                          [zinb, CONSTB, pbb], [tqb])
                        V(lambda e, tqv=tqv, ts=ts: e.tensor_tensor(out=zout[:, cc, ts], in0=tqv, in1=gate[:, cc, ts], op=ALU.mult), [tqb, gateb], [zoutb])
            out_proj([yhy[:, 0, :], yhy[:, 1, :]], [YHB], 256, 128)
            K.barrier()
            K.release(mk)

        yhy = K.sb("yhy", [128, 2, NT], BF16)
        YHB = Buf("yhy")
        hyena(yhy, YHB)
        yssd = K.sb("yssd", [64, 4, NT], BF16)
        YSB = Buf("yssd")
        ssd(yssd, YSB)
        yret = K.sb("yret", [64, 4, NT], BF16)
        YRB = Buf("yret")
        ret(yret, YRB)
        yatt = K.sb("yatt", [64, 4, NT], BF16)
        YAB_ = Buf("yatt")
        att(yatt, YAB_)
        out_proj_all()
        K.barrier()
        K.release(mk_all)

    for l in range(nlayers):
        if l == 1:
            assert not pending_mod and not inflight_mod, (pending_mod, inflight_mod)
        ffn(l, 0)
        mixer(l)
        ffn(l, 1)

    K.dma("sync", lambda e: e.dma_start(out=yT, in_=x[:]), reads=[b for row in XB for b in row], semkey="sty")
    K.barrier()
    K.release(0)
    return nc, K


def _consts(is_s):
    f32 = np.float32
    cfv = np.zeros((128, CF.n), f32)

    def put(name, arr):
        c0, w = CF[name]
        cfv[:, c0:c0 + w] = np.asarray(arr, f32).reshape(128, w) if np.asarray(arr).ndim > 1 or w == 1 else np.broadcast_to(np.asarray(arr, f32), (128, w))
    j = np.arange(128)[:, None]
    i = np.arange(128)[None, :]
    put("trif", (i >= j).astype(f32))
    put("trib", (j >= i).astype(f32))
    put("ones", np.ones((128, 128), f32))
    put("relu_f", np.maximum(i - j, 0).astype(f32))
    put("relu_b", np.maximum(j - i, 0).astype(f32))
    put("ip1", np.broadcast_to((i + 1).astype(f32), (128, 128)))
    put("rmi", np.broadcast_to((128 - i).astype(f32), (128, 128)))
    put("tailf", (127 - j).astype(f32))
    put("tailb", j.astype(f32))
    nf = 16
    inv = (10000.0 ** (-np.arange(nf, dtype=f32) / nf)).astype(f32)
    cos = np.ones((NT, 32), f32)
    sin = np.zeros((NT, 32), f32)
    if is_s:
        t = np.arange(1024)
        rows = (t // 64).astype(f32)
        cols = (t % 64).astype(f32)
        angr = rows[:, None] * inv[None, :]
        angc = cols[:, None] * inv[None, :]
        cos[:1024, 0:16] = np.cos(angr)
        cos[:1024, 16:32] = np.cos(angc)
        sin[:1024, 0:16] = np.sin(angr)
        sin[:1024, 16:32] = np.sin(angc)
    put("cos", cos.reshape(NCH, 128, 32).transpose(1, 0, 2).reshape(128, 320))
    put("sin", sin.reshape(NCH, 128, 32).transpose(1, 0, 2).reshape(128, 320))
    cfb = np.full((NCH,), -30000.0, f32)
    hl = np.zeros((5,), f32)
    hr = np.zeros((5,), f32)
    if is_s:
        cfb[0:8] = 0.0
        hl[1:4] = 1.0
        hr[0:3] = 1.0
    put("cfb", cfb)
    put("hl", hl)
    put("hr", hr)
    put("flag", np.full((128, 1), 1.0 if is_s else 0.0, f32))
    put("negpi", np.full((128, 1), -math.pi, f32))
    put("nbf", ((i >= j).astype(f32) - 1.0) * 30000.0)
    put("nbb", ((j >= i).astype(f32) - 1.0) * 30000.0)

    cbv = np.zeros((128, CB.n), f32)

    def putb(name, arr):
        c0, w = CB[name]
        cbv[:, c0:c0 + w] = np.asarray(arr, f32).reshape(128, w)
    putb("ident", np.eye(128, dtype=f32))
    putb("ones", np.ones((128, 128), f32))
    am = np.zeros((128, NCH, 2, 128), f32)
    band_prev = (j >= i).astype(f32)
    band_next = (j <= i).astype(f32)
    for qc in range(NCH):
        if is_s and qc < 8:
            if qc >= 1:
                am[:, qc, 0, :] = band_prev
            if qc <= 6:
                am[:, qc, 1, :] = band_next
        else:
            if qc % 2 == 1:
                am[:, qc, 0, :] = 1.0
            else:
                am[:, qc, 1, :] = 1.0
    putb("am", am)
    om = 2 * np.pi * (np.arange(256) + 0.5) / 512.0
    row = np.arange(128)
    dft = np.zeros((128, 8, 2, 256), np.float64)
    for ty in range(8):
        if ty < 4:
            e = FWD_E0[ty] + row
        else:
            e = BWD_E0[ty - 4] - row
        valid = (np.abs(e) <= 255).astype(np.float64)
        ang = e[:, None] * om[None, :]
        dft[:, ty, 0, :] = np.cos(ang) * valid[:, None]
        dft[:, ty, 1, :] = -np.sin(ang) * valid[:, None]
    putb("dft", dft)
    tt = np.arange(256)
    idft = np.zeros((128, 2, 2, 256), np.float64)
    for fch in range(2):
        omf = om[fch * 128:(fch + 1) * 128]
        ang = omf[:, None] * tt[None, :]
        idft[:, fch, 0, :] = (2.0 / 512) * np.cos(ang)
        idft[:, fch, 1, :] = -(2.0 / 512) * np.sin(ang)
    putb("idft", idft)
    return cfv, cbv


def _hy_consts(LA):
    f32 = np.float32
    l = LA
    pos = np.arange(l, dtype=f32)
    t = pos / f32(l - 1)
    bands = np.linspace(1e-4, 15, 16, dtype=f32)
    ang = (f32(2.0 * math.pi / l)) * pos[:, None] * bands[None, :]
    feats = np.concatenate([t[:, None], np.cos(ang), -np.sin(ang)], axis=-1).astype(f32)
    max_decay = math.log(1e-2) / 0.3
    min_decay = math.log(1e-2) / 1.5
    deltas = np.abs(np.linspace(min_decay, max_decay, 256, dtype=f32))
    dec = np.exp(-t[:, None] * deltas[None, :]).astype(f32)
    return feats, dec


def _prepare(inp):
    f32 = np.float32
    g = lambda k: np.asarray(inp[k], dtype=f32)
    x_prompt, x_sample = g("x_prompt"), g("x_sample")
    cache_k, cache_v = g("cache_k"), g("cache_v")
    state_ssd, state_ret = g("state_ssd"), g("state_ret")
    c, c_ctx = g("c"), g("c_ctx")
    pt = np.zeros((128, PT.n), f32)

    def putp(name, off, arr):
        arr = np.asarray(arr, f32)
        c0 = PT[name][0] + off
        pt[0:arr.shape[0], c0:c0 + arr.shape[1]] = arr
    nw = g("norm_w")
    for l in range(2):
        for j in range(3):
            putp("norm_w", (l * 3 + j) * 8, nw[l, j].reshape(8, 128).T)
        putp("b_mod", l * 72, g("b_mod")[l].reshape(72, 128).T)
        cw, cbias = g("ssd_conv_w")[l], g("ssd_conv_b")[l]
        for q in range(8):
            if q < 4:
                f0, fs = q * 64, 64
            else:
                f0, fs = 256 + (q - 4) * 128, 128
            arr = np.stack([cw[0, f0:f0 + fs], cw[1, f0:f0 + fs], cw[2, f0:f0 + fs], cbias[f0:f0 + fs]], axis=1)
            putp("ssd_conv", (l * 8 + q) * 4, arr)
        hw, hb = g("hy_conv_w")[l], g("hy_conv_b")[l]
        for q in range(6):
            f0 = q * 128
            arr = np.stack([hw[0, f0:f0 + 128], hw[1, f0:f0 + 128], hw[2, f0:f0 + 128], hb[f0:f0 + 128]], axis=1)
            putp("hy_conv", (l * 6 + q) * 4, arr)
        hbias = g("hy_bias")[l]
        for o in range(2):
            putp("hy_bias", (l * 2 + o) * 2, hbias[o].reshape(2, 128).T)
        putp("ssd_nw", l * 4, g("ssd_norm_w")[l].reshape(4, 64).T)
        putp("ssd_d", l * 4, np.broadcast_to(g("ssd_d")[l][None, :], (64, 4)))
        putp("ret_gn", l * 4, g("ret_gn_w")[l].reshape(4, 64).T)
        putp("hyp", l * 3, np.stack([g("hy_b1")[l], g("hy_freq")[l], g("hy_b2")[l]], axis=1))
    ft = np.zeros((128, FT.n), f32)

    def putf(name, off, vec):
        vec = np.asarray(vec, f32).reshape(-1)
        c0 = FT[name][0] + off
        ft[:, c0:c0 + vec.size] = vec[None, :]
    for l in range(2):
        putf("dt_bias", l * 80, np.tile(g("ssd_dt_bias")[l].reshape(8), NCH))
        putf("a_log", l * 80, np.tile(g("ssd_a_log")[l].reshape(8), NCH))
        putf("ret_logit", l * 8, g("ret_decay_logit")[l].reshape(8))
        putf("qkw", l * 384, np.concatenate([np.tile(g("attn_q_norm")[l], 4), np.tile(g("attn_k_norm")[l], 2)]))
        putf("sink", l * 4, g("attn_sink")[l])
    featsB, decB = _hy_consts(256)
    featsA_s, decA_s = _hy_consts(1024)
    wm = g("w_mod").reshape(2, 8, 128, 18, 512).transpose(0, 3, 2, 1, 4).reshape(2, 18, 128, 4096)
    wi = g("ffn_w_in").reshape(2, 2, 8, 128, 2, 11, 256)
    wi = wi.transpose(0, 1, 5, 3, 2, 4, 6).reshape(2, 2, 11, 128, 4096)
    wo = g("ffn_w_out").reshape(2, 2, 11, 2, 128, 1024).transpose(0, 1, 2, 4, 3, 5).reshape(2, 2, 11, 128, 2048)
    shared = dict(ptab=pt, ftab=ft, w_mod=np.ascontiguousarray(wm), ffn_w_in=np.ascontiguousarray(wi), ffn_w_out=np.ascontiguousarray(wo), mix_w_in=g("mix_w_in"),
                  mix_w_out=g("mix_w_out"), hy_w1=g("hy_w1"), hy_w2=g("hy_w2"), hy_w3=g("hy_w3"),
                  featsB=np.ascontiguousarray(featsB.T))
    consts = {True: _consts(True), False: _consts(False)}
    in_maps = []
    plan = []
    for cid in range(NCORE):
        is_s = cid < 2
        if is_s:
            xs = np.concatenate([x_sample[cid], x_prompt[30 + cid]], axis=0)
            seqs = [30 + cid]
            condA = c[cid]
        else:
            seqs = list(range(5 * (cid - 2), 5 * (cid - 2) + 5))
            xs = x_prompt[seqs].reshape(NT, D)
            condA = c_ctx
        plan.append((is_s, seqs))
        xTm = np.ascontiguousarray(xs.T.reshape(8, 128, NT).transpose(1, 0, 2))
        cond = np.stack([condA, c_ctx], axis=-1)
        condTm = np.ascontiguousarray(cond.reshape(8, 128, 2).transpose(1, 0, 2))
        s0s = np.zeros((2, 5, 2, 4, 128, 64), f32)
        s0r = np.zeros((2, 5, 2, 4, 64, 64), f32)
        ck = np.zeros((2, 2, 64, 512), f32)
        cv = np.zeros((2, 512, 128), f32)
        if is_s:
            for l in range(2):
                s0s[l, 0, 0] = state_ssd[cid, l, 0]
                s0s[l, 3, 1] = state_ssd[cid, l, 1]
                s0r[l, 0, 0] = state_ret[cid, l, 0]
                s0r[l, 3, 1] = state_ret[cid, l, 1]
                ck[l] = cache_k[cid, l].transpose(1, 2, 0)
                cv[l] = cache_v[cid, l].reshape(512, 128)
            featsA = featsA_s
            decFA = decA_s.copy()
        else:
            featsA = np.zeros((1024, 33), f32)
            featsA[:256] = featsB
            decFA = np.zeros((1024, 256), f32)
            decFA[:256] = decB
        decBA = decFA.copy()
        decBA[0] = 0.0
        decFB = decB.copy()
        decBB = decB.copy()
        decBB[0] = 0.0
        dec = np.concatenate([decFA.reshape(8, 128, 256), decBA.reshape(8, 128, 256), decFB.reshape(2, 128, 256), decBB.reshape(2, 128, 256)], axis=0)
        cfv, cbv = consts[is_s]
        m = dict(shared)
        m.update(xT=xTm, condT=condTm,
                 s0ssd=np.ascontiguousarray(s0s.reshape(80, 128, 64).transpose(1, 0, 2)),
                 s0ret=np.ascontiguousarray(s0r.reshape(80, 64, 64).transpose(1, 0, 2)),
                 ckT=np.ascontiguousarray(ck.reshape(4, 64, 512).transpose(1, 0, 2)),
                 cvt=np.ascontiguousarray(cv.reshape(8, 128, 128).transpose(1, 0, 2)),
                 cf32=cfv, cbf=cbv, featsA=np.ascontiguousarray(featsA.T), dec=np.ascontiguousarray(dec.transpose(1, 0, 2)))
        in_maps.append(m)
    return in_maps, plan


_CACHE = {}


def kernel(**inputs):
    in_maps, plan = _prepare(inputs)
    if "nc" not in _CACHE:
        _CACHE["nc"] = build_program()[0]
    nc = _CACHE["nc"]
    res = run_bass_kernel_spmd(nc, in_maps, core_ids=list(range(NCORE)))
    f32 = np.float32
    y_prompt = np.zeros((32, 256, D), f32)
    y_sample = np.zeros((2, 1024, D), f32)
    nck = np.zeros((32, 2, 256, 2, 64), f32)
    ncv = np.zeros((32, 2, 256, 2, 64), f32)
    nssd = np.zeros((32, 2, 2, 4, 128, 64), f32)
    nret = np.zeros((32, 2, 2, 4, 64, 64), f32)
    for cid, (is_s, seqs) in enumerate(plan):
        r = res.results[cid]
        y = np.asarray(r["yT"]).transpose(1, 0, 2).reshape(D, NT).T
        nk = np.asarray(r["nk"])
        nv = np.asarray(r["nv"])
        ss = np.asarray(r["nssd"]).transpose(1, 0, 2).reshape(2, 5, 2, 4, 128, 64)
        sr = np.asarray(r["nret"]).transpose(1, 0, 2).reshape(2, 5, 2, 4, 64, 64)
        if is_s:
            y_sample[cid] = y[:1024]
            blks = [(4, seqs[0])]
        else:
            blks = list(enumerate(seqs))
        for blk, b in blks:
            ts = slice(blk * 256, (blk + 1) * 256)
            y_prompt[b] = y[ts]
            for l in range(2):
                nck[b, l] = nk[l, ts].reshape(256, 2, 64)
                ncv[b, l] = nv[l, ts].reshape(256, 2, 64)
                nssd[b, l] = ss[l, blk]
                nret[b, l] = sr[l, blk]
    return (y_prompt, y_sample, nck, ncv, nssd, nret)
```

```python
import math
import numpy as np
import concourse.bass as bass
import concourse.mybir as mybir
from concourse.bass_utils import run_bass_kernel_spmd

F32 = mybir.dt.float32
BF16 = mybir.dt.bfloat16
F32R = mybir.dt.float32r
AF = mybir.ActivationFunctionType
ALU = mybir.AluOpType
AX = mybir.AxisListType

NCORE = 8
NT = 1280
NB = 5
NCH = 10
TT = [(0, 512, 0), (512, 512, 0), (1024, 256, 1)]
D = 1024
DFF = 2816
EPS = 1e-6
ENGS = ["tensor", "vector", "scalar", "gpsimd", "sync"]
SAME_ENGINE_NOSYNC = ("tensor",)


class Buf:
    __slots__ = ("name", "w", "r", "excl")

    def __init__(self, name="", excl=False):
        self.name = name
        self.w = None
        self.r = {}
        self.excl = excl


def bufs(n, name=""):
    return [Buf(name + str(i)) for i in range(n)]


class KB:
    def __init__(self, nc):
        self.nc = nc
        self.cnt = {e: 0 for e in ENGS}
        self.seen = {e: {} for e in ENGS}
        self.sems = {}
        self.dcount = {}
        self._stack = []
        self.n_inst = 0
        self.uid = 0

    def enter(self, cm):
        v = cm.__enter__()
        self._stack.append(cm)
        return v

    def mark(self):
        return len(self._stack)

    def release(self, m):
        while len(self._stack) > m:
            self._stack.pop().__exit__(None, None, None)

    def sem(self, key):
        if key not in self.sems:
            self.sems[key] = self.enter(self.nc.semaphore("s_" + key))
        return self.sems[key]

    def sb(self, name, shape, dt=F32):
        self.uid += 1
        return self.enter(self.nc.sbuf_tensor(f"{name}_{self.uid}", list(shape), dt))

    def ps(self, name, shape, dt=F32):
        return self.enter(self.nc.psum_tensor(name, list(shape), dt))

    @staticmethod
    def _flat(bl):
        out = []
        for b in bl:
            if isinstance(b, (list, tuple)):
                out.extend(KB._flat(b))
            else:
                out.append(b)
        return out

    def _need(self, eng, reads, writes):
        need = {}

        def add(k, v):
            if need.get(k, 0) < v:
                need[k] = v
        for b in reads:
            if b.w is not None:
                add(*b.w)
        for b in writes:
            if b.w is not None:
                add(*b.w)
            for k, v in b.r.items():
                add(k, v)
        out = []
        for k, v in need.items():
            if k == "p_" + eng and eng in SAME_ENGINE_NOSYNC:
                continue
            if self.seen[eng].get(k, 0) >= v:
                continue
            self.seen[eng][k] = v
            out.append((k, v))
        return out

    def _emit(self, eng, waits, fn, key, inc):
        e = getattr(self.nc, eng)
        for k, v in waits:
            e.wait_ge(self.sems[k], v)
        if fn is not None:
            fn(e).then_inc(self.sems[key], inc)

    def op(self, eng, fn, reads=(), writes=()):
        reads = self._flat(reads)
        writes = self._flat(writes)
        ex = [b for b in reads if b.excl]
        if ex:
            reads = [b for b in reads if not b.excl]
            writes = writes + [b for b in ex if b not in writes]
        waits = self._need(eng, reads, writes)
        key = "p_" + eng
        self.sem(key)
        self.cnt[eng] += 1
        val = self.cnt[eng]
        for b in reads:
            if b.r.get(key, 0) < val:
                b.r[key] = val
        for b in writes:
            b.w = (key, val)
            b.r = {}
        self._emit(eng, waits, fn, key, 1)
        self.n_inst += 1

    def dma(self, eng, fn, reads=(), writes=(), semkey=None):
        reads = self._flat(reads)
        writes = self._flat(writes)
        waits = self._need(eng, reads, writes)
        self.sem(semkey)
        self.dcount[semkey] = self.dcount.get(semkey, 0) + 16
        val = self.dcount[semkey]
        for b in reads:
            if b.r.get(semkey, 0) < val:
                b.r[semkey] = val
        for b in writes:
            b.w = (semkey, val)
            b.r = {}
        self._emit(eng, waits, fn, semkey, 16)
        self.n_inst += 1

    def barrier(self):
        tot = [("p_" + e, self.cnt[e]) for e in ENGS if self.cnt[e] > 0]
        tot += list(self.dcount.items())
        for eng in ENGS:
            waits = []
            for k, v in tot:
                if k == "p_" + eng and eng == "tensor":
                    continue
                if self.seen[eng].get(k, 0) >= v:
                    continue
                self.seen[eng][k] = v
                waits.append((k, v))
            self._emit(eng, waits, None, None, 0)

    def V(self, fn, r=(), w=()):
        self.op("vector", fn, r, w)

    def A(self, fn, r=(), w=()):
        self.op("scalar", fn, r, w)

    def T(self, fn, r=(), w=()):
        self.op("tensor", fn, r, w)

    def G(self, fn, r=(), w=()):
        self.op("gpsimd", fn, r, w)


class Rot:
    def __init__(self, K, name, shape, dt, n):
        self.t = [K.sb(f"{name}{i}", shape, dt) for i in range(n)]
        self.b = bufs(n, name)
        self.i = 0

    def next(self):
        i = self.i
        self.i = (i + 1) % len(self.t)
        return self.t[i], self.b[i]


class Cols:
    def __init__(self):
        self.m = {}
        self.n = 0

    def add(self, name, w):
        self.m[name] = (self.n, w)
        self.n += w

    def __getitem__(self, name):
        return self.m[name]


def ptab_cols():
    c = Cols()
    c.add("norm_w", 48)
    c.add("b_mod", 144)
    c.add("ssd_conv", 64)
    c.add("hy_conv", 48)
    c.add("hy_bias", 8)
    c.add("ssd_nw", 8)
    c.add("ssd_d", 8)
    c.add("ret_gn", 8)
    c.add("hyp", 6)
    return c


def ftab_cols():
    c = Cols()
    c.add("dt_bias", 160)
    c.add("a_log", 160)
    c.add("ret_logit", 16)
    c.add("qkw", 768)
    c.add("sink", 8)
    return c


def cf32_cols():
    c = Cols()
    c.add("trif", 128)
    c.add("trib", 128)
    c.add("ones", 128)
    c.add("relu_f", 128)
    c.add("relu_b", 128)
    c.add("ip1", 128)
    c.add("rmi", 128)
    c.add("tailf", 1)
    c.add("tailb", 1)
    c.add("cos", 320)
    c.add("sin", 320)
    c.add("cfb", 10)
    c.add("hl", 5)
    c.add("hr", 5)
    c.add("flag", 1)
    c.add("negpi", 1)
    c.add("nbf", 128)
    c.add("nbb", 128)
    return c


def cbf_cols():
    c = Cols()
    c.add("ident", 128)
    c.add("ones", 128)
    c.add("am", 2560)
    c.add("dft", 4096)
    c.add("idft", 1024)
    return c


PT = ptab_cols()
FT = ftab_cols()
CF = cf32_cols()
CB = cbf_cols()

FWD_E0 = [-256, -128, 0, 128]
BWD_E0 = [256, 128, 0, -128]


def dft_type(src, e0):
    return (FWD_E0.index(e0) if src == 0 else 4 + BWD_E0.index(e0))


def spectrum_entries(delta, nchunks):
    out = []
    for k in range(nchunks):
        e0 = 128 * k - 256 * delta
        if e0 in FWD_E0:
            out.append((0, k, dft_type(0, e0)))
        e0b = -128 * k - 256 * delta
        if e0b in BWD_E0:
            out.append((1, k, dft_type(1, e0b)))
    return out


def build_program(nlayers=2):
    nc = bass.Bass("TRN2", target_bir_lowering=False)

    def din(name, shape):
        return nc.dram_tensor(name, list(shape), F32, kind="ExternalInput").ap()

    def dout(name, shape):
        return nc.dram_tensor(name, list(shape), F32, kind="ExternalOutput").ap()

    xT = din("xT", [128, 8, NT])
    condT = din("condT", [128, 8, 2])
    s0ssd = din("s0ssd", [128, 80, 64])
    s0ret = din("s0ret", [64, 80, 64])
    ckT = din("ckT", [64, 4, 512])
    cvt = din("cvt", [128, 8, 128])
    ptab_d = din("ptab", [128, PT.n])
    ftab_d = din("ftab", [128, FT.n])
    cf32_d = din("cf32", [128, CF.n])
    cbf_d = din("cbf", [128, CB.n])
    featsA_d = din("featsA", [33, 1024])
    featsB_d = din("featsB", [33, 256])
    dec_d = din("dec", [128, 20, 256])
    w_mod = din("w_mod", [2, 18, 128, 4096])
    ffn_w_in = din("ffn_w_in", [2, 2, 11, 128, 4096])
    ffn_w_out = din("ffn_w_out", [2, 2, 11, 128, 2048])
    mix_w_in = din("mix_w_in", [2, D, 3336])
    mix_w_out = din("mix_w_out", [2, D, D])
    hy_w1 = din("hy_w1", [2, 33, 64])
    hy_w2 = din("hy_w2", [2, 64, 64])
    hy_w3 = din("hy_w3", [2, 64, 1024])

    yT = dout("yT", [128, 8, NT])
    nk_o = dout("nk", [2, NT, 128])
    nv_o = dout("nv", [2, NT, 128])
    nssd_o = dout("nssd", [128, 80, 64])
    nret_o = dout("nret", [64, 80, 64])

    K = KB(nc)
    V, A, T, G = K.V, K.A, K.T, K.G

    x = K.sb("x", [128, 8, NT])
    XB = [[Buf(f"x{m}_{t}") for t in range(3)] for m in range(8)]
    modt = K.sb("modt", [128, 2, 2, 72])
    MODB = Buf("mod")
    ptab = K.sb("ptab", [128, PT.n])
    ftab = K.sb("ftab", [128, FT.n])
    cf = K.sb("cf", [128, CF.n])
    cb = K.sb("cb", [128, CB.n], BF16)
    CONSTB = Buf("const")
    CONSTB2 = Buf("constb")
    WA = [K.sb(f"WA{i}", [128, 8, 512], BF16) for i in range(2)]
    WAB = bufs(2, "WA")
    WB = [K.sb(f"WB{i}", [128, 4096], BF16) for i in range(2)]
    WBB = bufs(2, "WB")
    wctr = {"a": 0, "b": 0}
    PS = [K.ps(f"ps{i}", [128, 512]) for i in range(8)]
    PSBK = [Buf(f"psb{i}", excl=True) for i in range(8)]
    PSR = [PSBK[i // 4] for i in range(32)]
    PSB = [[PSBK[i]] for i in range(8)]
    psctr = [0, 0, 8]
    prc = [0]

    def bank():
        lo, hi = psctr[1], psctr[2]
        i = psctr[0]
        if i < lo or i >= hi:
            i = lo
        psctr[0] = i + 1 if i + 1 < hi else lo
        return PS[i], PSB[i]

    def set_banks(lo, hi):
        psctr[1], psctr[2] = lo, hi

    def pr(ncols):
        n = (ncols + 127) // 128
        i = prc[0]
        if (i % 4) + n > 4:
            i = (i // 4 + 1) * 4
        if i + n > 32:
            i = 0
        prc[0] = (i + n) % 32
        b, r = divmod(i, 4)
        return PS[b][:, r * 128:r * 128 + n * 128], [PSBK[b]]

    def interleave(gens):
        gens = list(gens)
        while gens:
            nxt = []
            for g in gens:
                try:
                    next(g)
                    nxt.append(g)
                except StopIteration:
                    pass
            gens = nxt

    def nextA():
        i = wctr["a"] % 2
        wctr["a"] += 1
        return WA[i], WAB[i], f"wa{i}"

    def nextB():
        i = wctr["b"] % 2
        wctr["b"] += 1
        return WB[i], WBB[i], f"wb{i}"

    def pcol(name, off=0, w=1, rows=128):
        c0 = PT[name][0] + off
        return ptab[0:rows, c0:c0 + w]

    def fcol(name, off=0, w=1, rows=128):
        c0 = FT[name][0] + off
        return ftab[0:rows, c0:c0 + w]

    def ccol(name, off=0, w=None, rows=128):
        c0, ww = CF[name]
        if w is None:
            w = ww
        return cf[0:rows, c0 + off:c0 + off + w]

    def bcol(name, off=0, w=None, rows=128):
        c0, ww = CB[name]
        if w is None:
            w = ww
        return cb[0:rows, c0 + off:c0 + off + w]

    K.dma("sync", lambda e: e.dma_start(out=x[:], in_=xT), writes=[b for row in XB for b in row], semkey="ldx")
    K.dma("sync", lambda e: e.dma_start(out=ptab[:], in_=ptab_d), writes=[CONSTB], semkey="ldc")
    K.dma("sync", lambda e: e.dma_start(out=ftab[:], in_=ftab_d), writes=[CONSTB], semkey="ldc")
    K.dma("sync", lambda e: e.dma_start(out=cf[:], in_=cf32_d), writes=[CONSTB], semkey="ldc")
    K.dma("gpsimd", lambda e: e.dma_start(out=cb[:], in_=cbf_d, max_dma_last_dim=2048), writes=[CONSTB2], semkey="ldcb")

    K.barrier()

    flag = ccol("flag")
    ident_b = bcol("ident")
    ones_b = bcol("ones")
    ones_f = ccol("ones")
    trif = ccol("trif")
    trib = ccol("trib")

    condf = K.sb("condf", [128, 8, 2])
    condb = K.sb("condb", [128, 8, 2], BF16)
    CB_ = Buf("cond")
    MODL = [Buf("mod0"), Buf("mod1")]
    K.dma("sync", lambda e: e.dma_start(out=condf[:], in_=condT), writes=[CB_], semkey="ldcond")
    A(lambda e: e.activation(out=condb[:], in_=condf[:], func=AF.Silu), [CB_], [CB_])

    def mod_dma(l, ci, wa, wab, sk):
        K.dma("gpsimd", lambda e: e.dma_start(out=wa[:].rearrange("p k n -> p (k n)"), in_=w_mod[l, ci], max_dma_last_dim=8192),
              writes=[wab], semkey=sk)

    def mod_chunk(l, ci, wa, wab, sk, dma=True):
        if dma:
            mod_dma(l, ci, wa, wab, sk)
        pb, pbb = bank()
        for mb in range(4):
            for k in range(8):
                T(lambda e: e.matmul(pb[:, mb * 2:mb * 2 + 2], lhsT=wa[:, k, mb * 128:(mb + 1) * 128], rhs=condb[:, k, :],
                                     start=(k == 0), stop=(k == 7)), [wab, CB_], [pbb])
        for cnd in range(2):
            V(lambda e: e.tensor_tensor(out=modt[:, l, cnd, ci * 4:ci * 4 + 4], in0=pb[:, 0:8].rearrange("p (m c) -> p m c", c=2)[:, :, cnd],
                                        in1=pcol("b_mod", l * 72 + ci * 4, 4), op=ALU.add), [pbb, CONSTB], [MODL[l]])

    def mod_finish_j(l, j):
        for cnd in range(2):
            V(lambda e: e.scalar_tensor_tensor(
                out=modt[:, l, cnd, (3 * j + 1) * 8:(3 * j + 2) * 8], in0=modt[:, l, cnd, (3 * j + 1) * 8:(3 * j + 2) * 8],
                scalar=1.0, op0=ALU.add, in1=pcol("norm_w", (l * 3 + j) * 8, 8), op1=ALU.mult), [MODL[l], CONSTB], [MODL[l]])
            if j in (0, 2):
                V(lambda e: e.tensor_scalar(
                    out=modt[:, l, cnd, (3 * j + 2) * 8:(3 * j + 3) * 8], in0=modt[:, l, cnd, (3 * j + 2) * 8:(3 * j + 3) * 8],
                    scalar1=0.5, scalar2=None, op0=ALU.mult), [MODL[l]], [MODL[l]])

    mod_done = {}

    def mod_mark(l, ci):
        j = ci // 6
        mod_done[(l, j)] = mod_done.get((l, j), 0) + 1
        if mod_done[(l, j)] == 6:
            mod_finish_j(l, j)

    for ci in range(6):
        wa, wab, sk = nextA()
        mod_chunk(0, ci, wa, wab, sk)
        mod_mark(0, ci)
    K.barrier()
    pending_mod = [(0, ci) for ci in range(6, 18)] + ([(1, ci) for ci in range(18)] if nlayers > 1 else [])
    inflight_mod = []
    ffn_gi = [0]

    def modc(l, cnd, j, m):
        return modt[:, l, cnd, j * 8 + m:j * 8 + m + 1]

    def make_h(l, j, hbuf, HB, tiles=(0, 1, 2), rs_pool=None):
        for tt in tiles:
            t0, w, cnd = TT[tt]
            pb, pbb = bank()
            for c in range(8):
                sq, sqb = rs_pool["sq"].next()
                A(lambda e, sq=sq, c=c, t0=t0, w=w: e.activation(out=sq[:, 0:w], in_=x[:, c, t0:t0 + w], func=AF.Square),
                  [XB[c][tt]], [sqb])
                T(lambda e, pb=pb, sq=sq, c=c, w=w: e.matmul(pb[:, 0:w], lhsT=ones_b, rhs=sq[:, 0:w],
                                                               start=(c == 0), stop=(c == 7)), [sqb, CONSTB], [pbb])
            rs, rsb = rs_pool["rs"].next()
            A(lambda e, rs=rs, pb=pb, w=w: e.activation(out=rs[:, 0:w], in_=pb[:, 0:w], func=AF.Sqrt, bias=EPS, scale=1.0 / D),
              [pbb], [rsb])
            V(lambda e, rs=rs, w=w: e.reciprocal(rs[:, 0:w], rs[:, 0:w]), [rsb], [rsb])
            for c in range(8):
                tm, tmb = rs_pool["tm"].next()
                V(lambda e, tm=tm, rs=rs, c=c, t0=t0, w=w: e.tensor_tensor(out=tm[:, 0:w], in0=x[:, c, t0:t0 + w], in1=rs[:, 0:w], op=ALU.mult),
                  [XB[c][tt], rsb], [tmb])
                A(lambda e, tm=tm, c=c, t0=t0, w=w, cnd=cnd: e.activation(
                    out=hbuf[:, c, t0:t0 + w], in_=tm[:, 0:w], func=AF.Identity,
                    scale=modc(l, cnd, 3 * j + 1, c), bias=modc(l, cnd, 3 * j, c)), [tmb, MODL[l]], [HB[tt]])

    def resid_update(pb, pbb, m, tt, gate_ap, extra_reads=()):
        t0, w, cnd = TT[tt]
        V(lambda e: e.scalar_tensor_tensor(out=x[:, m, t0:t0 + w], in0=pb[:, 0:w], scalar=gate_ap, op0=ALU.mult,
                                           in1=x[:, m, t0:t0 + w], op1=ALU.add), [pbb, MODL[0], MODL[1]] + list(extra_reads), [XB[m][tt]])

    def ffn(l, f):
        j = 0 if f == 0 else 2
        mk = K.mark()
        hbuf = K.sb("h", [128, 8, NT], BF16)
        HB = bufs(3, "h")
        hid = [K.sb(f"hid{i}", [128, 2, NT], BF16) for i in range(2)]
        HIDB = [bufs(3, "hid0_"), bufs(3, "hid1_")]
        pool = {"sq": Rot(K, "sq", [128, 512], BF16, 3), "rs": Rot(K, "rs", [128, 512], F32, 2),
                "tm": Rot(K, "tm", [128, 512], F32, 3)}
        sgp = Rot(K, "sg", [128, 512], F32, 3)
        make_h(l, j, hbuf, HB, rs_pool=pool)
        stream_mod = (l == 0 and (len(pending_mod) > 0 or len(inflight_mod) > 0))
        if stream_mod:
            WM = [K.sb(f"WM{i}", [128, 8, 512], BF16) for i in range(4)]
            WMB = bufs(4, "WM")
            assert not inflight_mod
        for g in range(11):
            if stream_mod:
                while inflight_mod:
                    (ml, ci, slot) = inflight_mod.pop(0)
                    mod_chunk(ml, ci, WM[slot], WMB[slot], f"wm{slot}", dma=False)
                    mod_mark(ml, ci)
                ffn_gi[0] += 1
            wa, wab, ska = nextA()
            wb, wbb, skb = nextB()
            K.dma("gpsimd", lambda e, wa=wa, g=g: e.dma_start(out=wa[:].rearrange("p k n -> p (k n)"), in_=ffn_w_in[l, f, g], max_dma_last_dim=8192),
                  writes=[wab], semkey=ska)
            K.dma("gpsimd", lambda e, wb=wb, g=g: e.dma_start(out=wb[:, 0:2048], in_=ffn_w_out[l, f, g], max_dma_last_dim=8192),
                  writes=[wbb], semkey=skb)
            if stream_mod and g < 10:
                nper = 2 if ffn_gi[0] <= 10 else 1
                for i in range(nper):
                    if pending_mod:
                        slot = 2 * (g % 2) + i
                        (ml, ci) = pending_mod.pop(0)
                        mod_dma(ml, ci, WM[slot], WMB[slot], f"wm{slot}")
                        inflight_mod.append((ml, ci, slot))
            hd = hid[g % 2]
            hdb = HIDB[g % 2]
            for jj in range(2):
                for tt in range(3):
                    t0, w, cnd = TT[tt]
                    pg, pgb = bank()
                    pu, pub = bank()
                    for k in range(8):
                        T(lambda e, pg=pg, wa=wa, k=k, jj=jj, t0=t0, w=w: e.matmul(
                            pg[:, 0:w], lhsT=wa[:, k, jj * 128:(jj + 1) * 128], rhs=hbuf[:, k, t0:t0 + w], start=(k == 0), stop=(k == 7)),
                            [wab, HB[tt]], [pgb])
                    for k in range(8):
                        T(lambda e, pu=pu, wa=wa, k=k, jj=jj, t0=t0, w=w: e.matmul(
                            pu[:, 0:w], lhsT=wa[:, k, 256 + jj * 128:256 + (jj + 1) * 128], rhs=hbuf[:, k, t0:t0 + w], start=(k == 0), stop=(k == 7)),
                            [wab, HB[tt]], [pub])
                    sg, sgb = sgp.next()
                    A(lambda e, sg=sg, pg=pg, w=w: e.activation(out=sg[:, 0:w], in_=pg[:, 0:w], func=AF.Silu), [pgb], [sgb])
                    V(lambda e, sg=sg, pu=pu, hd=hd, jj=jj, t0=t0, w=w: e.tensor_tensor(
                        out=hd[:, jj, t0:t0 + w], in0=sg[:, 0:w], in1=pu[:, 0:w], op=ALU.mult), [sgb, pub], [hdb[tt]])
            wbv = wb[:, 0:2048].rearrange("p (j n) -> p j n", j=2)
            for m in range(8):
                for tt in range(3):
                    t0, w, cnd = TT[tt]
                    po, pob = bank()
                    for jj in range(2):
                        T(lambda e, po=po, wbv=wbv, jj=jj, m=m, hd=hd, t0=t0, w=w: e.matmul(
                            po[:, 0:w], lhsT=wbv[:, jj, m * 128:(m + 1) * 128], rhs=hd[:, jj, t0:t0 + w], start=(jj == 0), stop=(jj == 1)),
                            [wbb, hdb[tt]], [pob])
                    resid_update(po, pob, m, tt, modc(l, cnd, 3 * j + 2, m))
        if stream_mod:
            while inflight_mod:
                (ml, ci, slot) = inflight_mod.pop(0)
                mod_chunk(ml, ci, WM[slot], WMB[slot], f"wm{slot}", dma=False)
                mod_mark(ml, ci)
        K.barrier()
        K.release(mk)

    def mixer(l):
        mk_all = K.mark()
        wmi = mix_w_in[l]
        wmo = mix_w_out[l]
        cvx = {}

        def load_wa(col0, ncols):
            wa, wab, sk = nextA()
            K.dma("gpsimd", lambda e: e.dma_start(out=wa[:, :, 0:ncols],
                                                  in_=wmi[:, col0:col0 + ncols].rearrange("(k p) n -> p k n", p=128)),
                  writes=[wab], semkey=sk)
            return wa, wab

        def proj_fm(pb, pbb, wa, wab, c0, M, hbuf, HB, tt):
            t0, w, cnd = TT[tt]
            for k in range(8):
                T(lambda e, k=k: e.matmul(pb[0:M, 0:w], lhsT=wa[:, k, c0:c0 + M], rhs=hbuf[:, k, t0:t0 + w],
                                          start=(k == 0), stop=(k == 7)), [wab, HB[tt]], [pbb])

        yreg = []

        def out_proj(ychunks, YB, wrow0, kp, tile=None):
            yreg.append((ychunks, YB, wrow0, kp, tile))

        def out_proj_all():
            packed = []
            for (ychunks, YB, wrow0, kp, tile) in yreg:
                if kp == 64 and tile is not None:
                    yp = K.sb("ypair", [128, 2, NT], BF16)
                    YPB = Buf("ypair")
                    tv = tile[:].rearrange("p (j two) t -> p j two t", two=2)
                    K.dma("sync", lambda e, yp=yp, tv=tv: e.dma_start(out=yp[0:64, :, :], in_=tv[:, :, 0, :]), reads=YB, writes=[YPB], semkey=f"ldyp{len(packed)}")
                    K.dma("sync", lambda e, yp=yp, tv=tv: e.dma_start(out=yp[64:128, :, :], in_=tv[:, :, 1, :]), reads=YB, writes=[YPB], semkey=f"ldyp{len(packed)}")
                    packed.append(([yp[:, 0, :], yp[:, 1, :]], [YPB], wrow0, 128))
                else:
                    packed.append((ychunks, YB, wrow0, kp))
            yreg[:] = packed
            slots = []
            for (ychunks, YB, wrow0, kp) in yreg:
                nk_ = len(ychunks)
                if len(slots) % 2 == 0:
                    wt_, wtb_, sk = nextB()
                    flat = wt_[:, :]
                else:
                    wt_, wtb_, sk = nextA()
                    flat = wt_[:].rearrange("p k n -> p (k n)")
                wv = flat[0:kp, 0:nk_ * 1024].rearrange("p (j n) -> p j n", j=nk_)
                K.dma("gpsimd", lambda e, wv=wv, wrow0=wrow0, nk_=nk_, kp=kp: e.dma_start(
                    out=wv, in_=wmo[wrow0:wrow0 + nk_ * kp, :].rearrange("(j p) n -> p j n", p=kp)), writes=[wtb_], semkey=sk)
                slots.append((wv, wtb_))
            total = sum(len(y[0]) for y in yreg)
            for m in range(8):
                for tt in range(3):
                    t0, w, cnd = TT[tt]
                    po, pob = bank()
                    i = 0
                    for (ychunks, YB, wrow0, kp), (wv, wtb_) in zip(yreg, slots):
                        for jj, yc in enumerate(ychunks):
                            T(lambda e, wv=wv, jj=jj, yc=yc, i=i: e.matmul(po[:, 0:w], lhsT=wv[:, jj, m * 128:(m + 1) * 128],
                                                                        rhs=yc[:, t0:t0 + w], start=(i == 0), stop=(i == total - 1)),
                              [wtb_] + YB, [pob])
                            i += 1
                    resid_update(po, pob, m, tt, modc(l, cnd, 5, m))

        def conv_chunk(raw, rawb, P, pc0, out_ap, outb, silu):
            hl = ccol("hl")
            hr = ccol("hr")
            V(lambda e: e.tensor_tensor(out=raw[0:P, 1:5, 0:1], in0=raw[0:P, 0:4, 256:257], in1=hl[0:P, 1:5].unsqueeze(2), op=ALU.mult),
              [rawb, CONSTB], [rawb])
            V(lambda e: e.tensor_tensor(out=raw[0:P, 0:4, 257:258], in0=raw[0:P, 1:5, 1:2], in1=hr[0:P, 0:4].unsqueeze(2), op=ALU.mult),
              [rawb, CONSTB], [rawb])
            acc, accb = cvx["convacc"].next()
            accv = acc[0:P, :].rearrange("p (b t) -> p b t", t=256)
            w0 = ptab[0:P, pc0:pc0 + 1]
            w1 = ptab[0:P, pc0 + 1:pc0 + 2]
            w2 = ptab[0:P, pc0 + 2:pc0 + 3]
            bb = ptab[0:P, pc0 + 3:pc0 + 4]
            V(lambda e: e.tensor_scalar(out=accv, in0=raw[0:P, :, 1:257], scalar1=w1, scalar2=None, op0=ALU.mult), [rawb, CONSTB], [accb])
            V(lambda e: e.scalar_tensor_tensor(out=accv, in0=raw[0:P, :, 0:256], scalar=w0, op0=ALU.mult, in1=accv, op1=ALU.add),
              [rawb, CONSTB, accb], [accb])
            V(lambda e: e.scalar_tensor_tensor(out=accv, in0=raw[0:P, :, 2:258], scalar=w2, op0=ALU.mult, in1=accv, op1=ALU.add),
              [rawb, CONSTB, accb], [accb])
            A(lambda e: e.activation(out=out_ap, in_=acc[0:P, :], func=(AF.Silu if silu else AF.Identity), bias=bb), [accb, CONSTB], [outb])

        def raw_fill(raw, rawb, P, pb, pbb, tt):
            t0, w, cnd = TT[tt]
            b0 = t0 // 256
            nb = w // 256
            A(lambda e: e.activation(out=raw[0:P, b0:b0 + nb, 1:257], in_=pb[0:P, 0:w].rearrange("p (b t) -> p b t", t=256), func=AF.Copy),
              [pbb], [rawb])

        hbuf_m = K.sb("hmix", [128, 8, NT], BF16)
        HB_m = bufs(3, "hmix")
        mkp = K.mark()
        pool_m = {"sq": Rot(K, "sq", [128, 512], BF16, 3), "rs": Rot(K, "rs", [128, 512], F32, 2),
                  "tm": Rot(K, "tm", [128, 512], F32, 2)}
        make_h(l, 1, hbuf_m, HB_m, rs_pool=pool_m)
        K.barrier()
        K.release(mkp)

        def with_h(fn, conv=False):
            mk = K.mark()
            if conv:
                cvx["convacc"] = Rot(K, "cacc", [128, NT], F32, 1)
                raws = Rot(K, "raw", [128, 5, 258], F32, 2)
                cvx["raws"] = raws
                for i in range(2):
                    V(lambda e, i=i: e.memset(raws.t[i][:], 0.0), [], [raws.b[i]])
            fn(hbuf_m, HB_m)
            K.barrier()
            K.release(mk)

        def ssd(szb, SZB):
            mk = K.mark()
            xsf = K.sb("xsf", [64, 4, NT], BF16)
            XSF = Buf("xsf")
            bcf = K.sb("bcf", [128, 4, NT], BF16)
            BCF = Buf("bcf")
            dtr = K.sb("dtr", [128, 10, 8])
            dtt = K.sb("dtt", [128, 10, 8])
            lat = K.sb("lat", [128, 10, 8])
            DTB = Buf("dt")

            def inproj(hbuf, HB):
                wa0, wab0 = load_wa(0, 512)
                wa1, wab1 = load_wa(512, 512)
                for hh in range(4):
                    for tt in range(3):
                        t0, w, cnd = TT[tt]
                        pb, pbb = bank()
                        proj_fm(pb, pbb, wa0, wab0, hh * 64, 64, hbuf, HB, tt)
                        A(lambda e, pb=pb, hh=hh, t0=t0, w=w: e.activation(out=szb[:, hh, t0:t0 + w], in_=pb[0:64, 0:w], func=AF.Silu), [pbb], [SZB])
                for q in range(8):
                    raw, rawb = cvx["raws"].next()
                    P = 64 if q < 4 else 128
                    for tt in range(3):
                        pb, pbb = bank()
                        if q < 4:
                            proj_fm(pb, pbb, wa0, wab0, 256 + q * 64, 64, hbuf, HB, tt)
                        else:
                            proj_fm(pb, pbb, wa1, wab1, (q - 4) * 128, 128, hbuf, HB, tt)
                        raw_fill(raw, rawb, P, pb, pbb, tt)
                    pc0 = PT["ssd_conv"][0] + (l * 8 + q) * 4
                    if q < 4:
                        conv_chunk(raw, rawb, 64, pc0, xsf[:, q, :], XSF, True)
                    else:
                        conv_chunk(raw, rawb, 128, pc0, bcf[:, q - 4, :], BCF, True)
                wdt = K.sb("wdt", [128, 8, 8], BF16)
                WDT = Buf("wdt")
                K.dma("gpsimd", lambda e: e.dma_start(out=wdt[:], in_=wmi[:, 1024:1032].rearrange("(k p) n -> p k n", p=128)),
                      writes=[WDT], semkey="wdt")
                pb, pbb = bank()
                for tc in range(NCH):
                    tt = 0 if tc < 4 else (1 if tc < 8 else 2)
                    for k in range(8):
                        T(lambda e, tc=tc, k=k: e.matmul(pb[:, tc * 8:tc * 8 + 8], lhsT=hbuf[:, k, tc * 128:(tc + 1) * 128], rhs=wdt[:, k, :],
                                                         start=(k == 0), stop=(k == 7)), [HB[tt], WDT], [pbb])
                V(lambda e: e.tensor_tensor(out=dtr[:].rearrange("p a b -> p (a b)"), in0=pb[:, 0:80], in1=fcol("dt_bias", l * 80, 80), op=ALU.add),
                  [pbb, CONSTB], [DTB])
            with_h(inproj, conv=True)
            A(lambda e: e.activation(out=dtt[:], in_=dtr[:], func=AF.Exp), [DTB], [DTB])
            A(lambda e: e.activation(out=dtt[:], in_=dtt[:], func=AF.Ln, bias=1.0), [DTB], [DTB])
            A(lambda e: e.activation(out=dtr[:].rearrange("p a b -> p (a b)"), in_=fcol("a_log", l * 80, 80), func=AF.Exp), [DTB, CONSTB], [DTB])
            V(lambda e: e.scalar_tensor_tensor(out=lat[:], in0=dtr[:], scalar=-1.0, op0=ALU.mult, in1=dtt[:], op1=ALU.mult), [DTB], [DTB])
            xbtm = K.sb("xbtm", [128, NCH, 512], BF16)
            XBT = bufs(NCH, "xbtm")
            for tc in range(NCH):
                pb, pbb = bank()
                pbv = pb[:].bitcast(BF16)
                for hh in range(4):
                    T(lambda e, hh=hh, tc=tc: e.transpose(pbv[:, hh * 64:(hh + 1) * 64], xsf[:, hh, tc * 128:(tc + 1) * 128], ident_b[0:64, 0:64]),
                      [XSF, CONSTB], [pbb])
                for g in range(2):
                    T(lambda e, g=g, tc=tc: e.transpose(pbv[:, 256 + g * 128:256 + (g + 1) * 128], bcf[:, g, tc * 128:(tc + 1) * 128], ident_b),
                      [BCF, CONSTB], [pbb])
                A(lambda e, tc=tc, pbv=pbv: e.activation(out=xbtm[:, tc, :], in_=pbv[:, 0:512], func=AF.Copy), [pbb], [XBT[tc]])
            cst = K.sb("cst", [128, NCH, 16])
            wBt = K.sb("wBt", [128, NCH, 8])
            edt = K.sb("edt", [128, NCH, 8])
            pb, pbb = bank()
            for tc in range(NCH):
                T(lambda e, tc=tc: e.matmul(pb[:, tc * 16:tc * 16 + 4], lhsT=trif, rhs=lat[:, tc, 0:4], start=True, stop=True), [DTB, CONSTB], [pbb])
                T(lambda e, tc=tc: e.matmul(pb[:, tc * 16 + 4:tc * 16 + 8], lhsT=trib, rhs=lat[:, tc, 4:8], start=True, stop=True), [DTB, CONSTB], [pbb])
                T(lambda e, tc=tc: e.matmul(pb[:, tc * 16 + 8:tc * 16 + 16], lhsT=ones_f, rhs=lat[:, tc, 0:8], start=True, stop=True), [DTB, CONSTB], [pbb])
            V(lambda e: e.tensor_copy(cst[:].rearrange("p a b -> p (a b)"), pb[:, 0:160]), [pbb], [DTB])
            V(lambda e: e.tensor_tensor(out=wBt[:], in0=cst[:, :, 8:16], in1=cst[:, :, 0:8], op=ALU.subtract), [DTB], [DTB])
            A(lambda e: e.activation(out=wBt[:], in_=wBt[:], func=AF.Exp), [DTB], [DTB])
            V(lambda e: e.tensor_tensor(out=wBt[:], in0=wBt[:], in1=dtt[:], op=ALU.mult), [DTB], [DTB])
            A(lambda e: e.activation(out=edt[:], in_=cst[:, :, 8:16], func=AF.Exp), [DTB], [DTB])
            S32 = K.sb("S32", [128, 8, 64])
            SB_ = bufs(8, "S32")
            SP = K.sb("SP", [128, NCH, 8, 64], BF16)
            SPB = [bufs(8, f"SP{c}_") for c in range(NCH)]
            mk_st = K.mark()
            s0t = K.sb("s0t", [128, 40, 64])
            S0B = Buf("s0")
            K.dma("sync", lambda e: e.dma_start(out=s0t[:], in_=s0ssd[:, l * 40:(l + 1) * 40, :]), writes=[S0B], semkey="lds0")
            V(lambda e: e.memset(S32[:], 0.0), [], SB_)
            hl = ccol("hl")
            hr = ccol("hr")
            bsp = Rot(K, "bs", [128, 128], BF16, 12)

            def state_chain(d, hh):
                col = d * 4 + hh
                g = hh // 2
                for k in range(NCH):
                    c = k if d == 0 else NCH - 1 - k
                    blk = c // 2
                    first = (c % 2 == 0) if d == 0 else (c % 2 == 1)
                    sidx = (blk * 2 + d) * 4 + hh
                    if first:
                        fl = hl[:, blk:blk + 1] if d == 0 else hr[:, blk:blk + 1]
                        V(lambda e: e.scalar_tensor_tensor(out=S32[:, col, :], in0=S32[:, col, :], scalar=fl, op0=ALU.mult, in1=s0t[:, sidx, :], op1=ALU.add),
                          [SB_[col], S0B, CONSTB], [SB_[col]])
                    A(lambda e: e.activation(out=SP[:, c, col, :], in_=S32[:, col, :], func=AF.Copy), [SB_[col]], [SPB[c][col]])
                    bs, bsb = bsp.next()
                    V(lambda e: e.tensor_scalar(out=bs[:], in0=xbtm[:, c, 256 + g * 128:256 + (g + 1) * 128], scalar1=wBt[:, c, col:col + 1], scalar2=None, op0=ALU.mult),
                      [XBT[c], DTB], [bsb])
                    yield
                    pb, pbb = pr(64)
                    T(lambda e: e.matmul(pb[:, 0:64], lhsT=bs[:], rhs=xbtm[:, c, hh * 64:(hh + 1) * 64], start=True, stop=True), [bsb, XBT[c]], [pbb])
                    yield
                    V(lambda e: e.scalar_tensor_tensor(out=S32[:, col, :], in0=S32[:, col, :], scalar=edt[:, c, col:col + 1], op0=ALU.mult, in1=pb[:, 0:64], op1=ALU.add),
                      [SB_[col], DTB, pbb], [SB_[col]])
                    if not first:
                        oidx = l * 40 + sidx
                        sslot = sstp.i
                        stg, stgb = sstp.next()
                        A(lambda e: e.activation(out=stg[:], in_=S32[:, col, :], func=AF.Copy), [SB_[col]], [stgb])
                        K.dma("sync", lambda e: e.dma_start(out=nssd_o[:, oidx, :], in_=stg[:]), reads=[stgb], semkey=f"stS{sslot}")
                    yield
            sstp = Rot(K, "sstg", [128, 64], F32, 8)
            interleave([state_chain(d, hh) for d in range(2) for hh in range(4)])
            K.barrier()
            K.release(mk_st)

            dsk = pcol("ssd_d", l * 4, 4, rows=64)
            nws = pcol("ssd_nw", l * 4, 4, rows=64)
            ygp = Rot(K, "yg", [64, 4, 128], F32, 2)
            wtp = Rot(K, "wt", [128, 128], F32, 16)
            sgp2 = Rot(K, "sg2", [128, 128], F32, 16)
            csp = Rot(K, "cs", [128, 128], BF16, 16)
            sqp = Rot(K, "sq4", [64, 128], BF16, 8)

            def ssd_chain(c, hh, d, psc, pscb, res):
                cs = slice(c * 128, (c + 1) * 128)
                col = d * 4 + hh
                g = hh // 2
                U = trif if d == 0 else trib
                NB = ccol("nbf") if d == 0 else ccol("nbb")
                wt, wtb = wtp.next()
                G(lambda e: e.tensor_scalar(out=wt[:], in0=U, scalar1=lat[:, c, col:col + 1], scalar2=0.0, op0=ALU.mult, op1=ALU.add), [CONSTB, DTB], [wtb])
                yield
                pc, pcb = pr(128)
                T(lambda e: e.matmul(pc[:, 0:128], lhsT=ones_f, rhs=wt[:], start=True, stop=True), [wtb, CONSTB], [pcb])
                yield
                sg, sgb = sgp2.next()
                V(lambda e: e.scalar_tensor_tensor(out=sg[:], in0=pc[:, 0:128], scalar=cst[:, c, col:col + 1], op0=ALU.subtract, in1=NB, op1=ALU.add),
                  [pcb, DTB, CONSTB], [sgb])
                yield
                ec, ecb = wt, wtb
                A(lambda e: e.activation(out=ec[:], in_=pc[:, 0:128], func=AF.Exp), [pcb], [ecb])
                yield
                A(lambda e: e.activation(out=sg[:], in_=sg[:], func=AF.Exp), [sgb], [sgb])
                csb, csbb = csp.next()
                G(lambda e: e.tensor_tensor(out=csb[:], in0=bcf[:, 2 + g, cs], in1=ec[:], op=ALU.mult), [BCF, ecb], [csbb])
                yield
                yield
                st, stb = wt[:].bitcast(BF16)[:, 0:128], wtb
                V(lambda e: e.scalar_tensor_tensor(out=st, in0=psc[:, g * 128:(g + 1) * 128], scalar=dtt[:, c, col:col + 1], op0=ALU.mult, in1=sg[:], op1=ALU.mult),
                  [pscb, sgb, DTB], [stb])
                res[(hh, d)] = (st, stb, csb, csbb)
                yield

            def ssd_chunk(c):
                cs = slice(c * 128, (c + 1) * 128)
                psc, pscb = pr(256)
                for g in range(2):
                    T(lambda e: e.matmul(psc[:, g * 128:(g + 1) * 128], lhsT=bcf[:, g, cs], rhs=bcf[:, 2 + g, cs], start=True, stop=True), [BCF], [pscb])
                res = {}
                chains = [ssd_chain(c, hh, d, psc, pscb, res) for hh in range(4) for d in range(2)]
                while chains:
                    nxt = []
                    for gch in chains:
                        try:
                            next(gch)
                            nxt.append(gch)
                        except StopIteration:
                            pass
                    chains = nxt
                    yield
                yg, ygb = ygp.next()
                pys = []
                for hh in range(4):
                    py, pyb = pr(128)
                    pys.append((py, pyb))
                    for d in range(2):
                        col = d * 4 + hh
                        st, stb, csb, csbb = res[(hh, d)]
                        T(lambda e: e.matmul(py[0:64, 0:128], lhsT=xbtm[:, c, hh * 64:(hh + 1) * 64], rhs=st, start=(d == 0), stop=False), [XBT[c], stb], [pyb])
                        T(lambda e: e.matmul(py[0:64, 0:128], lhsT=SP[:, c, col, :], rhs=csb[:], start=False, stop=(d == 1)), [SPB[c][col], csbb], [pyb])
                yield
                sqs = []
                for hh in range(4):
                    py, pyb = pys[hh]
                    V(lambda e: e.scalar_tensor_tensor(out=yg[:, hh, :], in0=xsf[:, hh, cs], scalar=dsk[:, hh:hh + 1], op0=ALU.mult, in1=py[0:64, 0:128], op1=ALU.add),
                      [XSF, CONSTB, pyb], [ygb])
                    V(lambda e: e.tensor_tensor(out=yg[:, hh, :], in0=yg[:, hh, :], in1=szb[:, hh, cs], op=ALU.mult), [ygb, SZB], [ygb])
                    sq, sqb = sqp.next()
                    A(lambda e: e.activation(out=sq[:], in_=yg[:, hh, :], func=AF.Square), [ygb], [sqb])
                    sqs.append((sq, sqb))
                    yield
                pq, pqb = pr(128)
                for hh in range(4):
                    sq, sqb = sqs[hh]
                    T(lambda e: e.matmul(pq[0:64, 0:128], lhsT=ones_b[0:64, 0:64], rhs=sq[:], start=(hh == 0), stop=(hh == 3)), [sqb, CONSTB], [pqb])
                yield
                rs, rsb = rsp2.next()
                A(lambda e: e.activation(out=rs[:], in_=pq[0:64, 0:128], func=AF.Sqrt, bias=EPS, scale=1.0 / 256), [pqb], [rsb])
                yield
                V(lambda e: e.reciprocal(rs[:], rs[:]), [rsb], [rsb])
                yield
                for hh in range(4):
                    V(lambda e: e.scalar_tensor_tensor(out=szb[:, hh, cs], in0=yg[:, hh, :], scalar=nws[:, hh:hh + 1], op0=ALU.mult, in1=rs[:], op1=ALU.mult),
                      [ygb, rsb, CONSTB], [SZB])
                    yield
            rsp2 = Rot(K, "rs2", [64, 128], F32, 2)
            for c0 in range(0, NCH, 2):
                interleave([ssd_chunk(c0), ssd_chunk(c0 + 1)])
            out_proj([szb[:, hh, :] for hh in range(4)], [SZB], 0, 64, tile=szb)
            K.barrier()
            K.release(mk)

        def ret(sgf, SGF):
            mk = K.mark()
            qf = K.sb("qf", [64, 4, NT], BF16)
            kf = K.sb("kf", [64, 4, NT], BF16)
            kvt = K.sb("kvt", [128, NCH, 256], BF16)
            QF, KF = Buf("qf"), Buf("kf")
            KVT = bufs(NCH, "kvt")
            KST = Buf("kst")
            lg = K.sb("lg", [128, 8])
            LG = Buf("lg")
            A(lambda e: e.activation(out=lg[:], in_=fcol("ret_logit", l * 8, 8), func=AF.Exp, scale=-1.0), [CONSTB], [LG])
            A(lambda e: e.activation(out=lg[:], in_=lg[:], func=AF.Ln, bias=1.0), [LG], [LG])
            V(lambda e: e.tensor_scalar(out=lg[:], in0=lg[:], scalar1=-1.0, scalar2=None, op0=ALU.mult), [LG], [LG])
            Tl = K.sb("Tl", [128, 8])
            G128 = K.sb("G128", [128, 8])
            RC = Buf("retc")
            for hh in range(4):
                A(lambda e, hh=hh: e.activation(out=Tl[:, hh:hh + 1], in_=ccol("tailf"), func=AF.Exp, scale=lg[:, hh:hh + 1]), [LG, CONSTB], [RC])
                A(lambda e, hh=hh: e.activation(out=Tl[:, 4 + hh:5 + hh], in_=ccol("tailb"), func=AF.Exp, scale=lg[:, 4 + hh:5 + hh]), [LG, CONSTB], [RC])
            V(lambda e: e.tensor_scalar(out=Tl[:], in0=Tl[:], scalar1=0.125, scalar2=None, op0=ALU.mult), [RC], [RC])
            A(lambda e: e.activation(out=G128[:], in_=lg[:], func=AF.Exp, scale=128.0), [LG], [RC])
            S32 = K.sb("R32", [64, 8, 64])
            SB_ = bufs(8, "R32")
            SP = K.sb("RSP", [64, NCH, 8, 64], BF16)
            SPB = [bufs(8, f"RSP{c}_") for c in range(NCH)]
            mk_k = K.mark()
            kst = K.sb("kst", [128, 2, NCH, 256], BF16)

            def inproj(hbuf, HB):
                wa0, wab0 = load_wa(1800, 512)
                wa1, wab1 = load_wa(2312, 512)
                qp = K.sb("qpair", [128, 2, 2, NT], BF16)
                QPB = bufs(2, "qpair")
                specs = [(wa0, wab0, 0, AF.Copy, None, qf, QF), (wa0, wab0, 256, AF.Copy, 0.125, kf, KF), (wa1, wab1, 256, AF.Silu, None, sgf, SGF)]
                for i, (wsrc, wsrcb, c0, fn_, scl, dst, dstb) in enumerate(specs):
                    slot = i % 2
                    for j in range(2):
                        for tt in range(3):
                            t0, w, cnd = TT[tt]
                            pb, pbb = bank()
                            proj_fm(pb, pbb, wsrc, wsrcb, c0 + j * 128, 128, hbuf, HB, tt)
                            if scl is None:
                                A(lambda e: e.activation(out=qp[:, slot, j, t0:t0 + w], in_=pb[:, 0:w], func=fn_), [pbb], [QPB[slot]])
                            else:
                                A(lambda e: e.activation(out=qp[:, slot, j, t0:t0 + w], in_=pb[:, 0:w], func=fn_, scale=scl), [pbb], [QPB[slot]])
                    dv = dst[:].rearrange("p (j two) t -> p j two t", two=2)
                    for half in range(2):
                        K.dma("sync", lambda e: e.dma_start(out=dv[:, :, half, :], in_=qp[half * 64:(half + 1) * 64, slot, :, :]),
                              reads=[QPB[slot]], writes=[dstb], semkey=f"ldrp{i}")
                for tc in range(NCH):
                    tt = 0 if tc < 4 else (1 if tc < 8 else 2)
                    pb, pbb = bank()
                    for k in range(8):
                        T(lambda e, pb=pb, tc=tc, k=k: e.matmul(pb[:, 0:256], lhsT=hbuf[:, k, tc * 128:(tc + 1) * 128], rhs=wa0[:, k, 256:512],
                                                                 start=(k == 0), stop=(k == 7)), [HB[tt], wab0], [pbb])
                    for k in range(8):
                        T(lambda e, pb=pb, tc=tc, k=k: e.matmul(pb[:, 256:512], lhsT=hbuf[:, k, tc * 128:(tc + 1) * 128], rhs=wa1[:, k, 0:256],
                                                                 start=(k == 0), stop=(k == 7)), [HB[tt], wab1], [pbb])
                    for d in range(2):
                        V(lambda e, pb=pb, tc=tc, d=d: e.tensor_tensor(out=kst[:, d, tc, :].rearrange("p (h n) -> p h n", h=4),
                                                                       in0=pb[:, 0:256].rearrange("p (h n) -> p h n", h=4),
                                                                       in1=Tl[:, d * 4:(d + 1) * 4].unsqueeze(2).broadcast_to([128, 4, 64]), op=ALU.mult),
                          [pbb, RC], [KST])
                    A(lambda e, pb=pb, tc=tc: e.activation(out=kvt[:, tc, :], in_=pb[:, 256:512], func=AF.Copy), [pbb], [KVT[tc]])
            with_h(inproj)
            mk_st = K.mark()
            s0t = K.sb("rs0t", [64, 40, 64])
            S0B = Buf("rs0")
            K.dma("sync", lambda e: e.dma_start(out=s0t[:], in_=s0ret[:, l * 40:(l + 1) * 40, :]), writes=[S0B], semkey="lds0")
            V(lambda e: e.memset(S32[:], 0.0), [], SB_)
            hl = ccol("hl")
            hr = ccol("hr")
            def rstate_chain(d, hh):
                col = d * 4 + hh
                for k in range(NCH):
                    c = k if d == 0 else NCH - 1 - k
                    blk = c // 2
                    first = (c % 2 == 0) if d == 0 else (c % 2 == 1)
                    sidx = (blk * 2 + d) * 4 + hh
                    if first:
                        fl = hl[0:64, blk:blk + 1] if d == 0 else hr[0:64, blk:blk + 1]
                        V(lambda e: e.scalar_tensor_tensor(out=S32[:, col, :], in0=S32[:, col, :], scalar=fl, op0=ALU.mult, in1=s0t[:, sidx, :], op1=ALU.add),
                          [SB_[col], S0B, CONSTB], [SB_[col]])
                    A(lambda e: e.activation(out=SP[:, c, col, :], in_=S32[:, col, :], func=AF.Copy), [SB_[col]], [SPB[c][col]])
                    yield
                    pb, pbb = pr(64)
                    T(lambda e: e.matmul(pb[0:64, 0:64], lhsT=kst[:, d, c, hh * 64:(hh + 1) * 64], rhs=kvt[:, c, hh * 64:(hh + 1) * 64],
                                         start=True, stop=True), [KST, KVT[c]], [pbb])
                    yield
                    V(lambda e: e.scalar_tensor_tensor(out=S32[:, col, :], in0=S32[:, col, :], scalar=G128[0:64, col:col + 1], op0=ALU.mult, in1=pb[0:64, 0:64], op1=ALU.add),
                      [SB_[col], RC, pbb], [SB_[col]])
                    if not first:
                        oidx = l * 40 + sidx
                        sslot = rstp.i
                        stg, stgb = rstp.next()
                        A(lambda e: e.activation(out=stg[:], in_=S32[:, col, :], func=AF.Copy), [SB_[col]], [stgb])
                        K.dma("sync", lambda e: e.dma_start(out=nret_o[:, oidx, :], in_=stg[:]), reads=[stgb], semkey=f"stR{sslot}")
                    yield
            rstp = Rot(K, "rstg", [64, 64], F32, 8)
            interleave([rstate_chain(d, hh) for d in range(2) for hh in range(4)])
            K.barrier()
            K.release(mk_k)

            gnw = pcol("ret_gn", l * 4, 4, rows=64)
            Dm = K.sb("Dm", [128, 4, 128])
            Ef = K.sb("Ef", [64, 8, 128])
            tmpf = Rot(K, "tmpf", [128, 128], F32, 2)
            for hh in range(4):
                t1, t1b = tmpf.next()
                A(lambda e, t1=t1, hh=hh: e.activation(out=t1[:], in_=ccol("relu_f"), func=AF.Exp, scale=lg[:, hh:hh + 1]), [LG, CONSTB], [t1b])
                V(lambda e, t1=t1, hh=hh: e.tensor_tensor(out=Dm[:, hh, :], in0=t1[:], in1=trif, op=ALU.mult), [t1b, CONSTB], [RC])
                t2, t2b = tmpf.next()
                A(lambda e, t2=t2, hh=hh: e.activation(out=t2[:], in_=ccol("relu_b"), func=AF.Exp, scale=lg[:, 4 + hh:5 + hh]), [LG, CONSTB], [t2b])
                V(lambda e, t2=t2: e.tensor_tensor(out=t2[:], in0=t2[:], in1=trib, op=ALU.mult), [t2b, CONSTB], [t2b])
                V(lambda e, t2=t2, hh=hh: e.tensor_tensor(out=Dm[:, hh, :], in0=Dm[:, hh, :], in1=t2[:], op=ALU.add), [t2b, RC], [RC])
                A(lambda e, hh=hh: e.activation(out=Ef[:, hh, :], in_=ccol("ip1", rows=64), func=AF.Exp, scale=lg[0:64, hh:hh + 1]), [LG, CONSTB], [RC])
                A(lambda e, hh=hh: e.activation(out=Ef[:, 4 + hh, :], in_=ccol("rmi", rows=64), func=AF.Exp, scale=lg[0:64, 4 + hh:5 + hh]), [LG, CONSTB], [RC])
            qsp = Rot(K, "qs4", [64, 2, 4, 128], BF16, 2)
            st4p = Rot(K, "st4", [128, 4, 128], BF16, 2)
            yvp = Rot(K, "yv4", [64, 4, 128], F32, 2)
            sq4p = Rot(K, "sq4", [64, 4, 128], F32, 2)

            def ret_chunk(c):
                cs = slice(c * 128, (c + 1) * 128)
                ps_, psb = bank()
                for hh in range(4):
                    T(lambda e: e.matmul(ps_[:, hh * 128:(hh + 1) * 128], lhsT=kf[:, hh, cs], rhs=qf[:, hh, cs], start=True, stop=True), [KF, QF], [psb])
                yield
                st, stb = st4p.next()
                V(lambda e: e.tensor_tensor(out=st[:], in0=ps_[:, 0:512].rearrange("p (h t) -> p h t", h=4), in1=Dm[:], op=ALU.mult), [psb, RC], [stb])
                qs, qsb = qsp.next()
                V(lambda e: e.tensor_tensor(out=qs[:], in0=qf[:, :, cs].unsqueeze(1).broadcast_to([64, 2, 4, 128]),
                                            in1=Ef[:].rearrange("p (d h) t -> p d h t", d=2), op=ALU.mult), [QF, RC], [qsb])
                yield
                py, pyb = bank()
                for hh in range(4):
                    T(lambda e: e.matmul(py[0:64, hh * 128:(hh + 1) * 128], lhsT=kvt[:, c, hh * 64:(hh + 1) * 64], rhs=st[:, hh, :],
                                         start=True, stop=False), [KVT[c], stb], [pyb])
                    T(lambda e: e.matmul(py[0:64, hh * 128:(hh + 1) * 128], lhsT=SP[:, c, hh, :], rhs=qs[:, 0, hh, :], start=False, stop=False),
                      [SPB[c][hh], qsb], [pyb])
                    T(lambda e: e.matmul(py[0:64, hh * 128:(hh + 1) * 128], lhsT=SP[:, c, 4 + hh, :], rhs=qs[:, 1, hh, :], start=False, stop=True),
                      [SPB[c][4 + hh], qsb], [pyb])
                yield
                yv, yvb = yvp.next()
                yvf = yv[:].rearrange("p h t -> p (h t)")
                V(lambda e: e.tensor_copy(yvf, py[0:64, 0:512]), [pyb], [yvb])
                yield
                pm, pmb = bank()
                T(lambda e: e.matmul(pm[0:64, 0:512], lhsT=ones_f[0:64, 0:64], rhs=yvf, start=True, stop=True), [yvb, CONSTB], [pmb])
                yield
                V(lambda e: e.scalar_tensor_tensor(out=yvf, in0=pm[0:64, 0:512], scalar=-1.0 / 64, op0=ALU.mult, in1=yvf, op1=ALU.add), [pmb, yvb], [yvb])
                yield
                sq, sqb = sq4p.next()
                sqf = sq[:].rearrange("p h t -> p (h t)")
                A(lambda e: e.activation(out=sqf, in_=yvf, func=AF.Square), [yvb], [sqb])
                yield
                pv_, pvb = bank()
                T(lambda e: e.matmul(pv_[0:64, 0:512], lhsT=ones_f[0:64, 0:64], rhs=sqf, start=True, stop=True), [sqb, CONSTB], [pvb])
                yield
                A(lambda e: e.activation(out=sqf, in_=pv_[0:64, 0:512], func=AF.Sqrt, bias=EPS, scale=1.0 / 64), [pvb], [sqb])
                yield
                V(lambda e: e.reciprocal(sqf, sqf), [sqb], [sqb])
                yield
                V(lambda e: e.tensor_tensor(out=yvf, in0=yvf, in1=sqf, op=ALU.mult), [yvb, sqb], [yvb])
                V(lambda e: e.tensor_tensor(out=yv[:], in0=yv[:], in1=gnw.unsqueeze(2).broadcast_to([64, 4, 128]), op=ALU.mult), [yvb, CONSTB], [yvb])
                yield
                V(lambda e: e.tensor_tensor(out=sgf[:, :, cs], in0=yv[:], in1=sgf[:, :, cs], op=ALU.mult), [yvb, SGF], [SGF])
                yield
            for c0 in range(0, NCH, 2):
                interleave([ret_chunk(c0), ret_chunk(c0 + 1)])
            out_proj([sgf[:, hh, :] for hh in range(4)], [SGF], 512, 64, tile=sgf)
            K.barrier()
            K.release(mk)

        def att(yat, YA):
            mk = K.mark()
            qfm = K.sb("aq", [64, 4, NT], BF16)
            kfm = K.sb("ak", [64, 2, NT], BF16)
            vtm = K.sb("av", [128, NCH, 128], BF16)
            QF, KF = Buf("aq"), Buf("ak")
            VT = bufs(NCH, "av")
            ckf = K.sb("ckf", [64, 2, 512], BF16)
            cvs = K.sb("cvs", [128, 4, 128], BF16)
            CK = Buf("ck")
            K.dma("gpsimd", lambda e: e.dma_start(out=ckf[:], in_=ckT[:, l * 2:l * 2 + 2, :]), writes=[CK], semkey="ldck")
            K.dma("gpsimd", lambda e: e.dma_start(out=cvs[:], in_=cvt[:, l * 4:l * 4 + 4, :]), writes=[CK], semkey="ldck")
            es = K.sb("es", [64, 4])
            A(lambda e: e.activation(out=es[:], in_=fcol("sink", l * 4, 4, rows=64), func=AF.Exp), [CONSTB], [CK])
            mk_in = K.mark()
            qkp = Rot(K, "qk", [128, 512], F32, 3)
            qnp = Rot(K, "qn", [128, 384], F32, 3)
            qbp = Rot(K, "qb", [128, 384], BF16, 3)
            smp = Rot(K, "sm", [128, 8], F32, 3)
            rtp = Rot(K, "rt", [128, 6, 2, 16], F32, 8)
            kvo = Rot(K, "kvo", [128, 256], F32, 3)

            def inproj(hbuf, HB):
                wa, wab = load_wa(2824, 512)

                def in_chunk(tc):
                    tt = 0 if tc < 4 else (1 if tc < 8 else 2)
                    pb, pbb = bank()
                    for k in range(8):
                        T(lambda e: e.matmul(pb[:, 0:512], lhsT=hbuf[:, k, tc * 128:(tc + 1) * 128], rhs=wa[:, k, :], start=(k == 0), stop=(k == 7)),
                          [HB[tt], wab], [pbb])
                    yield
                    qk, qkb = qkp.next()
                    A(lambda e: e.activation(out=qk[:], in_=pb[:, 0:512], func=AF.Copy), [pbb], [qkb])
                    yield
                    qn, qnb = qnp.next()
                    sm, smb = smp.next()
                    A(lambda e: e.activation(out=qn[:], in_=qk[:, 0:384], func=AF.Square), [qkb], [qnb])
                    A(lambda e: e.activation(out=vtm[:, tc, :], in_=qk[:, 384:512], func=AF.Copy), [qkb], [VT[tc]])
                    yield
                    V(lambda e: e.tensor_reduce(out=sm[:, 0:6], in_=qn[:].rearrange("p (h d) -> p h d", d=64), axis=AX.X, op=ALU.add), [qnb], [smb])
                    yield
                    A(lambda e: e.activation(out=sm[:, 0:6], in_=sm[:, 0:6], func=AF.Sqrt, bias=EPS, scale=1.0 / 64), [smb], [smb])
                    yield
                    V(lambda e: e.reciprocal(sm[:, 0:6], sm[:, 0:6]), [smb], [smb])
                    yield
                    V(lambda e: e.tensor_tensor(out=qn[:].rearrange("p (h d) -> p h d", d=64), in0=qk[:, 0:384].rearrange("p (h d) -> p h d", d=64),
                                                in1=sm[:, 0:6].unsqueeze(2).broadcast_to([128, 6, 64]), op=ALU.mult), [qkb, smb], [qnb])
                    yield
                    V(lambda e: e.tensor_tensor(out=qn[:], in0=qn[:], in1=fcol("qkw", l * 384, 384), op=ALU.mult), [qnb, CONSTB], [qnb])
                    yield
                    qv = qn[:].rearrange("p (h a b f) -> p h a b f", h=6, a=2, b=2)
                    cosv = ccol("cos", tc * 32, 32).rearrange("p (a f) -> p a f", a=2).unsqueeze(1).broadcast_to([128, 6, 2, 16])
                    sinv = ccol("sin", tc * 32, 32).rearrange("p (a f) -> p a f", a=2).unsqueeze(1).broadcast_to([128, 6, 2, 16])
                    x1 = qv[:, :, :, 0, :]
                    x2 = qv[:, :, :, 1, :]
                    t1, t1b = rtp.next()
                    t2, t2b = rtp.next()
                    t3, t3b = rtp.next()
                    t4, t4b = rtp.next()
                    V(lambda e: e.tensor_tensor(out=t1[:], in0=x1, in1=cosv, op=ALU.mult), [qnb, CONSTB], [t1b])
                    V(lambda e: e.tensor_tensor(out=t3[:], in0=x1, in1=sinv, op=ALU.mult), [qnb, CONSTB], [t3b])
                    yield
                    V(lambda e: e.tensor_tensor(out=t2[:], in0=x2, in1=sinv, op=ALU.mult), [qnb, CONSTB], [t2b])
                    V(lambda e: e.tensor_tensor(out=t4[:], in0=x2, in1=cosv, op=ALU.mult), [qnb, CONSTB], [t4b])
                    yield
                    V(lambda e: e.tensor_tensor(out=x1, in0=t1[:], in1=t2[:], op=ALU.subtract), [t1b, t2b], [qnb])
                    yield
                    V(lambda e: e.tensor_tensor(out=x2, in0=t3[:], in1=t4[:], op=ALU.add), [t3b, t4b], [qnb])
                    yield
                    kslot = kvo.i
                    ko, kob = kvo.next()
                    V(lambda e: e.tensor_copy(ko[:, 0:128], qn[:, 256:384]), [qnb], [kob])
                    V(lambda e: e.tensor_copy(ko[:, 128:256], qk[:, 384:512]), [qkb], [kob])
                    qb, qbb = qbp.next()
                    A(lambda e: e.activation(out=qb[:], in_=qn[:], func=AF.Copy), [qnb], [qbb])
                    yield
                    K.dma("sync", lambda e: e.dma_start(out=nk_o[l, tc * 128:(tc + 1) * 128, :], in_=ko[:, 0:128]), reads=[kob], semkey=f"stk{kslot}")
                    K.dma("sync", lambda e: e.dma_start(out=nv_o[l, tc * 128:(tc + 1) * 128, :], in_=ko[:, 128:256]), reads=[kob], semkey=f"stk{kslot}")
                    pt, ptb = bank()
                    ptv = pt[:].bitcast(BF16)
                    for hh in range(6):
                        T(lambda e: e.transpose(ptv[0:64, hh * 128:(hh + 1) * 128], qb[:, hh * 64:(hh + 1) * 64], ident_b), [qbb, CONSTB], [ptb])
                    yield
                    V(lambda e: e.tensor_copy(qfm[:, :, tc * 128:(tc + 1) * 128], ptv[0:64, 0:512].rearrange("p (h t) -> p h t", h=4)), [ptb], [QF])
                    V(lambda e: e.tensor_copy(kfm[:, :, tc * 128:(tc + 1) * 128], ptv[0:64, 512:768].rearrange("p (h t) -> p h t", h=2)), [ptb], [KF])
                    yield
                for c0 in range(0, NCH, 2):
                    interleave([in_chunk(c0), in_chunk(c0 + 1)])
            with_h(inproj)
            K.release(mk_in)
            ptp = Rot(K, "pt", [128, 7, 2, 128], BF16, 3)
            rcp = Rot(K, "rc", [64, 2, 128], F32, 3)
            am = bcol("am").rearrange("p (c a t) -> p c a t", c=NCH, a=2)
            cfb = ccol("cfb")

            def att_chain(qc, kv):
                qs = slice(qc * 128, (qc + 1) * 128)
                pc_ = max(qc - 1, 0)
                nc_ = min(qc + 1, NCH - 1)
                qrhs = qfm[:, 2 * kv:2 * kv + 2, qs]
                pa, pab = bank()
                pb_, pbb_ = bank()
                for sc in range(4):
                    dst, dstb = (pa, pab) if sc < 2 else (pb_, pbb_)
                    T(lambda e: e.matmul(dst[:, (sc % 2) * 256:(sc % 2) * 256 + 256], lhsT=ckf[:, kv, sc * 128:(sc + 1) * 128], rhs=qrhs, start=True, stop=True),
                      [CK, QF], [dstb])
                yield
                pt_, ptb_ = ptp.next()
                A(lambda e: e.activation(out=pt_[:, 0:2, :, :], in_=pa[:, 0:512].rearrange("p (c h t) -> p c h t", c=2, h=2), func=AF.Exp,
                                         scale=0.125, bias=cfb[:, qc:qc + 1]), [pab, CONSTB], [ptb_])
                A(lambda e: e.activation(out=pt_[:, 2:4, :, :], in_=pb_[:, 0:512].rearrange("p (c h t) -> p c h t", c=2, h=2), func=AF.Exp,
                                         scale=0.125, bias=cfb[:, qc:qc + 1]), [pbb_, CONSTB], [ptb_])
                pc2, pc2b = bank()
                pd2, pd2b = bank()
                for i, kc in enumerate((pc_, nc_, qc)):
                    dst, dstb = (pc2, pc2b) if i < 2 else (pd2, pd2b)
                    T(lambda e: e.matmul(dst[:, (i % 2) * 256:(i % 2) * 256 + 256], lhsT=kfm[:, kv, kc * 128:(kc + 1) * 128], rhs=qrhs, start=True, stop=True),
                      [KF, QF], [dstb])
                yield
                A(lambda e: e.activation(out=pt_[:, 4:6, :, :], in_=pc2[:, 0:512].rearrange("p (c h t) -> p c h t", c=2, h=2), func=AF.Exp, scale=0.125),
                  [pc2b], [ptb_])
                A(lambda e: e.activation(out=pt_[:, 6, :, :], in_=pd2[:, 0:256].rearrange("p (h t) -> p h t", h=2), func=AF.Exp, scale=0.125),
                  [pd2b], [ptb_])
                yield
                V(lambda e: e.tensor_tensor(out=pt_[:, 4:6, :, :], in0=pt_[:, 4:6, :, :], in1=am[:, qc, :, :].unsqueeze(2).broadcast_to([128, 2, 2, 128]), op=ALU.mult),
                  [ptb_, CONSTB], [ptb_])
                yield
                po, pob = bank()
                vlist = [(cvs[:, sc, kv * 64:(kv + 1) * 64], CK) for sc in range(4)] + \
                        [(vtm[:, kc, kv * 64:(kv + 1) * 64], VT[kc]) for kc in (pc_, nc_, qc)]
                for i, (vap, vb) in enumerate(vlist):
                    T(lambda e: e.matmul(po[0:64, 0:256], lhsT=vap, rhs=pt_[:, i, :, :], start=(i == 0), stop=(i == 6)), [vb, ptb_], [pob])
                for i in range(7):
                    T(lambda e: e.matmul(po[0:64, 256:512], lhsT=ones_b[:, 0:64], rhs=pt_[:, i, :, :], start=(i == 0), stop=(i == 6)), [CONSTB, ptb_], [pob])
                yield
                rc, rcb = rcp.next()
                V(lambda e: e.tensor_tensor(out=rc[:], in0=po[0:64, 256:512].rearrange("p (h t) -> p h t", h=2),
                                            in1=es[:, 2 * kv:2 * kv + 2].unsqueeze(2).broadcast_to([64, 2, 128]), op=ALU.add), [pob, CK], [rcb])
                V(lambda e: e.reciprocal(rc[:], rc[:]), [rcb], [rcb])
                V(lambda e: e.tensor_tensor(out=yat[:, 2 * kv:2 * kv + 2, qs], in0=po[0:64, 0:256].rearrange("p (h t) -> p h t", h=2), in1=rc[:], op=ALU.mult),
                  [pob, rcb], [YA])
                yield
            for qc in range(NCH):
                interleave([att_chain(qc, 0), att_chain(qc, 1)])
            out_proj([yat[:, hh, :] for hh in range(4)], [YA], 768, 64, tile=yat)
            K.barrier()
            K.release(mk)

        def hyena(yhy, YHB):
            mk = K.mark()
            vb_ = K.sb("hv", [128, 2, NT], BF16)
            x1f = K.sb("hx1", [128, 2, NT], BF16)
            x2f = K.sb("hx2", [128, 2, NT], BF16)
            HVB, HX1, HX2 = Buf("hv"), Buf("hx1"), Buf("hx2")

            def inproj(hbuf, HB):
                wa0, wab0 = load_wa(1032, 512)
                wa1, wab1 = load_wa(1544, 256)
                dests = [(vb_, HVB), (vb_, HVB), (x1f, HX1), (x1f, HX1), (x2f, HX2), (x2f, HX2)]
                for q in range(6):
                    raw, rawb = cvx["raws"].next()
                    for tt in range(3):
                        pb, pbb = bank()
                        if q < 4:
                            proj_fm(pb, pbb, wa0, wab0, q * 128, 128, hbuf, HB, tt)
                        else:
                            proj_fm(pb, pbb, wa1, wab1, (q - 4) * 128, 128, hbuf, HB, tt)
                        raw_fill(raw, rawb, 128, pb, pbb, tt)
                    pc0 = PT["hy_conv"][0] + (l * 6 + q) * 4
                    dt_, db_ = dests[q]
                    conv_chunk(raw, rawb, 128, pc0, dt_[:, q % 2, :], db_, False)
            with_h(inproj, conv=True)
            FEB = Buf("feats")
            w3s = K.sb("hw3", [64, 1024])
            K.dma("sync", lambda e: e.dma_start(out=w3s[:], in_=hy_w3[l]), writes=[FEB], semkey="ldf")
            h2 = K.sb("hh2", [64, NT])
            H2B = Buf("h2")
            mk2 = K.mark()
            feats = K.sb("feats", [33, NT])
            K.dma("sync", lambda e: e.dma_start(out=feats[:, 0:1024], in_=featsA_d), writes=[FEB], semkey="ldf")
            K.dma("sync", lambda e: e.dma_start(out=feats[:, 1024:1280], in_=featsB_d), writes=[FEB], semkey="ldf")
            w1s = K.sb("hw1", [33, 64])
            w2s = K.sb("hw2", [64, 64])
            K.dma("sync", lambda e: e.dma_start(out=w1s[:], in_=hy_w1[l]), writes=[FEB], semkey="ldf")
            K.dma("sync", lambda e: e.dma_start(out=w2s[:], in_=hy_w2[l]), writes=[FEB], semkey="ldf")
            hp = pcol("hyp", l * 3, 3, rows=64)
            fb = K.sb("fb", [64, 2])
            V(lambda e: e.tensor_tensor(out=fb[:, 0:1], in0=hp[:, 0:1], in1=hp[:, 1:2], op=ALU.mult), [CONSTB], [FEB])
            V(lambda e: e.tensor_tensor(out=fb[:, 1:2], in0=hp[:, 2:3], in1=hp[:, 1:2], op=ALU.mult), [CONSTB], [FEB])
            h1 = K.sb("hh1", [64, NT])
            H1B = Buf("h1")
            MAGIC = 12582912.0
            argp = Rot(K, "arg", [64, 512], F32, 2)
            nrp = Rot(K, "nr", [64, 512], F32, 2)

            def sin_layer(lhsT, src, srcb, KK, dst, dstb, fbcol):
                for tt in range(3):
                    t0, w, cnd = TT[tt]
                    pb, pbb = bank()
                    T(lambda e, pb=pb, t0=t0, w=w: e.matmul(pb[0:64, 0:w], lhsT=lhsT, rhs=src[0:KK, t0:t0 + w], start=True, stop=True), [FEB, srcb], [pbb])
                    ar, arb = argp.next()
                    nr, nrb = nrp.next()
                    V(lambda e, ar=ar, pb=pb, w=w: e.tensor_scalar(out=ar[:, 0:w], in0=pb[0:64, 0:w], scalar1=hp[:, 1:2], scalar2=fb[:, fbcol:fbcol + 1],
                                                                   op0=ALU.mult, op1=ALU.add), [pbb, CONSTB, FEB], [arb])
                    V(lambda e, ar=ar, nr=nr, w=w: e.tensor_scalar(out=nr[:, 0:w], in0=ar[:, 0:w], scalar1=float(1 / (2 * math.pi)), scalar2=MAGIC,
                                                                   op0=ALU.mult, op1=ALU.add), [arb], [nrb])
                    V(lambda e, nr=nr, w=w: e.tensor_scalar(out=nr[:, 0:w], in0=nr[:, 0:w], scalar1=MAGIC, scalar2=None, op0=ALU.subtract), [nrb], [nrb])
                    V(lambda e, ar=ar, nr=nr, w=w: e.scalar_tensor_tensor(out=ar[:, 0:w], in0=nr[:, 0:w], scalar=float(-2 * math.pi), op0=ALU.mult,
                                                                          in1=ar[:, 0:w], op1=ALU.add), [arb, nrb], [arb])
                    V(lambda e, ar=ar, w=w: e.tensor_scalar(out=ar[:, 0:w], in0=ar[:, 0:w], scalar1=3.1415925, scalar2=-3.1415925, op0=ALU.min, op1=ALU.max), [arb], [arb])
                    A(lambda e, ar=ar, t0=t0, w=w: e.activation(out=dst[:, t0:t0 + w], in_=ar[:, 0:w], func=AF.Sin), [arb], [dstb])
            sin_layer(w1s[:], feats, FEB, 33, h1, H1B, 0)
            sin_layer(w2s[:], h1, H1B, 64, h2, H2B, 1)
            K.barrier()
            K.release(mk2)

            gA = K.sb("gA", [128, 2, 2, 7, 256], BF16)
            gB = K.sb("gB", [128, 2, 2, 256], BF16)
            GB_ = Buf("g")
            hfa = K.sb("hfa", [128, 10, 2, 256], BF16)
            HFB = Buf("hf")
            ztm = K.sb("ztm", [128, NCH, 128], BF16)
            ZTB = bufs(NCH, "ztm")
            Yb = K.sb("Yb", [128, 2, 2, 5, 128], BF16)
            YBB = Buf("Yb")
            ytp = Rot(K, "yt", [128, 4, 128], F32, 2)
            tqp = Rot(K, "tqr", [128, 4, 128], F32, 4)
            identr = K.sb("identr", [128, 2, 128])
            IDR = Buf("identr")
            V(lambda e: e.tensor_copy(identr[:, 0, :].bitcast(F32R), ident_b), [CONSTB], [IDR])
            V(lambda e: e.tensor_scalar(out=identr[:, 1, :].bitcast(F32R), in0=ident_b, scalar1=-1.0, scalar2=None, op0=ALU.mult), [CONSTB], [IDR])
            dftb = bcol("dft").rearrange("p (t r f) -> p t r f", t=8, r=2)
            idft = bcol("idft").rearrange("p (a r t) -> p a r t", a=2, r=2)

            decs = K.sb("decs", [128, 20, 128])
            DCB = Buf("decs")
            for o in range(2):
                zin, zinb = (vb_, HVB) if o == 0 else (x1f, HX1)
                gate, gateb = (x1f, HX1) if o == 0 else (x2f, HX2)
                zout, zoutb = (x1f, HX1) if o == 0 else (yhy, YHB)
                for cc in range(2):
                    K.dma("sync", lambda e, cc=cc: e.dma_start(out=decs[:], in_=dec_d[:, :, cc * 128:(cc + 1) * 128]), writes=[DCB], semkey="lddec")
                    for pk in range(10):
                        pb, pbb = bank()
                        pos0 = pk * 128
                        for sd in range(2):
                            wc0 = sd * 512 + o * 256 + cc * 128
                            T(lambda e, pb=pb, sd=sd, wc0=wc0, pos0=pos0: e.matmul(pb[:, sd * 128:(sd + 1) * 128], lhsT=h2[:, pos0:pos0 + 128],
                                                                                   rhs=w3s[:, wc0:wc0 + 128], start=True, stop=True), [H2B, FEB], [pbb])
                        if pk < 8:
                            di = [pk, 8 + pk]
                        else:
                            di = [16 + pk - 8, 18 + pk - 8]
                        for sd in range(2):
                            V(lambda e, pb=pb, sd=sd, pk=pk, di=di, cc=cc: e.tensor_tensor(out=hfa[:, pk, sd, cc * 128:(cc + 1) * 128], in0=pb[:, sd * 128:(sd + 1) * 128],
                                                                                          in1=decs[:, di[sd], :], op=ALU.mult), [pbb, DCB], [HFB])
                for delta in range(-3, 4):
                    ents = spectrum_entries(delta, 8)
                    for fch in range(2):
                        pb, pbb = bank()
                        for r in range(2):
                            for i, (src, k, ty) in enumerate(ents):
                                T(lambda e, pb=pb, r=r, i=i, src=src, k=k, ty=ty, fch=fch: e.matmul(
                                    pb[:, r * 256:(r + 1) * 256], lhsT=dftb[:, ty, r, fch * 128:(fch + 1) * 128], rhs=hfa[:, k, src, :],
                                    start=(i == 0), stop=(i == len(ents) - 1)), [CONSTB, HFB], [pbb])
                        if delta == 0:
                            A(lambda e, pb=pb, fch=fch, delta=delta: e.activation(out=gA[:, fch, :, delta + 3, :], in_=pb[:, 0:512].rearrange("p (r c) -> p r c", r=2),
                                                                                  func=AF.Copy), [pbb], [GB_])
                        else:
                            A(lambda e, pb=pb, fch=fch, delta=delta: e.activation(out=gA[:, fch, :, delta + 3, :], in_=pb[:, 0:512].rearrange("p (r c) -> p r c", r=2),
                                                                                  func=AF.Copy, scale=flag), [pbb, CONSTB], [GB_])
                entsB = spectrum_entries(0, 2)
                for fch in range(2):
                    pb, pbb = bank()
                    for r in range(2):
                        for i, (src, k, ty) in enumerate(entsB):
                            T(lambda e, pb=pb, r=r, i=i, src=src, k=k, ty=ty, fch=fch: e.matmul(
                                pb[:, r * 256:(r + 1) * 256], lhsT=dftb[:, ty, r, fch * 128:(fch + 1) * 128], rhs=hfa[:, 8 + k, src, :],
                                start=(i == 0), stop=(i == len(entsB) - 1)), [CONSTB, HFB], [pbb])
                    A(lambda e, pb=pb, fch=fch: e.activation(out=gB[:, fch, :, :], in_=pb[:, 0:512].rearrange("p (r c) -> p r c", r=2), func=AF.Copy), [pbb], [GB_])
                for cc in range(2):
                    gcs = slice(cc * 128, (cc + 1) * 128)
                    for tc in range(NCH):
                        pb, pbb = bank()
                        pbv = pb[:].bitcast(BF16)
                        T(lambda e, pbv=pbv, tc=tc: e.transpose(pbv[:, 0:128], zin[:, cc, tc * 128:(tc + 1) * 128], ident_b), [zinb, CONSTB], [pbb])
                        A(lambda e, pbv=pbv, tc=tc: e.activation(out=ztm[:, tc, :], in_=pbv[:, 0:128], func=AF.Copy), [pbb], [ZTB[tc]])
                    ty_f = [dft_type(0, 0), dft_type(0, 128)]
                    set_banks(6, 8)
                    for fch in range(2):
                        for r in range(2):
                            for blk in range(NB):
                                if blk < 4:
                                    dst, dstb = PS[r][:, blk * 128:(blk + 1) * 128], PSB[r]
                                else:
                                    dst, dstb = PS[2][:, r * 128:(r + 1) * 128], PSB[2]
                                for tk in range(2):
                                    T(lambda e: e.matmul(dst, lhsT=dftb[:, ty_f[tk], r, fch * 128:(fch + 1) * 128], rhs=ztm[:, 2 * blk + tk, :],
                                                         start=(tk == 0), stop=(tk == 1)), [CONSTB, ZTB[2 * blk + tk]], [dstb])
                        terms = []
                        for delta in [0, 1, -1, 2, -2, 3, -3]:
                            nbk = 4 - abs(delta)
                            tb0 = max(delta, 0)
                            sb0 = tb0 - delta
                            for (acc, gr, zr, sgn) in ((3, 0, 0, 0), (3, 1, 1, 1), (4, 0, 1, 0), (4, 1, 0, 0)):
                                g_ap = gA[:, fch, gr, delta + 3, gcs].unsqueeze(1).broadcast_to([128, nbk, 128])
                                z_ap = PS[zr][:, sb0 * 128:(sb0 + nbk) * 128].rearrange("p (b c) -> p b c", b=nbk)
                                terms.append((acc, tb0 * 128, nbk * 128, sgn, z_ap, PSB[zr], g_ap, nbk))
                        cnt = {3: 0, 4: 0}
                        tot = {3: 14, 4: 14}
                        for (acc, c0, ncol, sgn, z_ap, zb, g_ap, nbk) in terms:
                            tq, tqb = tqp.next()
                            V(lambda e: e.tensor_tensor(out=tq[:, 0:nbk, :].bitcast(F32R), in0=z_ap, in1=g_ap, op=ALU.mult), [zb, GB_], [tqb])
                            T(lambda e: e.matmul(PS[acc][:, c0:c0 + ncol], lhsT=identr[:, sgn, :].bitcast(F32R),
                                                 rhs=tq[:, 0:nbk, :].rearrange("p b c -> p (b c)").bitcast(F32R),
                                                 start=(cnt[acc] == 0), stop=(cnt[acc] == tot[acc] - 1)), [tqb, IDR], [PSB[acc]])
                            cnt[acc] += 1
                        tqs = []
                        for (gr, zr, sgn) in ((0, 0, 0), (1, 1, 1), (0, 1, 0), (1, 0, 0)):
                            tq, tqb = tqp.next()
                            V(lambda e: e.tensor_tensor(out=tq[:, 0, :].bitcast(F32R), in0=PS[2][:, zr * 128:(zr + 1) * 128], in1=gB[:, fch, gr, gcs], op=ALU.mult),
                              [PSB[2], GB_], [tqb])
                            tqs.append((tq, tqb, sgn))
                        for i, (tq, tqb, sgn) in enumerate(tqs):
                            ro = i // 2
                            T(lambda e: e.matmul(PS[5][:, ro * 128:(ro + 1) * 128], lhsT=identr[:, sgn, :].bitcast(F32R), rhs=tq[:, 0, :].bitcast(F32R),
                                                 start=(i % 2 == 0), stop=(i % 2 == 1)), [tqb, IDR], [PSB[5]])
                        for r in range(2):
                            A(lambda e: e.activation(out=Yb[:, fch, r, 0:4, :], in_=PS[3 + r][:, 0:512].rearrange("p (b c) -> p b c", b=4), func=AF.Copy),
                              [PSB[3 + r]], [YBB])
                        A(lambda e: e.activation(out=Yb[:, fch, :, 4, :], in_=PS[5][:, 0:256].rearrange("p (r c) -> p r c", r=2), func=AF.Copy), [PSB[5]], [YBB])
                    set_banks(0, 8)
                    hbcol = pcol("hy_bias", (l * 2 + o) * 2 + cc, 1)
                    for blk in range(NB):
                        pb, pbb = bank()
                        i = 0
                        for fch in range(2):
                            for r in range(2):
                                T(lambda e, pb=pb, fch=fch, r=r, blk=blk, i=i: e.matmul(pb[:, 0:256], lhsT=Yb[:, fch, r, blk, :], rhs=idft[:, fch, r, :],
                                                                                       start=(i == 0), stop=(i == 3)), [YBB, CONSTB], [pbb])
                                i += 1
                        ts = slice(blk * 256, (blk + 1) * 256)
                        tq, tqb = ytp.next()
                        tqv = tq[:].rearrange("p a b -> p (a b)")[:, 0:256]
                        V(lambda e, tqv=tqv, pb=pb, ts=ts: e.scalar_tensor_tensor(out=tqv, in0=zin[:, cc, ts], scalar=hbcol, op0=ALU.mult, in1=pb[:, 0:256], op1=ALU.add),
                          [zinb, CONSTB, pbb], [tqb])
                        V(lambda e, tqv=tqv, ts=ts: e.tensor_tensor(out=zout[:, cc, ts], in0=tqv, in1=gate[:, cc, ts], op=ALU.mult), [tqb, gateb], [zoutb])
            out_proj([yhy[:, 0, :], yhy[:, 1, :]], [YHB], 256, 128)
            K.barrier()
            K.release(mk)

        yhy = K.sb("yhy", [128, 2, NT], BF16)
        YHB = Buf("yhy")
        hyena(yhy, YHB)
        yssd = K.sb("yssd", [64, 4, NT], BF16)
        YSB = Buf("yssd")
        ssd(yssd, YSB)
        yret = K.sb("yret", [64, 4, NT], BF16)
        YRB = Buf("yret")
        ret(yret, YRB)
        yatt = K.sb("yatt", [64, 4, NT], BF16)
        YAB_ = Buf("yatt")
        att(yatt, YAB_)
        out_proj_all()
        K.barrier()
        K.release(mk_all)

    for l in range(nlayers):
        if l == 1:
            assert not pending_mod and not inflight_mod, (pending_mod, inflight_mod)
        ffn(l, 0)
        mixer(l)
        ffn(l, 1)

    K.dma("sync", lambda e: e.dma_start(out=yT, in_=x[:]), reads=[b for row in XB for b in row], semkey="sty")
    K.barrier()
    K.release(0)
    return nc, K


def _consts(is_s):
    f32 = np.float32
    cfv = np.zeros((128, CF.n), f32)

    def put(name, arr):
        c0, w = CF[name]
        cfv[:, c0:c0 + w] = np.asarray(arr, f32).reshape(128, w) if np.asarray(arr).ndim > 1 or w == 1 else np.broadcast_to(np.asarray(arr, f32), (128, w))
    j = np.arange(128)[:, None]
    i = np.arange(128)[None, :]
    put("trif", (i >= j).astype(f32))
    put("trib", (j >= i).astype(f32))
    put("ones", np.ones((128, 128), f32))
    put("relu_f", np.maximum(i - j, 0).astype(f32))
    put("relu_b", np.maximum(j - i, 0).astype(f32))
    put("ip1", np.broadcast_to((i + 1).astype(f32), (128, 128)))
    put("rmi", np.broadcast_to((128 - i).astype(f32), (128, 128)))
    put("tailf", (127 - j).astype(f32))
    put("tailb", j.astype(f32))
    nf = 16
    inv = (10000.0 ** (-np.arange(nf, dtype=f32) / nf)).astype(f32)
    cos = np.ones((NT, 32), f32)
    sin = np.zeros((NT, 32), f32)
    if is_s:
        t = np.arange(1024)
        rows = (t // 64).astype(f32)
        cols = (t % 64).astype(f32)
        angr = rows[:, None] * inv[None, :]
        angc = cols[:, None] * inv[None, :]
        cos[:1024, 0:16] = np.cos(angr)
        cos[:1024, 16:32] = np.cos(angc)
        sin[:1024, 0:16] = np.sin(angr)
        sin[:1024, 16:32] = np.sin(angc)
    put("cos", cos.reshape(NCH, 128, 32).transpose(1, 0, 2).reshape(128, 320))
    put("sin", sin.reshape(NCH, 128, 32).transpose(1, 0, 2).reshape(128, 320))
    cfb = np.full((NCH,), -30000.0, f32)
    hl = np.zeros((5,), f32)
    hr = np.zeros((5,), f32)
    if is_s:
        cfb[0:8] = 0.0
        hl[1:4] = 1.0
        hr[0:3] = 1.0
    put("cfb", cfb)
    put("hl", hl)
    put("hr", hr)
    put("flag", np.full((128, 1), 1.0 if is_s else 0.0, f32))
    put("negpi", np.full((128, 1), -math.pi, f32))
    put("nbf", ((i >= j).astype(f32) - 1.0) * 30000.0)
    put("nbb", ((j >= i).astype(f32) - 1.0) * 30000.0)

    cbv = np.zeros((128, CB.n), f32)

    def putb(name, arr):
        c0, w = CB[name]
        cbv[:, c0:c0 + w] = np.asarray(arr, f32).reshape(128, w)
    putb("ident", np.eye(128, dtype=f32))
    putb("ones", np.ones((128, 128), f32))
    am = np.zeros((128, NCH, 2, 128), f32)
    band_prev = (j >= i).astype(f32)
    band_next = (j <= i).astype(f32)
    for qc in range(NCH):
        if is_s and qc < 8:
            if qc >= 1:
                am[:, qc, 0, :] = band_prev
            if qc <= 6:
                am[:, qc, 1, :] = band_next
        else:
            if qc % 2 == 1:
                am[:, qc, 0, :] = 1.0
            else:
                am[:, qc, 1, :] = 1.0
    putb("am", am)
    om = 2 * np.pi * (np.arange(256) + 0.5) / 512.0
    row = np.arange(128)
    dft = np.zeros((128, 8, 2, 256), np.float64)
    for ty in range(8):
        if ty < 4:
            e = FWD_E0[ty] + row
        else:
            e = BWD_E0[ty - 4] - row
        valid = (np.abs(e) <= 255).astype(np.float64)
        ang = e[:, None] * om[None, :]
        dft[:, ty, 0, :] = np.cos(ang) * valid[:, None]
        dft[:, ty, 1, :] = -np.sin(ang) * valid[:, None]
    putb("dft", dft)
    tt = np.arange(256)
    idft = np.zeros((128, 2, 2, 256), np.float64)
    for fch in range(2):
        omf = om[fch * 128:(fch + 1) * 128]
        ang = omf[:, None] * tt[None, :]
        idft[:, fch, 0, :] = (2.0 / 512) * np.cos(ang)
        idft[:, fch, 1, :] = -(2.0 / 512) * np.sin(ang)
    putb("idft", idft)
    return cfv, cbv


def _hy_consts(LA):
    f32 = np.float32
    l = LA
    pos = np.arange(l, dtype=f32)
    t = pos / f32(l - 1)
    bands = np.linspace(1e-4, 15, 16, dtype=f32)
    ang = (f32(2.0 * math.pi / l)) * pos[:, None] * bands[None, :]
    feats = np.concatenate([t[:, None], np.cos(ang), -np.sin(ang)], axis=-1).astype(f32)
    max_decay = math.log(1e-2) / 0.3
    min_decay = math.log(1e-2) / 1.5
    deltas = np.abs(np.linspace(min_decay, max_decay, 256, dtype=f32))
    dec = np.exp(-t[:, None] * deltas[None, :]).astype(f32)
    return feats, dec


def _prepare(inp):
    f32 = np.float32
    g = lambda k: np.asarray(inp[k], dtype=f32)
    x_prompt, x_sample = g("x_prompt"), g("x_sample")
    cache_k, cache_v = g("cache_k"), g("cache_v")
    state_ssd, state_ret = g("state_ssd"), g("state_ret")
    c, c_ctx = g("c"), g("c_ctx")
    pt = np.zeros((128, PT.n), f32)

    def putp(name, off, arr):
        arr = np.asarray(arr, f32)
        c0 = PT[name][0] + off
        pt[0:arr.shape[0], c0:c0 + arr.shape[1]] = arr
    nw = g("norm_w")
    for l in range(2):
        for j in range(3):
            putp("norm_w", (l * 3 + j) * 8, nw[l, j].reshape(8, 128).T)
        putp("b_mod", l * 72, g("b_mod")[l].reshape(72, 128).T)
        cw, cbias = g("ssd_conv_w")[l], g("ssd_conv_b")[l]
        for q in range(8):
            if q < 4:
                f0, fs = q * 64, 64
            else:
                f0, fs = 256 + (q - 4) * 128, 128
            arr = np.stack([cw[0, f0:f0 + fs], cw[1, f0:f0 + fs], cw[2, f0:f0 + fs], cbias[f0:f0 + fs]], axis=1)
            putp("ssd_conv", (l * 8 + q) * 4, arr)
        hw, hb = g("hy_conv_w")[l], g("hy_conv_b")[l]
        for q in range(6):
            f0 = q * 128
            arr = np.stack([hw[0, f0:f0 + 128], hw[1, f0:f0 + 128], hw[2, f0:f0 + 128], hb[f0:f0 + 128]], axis=1)
            putp("hy_conv", (l * 6 + q) * 4, arr)
        hbias = g("hy_bias")[l]
        for o in range(2):
            putp("hy_bias", (l * 2 + o) * 2, hbias[o].reshape(2, 128).T)
        putp("ssd_nw", l * 4, g("ssd_norm_w")[l].reshape(4, 64).T)
        putp("ssd_d", l * 4, np.broadcast_to(g("ssd_d")[l][None, :], (64, 4)))
        putp("ret_gn", l * 4, g("ret_gn_w")[l].reshape(4, 64).T)
        putp("hyp", l * 3, np.stack([g("hy_b1")[l], g("hy_freq")[l], g("hy_b2")[l]], axis=1))
    ft = np.zeros((128, FT.n), f32)

    def putf(name, off, vec):
        vec = np.asarray(vec, f32).reshape(-1)
        c0 = FT[name][0] + off
        ft[:, c0:c0 + vec.size] = vec[None, :]
    for l in range(2):
        putf("dt_bias", l * 80, np.tile(g("ssd_dt_bias")[l].reshape(8), NCH))
        putf("a_log", l * 80, np.tile(g("ssd_a_log")[l].reshape(8), NCH))
        putf("ret_logit", l * 8, g("ret_decay_logit")[l].reshape(8))
        putf("qkw", l * 384, np.concatenate([np.tile(g("attn_q_norm")[l], 4), np.tile(g("attn_k_norm")[l], 2)]))
        putf("sink", l * 4, g("attn_sink")[l])
    featsB, decB = _hy_consts(256)
    featsA_s, decA_s = _hy_consts(1024)
    wm = g("w_mod").reshape(2, 8, 128, 18, 512).transpose(0, 3, 2, 1, 4).reshape(2, 18, 128, 4096)
    wi = g("ffn_w_in").reshape(2, 2, 8, 128, 2, 11, 256)
    wi = wi.transpose(0, 1, 5, 3, 2, 4, 6).reshape(2, 2, 11, 128, 4096)
    wo = g("ffn_w_out").reshape(2, 2, 11, 2, 128, 1024).transpose(0, 1, 2, 4, 3, 5).reshape(2, 2, 11, 128, 2048)
    shared = dict(ptab=pt, ftab=ft, w_mod=np.ascontiguousarray(wm), ffn_w_in=np.ascontiguousarray(wi), ffn_w_out=np.ascontiguousarray(wo), mix_w_in=g("mix_w_in"),
                  mix_w_out=g("mix_w_out"), hy_w1=g("hy_w1"), hy_w2=g("hy_w2"), hy_w3=g("hy_w3"),
                  featsB=np.ascontiguousarray(featsB.T))
    consts = {True: _consts(True), False: _consts(False)}
    in_maps = []
    plan = []
    for cid in range(NCORE):
        is_s = cid < 2
        if is_s:
            xs = np.concatenate([x_sample[cid], x_prompt[30 + cid]], axis=0)
            seqs = [30 + cid]
            condA = c[cid]
        else:
            seqs = list(range(5 * (cid - 2), 5 * (cid - 2) + 5))
            xs = x_prompt[seqs].reshape(NT, D)
            condA = c_ctx
        plan.append((is_s, seqs))
        xTm = np.ascontiguousarray(xs.T.reshape(8, 128, NT).transpose(1, 0, 2))
        cond = np.stack([condA, c_ctx], axis=-1)
        condTm = np.ascontiguousarray(cond.reshape(8, 128, 2).transpose(1, 0, 2))
        s0s = np.zeros((2, 5, 2, 4, 128, 64), f32)
        s0r = np.zeros((2, 5, 2, 4, 64, 64), f32)
        ck = np.zeros((2, 2, 64, 512), f32)
        cv = np.zeros((2, 512, 128), f32)
        if is_s:
            for l in range(2):
                s0s[l, 0, 0] = state_ssd[cid, l, 0]
                s0s[l, 3, 1] = state_ssd[cid, l, 1]
                s0r[l, 0, 0] = state_ret[cid, l, 0]
                s0r[l, 3, 1] = state_ret[cid, l, 1]
                ck[l] = cache_k[cid, l].transpose(1, 2, 0)
                cv[l] = cache_v[cid, l].reshape(512, 128)
            featsA = featsA_s
            decFA = decA_s.copy()
        else:
            featsA = np.zeros((1024, 33), f32)
            featsA[:256] = featsB
            decFA = np.zeros((1024, 256), f32)
            decFA[:256] = decB
        decBA = decFA.copy()
        decBA[0] = 0.0
        decFB = decB.copy()
        decBB = decB.copy()
        decBB[0] = 0.0
        dec = np.concatenate([decFA.reshape(8, 128, 256), decBA.reshape(8, 128, 256), decFB.reshape(2, 128, 256), decBB.reshape(2, 128, 256)], axis=0)
        cfv, cbv = consts[is_s]
        m = dict(shared)
        m.update(xT=xTm, condT=condTm,
                 s0ssd=np.ascontiguousarray(s0s.reshape(80, 128, 64).transpose(1, 0, 2)),
                 s0ret=np.ascontiguousarray(s0r.reshape(80, 64, 64).transpose(1, 0, 2)),
                 ckT=np.ascontiguousarray(ck.reshape(4, 64, 512).transpose(1, 0, 2)),
                 cvt=np.ascontiguousarray(cv.reshape(8, 128, 128).transpose(1, 0, 2)),
                 cf32=cfv, cbf=cbv, featsA=np.ascontiguousarray(featsA.T), dec=np.ascontiguousarray(dec.transpose(1, 0, 2)))
        in_maps.append(m)
    return in_maps, plan


_CACHE = {}


def kernel(**inputs):
    in_maps, plan = _prepare(inputs)
    if "nc" not in _CACHE:
        _CACHE["nc"] = build_program()[0]
    nc = _CACHE["nc"]
    res = run_bass_kernel_spmd(nc, in_maps, core_ids=list(range(NCORE)))
    f32 = np.float32
    y_prompt = np.zeros((32, 256, D), f32)
    y_sample = np.zeros((2, 1024, D), f32)
    nck = np.zeros((32, 2, 256, 2, 64), f32)
    ncv = np.zeros((32, 2, 256, 2, 64), f32)
    nssd = np.zeros((32, 2, 2, 4, 128, 64), f32)
    nret = np.zeros((32, 2, 2, 4, 64, 64), f32)
    for cid, (is_s, seqs) in enumerate(plan):
        r = res.results[cid]
        y = np.asarray(r["yT"]).transpose(1, 0, 2).reshape(D, NT).T
        nk = np.asarray(r["nk"])
        nv = np.asarray(r["nv"])
        ss = np.asarray(r["nssd"]).transpose(1, 0, 2).reshape(2, 5, 2, 4, 128, 64)
        sr = np.asarray(r["nret"]).transpose(1, 0, 2).reshape(2, 5, 2, 4, 64, 64)
        if is_s:
            y_sample[cid] = y[:1024]
            blks = [(4, seqs[0])]
        else:
            blks = list(enumerate(seqs))
        for blk, b in blks:
            ts = slice(blk * 256, (blk + 1) * 256)
            y_prompt[b] = y[ts]
            for l in range(2):
                nck[b, l] = nk[l, ts].reshape(256, 2, 64)
                ncv[b, l] = nv[l, ts].reshape(256, 2, 64)
                nssd[b, l] = ss[l, blk]
                nret[b, l] = sr[l, blk]
    return (y_prompt, y_sample, nck, ncv, nssd, nret)
```

```python
import math
import numpy as np
import concourse.bass as bass
import concourse.mybir as mybir
from concourse.bass_utils import run_bass_kernel_spmd

F32 = mybir.dt.float32
BF16 = mybir.dt.bfloat16
F32R = mybir.dt.float32r
AF = mybir.ActivationFunctionType
ALU = mybir.AluOpType
AX = mybir.AxisListType

NCORE = 8
NT = 1280
NB = 5
NCH = 10
TT = [(0, 512, 0), (512, 512, 0), (1024, 256, 1)]
D = 1024
DFF = 2816
EPS = 1e-6
ENGS = ["tensor", "vector", "scalar", "gpsimd", "sync"]
SAME_ENGINE_NOSYNC = ("tensor",)


class Buf:
    __slots__ = ("name", "w", "r", "excl")

    def __init__(self, name="", excl=False):
        self.name = name
        self.w = None
        self.r = {}
        self.excl = excl


def bufs(n, name=""):
    return [Buf(name + str(i)) for i in range(n)]


class KB:
    def __init__(self, nc):
        self.nc = nc
        self.cnt = {e: 0 for e in ENGS}
        self.seen = {e: {} for e in ENGS}
        self.sems = {}
        self.dcount = {}
        self._stack = []
        self.n_inst = 0
        self.uid = 0

    def enter(self, cm):
        v = cm.__enter__()
        self._stack.append(cm)
        return v

    def mark(self):
        return len(self._stack)

    def release(self, m):
        while len(self._stack) > m:
            self._stack.pop().__exit__(None, None, None)

    def sem(self, key):
        if key not in self.sems:
            self.sems[key] = self.enter(self.nc.semaphore("s_" + key))
        return self.sems[key]

    def sb(self, name, shape, dt=F32):
        self.uid += 1
        return self.enter(self.nc.sbuf_tensor(f"{name}_{self.uid}", list(shape), dt))

    def ps(self, name, shape, dt=F32):
        return self.enter(self.nc.psum_tensor(name, list(shape), dt))

    @staticmethod
    def _flat(bl):
        out = []
        for b in bl:
            if isinstance(b, (list, tuple)):
                out.extend(KB._flat(b))
            else:
                out.append(b)
        return out

    def _need(self, eng, reads, writes):
        need = {}

        def add(k, v):
            if need.get(k, 0) < v:
                need[k] = v
        for b in reads:
            if b.w is not None:
                add(*b.w)
        for b in writes:
            if b.w is not None:
                add(*b.w)
            for k, v in b.r.items():
                add(k, v)
        out = []
        for k, v in need.items():
            if k == "p_" + eng and eng in SAME_ENGINE_NOSYNC:
                continue
            if self.seen[eng].get(k, 0) >= v:
                continue
            self.seen[eng][k] = v
            out.append((k, v))
        return out

    def _emit(self, eng, waits, fn, key, inc):
        e = getattr(self.nc, eng)
        for k, v in waits:
            e.wait_ge(self.sems[k], v)
        if fn is not None:
            fn(e).then_inc(self.sems[key], inc)

    def op(self, eng, fn, reads=(), writes=()):
        reads = self._flat(reads)
        writes = self._flat(writes)
        ex = [b for b in reads if b.excl]
        if ex:
            reads = [b for b in reads if not b.excl]
            writes = writes + [b for b in ex if b not in writes]
        waits = self._need(eng, reads, writes)
        key = "p_" + eng
        self.sem(key)
        self.cnt[eng] += 1
        val = self.cnt[eng]
        for b in reads:
            if b.r.get(key, 0) < val:
                b.r[key] = val
        for b in writes:
            b.w = (key, val)
            b.r = {}
        self._emit(eng, waits, fn, key, 1)
        self.n_inst += 1

    def dma(self, eng, fn, reads=(), writes=(), semkey=None):
        reads = self._flat(reads)
        writes = self._flat(writes)
        waits = self._need(eng, reads, writes)
        self.sem(semkey)
        self.dcount[semkey] = self.dcount.get(semkey, 0) + 16
        val = self.dcount[semkey]
        for b in reads:
            if b.r.get(semkey, 0) < val:
                b.r[semkey] = val
        for b in writes:
            b.w = (semkey, val)
            b.r = {}
        self._emit(eng, waits, fn, semkey, 16)
        self.n_inst += 1

    def barrier(self):
        tot = [("p_" + e, self.cnt[e]) for e in ENGS if self.cnt[e] > 0]
        tot += list(self.dcount.items())
        for eng in ENGS:
            waits = []
            for k, v in tot:
                if k == "p_" + eng and eng == "tensor":
                    continue
                if self.seen[eng].get(k, 0) >= v:
                    continue
                self.seen[eng][k] = v
                waits.append((k, v))
            self._emit(eng, waits, None, None, 0)

    def V(self, fn, r=(), w=()):
        self.op("vector", fn, r, w)

    def A(self, fn, r=(), w=()):
        self.op("scalar", fn, r, w)

    def T(self, fn, r=(), w=()):
        self.op("tensor", fn, r, w)

    def G(self, fn, r=(), w=()):
        self.op("gpsimd", fn, r, w)


class Rot:
    def __init__(self, K, name, shape, dt, n):
        self.t = [K.sb(f"{name}{i}", shape, dt) for i in range(n)]
        self.b = bufs(n, name)
        self.i = 0

    def next(self):
        i = self.i
        self.i = (i + 1) % len(self.t)
        return self.t[i], self.b[i]


class Cols:
    def __init__(self):
        self.m = {}
        self.n = 0

    def add(self, name, w):
        self.m[name] = (self.n, w)
        self.n += w

    def __getitem__(self, name):
        return self.m[name]


def ptab_cols():
    c = Cols()
    c.add("norm_w", 48)
    c.add("b_mod", 144)
    c.add("ssd_conv", 64)
    c.add("hy_conv", 48)
    c.add("hy_bias", 8)
    c.add("ssd_nw", 8)
    c.add("ssd_d", 8)
    c.add("ret_gn", 8)
    c.add("hyp", 6)
    c.add("ssd_convp", 16)
    return c


def ftab_cols():
    c = Cols()
    c.add("dt_bias", 160)
    c.add("a_log", 160)
    c.add("ret_logit", 16)
    c.add("qkw", 768)
    c.add("sink", 8)
    return c


def cf32_cols():
    c = Cols()
    c.add("trif", 128)
    c.add("trib", 128)
    c.add("ones", 128)
    c.add("relu_f", 128)
    c.add("relu_b", 128)
    c.add("ip1", 128)
    c.add("rmi", 128)
    c.add("tailf", 1)
    c.add("tailb", 1)
    c.add("cos", 320)
    c.add("sin", 320)
    c.add("cfb", 10)
    c.add("hl", 5)
    c.add("hr", 5)
    c.add("flag", 1)
    c.add("negpi", 1)
    c.add("nbf", 128)
    c.add("nbb", 128)
    return c


def cbf_cols():
    c = Cols()
    c.add("ident", 128)
    c.add("ones", 128)
    c.add("am", 2560)
    c.add("dft", 4096)
    c.add("idft", 1024)
    return c


PT = ptab_cols()
FT = ftab_cols()
CF = cf32_cols()
CB = cbf_cols()

FWD_E0 = [-256, -128, 0, 128]
BWD_E0 = [256, 128, 0, -128]


def dft_type(src, e0):
    return (FWD_E0.index(e0) if src == 0 else 4 + BWD_E0.index(e0))


def spectrum_entries(delta, nchunks):
    out = []
    for k in range(nchunks):
        e0 = 128 * k - 256 * delta
        if e0 in FWD_E0:
            out.append((0, k, dft_type(0, e0)))
        e0b = -128 * k - 256 * delta
        if e0b in BWD_E0:
            out.append((1, k, dft_type(1, e0b)))
    return out


def build_program(nlayers=2):
    nc = bass.Bass("TRN2", target_bir_lowering=False)

    def din(name, shape):
        return nc.dram_tensor(name, list(shape), F32, kind="ExternalInput").ap()

    def dout(name, shape):
        return nc.dram_tensor(name, list(shape), F32, kind="ExternalOutput").ap()

    xT = din("xT", [128, 8, NT])
    condT = din("condT", [128, 8, 2])
    s0ssd = din("s0ssd", [128, 80, 64])
    s0ret = din("s0ret", [64, 80, 64])
    ckT = din("ckT", [64, 4, 512])
    cvt = din("cvt", [128, 8, 128])
    ptab_d = din("ptab", [128, PT.n])
    ftab_d = din("ftab", [128, FT.n])
    cf32_d = din("cf32", [128, CF.n])
    cbf_d = din("cbf", [128, CB.n])
    featsA_d = din("featsA", [33, 1024])
    featsB_d = din("featsB", [33, 256])
    dec_d = din("dec", [128, 20, 256])
    w_mod = din("w_mod", [2, 18, 128, 4096])
    ffn_w_in = din("ffn_w_in", [2, 2, 11, 128, 4096])
    ffn_w_out = din("ffn_w_out", [2, 2, 11, 128, 2048])
    mix_w_in = din("mix_w_in", [2, D, 3336])
    mix_w_out = din("mix_w_out", [2, D, D])
    hy_w1 = din("hy_w1", [2, 33, 64])
    hy_w2 = din("hy_w2", [2, 64, 64])
    hy_w3 = din("hy_w3", [2, 64, 1024])

    yT = dout("yT", [128, 8, NT])
    nk_o = dout("nk", [2, NT, 128])
    nv_o = dout("nv", [2, NT, 128])
    nssd_o = dout("nssd", [128, 80, 64])
    nret_o = dout("nret", [64, 80, 64])

    K = KB(nc)
    V, A, T, G = K.V, K.A, K.T, K.G

    x = K.sb("x", [128, 8, NT])
    XB = [[Buf(f"x{m}_{t}") for t in range(3)] for m in range(8)]
    modt = K.sb("modt", [128, 2, 2, 72])
    MODB = Buf("mod")
    ptab = K.sb("ptab", [128, PT.n])
    ftab = K.sb("ftab", [128, FT.n])
    cf = K.sb("cf", [128, CF.n])
    cb = K.sb("cb", [128, CB.n], BF16)
    CONSTB = Buf("const")
    CONSTB2 = Buf("constb")
    WA = [K.sb(f"WA{i}", [128, 8, 512], BF16) for i in range(2)]
    WAB = bufs(2, "WA")
    WB = [K.sb(f"WB{i}", [128, 4096], BF16) for i in range(2)]
    WBB = bufs(2, "WB")
    wctr = {"a": 0, "b": 0}
    PS = [K.ps(f"ps{i}", [128, 512]) for i in range(8)]
    PSBK = [Buf(f"psb{i}", excl=True) for i in range(8)]
    PSR = [PSBK[i // 4] for i in range(32)]
    PSB = [[PSBK[i]] for i in range(8)]
    psctr = [0, 0, 8]
    prc = [0]

    def bank():
        lo, hi = psctr[1], psctr[2]
        i = psctr[0]
        if i < lo or i >= hi:
            i = lo
        psctr[0] = i + 1 if i + 1 < hi else lo
        return PS[i], PSB[i]

    def set_banks(lo, hi):
        psctr[1], psctr[2] = lo, hi

    def pr(ncols):
        n = (ncols + 127) // 128
        i = prc[0]
        if (i % 4) + n > 4:
            i = (i // 4 + 1) * 4
        if i + n > 32:
            i = 0
        prc[0] = (i + n) % 32
        b, r = divmod(i, 4)
        return PS[b][:, r * 128:r * 128 + n * 128], [PSBK[b]]

    def interleave(gens):
        gens = list(gens)
        while gens:
            nxt = []
            for g in gens:
                try:
                    next(g)
                    nxt.append(g)
                except StopIteration:
                    pass
            gens = nxt

    def nextA():
        i = wctr["a"] % 2
        wctr["a"] += 1
        return WA[i], WAB[i], f"wa{i}"

    def nextB():
        i = wctr["b"] % 2
        wctr["b"] += 1
        return WB[i], WBB[i], f"wb{i}"

    def pcol(name, off=0, w=1, rows=128):
        c0 = PT[name][0] + off
        return ptab[0:rows, c0:c0 + w]

    def fcol(name, off=0, w=1, rows=128):
        c0 = FT[name][0] + off
        return ftab[0:rows, c0:c0 + w]

    def ccol(name, off=0, w=None, rows=128):
        c0, ww = CF[name]
        if w is None:
            w = ww
        return cf[0:rows, c0 + off:c0 + off + w]

    def bcol(name, off=0, w=None, rows=128):
        c0, ww = CB[name]
        if w is None:
            w = ww
        return cb[0:rows, c0 + off:c0 + off + w]

    K.dma("sync", lambda e: e.dma_start(out=x[:], in_=xT), writes=[b for row in XB for b in row], semkey="ldx")
    K.dma("sync", lambda e: e.dma_start(out=ptab[:], in_=ptab_d), writes=[CONSTB], semkey="ldc")
    K.dma("sync", lambda e: e.dma_start(out=ftab[:], in_=ftab_d), writes=[CONSTB], semkey="ldc")
    K.dma("sync", lambda e: e.dma_start(out=cf[:], in_=cf32_d), writes=[CONSTB], semkey="ldc")
    K.dma("gpsimd", lambda e: e.dma_start(out=cb[:], in_=cbf_d, max_dma_last_dim=2048), writes=[CONSTB2], semkey="ldcb")

    K.barrier()

    flag = ccol("flag")
    ident_b = bcol("ident")
    ones_b = bcol("ones")
    ones_f = ccol("ones")
    trif = ccol("trif")
    trib = ccol("trib")

    condf = K.sb("condf", [128, 8, 2])
    condb = K.sb("condb", [128, 8, 2], BF16)
    CB_ = Buf("cond")
    MODL = [Buf("mod0"), Buf("mod1")]
    K.dma("sync", lambda e: e.dma_start(out=condf[:], in_=condT), writes=[CB_], semkey="ldcond")
    A(lambda e: e.activation(out=condb[:], in_=condf[:], func=AF.Silu), [CB_], [CB_])

    def mod_dma(l, ci, wa, wab, sk):
        K.dma("gpsimd", lambda e: e.dma_start(out=wa[:].rearrange("p k n -> p (k n)"), in_=w_mod[l, ci], max_dma_last_dim=8192),
              writes=[wab], semkey=sk)

    def mod_chunk(l, ci, wa, wab, sk, dma=True):
        if dma:
            mod_dma(l, ci, wa, wab, sk)
        pb, pbb = bank()
        for mb in range(4):
            for k in range(8):
                T(lambda e: e.matmul(pb[:, mb * 2:mb * 2 + 2], lhsT=wa[:, k, mb * 128:(mb + 1) * 128], rhs=condb[:, k, :],
                                     start=(k == 0), stop=(k == 7)), [wab, CB_], [pbb])
        for cnd in range(2):
            V(lambda e: e.tensor_tensor(out=modt[:, l, cnd, ci * 4:ci * 4 + 4], in0=pb[:, 0:8].rearrange("p (m c) -> p m c", c=2)[:, :, cnd],
                                        in1=pcol("b_mod", l * 72 + ci * 4, 4), op=ALU.add), [pbb, CONSTB], [MODL[l]])

    def mod_finish_j(l, j):
        for cnd in range(2):
            V(lambda e: e.scalar_tensor_tensor(
                out=modt[:, l, cnd, (3 * j + 1) * 8:(3 * j + 2) * 8], in0=modt[:, l, cnd, (3 * j + 1) * 8:(3 * j + 2) * 8],
                scalar=1.0, op0=ALU.add, in1=pcol("norm_w", (l * 3 + j) * 8, 8), op1=ALU.mult), [MODL[l], CONSTB], [MODL[l]])
            if j in (0, 2):
                V(lambda e: e.tensor_scalar(
                    out=modt[:, l, cnd, (3 * j + 2) * 8:(3 * j + 3) * 8], in0=modt[:, l, cnd, (3 * j + 2) * 8:(3 * j + 3) * 8],
                    scalar1=0.5, scalar2=None, op0=ALU.mult), [MODL[l]], [MODL[l]])

    mod_done = {}

    def mod_mark(l, ci):
        j = ci // 6
        mod_done[(l, j)] = mod_done.get((l, j), 0) + 1
        if mod_done[(l, j)] == 6:
            mod_finish_j(l, j)

    for ci in range(6):
        wa, wab, sk = nextA()
        mod_chunk(0, ci, wa, wab, sk)
        mod_mark(0, ci)
    K.barrier()
    pending_mod = [(0, ci) for ci in range(6, 18)] + ([(1, ci) for ci in range(18)] if nlayers > 1 else [])
    inflight_mod = []
    ffn_gi = [0]

    def modc(l, cnd, j, m):
        return modt[:, l, cnd, j * 8 + m:j * 8 + m + 1]

    def make_h(l, j, hbuf, HB, tiles=(0, 1, 2), rs_pool=None):
        for tt in tiles:
            t0, w, cnd = TT[tt]
            pb, pbb = bank()
            for c in range(8):
                sq, sqb = rs_pool["sq"].next()
                A(lambda e, sq=sq, c=c, t0=t0, w=w: e.activation(out=sq[:, 0:w], in_=x[:, c, t0:t0 + w], func=AF.Square),
                  [XB[c][tt]], [sqb])
                T(lambda e, pb=pb, sq=sq, c=c, w=w: e.matmul(pb[:, 0:w], lhsT=ones_b, rhs=sq[:, 0:w],
                                                               start=(c == 0), stop=(c == 7)), [sqb, CONSTB], [pbb])
            rs, rsb = rs_pool["rs"].next()
            A(lambda e, rs=rs, pb=pb, w=w: e.activation(out=rs[:, 0:w], in_=pb[:, 0:w], func=AF.Sqrt, bias=EPS, scale=1.0 / D),
              [pbb], [rsb])
            V(lambda e, rs=rs, w=w: e.reciprocal(rs[:, 0:w], rs[:, 0:w]), [rsb], [rsb])
            for c in range(8):
                tm, tmb = rs_pool["tm"].next()
                V(lambda e, tm=tm, rs=rs, c=c, t0=t0, w=w: e.tensor_tensor(out=tm[:, 0:w], in0=x[:, c, t0:t0 + w], in1=rs[:, 0:w], op=ALU.mult),
                  [XB[c][tt], rsb], [tmb])
                A(lambda e, tm=tm, c=c, t0=t0, w=w, cnd=cnd: e.activation(
                    out=hbuf[:, c, t0:t0 + w], in_=tm[:, 0:w], func=AF.Identity,
                    scale=modc(l, cnd, 3 * j + 1, c), bias=modc(l, cnd, 3 * j, c)), [tmb, MODL[l]], [HB[tt]])

    def resid_update(pb, pbb, m, tt, gate_ap, extra_reads=()):
        t0, w, cnd = TT[tt]
        V(lambda e: e.scalar_tensor_tensor(out=x[:, m, t0:t0 + w], in0=pb[:, 0:w], scalar=gate_ap, op0=ALU.mult,
                                           in1=x[:, m, t0:t0 + w], op1=ALU.add), [pbb, MODL[0], MODL[1]] + list(extra_reads), [XB[m][tt]])

    def ffn(l, f):
        j = 0 if f == 0 else 2
        mk = K.mark()
        hbuf = K.sb("h", [128, 8, NT], BF16)
        HB = bufs(3, "h")
        hid = [K.sb(f"hid{i}", [128, 2, NT], BF16) for i in range(2)]
        HIDB = [bufs(3, "hid0_"), bufs(3, "hid1_")]
        pool = {"sq": Rot(K, "sq", [128, 512], BF16, 3), "rs": Rot(K, "rs", [128, 512], F32, 2),
                "tm": Rot(K, "tm", [128, 512], F32, 3)}
        sgp = Rot(K, "sg", [128, 512], F32, 3)
        make_h(l, j, hbuf, HB, rs_pool=pool)
        stream_mod = (l == 0 and (len(pending_mod) > 0 or len(inflight_mod) > 0))
        if stream_mod:
            WM = [K.sb(f"WM{i}", [128, 8, 512], BF16) for i in range(4)]
            WMB = bufs(4, "WM")
            assert not inflight_mod
        for g in range(11):
            if stream_mod:
                while inflight_mod:
                    (ml, ci, slot) = inflight_mod.pop(0)
                    mod_chunk(ml, ci, WM[slot], WMB[slot], f"wm{slot}", dma=False)
                    mod_mark(ml, ci)
                ffn_gi[0] += 1
            wa, wab, ska = nextA()
            wb, wbb, skb = nextB()
            K.dma("gpsimd", lambda e, wa=wa, g=g: e.dma_start(out=wa[:].rearrange("p k n -> p (k n)"), in_=ffn_w_in[l, f, g], max_dma_last_dim=8192),
                  writes=[wab], semkey=ska)
            K.dma("gpsimd", lambda e, wb=wb, g=g: e.dma_start(out=wb[:, 0:2048], in_=ffn_w_out[l, f, g], max_dma_last_dim=8192),
                  writes=[wbb], semkey=skb)
            if stream_mod and g < 10:
                nper = 2 if ffn_gi[0] <= 10 else 1
                for i in range(nper):
                    if pending_mod:
                        slot = 2 * (g % 2) + i
                        (ml, ci) = pending_mod.pop(0)
                        mod_dma(ml, ci, WM[slot], WMB[slot], f"wm{slot}")
                        inflight_mod.append((ml, ci, slot))
            hd = hid[g % 2]
            hdb = HIDB[g % 2]
            for jj in range(2):
                for tt in range(3):
                    t0, w, cnd = TT[tt]
                    pg, pgb = bank()
                    pu, pub = bank()
                    for k in range(8):
                        T(lambda e, pg=pg, wa=wa, k=k, jj=jj, t0=t0, w=w: e.matmul(
                            pg[:, 0:w], lhsT=wa[:, k, jj * 128:(jj + 1) * 128], rhs=hbuf[:, k, t0:t0 + w], start=(k == 0), stop=(k == 7)),
                            [wab, HB[tt]], [pgb])
                    for k in range(8):
                        T(lambda e, pu=pu, wa=wa, k=k, jj=jj, t0=t0, w=w: e.matmul(
                            pu[:, 0:w], lhsT=wa[:, k, 256 + jj * 128:256 + (jj + 1) * 128], rhs=hbuf[:, k, t0:t0 + w], start=(k == 0), stop=(k == 7)),
                            [wab, HB[tt]], [pub])
                    sg, sgb = sgp.next()
                    A(lambda e, sg=sg, pg=pg, w=w: e.activation(out=sg[:, 0:w], in_=pg[:, 0:w], func=AF.Silu), [pgb], [sgb])
                    V(lambda e, sg=sg, pu=pu, hd=hd, jj=jj, t0=t0, w=w: e.tensor_tensor(
                        out=hd[:, jj, t0:t0 + w], in0=sg[:, 0:w], in1=pu[:, 0:w], op=ALU.mult), [sgb, pub], [hdb[tt]])
            wbv = wb[:, 0:2048].rearrange("p (j n) -> p j n", j=2)
            for m in range(8):
                for tt in range(3):
                    t0, w, cnd = TT[tt]
                    po, pob = bank()
                    for jj in range(2):
                        T(lambda e, po=po, wbv=wbv, jj=jj, m=m, hd=hd, t0=t0, w=w: e.matmul(
                            po[:, 0:w], lhsT=wbv[:, jj, m * 128:(m + 1) * 128], rhs=hd[:, jj, t0:t0 + w], start=(jj == 0), stop=(jj == 1)),
                            [wbb, hdb[tt]], [pob])
                    resid_update(po, pob, m, tt, modc(l, cnd, 3 * j + 2, m))
        if stream_mod:
            while inflight_mod:
                (ml, ci, slot) = inflight_mod.pop(0)
                mod_chunk(ml, ci, WM[slot], WMB[slot], f"wm{slot}", dma=False)
                mod_mark(ml, ci)
        K.barrier()
        K.release(mk)

    def mixer(l):
        mk_all = K.mark()
        wmi = mix_w_in[l]
        wmo = mix_w_out[l]
        cvx = {}

        def load_wa(col0, ncols):
            wa, wab, sk = nextA()
            K.dma("gpsimd", lambda e: e.dma_start(out=wa[:, :, 0:ncols],
                                                  in_=wmi[:, col0:col0 + ncols].rearrange("(k p) n -> p k n", p=128)),
                  writes=[wab], semkey=sk)
            return wa, wab

        def proj_fm(pb, pbb, wa, wab, c0, M, hbuf, HB, tt):
            t0, w, cnd = TT[tt]
            for k in range(8):
                T(lambda e, k=k: e.matmul(pb[0:M, 0:w], lhsT=wa[:, k, c0:c0 + M], rhs=hbuf[:, k, t0:t0 + w],
                                          start=(k == 0), stop=(k == 7)), [wab, HB[tt]], [pbb])

        yreg = []

        def out_proj(ychunks, YB, wrow0, kp, tile=None):
            yreg.append((ychunks, YB, wrow0, kp, tile))

        def out_proj_all():
            packed = []
            for (ychunks, YB, wrow0, kp, tile) in yreg:
                if kp == 64 and tile is not None:
                    yp = K.sb("ypair", [128, 2, NT], BF16)
                    YPB = Buf("ypair")
                    tv = tile[:].rearrange("p (j two) t -> p j two t", two=2)
                    K.dma("sync", lambda e, yp=yp, tv=tv: e.dma_start(out=yp[0:64, :, :], in_=tv[:, :, 0, :]), reads=YB, writes=[YPB], semkey=f"ldyp{len(packed)}")
                    K.dma("sync", lambda e, yp=yp, tv=tv: e.dma_start(out=yp[64:128, :, :], in_=tv[:, :, 1, :]), reads=YB, writes=[YPB], semkey=f"ldyp{len(packed)}")
                    packed.append(([yp[:, 0, :], yp[:, 1, :]], [YPB], wrow0, 128))
                else:
                    packed.append((ychunks, YB, wrow0, kp))
            yreg[:] = packed
            slots = []
            for (ychunks, YB, wrow0, kp) in yreg:
                nk_ = len(ychunks)
                if len(slots) % 2 == 0:
                    wt_, wtb_, sk = nextB()
                    flat = wt_[:, :]
                else:
                    wt_, wtb_, sk = nextA()
                    flat = wt_[:].rearrange("p k n -> p (k n)")
                wv = flat[0:kp, 0:nk_ * 1024].rearrange("p (j n) -> p j n", j=nk_)
                K.dma("gpsimd", lambda e, wv=wv, wrow0=wrow0, nk_=nk_, kp=kp: e.dma_start(
                    out=wv, in_=wmo[wrow0:wrow0 + nk_ * kp, :].rearrange("(j p) n -> p j n", p=kp)), writes=[wtb_], semkey=sk)
                slots.append((wv, wtb_))
            total = sum(len(y[0]) for y in yreg)
            for m in range(8):
                for tt in range(3):
                    t0, w, cnd = TT[tt]
                    po, pob = bank()
                    i = 0
                    for (ychunks, YB, wrow0, kp), (wv, wtb_) in zip(yreg, slots):
                        for jj, yc in enumerate(ychunks):
                            T(lambda e, wv=wv, jj=jj, yc=yc, i=i: e.matmul(po[:, 0:w], lhsT=wv[:, jj, m * 128:(m + 1) * 128],
                                                                        rhs=yc[:, t0:t0 + w], start=(i == 0), stop=(i == total - 1)),
                              [wtb_] + YB, [pob])
                            i += 1
                    resid_update(po, pob, m, tt, modc(l, cnd, 5, m))

        def conv_chunk(raw, rawb, P, pc0, out_ap, outb, silu):
            hl = ccol("hl")
            hr = ccol("hr")
            V(lambda e: e.tensor_tensor(out=raw[0:P, 1:5, 0:1], in0=raw[0:P, 0:4, 256:257], in1=hl[0:P, 1:5].unsqueeze(2), op=ALU.mult),
              [rawb, CONSTB], [rawb])
            V(lambda e: e.tensor_tensor(out=raw[0:P, 0:4, 257:258], in0=raw[0:P, 1:5, 1:2], in1=hr[0:P, 0:4].unsqueeze(2), op=ALU.mult),
              [rawb, CONSTB], [rawb])
            acc, accb = cvx["convacc"].next()
            accv = acc[0:P, :].rearrange("p (b t) -> p b t", t=256)
            w0 = ptab[0:P, pc0:pc0 + 1]
            w1 = ptab[0:P, pc0 + 1:pc0 + 2]
            w2 = ptab[0:P, pc0 + 2:pc0 + 3]
            bb = ptab[0:P, pc0 + 3:pc0 + 4]
            V(lambda e: e.tensor_scalar(out=accv, in0=raw[0:P, :, 1:257], scalar1=w1, scalar2=None, op0=ALU.mult), [rawb, CONSTB], [accb])
            V(lambda e: e.scalar_tensor_tensor(out=accv, in0=raw[0:P, :, 0:256], scalar=w0, op0=ALU.mult, in1=accv, op1=ALU.add),
              [rawb, CONSTB, accb], [accb])
            V(lambda e: e.scalar_tensor_tensor(out=accv, in0=raw[0:P, :, 2:258], scalar=w2, op0=ALU.mult, in1=accv, op1=ALU.add),
              [rawb, CONSTB, accb], [accb])
            A(lambda e: e.activation(out=out_ap, in_=acc[0:P, :], func=(AF.Silu if silu else AF.Identity), bias=bb), [accb, CONSTB], [outb])

        def raw_fill(raw, rawb, P, pb, pbb, tt):
            t0, w, cnd = TT[tt]
            b0 = t0 // 256
            nb = w // 256
            A(lambda e: e.activation(out=raw[0:P, b0:b0 + nb, 1:257], in_=pb[0:P, 0:w].rearrange("p (b t) -> p b t", t=256), func=AF.Copy),
              [pbb], [rawb])

        hbuf_m = K.sb("hmix", [128, 8, NT], BF16)
        HB_m = bufs(3, "hmix")
        mkp = K.mark()
        pool_m = {"sq": Rot(K, "sq", [128, 512], BF16, 3), "rs": Rot(K, "rs", [128, 512], F32, 2),
                  "tm": Rot(K, "tm", [128, 512], F32, 2)}
        make_h(l, 1, hbuf_m, HB_m, rs_pool=pool_m)
        K.barrier()
        K.release(mkp)

        def with_h(fn, conv=False):
            mk = K.mark()
            if conv:
                cvx["convacc"] = Rot(K, "cacc", [128, NT], F32, 1)
                raws = Rot(K, "raw", [128, 5, 258], F32, 2)
                cvx["raws"] = raws
                for i in range(2):
                    V(lambda e, i=i: e.memset(raws.t[i][:], 0.0), [], [raws.b[i]])
            fn(hbuf_m, HB_m)
            K.barrier()
            K.release(mk)

        def ssd(szb, SZB):
            mk = K.mark()
            xsf = K.sb("xsf", [64, 4, NT], BF16)
            XSF = Buf("xsf")
            bcf = K.sb("bcf", [128, 4, NT], BF16)
            BCF = Buf("bcf")
            dtr = K.sb("dtr", [128, 10, 8])
            dtt = K.sb("dtt", [128, 10, 8])
            lat = K.sb("lat", [128, 10, 8])
            DTB = Buf("dt")

            def inproj(hbuf, HB):
                wa0, wab0 = load_wa(0, 512)
                wa1, wab1 = load_wa(512, 512)
                zxp = K.sb("zxpair", [128, 2, 2, NT], BF16)
                ZXB = bufs(2, "zxpair")
                for j in range(2):
                    for tt in range(3):
                        t0, w, cnd = TT[tt]
                        pb, pbb = bank()
                        proj_fm(pb, pbb, wa0, wab0, j * 128, 128, hbuf, HB, tt)
                        A(lambda e, pb=pb, j=j, t0=t0, w=w: e.activation(out=zxp[:, 0, j, t0:t0 + w], in_=pb[:, 0:w], func=AF.Silu), [pbb], [ZXB[0]])
                for j in range(2):
                    raw, rawb = cvx["raws"].next()
                    for tt in range(3):
                        pb, pbb = bank()
                        proj_fm(pb, pbb, wa0, wab0, 256 + j * 128, 128, hbuf, HB, tt)
                        raw_fill(raw, rawb, 128, pb, pbb, tt)
                    conv_chunk(raw, rawb, 128, PT["ssd_convp"][0] + (l * 2 + j) * 4, zxp[:, 1, j, :], ZXB[1], True)
                for slot, (dst, dstb) in enumerate(((szb, SZB), (xsf, XSF))):
                    dv = dst[:].rearrange("p (j two) t -> p j two t", two=2)
                    for half in range(2):
                        K.dma("sync", lambda e, dv=dv, slot=slot, half=half: e.dma_start(out=dv[:, :, half, :], in_=zxp[half * 64:(half + 1) * 64, slot, :, :]),
                              reads=[ZXB[slot]], writes=[dstb], semkey=f"ldzx{slot}")
                for q in range(4, 8):
                    raw, rawb = cvx["raws"].next()
                    P = 64 if q < 4 else 128
                    for tt in range(3):
                        pb, pbb = bank()
                        if q < 4:
                            proj_fm(pb, pbb, wa0, wab0, 256 + q * 64, 64, hbuf, HB, tt)
                        else:
                            proj_fm(pb, pbb, wa1, wab1, (q - 4) * 128, 128, hbuf, HB, tt)
                        raw_fill(raw, rawb, P, pb, pbb, tt)
                    pc0 = PT["ssd_conv"][0] + (l * 8 + q) * 4
                    if q < 4:
                        conv_chunk(raw, rawb, 64, pc0, xsf[:, q, :], XSF, True)
                    else:
                        conv_chunk(raw, rawb, 128, pc0, bcf[:, q - 4, :], BCF, True)
                wdt = K.sb("wdt", [128, 8, 8], BF16)
                WDT = Buf("wdt")
                K.dma("gpsimd", lambda e: e.dma_start(out=wdt[:], in_=wmi[:, 1024:1032].rearrange("(k p) n -> p k n", p=128)),
                      writes=[WDT], semkey="wdt")
                pb, pbb = bank()
                for tc in range(NCH):
                    tt = 0 if tc < 4 else (1 if tc < 8 else 2)
                    for k in range(8):
                        T(lambda e, tc=tc, k=k: e.matmul(pb[:, tc * 8:tc * 8 + 8], lhsT=hbuf[:, k, tc * 128:(tc + 1) * 128], rhs=wdt[:, k, :],
                                                         start=(k == 0), stop=(k == 7)), [HB[tt], WDT], [pbb])
                V(lambda e: e.tensor_tensor(out=dtr[:].rearrange("p a b -> p (a b)"), in0=pb[:, 0:80], in1=fcol("dt_bias", l * 80, 80), op=ALU.add),
                  [pbb, CONSTB], [DTB])
            with_h(inproj, conv=True)
            A(lambda e: e.activation(out=dtt[:], in_=dtr[:], func=AF.Exp), [DTB], [DTB])
            A(lambda e: e.activation(out=dtt[:], in_=dtt[:], func=AF.Ln, bias=1.0), [DTB], [DTB])
            A(lambda e: e.activation(out=dtr[:].rearrange("p a b -> p (a b)"), in_=fcol("a_log", l * 80, 80), func=AF.Exp), [DTB, CONSTB], [DTB])
            V(lambda e: e.scalar_tensor_tensor(out=lat[:], in0=dtr[:], scalar=-1.0, op0=ALU.mult, in1=dtt[:], op1=ALU.mult), [DTB], [DTB])
            xbtm = K.sb("xbtm", [128, NCH, 512], BF16)
            XBT = bufs(NCH, "xbtm")
            for tc in range(NCH):
                pb, pbb = bank()
                pbv = pb[:].bitcast(BF16)
                for hh in range(4):
                    T(lambda e, hh=hh, tc=tc: e.transpose(pbv[:, hh * 64:(hh + 1) * 64], xsf[:, hh, tc * 128:(tc + 1) * 128], ident_b[0:64, 0:64]),
                      [XSF, CONSTB], [pbb])
                for g in range(2):
                    T(lambda e, g=g, tc=tc: e.transpose(pbv[:, 256 + g * 128:256 + (g + 1) * 128], bcf[:, g, tc * 128:(tc + 1) * 128], ident_b),
                      [BCF, CONSTB], [pbb])
                A(lambda e, tc=tc, pbv=pbv: e.activation(out=xbtm[:, tc, :], in_=pbv[:, 0:512], func=AF.Copy), [pbb], [XBT[tc]])
            cst = K.sb("cst", [128, NCH, 16])
            wBt = K.sb("wBt", [128, NCH, 8])
            edt = K.sb("edt", [128, NCH, 8])
            pb, pbb = bank()
            for tc in range(NCH):
                T(lambda e, tc=tc: e.matmul(pb[:, tc * 16:tc * 16 + 4], lhsT=trif, rhs=lat[:, tc, 0:4], start=True, stop=True), [DTB, CONSTB], [pbb])
                T(lambda e, tc=tc: e.matmul(pb[:, tc * 16 + 4:tc * 16 + 8], lhsT=trib, rhs=lat[:, tc, 4:8], start=True, stop=True), [DTB, CONSTB], [pbb])
                T(lambda e, tc=tc: e.matmul(pb[:, tc * 16 + 8:tc * 16 + 16], lhsT=ones_f, rhs=lat[:, tc, 0:8], start=True, stop=True), [DTB, CONSTB], [pbb])
            V(lambda e: e.tensor_copy(cst[:].rearrange("p a b -> p (a b)"), pb[:, 0:160]), [pbb], [DTB])
            V(lambda e: e.tensor_tensor(out=wBt[:], in0=cst[:, :, 8:16], in1=cst[:, :, 0:8], op=ALU.subtract), [DTB], [DTB])
            A(lambda e: e.activation(out=wBt[:], in_=wBt[:], func=AF.Exp), [DTB], [DTB])
            V(lambda e: e.tensor_tensor(out=wBt[:], in0=wBt[:], in1=dtt[:], op=ALU.mult), [DTB], [DTB])
            A(lambda e: e.activation(out=edt[:], in_=cst[:, :, 8:16], func=AF.Exp), [DTB], [DTB])
            S32 = K.sb("S32", [128, 8, 64])
            SB_ = bufs(8, "S32")
            SP = K.sb("SP", [128, NCH, 8, 64], BF16)
            SPB = [bufs(8, f"SP{c}_") for c in range(NCH)]
            mk_st = K.mark()
            s0t = K.sb("s0t", [128, 40, 64])
            S0B = Buf("s0")
            K.dma("sync", lambda e: e.dma_start(out=s0t[:], in_=s0ssd[:, l * 40:(l + 1) * 40, :]), writes=[S0B], semkey="lds0")
            V(lambda e: e.memset(S32[:], 0.0), [], SB_)
            hl = ccol("hl")
            hr = ccol("hr")
            bsp = Rot(K, "bs", [128, 128], BF16, 12)

            def state_chain(d, hh):
                col = d * 4 + hh
                g = hh // 2
                for k in range(NCH):
                    c = k if d == 0 else NCH - 1 - k
                    blk = c // 2
                    first = (c % 2 == 0) if d == 0 else (c % 2 == 1)
                    sidx = (blk * 2 + d) * 4 + hh
                    if first:
                        fl = hl[:, blk:blk + 1] if d == 0 else hr[:, blk:blk + 1]
                        V(lambda e: e.scalar_tensor_tensor(out=S32[:, col, :], in0=S32[:, col, :], scalar=fl, op0=ALU.mult, in1=s0t[:, sidx, :], op1=ALU.add),
                          [SB_[col], S0B, CONSTB], [SB_[col]])
                    A(lambda e: e.activation(out=SP[:, c, col, :], in_=S32[:, col, :], func=AF.Copy), [SB_[col]], [SPB[c][col]])
                    bs, bsb = bsp.next()
                    V(lambda e: e.tensor_scalar(out=bs[:], in0=xbtm[:, c, 256 + g * 128:256 + (g + 1) * 128], scalar1=wBt[:, c, col:col + 1], scalar2=None, op0=ALU.mult),
                      [XBT[c], DTB], [bsb])
                    yield
                    pb, pbb = pr(64)
                    T(lambda e: e.matmul(pb[:, 0:64], lhsT=bs[:], rhs=xbtm[:, c, hh * 64:(hh + 1) * 64], start=True, stop=True), [bsb, XBT[c]], [pbb])
                    yield
                    V(lambda e: e.scalar_tensor_tensor(out=S32[:, col, :], in0=S32[:, col, :], scalar=edt[:, c, col:col + 1], op0=ALU.mult, in1=pb[:, 0:64], op1=ALU.add),
                      [SB_[col], DTB, pbb], [SB_[col]])
                    if not first:
                        oidx = l * 40 + sidx
                        sslot = sstp.i
                        stg, stgb = sstp.next()
                        A(lambda e: e.activation(out=stg[:], in_=S32[:, col, :], func=AF.Copy), [SB_[col]], [stgb])
                        K.dma("sync", lambda e: e.dma_start(out=nssd_o[:, oidx, :], in_=stg[:]), reads=[stgb], semkey=f"stS{sslot}")
                    yield
            sstp = Rot(K, "sstg", [128, 64], F32, 8)
            interleave([state_chain(d, hh) for d in range(2) for hh in range(4)])
            K.barrier()
            K.release(mk_st)

            dsk = pcol("ssd_d", l * 4, 4, rows=64)
            nws = pcol("ssd_nw", l * 4, 4, rows=64)
            ygp = Rot(K, "yg", [64, 4, 128], F32, 2)
            wtp = Rot(K, "wt", [128, 128], F32, 16)
            sgp2 = Rot(K, "sg2", [128, 128], F32, 16)
            csp = Rot(K, "cs", [128, 128], BF16, 16)
            sqp = Rot(K, "sq4", [64, 128], BF16, 8)

            def ssd_chain(c, hh, d, psc, pscb, res):
                cs = slice(c * 128, (c + 1) * 128)
                col = d * 4 + hh
                g = hh // 2
                U = trif if d == 0 else trib
                NB = ccol("nbf") if d == 0 else ccol("nbb")
                wt, wtb = wtp.next()
                G(lambda e: e.tensor_scalar(out=wt[:], in0=U, scalar1=lat[:, c, col:col + 1], scalar2=0.0, op0=ALU.mult, op1=ALU.add), [CONSTB, DTB], [wtb])
                yield
                pc, pcb = pr(128)
                T(lambda e: e.matmul(pc[:, 0:128], lhsT=ones_f, rhs=wt[:], start=True, stop=True), [wtb, CONSTB], [pcb])
                yield
                sg, sgb = sgp2.next()
                V(lambda e: e.scalar_tensor_tensor(out=sg[:], in0=pc[:, 0:128], scalar=cst[:, c, col:col + 1], op0=ALU.subtract, in1=NB, op1=ALU.add),
                  [pcb, DTB, CONSTB], [sgb])
                yield
                ec, ecb = wt, wtb
                A(lambda e: e.activation(out=ec[:], in_=pc[:, 0:128], func=AF.Exp), [pcb], [ecb])
                yield
                A(lambda e: e.activation(out=sg[:], in_=sg[:], func=AF.Exp), [sgb], [sgb])
                csb, csbb = csp.next()
                G(lambda e: e.tensor_tensor(out=csb[:], in0=bcf[:, 2 + g, cs], in1=ec[:], op=ALU.mult), [BCF, ecb], [csbb])
                yield
                yield
                st, stb = wt[:].bitcast(BF16)[:, 0:128], wtb
                V(lambda e: e.scalar_tensor_tensor(out=st, in0=psc[:, g * 128:(g + 1) * 128], scalar=dtt[:, c, col:col + 1], op0=ALU.mult, in1=sg[:], op1=ALU.mult),
                  [pscb, sgb, DTB], [stb])
                res[(hh, d)] = (st, stb, csb, csbb)
                yield

            def ssd_chunk(c):
                cs = slice(c * 128, (c + 1) * 128)
                psc, pscb = pr(256)
                for g in range(2):
                    T(lambda e: e.matmul(psc[:, g * 128:(g + 1) * 128], lhsT=bcf[:, g, cs], rhs=bcf[:, 2 + g, cs], start=True, stop=True), [BCF], [pscb])
                res = {}
                chains = [ssd_chain(c, hh, d, psc, pscb, res) for hh in range(4) for d in range(2)]
                while chains:
                    nxt = []
                    for gch in chains:
                        try:
                            next(gch)
                            nxt.append(gch)
                        except StopIteration:
                            pass
                    chains = nxt
                    yield
                yg, ygb = ygp.next()
                pys = []
                for hh in range(4):
                    py, pyb = pr(128)
                    pys.append((py, pyb))
                    for d in range(2):
                        col = d * 4 + hh
                        st, stb, csb, csbb = res[(hh, d)]
                        T(lambda e: e.matmul(py[0:64, 0:128], lhsT=xbtm[:, c, hh * 64:(hh + 1) * 64], rhs=st, start=(d == 0), stop=False), [XBT[c], stb], [pyb])
                        T(lambda e: e.matmul(py[0:64, 0:128], lhsT=SP[:, c, col, :], rhs=csb[:], start=False, stop=(d == 1)), [SPB[c][col], csbb], [pyb])
                yield
                sqs = []
                for hh in range(4):
                    py, pyb = pys[hh]
                    V(lambda e: e.scalar_tensor_tensor(out=yg[:, hh, :], in0=xsf[:, hh, cs], scalar=dsk[:, hh:hh + 1], op0=ALU.mult, in1=py[0:64, 0:128], op1=ALU.add),
                      [XSF, CONSTB, pyb], [ygb])
                    V(lambda e: e.tensor_tensor(out=yg[:, hh, :], in0=yg[:, hh, :], in1=szb[:, hh, cs], op=ALU.mult), [ygb, SZB], [ygb])
                    sq, sqb = sqp.next()
                    A(lambda e: e.activation(out=sq[:], in_=yg[:, hh, :], func=AF.Square), [ygb], [sqb])
                    sqs.append((sq, sqb))
                    yield
                pq, pqb = pr(128)
                for hh in range(4):
                    sq, sqb = sqs[hh]
                    T(lambda e: e.matmul(pq[0:64, 0:128], lhsT=ones_b[0:64, 0:64], rhs=sq[:], start=(hh == 0), stop=(hh == 3)), [sqb, CONSTB], [pqb])
                yield
                rs, rsb = rsp2.next()
                A(lambda e: e.activation(out=rs[:], in_=pq[0:64, 0:128], func=AF.Sqrt, bias=EPS, scale=1.0 / 256), [pqb], [rsb])
                yield
                V(lambda e: e.reciprocal(rs[:], rs[:]), [rsb], [rsb])
                yield
                for hh in range(4):
                    V(lambda e: e.scalar_tensor_tensor(out=szb[:, hh, cs], in0=yg[:, hh, :], scalar=nws[:, hh:hh + 1], op0=ALU.mult, in1=rs[:], op1=ALU.mult),
                      [ygb, rsb, CONSTB], [SZB])
                    yield
            rsp2 = Rot(K, "rs2", [64, 128], F32, 2)
            for c0 in range(0, NCH, 2):
                interleave([ssd_chunk(c0), ssd_chunk(c0 + 1)])
            out_proj([szb[:, hh, :] for hh in range(4)], [SZB], 0, 64, tile=szb)
            K.barrier()
            K.release(mk)

        def ret(sgf, SGF):
            mk = K.mark()
            qf = K.sb("qf", [64, 4, NT], BF16)
            kf = K.sb("kf", [64, 4, NT], BF16)
            kvt = K.sb("kvt", [128, NCH, 256], BF16)
            QF, KF = Buf("qf"), Buf("kf")
            KVT = bufs(NCH, "kvt")
            KST = Buf("kst")
            lg = K.sb("lg", [128, 8])
            LG = Buf("lg")
            A(lambda e: e.activation(out=lg[:], in_=fcol("ret_logit", l * 8, 8), func=AF.Exp, scale=-1.0), [CONSTB], [LG])
            A(lambda e: e.activation(out=lg[:], in_=lg[:], func=AF.Ln, bias=1.0), [LG], [LG])
            V(lambda e: e.tensor_scalar(out=lg[:], in0=lg[:], scalar1=-1.0, scalar2=None, op0=ALU.mult), [LG], [LG])
            Tl = K.sb("Tl", [128, 8])
            G128 = K.sb("G128", [128, 8])
            RC = Buf("retc")
            for hh in range(4):
                A(lambda e, hh=hh: e.activation(out=Tl[:, hh:hh + 1], in_=ccol("tailf"), func=AF.Exp, scale=lg[:, hh:hh + 1]), [LG, CONSTB], [RC])
                A(lambda e, hh=hh: e.activation(out=Tl[:, 4 + hh:5 + hh], in_=ccol("tailb"), func=AF.Exp, scale=lg[:, 4 + hh:5 + hh]), [LG, CONSTB], [RC])
            V(lambda e: e.tensor_scalar(out=Tl[:], in0=Tl[:], scalar1=0.125, scalar2=None, op0=ALU.mult), [RC], [RC])
            A(lambda e: e.activation(out=G128[:], in_=lg[:], func=AF.Exp, scale=128.0), [LG], [RC])
            S32 = K.sb("R32", [64, 8, 64])
            SB_ = bufs(8, "R32")
            SP = K.sb("RSP", [64, NCH, 8, 64], BF16)
            SPB = [bufs(8, f"RSP{c}_") for c in range(NCH)]
            mk_k = K.mark()
            kst = K.sb("kst", [128, 2, NCH, 256], BF16)

            def inproj(hbuf, HB):
                wa0, wab0 = load_wa(1800, 512)
                wa1, wab1 = load_wa(2312, 512)
                qp = K.sb("qpair", [128, 2, 2, NT], BF16)
                QPB = bufs(2, "qpair")
                specs = [(wa0, wab0, 0, AF.Copy, None, qf, QF), (wa0, wab0, 256, AF.Copy, 0.125, kf, KF), (wa1, wab1, 256, AF.Silu, None, sgf, SGF)]
                for i, (wsrc, wsrcb, c0, fn_, scl, dst, dstb) in enumerate(specs):
                    slot = i % 2
                    for j in range(2):
                        for tt in range(3):
                            t0, w, cnd = TT[tt]
                            pb, pbb = bank()
                            proj_fm(pb, pbb, wsrc, wsrcb, c0 + j * 128, 128, hbuf, HB, tt)
                            if scl is None:
                                A(lambda e: e.activation(out=qp[:, slot, j, t0:t0 + w], in_=pb[:, 0:w], func=fn_), [pbb], [QPB[slot]])
                            else:
                                A(lambda e: e.activation(out=qp[:, slot, j, t0:t0 + w], in_=pb[:, 0:w], func=fn_, scale=scl), [pbb], [QPB[slot]])
                    dv = dst[:].rearrange("p (j two) t -> p j two t", two=2)
                    for half in range(2):
                        K.dma("sync", lambda e: e.dma_start(out=dv[:, :, half, :], in_=qp[half * 64:(half + 1) * 64, slot, :, :]),
                              reads=[QPB[slot]], writes=[dstb], semkey=f"ldrp{i}")
                for tc in range(NCH):
                    tt = 0 if tc < 4 else (1 if tc < 8 else 2)
                    pb, pbb = bank()
                    for k in range(8):
                        T(lambda e, pb=pb, tc=tc, k=k: e.matmul(pb[:, 0:256], lhsT=hbuf[:, k, tc * 128:(tc + 1) * 128], rhs=wa0[:, k, 256:512],
                                                                 start=(k == 0), stop=(k == 7)), [HB[tt], wab0], [pbb])
                    for k in range(8):
                        T(lambda e, pb=pb, tc=tc, k=k: e.matmul(pb[:, 256:512], lhsT=hbuf[:, k, tc * 128:(tc + 1) * 128], rhs=wa1[:, k, 0:256],
                                                                 start=(k == 0), stop=(k == 7)), [HB[tt], wab1], [pbb])
                    for d in range(2):
                        V(lambda e, pb=pb, tc=tc, d=d: e.tensor_tensor(out=kst[:, d, tc, :].rearrange("p (h n) -> p h n", h=4),
                                                                       in0=pb[:, 0:256].rearrange("p (h n) -> p h n", h=4),
                                                                       in1=Tl[:, d * 4:(d + 1) * 4].unsqueeze(2).broadcast_to([128, 4, 64]), op=ALU.mult),
                          [pbb, RC], [KST])
                    A(lambda e, pb=pb, tc=tc: e.activation(out=kvt[:, tc, :], in_=pb[:, 256:512], func=AF.Copy), [pbb], [KVT[tc]])
            with_h(inproj)
            mk_st = K.mark()
            s0t = K.sb("rs0t", [64, 40, 64])
            S0B = Buf("rs0")
            K.dma("sync", lambda e: e.dma_start(out=s0t[:], in_=s0ret[:, l * 40:(l + 1) * 40, :]), writes=[S0B], semkey="lds0")
            V(lambda e: e.memset(S32[:], 0.0), [], SB_)
            hl = ccol("hl")
            hr = ccol("hr")
            def rstate_chain(d, hh):
                col = d * 4 + hh
                for k in range(NCH):
                    c = k if d == 0 else NCH - 1 - k
                    blk = c // 2
                    first = (c % 2 == 0) if d == 0 else (c % 2 == 1)
                    sidx = (blk * 2 + d) * 4 + hh
                    if first:
                        fl = hl[0:64, blk:blk + 1] if d == 0 else hr[0:64, blk:blk + 1]
                        V(lambda e: e.scalar_tensor_tensor(out=S32[:, col, :], in0=S32[:, col, :], scalar=fl, op0=ALU.mult, in1=s0t[:, sidx, :], op1=ALU.add),
                          [SB_[col], S0B, CONSTB], [SB_[col]])
                    A(lambda e: e.activation(out=SP[:, c, col, :], in_=S32[:, col, :], func=AF.Copy), [SB_[col]], [SPB[c][col]])
                    yield
                    pb, pbb = pr(64)
                    T(lambda e: e.matmul(pb[0:64, 0:64], lhsT=kst[:, d, c, hh * 64:(hh + 1) * 64], rhs=kvt[:, c, hh * 64:(hh + 1) * 64],
                                         start=True, stop=True), [KST, KVT[c]], [pbb])
                    yield
                    V(lambda e: e.scalar_tensor_tensor(out=S32[:, col, :], in0=S32[:, col, :], scalar=G128[0:64, col:col + 1], op0=ALU.mult, in1=pb[0:64, 0:64], op1=ALU.add),
                      [SB_[col], RC, pbb], [SB_[col]])
                    if not first:
                        oidx = l * 40 + sidx
                        sslot = rstp.i
                        stg, stgb = rstp.next()
                        A(lambda e: e.activation(out=stg[:], in_=S32[:, col, :], func=AF.Copy), [SB_[col]], [stgb])
                        K.dma("sync", lambda e: e.dma_start(out=nret_o[:, oidx, :], in_=stg[:]), reads=[stgb], semkey=f"stR{sslot}")
                    yield
            rstp = Rot(K, "rstg", [64, 64], F32, 8)
            interleave([rstate_chain(d, hh) for d in range(2) for hh in range(4)])
            K.barrier()
            K.release(mk_k)

            gnw = pcol("ret_gn", l * 4, 4, rows=64)
            Dm = K.sb("Dm", [128, 4, 128])
            Ef = K.sb("Ef", [64, 8, 128])
            tmpf = Rot(K, "tmpf", [128, 128], F32, 2)
            for hh in range(4):
                t1, t1b = tmpf.next()
                A(lambda e, t1=t1, hh=hh: e.activation(out=t1[:], in_=ccol("relu_f"), func=AF.Exp, scale=lg[:, hh:hh + 1]), [LG, CONSTB], [t1b])
                V(lambda e, t1=t1, hh=hh: e.tensor_tensor(out=Dm[:, hh, :], in0=t1[:], in1=trif, op=ALU.mult), [t1b, CONSTB], [RC])
                t2, t2b = tmpf.next()
                A(lambda e, t2=t2, hh=hh: e.activation(out=t2[:], in_=ccol("relu_b"), func=AF.Exp, scale=lg[:, 4 + hh:5 + hh]), [LG, CONSTB], [t2b])
                V(lambda e, t2=t2: e.tensor_tensor(out=t2[:], in0=t2[:], in1=trib, op=ALU.mult), [t2b, CONSTB], [t2b])
                V(lambda e, t2=t2, hh=hh: e.tensor_tensor(out=Dm[:, hh, :], in0=Dm[:, hh, :], in1=t2[:], op=ALU.add), [t2b, RC], [RC])
                A(lambda e, hh=hh: e.activation(out=Ef[:, hh, :], in_=ccol("ip1", rows=64), func=AF.Exp, scale=lg[0:64, hh:hh + 1]), [LG, CONSTB], [RC])
                A(lambda e, hh=hh: e.activation(out=Ef[:, 4 + hh, :], in_=ccol("rmi", rows=64), func=AF.Exp, scale=lg[0:64, 4 + hh:5 + hh]), [LG, CONSTB], [RC])
            qsp = Rot(K, "qs4", [64, 2, 4, 128], BF16, 2)
            st4p = Rot(K, "st4", [128, 4, 128], BF16, 2)
            yvp = Rot(K, "yv4", [64, 4, 128], F32, 2)
            sq4p = Rot(K, "sq4", [64, 4, 128], F32, 2)

            def ret_chunk(c):
                cs = slice(c * 128, (c + 1) * 128)
                ps_, psb = bank()
                for hh in range(4):
                    T(lambda e: e.matmul(ps_[:, hh * 128:(hh + 1) * 128], lhsT=kf[:, hh, cs], rhs=qf[:, hh, cs], start=True, stop=True), [KF, QF], [psb])
                yield
                st, stb = st4p.next()
                V(lambda e: e.tensor_tensor(out=st[:], in0=ps_[:, 0:512].rearrange("p (h t) -> p h t", h=4), in1=Dm[:], op=ALU.mult), [psb, RC], [stb])
                qs, qsb = qsp.next()
                V(lambda e: e.tensor_tensor(out=qs[:], in0=qf[:, :, cs].unsqueeze(1).broadcast_to([64, 2, 4, 128]),
                                            in1=Ef[:].rearrange("p (d h) t -> p d h t", d=2), op=ALU.mult), [QF, RC], [qsb])
                yield
                py, pyb = bank()
                for hh in range(4):
                    T(lambda e: e.matmul(py[0:64, hh * 128:(hh + 1) * 128], lhsT=kvt[:, c, hh * 64:(hh + 1) * 64], rhs=st[:, hh, :],
                                         start=True, stop=False), [KVT[c], stb], [pyb])
                    T(lambda e: e.matmul(py[0:64, hh * 128:(hh + 1) * 128], lhsT=SP[:, c, hh, :], rhs=qs[:, 0, hh, :], start=False, stop=False),
                      [SPB[c][hh], qsb], [pyb])
                    T(lambda e: e.matmul(py[0:64, hh * 128:(hh + 1) * 128], lhsT=SP[:, c, 4 + hh, :], rhs=qs[:, 1, hh, :], start=False, stop=True),
                      [SPB[c][4 + hh], qsb], [pyb])
                yield
                yv, yvb = yvp.next()
                yvf = yv[:].rearrange("p h t -> p (h t)")
                V(lambda e: e.tensor_copy(yvf, py[0:64, 0:512]), [pyb], [yvb])
                yield
                pm, pmb = bank()
                T(lambda e: e.matmul(pm[0:64, 0:512], lhsT=ones_f[0:64, 0:64], rhs=yvf, start=True, stop=True), [yvb, CONSTB], [pmb])
                yield
                V(lambda e: e.scalar_tensor_tensor(out=yvf, in0=pm[0:64, 0:512], scalar=-1.0 / 64, op0=ALU.mult, in1=yvf, op1=ALU.add), [pmb, yvb], [yvb])
                yield
                sq, sqb = sq4p.next()
                sqf = sq[:].rearrange("p h t -> p (h t)")
                A(lambda e: e.activation(out=sqf, in_=yvf, func=AF.Square), [yvb], [sqb])
                yield
                pv_, pvb = bank()
                T(lambda e: e.matmul(pv_[0:64, 0:512], lhsT=ones_f[0:64, 0:64], rhs=sqf, start=True, stop=True), [sqb, CONSTB], [pvb])
                yield
                A(lambda e: e.activation(out=sqf, in_=pv_[0:64, 0:512], func=AF.Sqrt, bias=EPS, scale=1.0 / 64), [pvb], [sqb])
                yield
                V(lambda e: e.reciprocal(sqf, sqf), [sqb], [sqb])
                yield
                V(lambda e: e.tensor_tensor(out=yvf, in0=yvf, in1=sqf, op=ALU.mult), [yvb, sqb], [yvb])
                V(lambda e: e.tensor_tensor(out=yv[:], in0=yv[:], in1=gnw.unsqueeze(2).broadcast_to([64, 4, 128]), op=ALU.mult), [yvb, CONSTB], [yvb])
                yield
                V(lambda e: e.tensor_tensor(out=sgf[:, :, cs], in0=yv[:], in1=sgf[:, :, cs], op=ALU.mult), [yvb, SGF], [SGF])
                yield
            for c0 in range(0, NCH, 2):
                interleave([ret_chunk(c0), ret_chunk(c0 + 1)])
            out_proj([sgf[:, hh, :] for hh in range(4)], [SGF], 512, 64, tile=sgf)
            K.barrier()
            K.release(mk)

        def att(yat, YA):
            mk = K.mark()
            qfm = K.sb("aq", [64, 4, NT], BF16)
            kfm = K.sb("ak", [64, 2, NT], BF16)
            vtm = K.sb("av", [128, NCH, 128], BF16)
            QF, KF = Buf("aq"), Buf("ak")
            VT = bufs(NCH, "av")
            ckf = K.sb("ckf", [64, 2, 512], BF16)
            cvs = K.sb("cvs", [128, 4, 128], BF16)
            CK = Buf("ck")
            K.dma("gpsimd", lambda e: e.dma_start(out=ckf[:], in_=ckT[:, l * 2:l * 2 + 2, :]), writes=[CK], semkey="ldck")
            K.dma("gpsimd", lambda e: e.dma_start(out=cvs[:], in_=cvt[:, l * 4:l * 4 + 4, :]), writes=[CK], semkey="ldck")
            es = K.sb("es", [64, 4])
            A(lambda e: e.activation(out=es[:], in_=fcol("sink", l * 4, 4, rows=64), func=AF.Exp), [CONSTB], [CK])
            mk_in = K.mark()
            qkp = Rot(K, "qk", [128, 512], F32, 3)
            qnp = Rot(K, "qn", [128, 384], F32, 3)
            qbp = Rot(K, "qb", [128, 384], BF16, 3)
            smp = Rot(K, "sm", [128, 8], F32, 3)
            rtp = Rot(K, "rt", [128, 6, 2, 16], F32, 8)
            kvo = Rot(K, "kvo", [128, 256], F32, 3)

            def inproj(hbuf, HB):
                wa, wab = load_wa(2824, 512)

                def in_chunk(tc):
                    tt = 0 if tc < 4 else (1 if tc < 8 else 2)
                    pb, pbb = bank()
                    for k in range(8):
                        T(lambda e: e.matmul(pb[:, 0:512], lhsT=hbuf[:, k, tc * 128:(tc + 1) * 128], rhs=wa[:, k, :], start=(k == 0), stop=(k == 7)),
                          [HB[tt], wab], [pbb])
                    yield
                    qk, qkb = qkp.next()
                    A(lambda e: e.activation(out=qk[:], in_=pb[:, 0:512], func=AF.Copy), [pbb], [qkb])
                    yield
                    qn, qnb = qnp.next()
                    sm, smb = smp.next()
                    A(lambda e: e.activation(out=qn[:], in_=qk[:, 0:384], func=AF.Square), [qkb], [qnb])
                    A(lambda e: e.activation(out=vtm[:, tc, :], in_=qk[:, 384:512], func=AF.Copy), [qkb], [VT[tc]])
                    yield
                    V(lambda e: e.tensor_reduce(out=sm[:, 0:6], in_=qn[:].rearrange("p (h d) -> p h d", d=64), axis=AX.X, op=ALU.add), [qnb], [smb])
                    yield
                    A(lambda e: e.activation(out=sm[:, 0:6], in_=sm[:, 0:6], func=AF.Sqrt, bias=EPS, scale=1.0 / 64), [smb], [smb])
                    yield
                    V(lambda e: e.reciprocal(sm[:, 0:6], sm[:, 0:6]), [smb], [smb])
                    yield
                    V(lambda e: e.tensor_tensor(out=qn[:].rearrange("p (h d) -> p h d", d=64), in0=qk[:, 0:384].rearrange("p (h d) -> p h d", d=64),
                                                in1=sm[:, 0:6].unsqueeze(2).broadcast_to([128, 6, 64]), op=ALU.mult), [qkb, smb], [qnb])
                    yield
                    V(lambda e: e.tensor_tensor(out=qn[:], in0=qn[:], in1=fcol("qkw", l * 384, 384), op=ALU.mult), [qnb, CONSTB], [qnb])
                    yield
                    qv = qn[:].rearrange("p (h a b f) -> p h a b f", h=6, a=2, b=2)
                    cosv = ccol("cos", tc * 32, 32).rearrange("p (a f) -> p a f", a=2).unsqueeze(1).broadcast_to([128, 6, 2, 16])
                    sinv = ccol("sin", tc * 32, 32).rearrange("p (a f) -> p a f", a=2).unsqueeze(1).broadcast_to([128, 6, 2, 16])
                    x1 = qv[:, :, :, 0, :]
                    x2 = qv[:, :, :, 1, :]
                    t1, t1b = rtp.next()
                    t2, t2b = rtp.next()
                    t3, t3b = rtp.next()
                    t4, t4b = rtp.next()
                    V(lambda e: e.tensor_tensor(out=t1[:], in0=x1, in1=cosv, op=ALU.mult), [qnb, CONSTB], [t1b])
                    V(lambda e: e.tensor_tensor(out=t3[:], in0=x1, in1=sinv, op=ALU.mult), [qnb, CONSTB], [t3b])
                    yield
                    V(lambda e: e.tensor_tensor(out=t2[:], in0=x2, in1=sinv, op=ALU.mult), [qnb, CONSTB], [t2b])
                    V(lambda e: e.tensor_tensor(out=t4[:], in0=x2, in1=cosv, op=ALU.mult), [qnb, CONSTB], [t4b])
                    yield
                    V(lambda e: e.tensor_tensor(out=x1, in0=t1[:], in1=t2[:], op=ALU.subtract), [t1b, t2b], [qnb])
                    yield
                    V(lambda e: e.tensor_tensor(out=x2, in0=t3[:], in1=t4[:], op=ALU.add), [t3b, t4b], [qnb])
                    yield
                    kslot = kvo.i
                    ko, kob = kvo.next()
                    V(lambda e: e.tensor_copy(ko[:, 0:128], qn[:, 256:384]), [qnb], [kob])
                    V(lambda e: e.tensor_copy(ko[:, 128:256], qk[:, 384:512]), [qkb], [kob])
                    qb, qbb = qbp.next()
                    A(lambda e: e.activation(out=qb[:], in_=qn[:], func=AF.Copy), [qnb], [qbb])
                    yield
                    K.dma("sync", lambda e: e.dma_start(out=nk_o[l, tc * 128:(tc + 1) * 128, :], in_=ko[:, 0:128]), reads=[kob], semkey=f"stk{kslot}")
                    K.dma("sync", lambda e: e.dma_start(out=nv_o[l, tc * 128:(tc + 1) * 128, :], in_=ko[:, 128:256]), reads=[kob], semkey=f"stk{kslot}")
                    pt, ptb = bank()
                    ptv = pt[:].bitcast(BF16)
                    for hh in range(6):
                        T(lambda e: e.transpose(ptv[0:64, hh * 128:(hh + 1) * 128], qb[:, hh * 64:(hh + 1) * 64], ident_b), [qbb, CONSTB], [ptb])
                    yield
                    V(lambda e: e.tensor_copy(qfm[:, :, tc * 128:(tc + 1) * 128], ptv[0:64, 0:512].rearrange("p (h t) -> p h t", h=4)), [ptb], [QF])
                    V(lambda e: e.tensor_copy(kfm[:, :, tc * 128:(tc + 1) * 128], ptv[0:64, 512:768].rearrange("p (h t) -> p h t", h=2)), [ptb], [KF])
                    yield
                for c0 in range(0, NCH, 2):
                    interleave([in_chunk(c0), in_chunk(c0 + 1)])
            with_h(inproj)
            K.release(mk_in)
            ptp = Rot(K, "pt", [128, 7, 2, 128], BF16, 3)
            rcp = Rot(K, "rc", [64, 2, 128], F32, 3)
            am = bcol("am").rearrange("p (c a t) -> p c a t", c=NCH, a=2)
            cfb = ccol("cfb")

            def att_chain(qc, kv):
                qs = slice(qc * 128, (qc + 1) * 128)
                pc_ = max(qc - 1, 0)
                nc_ = min(qc + 1, NCH - 1)
                qrhs = qfm[:, 2 * kv:2 * kv + 2, qs]
                pa, pab = bank()
                pb_, pbb_ = bank()
                for sc in range(4):
                    dst, dstb = (pa, pab) if sc < 2 else (pb_, pbb_)
                    T(lambda e: e.matmul(dst[:, (sc % 2) * 256:(sc % 2) * 256 + 256], lhsT=ckf[:, kv, sc * 128:(sc + 1) * 128], rhs=qrhs, start=True, stop=True),
                      [CK, QF], [dstb])
                yield
                pt_, ptb_ = ptp.next()
                A(lambda e: e.activation(out=pt_[:, 0:2, :, :], in_=pa[:, 0:512].rearrange("p (c h t) -> p c h t", c=2, h=2), func=AF.Exp,
                                         scale=0.125, bias=cfb[:, qc:qc + 1]), [pab, CONSTB], [ptb_])
                A(lambda e: e.activation(out=pt_[:, 2:4, :, :], in_=pb_[:, 0:512].rearrange("p (c h t) -> p c h t", c=2, h=2), func=AF.Exp,
                                         scale=0.125, bias=cfb[:, qc:qc + 1]), [pbb_, CONSTB], [ptb_])
                pc2, pc2b = bank()
                pd2, pd2b = bank()
                for i, kc in enumerate((pc_, nc_, qc)):
                    dst, dstb = (pc2, pc2b) if i < 2 else (pd2, pd2b)
                    T(lambda e: e.matmul(dst[:, (i % 2) * 256:(i % 2) * 256 + 256], lhsT=kfm[:, kv, kc * 128:(kc + 1) * 128], rhs=qrhs, start=True, stop=True),
                      [KF, QF], [dstb])
                yield
                A(lambda e: e.activation(out=pt_[:, 4:6, :, :], in_=pc2[:, 0:512].rearrange("p (c h t) -> p c h t", c=2, h=2), func=AF.Exp, scale=0.125),
                  [pc2b], [ptb_])
                A(lambda e: e.activation(out=pt_[:, 6, :, :], in_=pd2[:, 0:256].rearrange("p (h t) -> p h t", h=2), func=AF.Exp, scale=0.125),
                  [pd2b], [ptb_])
                yield
                V(lambda e: e.tensor_tensor(out=pt_[:, 4:6, :, :], in0=pt_[:, 4:6, :, :], in1=am[:, qc, :, :].unsqueeze(2).broadcast_to([128, 2, 2, 128]), op=ALU.mult),
                  [ptb_, CONSTB], [ptb_])
                yield
                po, pob = bank()
                vlist = [(cvs[:, sc, kv * 64:(kv + 1) * 64], CK) for sc in range(4)] + \
                        [(vtm[:, kc, kv * 64:(kv + 1) * 64], VT[kc]) for kc in (pc_, nc_, qc)]
                for i, (vap, vb) in enumerate(vlist):
                    T(lambda e: e.matmul(po[0:64, 0:256], lhsT=vap, rhs=pt_[:, i, :, :], start=(i == 0), stop=(i == 6)), [vb, ptb_], [pob])
                for i in range(7):
                    T(lambda e: e.matmul(po[0:64, 256:512], lhsT=ones_b[:, 0:64], rhs=pt_[:, i, :, :], start=(i == 0), stop=(i == 6)), [CONSTB, ptb_], [pob])
                yield
                rc, rcb = rcp.next()
                V(lambda e: e.tensor_tensor(out=rc[:], in0=po[0:64, 256:512].rearrange("p (h t) -> p h t", h=2),
                                            in1=es[:, 2 * kv:2 * kv + 2].unsqueeze(2).broadcast_to([64, 2, 128]), op=ALU.add), [pob, CK], [rcb])
                V(lambda e: e.reciprocal(rc[:], rc[:]), [rcb], [rcb])
                V(lambda e: e.tensor_tensor(out=yat[:, 2 * kv:2 * kv + 2, qs], in0=po[0:64, 0:256].rearrange("p (h t) -> p h t", h=2), in1=rc[:], op=ALU.mult),
                  [pob, rcb], [YA])
                yield
            for qc in range(NCH):
                interleave([att_chain(qc, 0), att_chain(qc, 1)])
            out_proj([yat[:, hh, :] for hh in range(4)], [YA], 768, 64, tile=yat)
            K.barrier()
            K.release(mk)

        def hyena(yhy, YHB):
            mk = K.mark()
            vb_ = K.sb("hv", [128, 2, NT], BF16)
            x1f = K.sb("hx1", [128, 2, NT], BF16)
            x2f = K.sb("hx2", [128, 2, NT], BF16)
            HVB, HX1, HX2 = Buf("hv"), Buf("hx1"), Buf("hx2")

            def inproj(hbuf, HB):
                wa0, wab0 = load_wa(1032, 512)
                wa1, wab1 = load_wa(1544, 256)
                dests = [(vb_, HVB), (vb_, HVB), (x1f, HX1), (x1f, HX1), (x2f, HX2), (x2f, HX2)]
                for q in range(6):
                    raw, rawb = cvx["raws"].next()
                    for tt in range(3):
                        pb, pbb = bank()
                        if q < 4:
                            proj_fm(pb, pbb, wa0, wab0, q * 128, 128, hbuf, HB, tt)
                        else:
                            proj_fm(pb, pbb, wa1, wab1, (q - 4) * 128, 128, hbuf, HB, tt)
                        raw_fill(raw, rawb, 128, pb, pbb, tt)
                    pc0 = PT["hy_conv"][0] + (l * 6 + q) * 4
                    dt_, db_ = dests[q]
                    conv_chunk(raw, rawb, 128, pc0, dt_[:, q % 2, :], db_, False)
            with_h(inproj, conv=True)
            FEB = Buf("feats")
            w3s = K.sb("hw3", [64, 1024])
            K.dma("sync", lambda e: e.dma_start(out=w3s[:], in_=hy_w3[l]), writes=[FEB], semkey="ldf")
            h2 = K.sb("hh2", [64, NT])
            H2B = Buf("h2")
            mk2 = K.mark()
            feats = K.sb("feats", [33, NT])
            K.dma("sync", lambda e: e.dma_start(out=feats[:, 0:1024], in_=featsA_d), writes=[FEB], semkey="ldf")
            K.dma("sync", lambda e: e.dma_start(out=feats[:, 1024:1280], in_=featsB_d), writes=[FEB], semkey="ldf")
            w1s = K.sb("hw1", [33, 64])
            w2s = K.sb("hw2", [64, 64])
            K.dma("sync", lambda e: e.dma_start(out=w1s[:], in_=hy_w1[l]), writes=[FEB], semkey="ldf")
            K.dma("sync", lambda e: e.dma_start(out=w2s[:], in_=hy_w2[l]), writes=[FEB], semkey="ldf")
            hp = pcol("hyp", l * 3, 3, rows=64)
            fb = K.sb("fb", [64, 2])
            V(lambda e: e.tensor_tensor(out=fb[:, 0:1], in0=hp[:, 0:1], in1=hp[:, 1:2], op=ALU.mult), [CONSTB], [FEB])
            V(lambda e: e.tensor_tensor(out=fb[:, 1:2], in0=hp[:, 2:3], in1=hp[:, 1:2], op=ALU.mult), [CONSTB], [FEB])
            h1 = K.sb("hh1", [64, NT])
            H1B = Buf("h1")
            MAGIC = 12582912.0
            argp = Rot(K, "arg", [64, 512], F32, 2)
            nrp = Rot(K, "nr", [64, 512], F32, 2)

            def sin_layer(lhsT, src, srcb, KK, dst, dstb, fbcol):
                for tt in range(3):
                    t0, w, cnd = TT[tt]
                    pb, pbb = bank()
                    T(lambda e, pb=pb, t0=t0, w=w: e.matmul(pb[0:64, 0:w], lhsT=lhsT, rhs=src[0:KK, t0:t0 + w], start=True, stop=True), [FEB, srcb], [pbb])
                    ar, arb = argp.next()
                    nr, nrb = nrp.next()
                    V(lambda e, ar=ar, pb=pb, w=w: e.tensor_scalar(out=ar[:, 0:w], in0=pb[0:64, 0:w], scalar1=hp[:, 1:2], scalar2=fb[:, fbcol:fbcol + 1],
                                                                   op0=ALU.mult, op1=ALU.add), [pbb, CONSTB, FEB], [arb])
                    V(lambda e, ar=ar, nr=nr, w=w: e.tensor_scalar(out=nr[:, 0:w], in0=ar[:, 0:w], scalar1=float(1 / (2 * math.pi)), scalar2=MAGIC,
                                                                   op0=ALU.mult, op1=ALU.add), [arb], [nrb])
                    V(lambda e, nr=nr, w=w: e.tensor_scalar(out=nr[:, 0:w], in0=nr[:, 0:w], scalar1=MAGIC, scalar2=None, op0=ALU.subtract), [nrb], [nrb])
                    V(lambda e, ar=ar, nr=nr, w=w: e.scalar_tensor_tensor(out=ar[:, 0:w], in0=nr[:, 0:w], scalar=float(-2 * math.pi), op0=ALU.mult,
                                                                          in1=ar[:, 0:w], op1=ALU.add), [arb, nrb], [arb])
                    V(lambda e, ar=ar, w=w: e.tensor_scalar(out=ar[:, 0:w], in0=ar[:, 0:w], scalar1=3.1415925, scalar2=-3.1415925, op0=ALU.min, op1=ALU.max), [arb], [arb])
                    A(lambda e, ar=ar, t0=t0, w=w: e.activation(out=dst[:, t0:t0 + w], in_=ar[:, 0:w], func=AF.Sin), [arb], [dstb])
            sin_layer(w1s[:], feats, FEB, 33, h1, H1B, 0)
            sin_layer(w2s[:], h1, H1B, 64, h2, H2B, 1)
            K.barrier()
            K.release(mk2)

            gA = K.sb("gA", [128, 2, 2, 7, 256], BF16)
            gB = K.sb("gB", [128, 2, 2, 256], BF16)
            GB_ = Buf("g")
            hfa = K.sb("hfa", [128, 10, 2, 256], BF16)
            HFB = Buf("hf")
            ztm = K.sb("ztm", [128, NCH, 128], BF16)
            ZTB = bufs(NCH, "ztm")
            Yb = K.sb("Yb", [128, 2, 2, 5, 128], BF16)
            YBB = Buf("Yb")
            ytp = Rot(K, "yt", [128, 4, 128], F32, 2)
            tqp = Rot(K, "tqr", [128, 4, 128], F32, 4)
            identr = K.sb("identr", [128, 2, 128])
            IDR = Buf("identr")
            V(lambda e: e.tensor_copy(identr[:, 0, :].bitcast(F32R), ident_b), [CONSTB], [IDR])
            V(lambda e: e.tensor_scalar(out=identr[:, 1, :].bitcast(F32R), in0=ident_b, scalar1=-1.0, scalar2=None, op0=ALU.mult), [CONSTB], [IDR])
            dftb = bcol("dft").rearrange("p (t r f) -> p t r f", t=8, r=2)
            idft = bcol("idft").rearrange("p (a r t) -> p a r t", a=2, r=2)

            decs = K.sb("decs", [128, 20, 128])
            DCB = Buf("decs")
            for o in range(2):
                zin, zinb = (vb_, HVB) if o == 0 else (x1f, HX1)
                gate, gateb = (x1f, HX1) if o == 0 else (x2f, HX2)
                zout, zoutb = (x1f, HX1) if o == 0 else (yhy, YHB)
                for cc in range(2):
                    K.dma("sync", lambda e, cc=cc: e.dma_start(out=decs[:], in_=dec_d[:, :, cc * 128:(cc + 1) * 128]), writes=[DCB], semkey="lddec")
                    for pk in range(10):
                        pb, pbb = bank()
                        pos0 = pk * 128
                        for sd in range(2):
                            wc0 = sd * 512 + o * 256 + cc * 128
                            T(lambda e, pb=pb, sd=sd, wc0=wc0, pos0=pos0: e.matmul(pb[:, sd * 128:(sd + 1) * 128], lhsT=h2[:, pos0:pos0 + 128],
                                                                                   rhs=w3s[:, wc0:wc0 + 128], start=True, stop=True), [H2B, FEB], [pbb])
                        if pk < 8:
                            di = [pk, 8 + pk]
                        else:
                            di = [16 + pk - 8, 18 + pk - 8]
                        for sd in range(2):
                            V(lambda e, pb=pb, sd=sd, pk=pk, di=di, cc=cc: e.tensor_tensor(out=hfa[:, pk, sd, cc * 128:(cc + 1) * 128], in0=pb[:, sd * 128:(sd + 1) * 128],
                                                                                          in1=decs[:, di[sd], :], op=ALU.mult), [pbb, DCB], [HFB])
                for delta in range(-3, 4):
                    ents = spectrum_entries(delta, 8)
                    for fch in range(2):
                        pb, pbb = bank()
                        for r in range(2):
                            for i, (src, k, ty) in enumerate(ents):
                                T(lambda e, pb=pb, r=r, i=i, src=src, k=k, ty=ty, fch=fch: e.matmul(
                                    pb[:, r * 256:(r + 1) * 256], lhsT=dftb[:, ty, r, fch * 128:(fch + 1) * 128], rhs=hfa[:, k, src, :],
                                    start=(i == 0), stop=(i == len(ents) - 1)), [CONSTB, HFB], [pbb])
                        if delta == 0:
                            A(lambda e, pb=pb, fch=fch, delta=delta: e.activation(out=gA[:, fch, :, delta + 3, :], in_=pb[:, 0:512].rearrange("p (r c) -> p r c", r=2),
                                                                                  func=AF.Copy), [pbb], [GB_])
                        else:
                            A(lambda e, pb=pb, fch=fch, delta=delta: e.activation(out=gA[:, fch, :, delta + 3, :], in_=pb[:, 0:512].rearrange("p (r c) -> p r c", r=2),
                                                                                  func=AF.Copy, scale=flag), [pbb, CONSTB], [GB_])
                entsB = spectrum_entries(0, 2)
                for fch in range(2):
                    pb, pbb = bank()
                    for r in range(2):
                        for i, (src, k, ty) in enumerate(entsB):
                            T(lambda e, pb=pb, r=r, i=i, src=src, k=k, ty=ty, fch=fch: e.matmul(
                                pb[:, r * 256:(r + 1) * 256], lhsT=dftb[:, ty, r, fch * 128:(fch + 1) * 128], rhs=hfa[:, 8 + k, src, :],
                                start=(i == 0), stop=(i == len(entsB) - 1)), [CONSTB, HFB], [pbb])
                    A(lambda e, pb=pb, fch=fch: e.activation(out=gB[:, fch, :, :], in_=pb[:, 0:512].rearrange("p (r c) -> p r c", r=2), func=AF.Copy), [pbb], [GB_])
                for cc in range(2):
                    gcs = slice(cc * 128, (cc + 1) * 128)
                    for tc in range(NCH):
                        pb, pbb = bank()
                        pbv = pb[:].bitcast(BF16)
                        T(lambda e, pbv=pbv, tc=tc: e.transpose(pbv[:, 0:128], zin[:, cc, tc * 128:(tc + 1) * 128], ident_b), [zinb, CONSTB], [pbb])
                        A(lambda e, pbv=pbv, tc=tc: e.activation(out=ztm[:, tc, :], in_=pbv[:, 0:128], func=AF.Copy), [pbb], [ZTB[tc]])
                    ty_f = [dft_type(0, 0), dft_type(0, 128)]
                    set_banks(6, 8)
                    for fch in range(2):
                        for r in range(2):
                            for blk in range(NB):
                                if blk < 4:
                                    dst, dstb = PS[r][:, blk * 128:(blk + 1) * 128], PSB[r]
                                else:
                                    dst, dstb = PS[2][:, r * 128:(r + 1) * 128], PSB[2]
                                for tk in range(2):
                                    T(lambda e: e.matmul(dst, lhsT=dftb[:, ty_f[tk], r, fch * 128:(fch + 1) * 128], rhs=ztm[:, 2 * blk + tk, :],
                                                         start=(tk == 0), stop=(tk == 1)), [CONSTB, ZTB[2 * blk + tk]], [dstb])
                        terms = []
                        for delta in [0, 1, -1, 2, -2, 3, -3]:
                            nbk = 4 - abs(delta)
                            tb0 = max(delta, 0)
                            sb0 = tb0 - delta
                            for (acc, gr, zr, sgn) in ((3, 0, 0, 0), (3, 1, 1, 1), (4, 0, 1, 0), (4, 1, 0, 0)):
                                g_ap = gA[:, fch, gr, delta + 3, gcs].unsqueeze(1).broadcast_to([128, nbk, 128])
                                z_ap = PS[zr][:, sb0 * 128:(sb0 + nbk) * 128].rearrange("p (b c) -> p b c", b=nbk)
                                terms.append((acc, tb0 * 128, nbk * 128, sgn, z_ap, PSB[zr], g_ap, nbk))
                        cnt = {3: 0, 4: 0}
                        tot = {3: 14, 4: 14}
                        for (acc, c0, ncol, sgn, z_ap, zb, g_ap, nbk) in terms:
                            tq, tqb = tqp.next()
                            V(lambda e: e.tensor_tensor(out=tq[:, 0:nbk, :].bitcast(F32R), in0=z_ap, in1=g_ap, op=ALU.mult), [zb, GB_], [tqb])
                            T(lambda e: e.matmul(PS[acc][:, c0:c0 + ncol], lhsT=identr[:, sgn, :].bitcast(F32R),
                                                 rhs=tq[:, 0:nbk, :].rearrange("p b c -> p (b c)").bitcast(F32R),
                                                 start=(cnt[acc] == 0), stop=(cnt[acc] == tot[acc] - 1)), [tqb, IDR], [PSB[acc]])
                            cnt[acc] += 1
                        tqs = []
                        for (gr, zr, sgn) in ((0, 0, 0), (1, 1, 1), (0, 1, 0), (1, 0, 0)):
                            tq, tqb = tqp.next()
                            V(lambda e: e.tensor_tensor(out=tq[:, 0, :].bitcast(F32R), in0=PS[2][:, zr * 128:(zr + 1) * 128], in1=gB[:, fch, gr, gcs], op=ALU.mult),
                              [PSB[2], GB_], [tqb])
                            tqs.append((tq, tqb, sgn))
                        for i, (tq, tqb, sgn) in enumerate(tqs):
                            ro = i // 2
                            T(lambda e: e.matmul(PS[5][:, ro * 128:(ro + 1) * 128], lhsT=identr[:, sgn, :].bitcast(F32R), rhs=tq[:, 0, :].bitcast(F32R),
                                                 start=(i % 2 == 0), stop=(i % 2 == 1)), [tqb, IDR], [PSB[5]])
                        for r in range(2):
                            A(lambda e: e.activation(out=Yb[:, fch, r, 0:4, :], in_=PS[3 + r][:, 0:512].rearrange("p (b c) -> p b c", b=4), func=AF.Copy),
                              [PSB[3 + r]], [YBB])
                        A(lambda e: e.activation(out=Yb[:, fch, :, 4, :], in_=PS[5][:, 0:256].rearrange("p (r c) -> p r c", r=2), func=AF.Copy), [PSB[5]], [YBB])
                    set_banks(0, 8)
                    hbcol = pcol("hy_bias", (l * 2 + o) * 2 + cc, 1)
                    for blk in range(NB):
                        pb, pbb = bank()
                        i = 0
                        for fch in range(2):
                            for r in range(2):
                                T(lambda e, pb=pb, fch=fch, r=r, blk=blk, i=i: e.matmul(pb[:, 0:256], lhsT=Yb[:, fch, r, blk, :], rhs=idft[:, fch, r, :],
                                                                                       start=(i == 0), stop=(i == 3)), [YBB, CONSTB], [pbb])
                                i += 1
                        ts = slice(blk * 256, (blk + 1) * 256)
                        tq, tqb = ytp.next()
                        tqv = tq[:].rearrange("p a b -> p (a b)")[:, 0:256]
                        V(lambda e, tqv=tqv, pb=pb, ts=ts: e.scalar_tensor_tensor(out=tqv, in0=zin[:, cc, ts], scalar=hbcol, op0=ALU.mult, in1=pb[:, 0:256], op1=ALU.add),
                          [zinb, CONSTB, pbb], [tqb])
                        V(lambda e, tqv=tqv, ts=ts: e.tensor_tensor(out=zout[:, cc, ts], in0=tqv, in1=gate[:, cc, ts], op=ALU.mult), [tqb, gateb], [zoutb])
            out_proj([yhy[:, 0, :], yhy[:, 1, :]], [YHB], 256, 128)
            K.barrier()
            K.release(mk)

        yhy = K.sb("yhy", [128, 2, NT], BF16)
        YHB = Buf("yhy")
        hyena(yhy, YHB)
        yssd = K.sb("yssd", [64, 4, NT], BF16)
        YSB = Buf("yssd")
        ssd(yssd, YSB)
        yret = K.sb("yret", [64, 4, NT], BF16)
        YRB = Buf("yret")
        ret(yret, YRB)
        yatt = K.sb("yatt", [64, 4, NT], BF16)
        YAB_ = Buf("yatt")
        att(yatt, YAB_)
        out_proj_all()
        K.barrier()
        K.release(mk_all)

    for l in range(nlayers):
        if l == 1:
            assert not pending_mod and not inflight_mod, (pending_mod, inflight_mod)
        ffn(l, 0)
        mixer(l)
        ffn(l, 1)

    K.dma("sync", lambda e: e.dma_start(out=yT, in_=x[:]), reads=[b for row in XB for b in row], semkey="sty")
    K.barrier()
    K.release(0)
    return nc, K


def _consts(is_s):
    f32 = np.float32
    cfv = np.zeros((128, CF.n), f32)

    def put(name, arr):
        c0, w = CF[name]
        cfv[:, c0:c0 + w] = np.asarray(arr, f32).reshape(128, w) if np.asarray(arr).ndim > 1 or w == 1 else np.broadcast_to(np.asarray(arr, f32), (128, w))
    j = np.arange(128)[:, None]
    i = np.arange(128)[None, :]
    put("trif", (i >= j).astype(f32))
    put("trib", (j >= i).astype(f32))
    put("ones", np.ones((128, 128), f32))
    put("relu_f", np.maximum(i - j, 0).astype(f32))
    put("relu_b", np.maximum(j - i, 0).astype(f32))
    put("ip1", np.broadcast_to((i + 1).astype(f32), (128, 128)))
    put("rmi", np.broadcast_to((128 - i).astype(f32), (128, 128)))
    put("tailf", (127 - j).astype(f32))
    put("tailb", j.astype(f32))
    nf = 16
    inv = (10000.0 ** (-np.arange(nf, dtype=f32) / nf)).astype(f32)
    cos = np.ones((NT, 32), f32)
    sin = np.zeros((NT, 32), f32)
    if is_s:
        t = np.arange(1024)
        rows = (t // 64).astype(f32)
        cols = (t % 64).astype(f32)
        angr = rows[:, None] * inv[None, :]
        angc = cols[:, None] * inv[None, :]
        cos[:1024, 0:16] = np.cos(angr)
        cos[:1024, 16:32] = np.cos(angc)
        sin[:1024, 0:16] = np.sin(angr)
        sin[:1024, 16:32] = np.sin(angc)
    put("cos", cos.reshape(NCH, 128, 32).transpose(1, 0, 2).reshape(128, 320))
    put("sin", sin.reshape(NCH, 128, 32).transpose(1, 0, 2).reshape(128, 320))
    cfb = np.full((NCH,), -30000.0, f32)
    hl = np.zeros((5,), f32)
    hr = np.zeros((5,), f32)
    if is_s:
        cfb[0:8] = 0.0
        hl[1:4] = 1.0
        hr[0:3] = 1.0
    put("cfb", cfb)
    put("hl", hl)
    put("hr", hr)
    put("flag", np.full((128, 1), 1.0 if is_s else 0.0, f32))
    put("negpi", np.full((128, 1), -math.pi, f32))
    put("nbf", ((i >= j).astype(f32) - 1.0) * 30000.0)
    put("nbb", ((j >= i).astype(f32) - 1.0) * 30000.0)

    cbv = np.zeros((128, CB.n), f32)

    def putb(name, arr):
        c0, w = CB[name]
        cbv[:, c0:c0 + w] = np.asarray(arr, f32).reshape(128, w)
    putb("ident", np.eye(128, dtype=f32))
    putb("ones", np.ones((128, 128), f32))
    am = np.zeros((128, NCH, 2, 128), f32)
    band_prev = (j >= i).astype(f32)
    band_next = (j <= i).astype(f32)
    for qc in range(NCH):
        if is_s and qc < 8:
            if qc >= 1:
                am[:, qc, 0, :] = band_prev
            if qc <= 6:
                am[:, qc, 1, :] = band_next
        else:
            if qc % 2 == 1:
                am[:, qc, 0, :] = 1.0
            else:
                am[:, qc, 1, :] = 1.0
    putb("am", am)
    om = 2 * np.pi * (np.arange(256) + 0.5) / 512.0
    row = np.arange(128)
    dft = np.zeros((128, 8, 2, 256), np.float64)
    for ty in range(8):
        if ty < 4:
            e = FWD_E0[ty] + row
        else:
            e = BWD_E0[ty - 4] - row
        valid = (np.abs(e) <= 255).astype(np.float64)
        ang = e[:, None] * om[None, :]
        dft[:, ty, 0, :] = np.cos(ang) * valid[:, None]
        dft[:, ty, 1, :] = -np.sin(ang) * valid[:, None]
    putb("dft", dft)
    tt = np.arange(256)
    idft = np.zeros((128, 2, 2, 256), np.float64)
    for fch in range(2):
        omf = om[fch * 128:(fch + 1) * 128]
        ang = omf[:, None] * tt[None, :]
        idft[:, fch, 0, :] = (2.0 / 512) * np.cos(ang)
        idft[:, fch, 1, :] = -(2.0 / 512) * np.sin(ang)
    putb("idft", idft)
    return cfv, cbv


def _hy_consts(LA):
    f32 = np.float32
    l = LA
    pos = np.arange(l, dtype=f32)
    t = pos / f32(l - 1)
    bands = np.linspace(1e-4, 15, 16, dtype=f32)
    ang = (f32(2.0 * math.pi / l)) * pos[:, None] * bands[None, :]
    feats = np.concatenate([t[:, None], np.cos(ang), -np.sin(ang)], axis=-1).astype(f32)
    max_decay = math.log(1e-2) / 0.3
    min_decay = math.log(1e-2) / 1.5
    deltas = np.abs(np.linspace(min_decay, max_decay, 256, dtype=f32))
    dec = np.exp(-t[:, None] * deltas[None, :]).astype(f32)
    return feats, dec


def _prepare(inp):
    f32 = np.float32
    g = lambda k: np.asarray(inp[k], dtype=f32)
    x_prompt, x_sample = g("x_prompt"), g("x_sample")
    cache_k, cache_v = g("cache_k"), g("cache_v")
    state_ssd, state_ret = g("state_ssd"), g("state_ret")
    c, c_ctx = g("c"), g("c_ctx")
    pt = np.zeros((128, PT.n), f32)

    def putp(name, off, arr):
        arr = np.asarray(arr, f32)
        c0 = PT[name][0] + off
        pt[0:arr.shape[0], c0:c0 + arr.shape[1]] = arr
    nw = g("norm_w")
    for l in range(2):
        for j in range(3):
            putp("norm_w", (l * 3 + j) * 8, nw[l, j].reshape(8, 128).T)
        putp("b_mod", l * 72, g("b_mod")[l].reshape(72, 128).T)
        cw, cbias = g("ssd_conv_w")[l], g("ssd_conv_b")[l]
        for q in range(8):
            if q < 4:
                f0, fs = q * 64, 64
            else:
                f0, fs = 256 + (q - 4) * 128, 128
            arr = np.stack([cw[0, f0:f0 + fs], cw[1, f0:f0 + fs], cw[2, f0:f0 + fs], cbias[f0:f0 + fs]], axis=1)
            putp("ssd_conv", (l * 8 + q) * 4, arr)
        for j in range(2):
            f0 = j * 128
            arr = np.stack([cw[0, f0:f0 + 128], cw[1, f0:f0 + 128], cw[2, f0:f0 + 128], cbias[f0:f0 + 128]], axis=1)
            putp("ssd_convp", (l * 2 + j) * 4, arr)
        hw, hb = g("hy_conv_w")[l], g("hy_conv_b")[l]
        for q in range(6):
            f0 = q * 128
            arr = np.stack([hw[0, f0:f0 + 128], hw[1, f0:f0 + 128], hw[2, f0:f0 + 128], hb[f0:f0 + 128]], axis=1)
            putp("hy_conv", (l * 6 + q) * 4, arr)
        hbias = g("hy_bias")[l]
        for o in range(2):
            putp("hy_bias", (l * 2 + o) * 2, hbias[o].reshape(2, 128).T)
        putp("ssd_nw", l * 4, g("ssd_norm_w")[l].reshape(4, 64).T)
        putp("ssd_d", l * 4, np.broadcast_to(g("ssd_d")[l][None, :], (64, 4)))
        putp("ret_gn", l * 4, g("ret_gn_w")[l].reshape(4, 64).T)
        putp("hyp", l * 3, np.stack([g("hy_b1")[l], g("hy_freq")[l], g("hy_b2")[l]], axis=1))
    ft = np.zeros((128, FT.n), f32)

    def putf(name, off, vec):
        vec = np.asarray(vec, f32).reshape(-1)
        c0 = FT[name][0] + off
        ft[:, c0:c0 + vec.size] = vec[None, :]
    for l in range(2):
        putf("dt_bias", l * 80, np.tile(g("ssd_dt_bias")[l].reshape(8), NCH))
        putf("a_log", l * 80, np.tile(g("ssd_a_log")[l].reshape(8), NCH))
        putf("ret_logit", l * 8, g("ret_decay_logit")[l].reshape(8))
        putf("qkw", l * 384, np.concatenate([np.tile(g("attn_q_norm")[l], 4), np.tile(g("attn_k_norm")[l], 2)]))
        putf("sink", l * 4, g("attn_sink")[l])
    featsB, decB = _hy_consts(256)
    featsA_s, decA_s = _hy_consts(1024)
    wm = g("w_mod").reshape(2, 8, 128, 18, 512).transpose(0, 3, 2, 1, 4).reshape(2, 18, 128, 4096)
    wi = g("ffn_w_in").reshape(2, 2, 8, 128, 2, 11, 256)
    wi = wi.transpose(0, 1, 5, 3, 2, 4, 6).reshape(2, 2, 11, 128, 4096)
    wo = g("ffn_w_out").reshape(2, 2, 11, 2, 128, 1024).transpose(0, 1, 2, 4, 3, 5).reshape(2, 2, 11, 128, 2048)
    shared = dict(ptab=pt, ftab=ft, w_mod=np.ascontiguousarray(wm), ffn_w_in=np.ascontiguousarray(wi), ffn_w_out=np.ascontiguousarray(wo), mix_w_in=g("mix_w_in"),
                  mix_w_out=g("mix_w_out"), hy_w1=g("hy_w1"), hy_w2=g("hy_w2"), hy_w3=g("hy_w3"),
                  featsB=np.ascontiguousarray(featsB.T))
    consts = {True: _consts(True), False: _consts(False)}
    in_maps = []
    plan = []
    for cid in range(NCORE):
        is_s = cid < 2
        if is_s:
            xs = np.concatenate([x_sample[cid], x_prompt[30 + cid]], axis=0)
            seqs = [30 + cid]
            condA = c[cid]
        else:
            seqs = list(range(5 * (cid - 2), 5 * (cid - 2) + 5))
            xs = x_prompt[seqs].reshape(NT, D)
            condA = c_ctx
        plan.append((is_s, seqs))
        xTm = np.ascontiguousarray(xs.T.reshape(8, 128, NT).transpose(1, 0, 2))
        cond = np.stack([condA, c_ctx], axis=-1)
        condTm = np.ascontiguousarray(cond.reshape(8, 128, 2).transpose(1, 0, 2))
        s0s = np.zeros((2, 5, 2, 4, 128, 64), f32)
        s0r = np.zeros((2, 5, 2, 4, 64, 64), f32)
        ck = np.zeros((2, 2, 64, 512), f32)
        cv = np.zeros((2, 512, 128), f32)
        if is_s:
            for l in range(2):
                s0s[l, 0, 0] = state_ssd[cid, l, 0]
                s0s[l, 3, 1] = state_ssd[cid, l, 1]
                s0r[l, 0, 0] = state_ret[cid, l, 0]
                s0r[l, 3, 1] = state_ret[cid, l, 1]
                ck[l] = cache_k[cid, l].transpose(1, 2, 0)
                cv[l] = cache_v[cid, l].reshape(512, 128)
            featsA = featsA_s
            decFA = decA_s.copy()
        else:
            featsA = np.zeros((1024, 33), f32)
            featsA[:256] = featsB
            decFA = np.zeros((1024, 256), f32)
            decFA[:256] = decB
        decBA = decFA.copy()
        decBA[0] = 0.0
        decFB = decB.copy()
        decBB = decB.copy()
        decBB[0] = 0.0
        dec = np.concatenate([decFA.reshape(8, 128, 256), decBA.reshape(8, 128, 256), decFB.reshape(2, 128, 256), decBB.reshape(2, 128, 256)], axis=0)
        cfv, cbv = consts[is_s]
        m = dict(shared)
        m.update(xT=xTm, condT=condTm,
                 s0ssd=np.ascontiguousarray(s0s.reshape(80, 128, 64).transpose(1, 0, 2)),
                 s0ret=np.ascontiguousarray(s0r.reshape(80, 64, 64).transpose(1, 0, 2)),
                 ckT=np.ascontiguousarray(ck.reshape(4, 64, 512).transpose(1, 0, 2)),
                 cvt=np.ascontiguousarray(cv.reshape(8, 128, 128).transpose(1, 0, 2)),
                 cf32=cfv, cbf=cbv, featsA=np.ascontiguousarray(featsA.T), dec=np.ascontiguousarray(dec.transpose(1, 0, 2)))
        in_maps.append(m)
    return in_maps, plan


_CACHE = {}


def kernel(**inputs):
    in_maps, plan = _prepare(inputs)
    if "nc" not in _CACHE:
        _CACHE["nc"] = build_program()[0]
    nc = _CACHE["nc"]
    res = run_bass_kernel_spmd(nc, in_maps, core_ids=list(range(NCORE)))
    f32 = np.float32
    y_prompt = np.zeros((32, 256, D), f32)
    y_sample = np.zeros((2, 1024, D), f32)
    nck = np.zeros((32, 2, 256, 2, 64), f32)
    ncv = np.zeros((32, 2, 256, 2, 64), f32)
    nssd = np.zeros((32, 2, 2, 4, 128, 64), f32)
    nret = np.zeros((32, 2, 2, 4, 64, 64), f32)
    for cid, (is_s, seqs) in enumerate(plan):
        r = res.results[cid]
        y = np.asarray(r["yT"]).transpose(1, 0, 2).reshape(D, NT).T
        nk = np.asarray(r["nk"])
        nv = np.asarray(r["nv"])
        ss = np.asarray(r["nssd"]).transpose(1, 0, 2).reshape(2, 5, 2, 4, 128, 64)
        sr = np.asarray(r["nret"]).transpose(1, 0, 2).reshape(2, 5, 2, 4, 64, 64)
        if is_s:
            y_sample[cid] = y[:1024]
            blks = [(4, seqs[0])]
        else:
            blks = list(enumerate(seqs))
        for blk, b in blks:
            ts = slice(blk * 256, (blk + 1) * 256)
            y_prompt[b] = y[ts]
            for l in range(2):
                nck[b, l] = nk[l, ts].reshape(256, 2, 64)
                ncv[b, l] = nv[l, ts].reshape(256, 2, 64)
                nssd[b, l] = ss[l, blk]
                nret[b, l] = sr[l, blk]
    return (y_prompt, y_sample, nck, ncv, nssd, nret)
```

```python
import math
import numpy as np
import concourse.bass as bass
import concourse.mybir as mybir
from concourse.bass_utils import run_bass_kernel_spmd

F32 = mybir.dt.float32
BF16 = mybir.dt.bfloat16
F32R = mybir.dt.float32r
AF = mybir.ActivationFunctionType
ALU = mybir.AluOpType
AX = mybir.AxisListType

NCORE = 8
NT = 1280
NB = 5
NCH = 10
TT = [(0, 512, 0), (512, 512, 0), (1024, 256, 1)]
D = 1024
DFF = 2816
EPS = 1e-6
ENGS = ["tensor", "vector", "scalar", "gpsimd", "sync"]
SAME_ENGINE_NOSYNC = ("tensor",)


class Buf:
    __slots__ = ("name", "w", "r", "excl")

    def __init__(self, name="", excl=False):
        self.name = name
        self.w = None
        self.r = {}
        self.excl = excl


def bufs(n, name=""):
    return [Buf(name + str(i)) for i in range(n)]


class KB:
    def __init__(self, nc):
        self.nc = nc
        self.cnt = {e: 0 for e in ENGS}
        self.seen = {e: {} for e in ENGS}
        self.sems = {}
        self.dcount = {}
        self._stack = []
        self.n_inst = 0
        self.uid = 0

    def enter(self, cm):
        v = cm.__enter__()
        self._stack.append(cm)
        return v

    def mark(self):
        return len(self._stack)

    def release(self, m):
        while len(self._stack) > m:
            self._stack.pop().__exit__(None, None, None)

    def sem(self, key):
        if key not in self.sems:
            self.sems[key] = self.enter(self.nc.semaphore("s_" + key))
        return self.sems[key]

    def sb(self, name, shape, dt=F32):
        self.uid += 1
        return self.enter(self.nc.sbuf_tensor(f"{name}_{self.uid}", list(shape), dt))

    def ps(self, name, shape, dt=F32):
        return self.enter(self.nc.psum_tensor(name, list(shape), dt))

    @staticmethod
    def _flat(bl):
        out = []
        for b in bl:
            if isinstance(b, (list, tuple)):
                out.extend(KB._flat(b))
            else:
                out.append(b)
        return out

    def _need(self, eng, reads, writes):
        need = {}

        def add(k, v):
            if need.get(k, 0) < v:
                need[k] = v
        for b in reads:
            if b.w is not None:
                add(*b.w)
        for b in writes:
            if b.w is not None:
                add(*b.w)
            for k, v in b.r.items():
                add(k, v)
        out = []
        for k, v in need.items():
            if k == "p_" + eng and eng in SAME_ENGINE_NOSYNC:
                continue
            if self.seen[eng].get(k, 0) >= v:
                continue
            self.seen[eng][k] = v
            out.append((k, v))
        return out

    def _emit(self, eng, waits, fn, key, inc):
        e = getattr(self.nc, eng)
        for k, v in waits:
            e.wait_ge(self.sems[k], v)
        if fn is not None:
            fn(e).then_inc(self.sems[key], inc)

    def op(self, eng, fn, reads=(), writes=()):
        reads = self._flat(reads)
        writes = self._flat(writes)
        ex = [b for b in reads if b.excl]
        if ex:
            reads = [b for b in reads if not b.excl]
            writes = writes + [b for b in ex if b not in writes]
        waits = self._need(eng, reads, writes)
        key = "p_" + eng
        self.sem(key)
        self.cnt[eng] += 1
        val = self.cnt[eng]
        for b in reads:
            if b.r.get(key, 0) < val:
                b.r[key] = val
        for b in writes:
            b.w = (key, val)
            b.r = {}
        self._emit(eng, waits, fn, key, 1)
        self.n_inst += 1

    def dma(self, eng, fn, reads=(), writes=(), semkey=None):
        reads = self._flat(reads)
        writes = self._flat(writes)
        waits = self._need(eng, reads, writes)
        self.sem(semkey)
        self.dcount[semkey] = self.dcount.get(semkey, 0) + 16
        val = self.dcount[semkey]
        for b in reads:
            if b.r.get(semkey, 0) < val:
                b.r[semkey] = val
        for b in writes:
            b.w = (semkey, val)
            b.r = {}
        self._emit(eng, waits, fn, semkey, 16)
        self.n_inst += 1

    def barrier(self):
        tot = [("p_" + e, self.cnt[e]) for e in ENGS if self.cnt[e] > 0]
        tot += list(self.dcount.items())
        for eng in ENGS:
            waits = []
            for k, v in tot:
                if k == "p_" + eng and eng == "tensor":
                    continue
                if self.seen[eng].get(k, 0) >= v:
                    continue
                self.seen[eng][k] = v
                waits.append((k, v))
            self._emit(eng, waits, None, None, 0)

    def V(self, fn, r=(), w=()):
        self.op("vector", fn, r, w)

    def A(self, fn, r=(), w=()):
        self.op("scalar", fn, r, w)

    def T(self, fn, r=(), w=()):
        self.op("tensor", fn, r, w)

    def G(self, fn, r=(), w=()):
        self.op("gpsimd", fn, r, w)


class Rot:
    def __init__(self, K, name, shape, dt, n):
        self.t = [K.sb(f"{name}{i}", shape, dt) for i in range(n)]
        self.b = bufs(n, name)
        self.i = 0

    def next(self):
        i = self.i
        self.i = (i + 1) % len(self.t)
        return self.t[i], self.b[i]


class Cols:
    def __init__(self):
        self.m = {}
        self.n = 0

    def add(self, name, w):
        self.m[name] = (self.n, w)
        self.n += w

    def __getitem__(self, name):
        return self.m[name]


def ptab_cols():
    c = Cols()
    c.add("norm_w", 48)
    c.add("b_mod", 144)
    c.add("ssd_conv", 64)
    c.add("hy_conv", 48)
    c.add("hy_bias", 8)
    c.add("ssd_nw", 8)
    c.add("ssd_d", 8)
    c.add("ret_gn", 8)
    c.add("hyp", 6)
    c.add("ssd_convp", 16)
    return c


def ftab_cols():
    c = Cols()
    c.add("dt_bias", 160)
    c.add("a_log", 160)
    c.add("ret_logit", 16)
    c.add("qkw", 768)
    c.add("sink", 8)
    return c


def cf32_cols():
    c = Cols()
    c.add("trif", 128)
    c.add("trib", 128)
    c.add("ones", 128)
    c.add("relu_f", 128)
    c.add("relu_b", 128)
    c.add("ip1", 128)
    c.add("rmi", 128)
    c.add("tailf", 1)
    c.add("tailb", 1)
    c.add("cos", 320)
    c.add("sin", 320)
    c.add("cfb", 10)
    c.add("hl", 5)
    c.add("hr", 5)
    c.add("flag", 1)
    c.add("negpi", 1)
    c.add("nbf", 128)
    c.add("nbb", 128)
    return c


def cbf_cols():
    c = Cols()
    c.add("ident", 128)
    c.add("ones", 128)
    c.add("am", 2560)
    c.add("dft", 4096)
    c.add("idft", 1024)
    return c


PT = ptab_cols()
FT = ftab_cols()
CF = cf32_cols()
CB = cbf_cols()

FWD_E0 = [-256, -128, 0, 128]
BWD_E0 = [256, 128, 0, -128]


def dft_type(src, e0):
    return (FWD_E0.index(e0) if src == 0 else 4 + BWD_E0.index(e0))


def spectrum_entries(delta, nchunks):
    out = []
    for k in range(nchunks):
        e0 = 128 * k - 256 * delta
        if e0 in FWD_E0:
            out.append((0, k, dft_type(0, e0)))
        e0b = -128 * k - 256 * delta
        if e0b in BWD_E0:
            out.append((1, k, dft_type(1, e0b)))
    return out


def build_program(nlayers=2):
    nc = bass.Bass("TRN2", target_bir_lowering=False)

    def din(name, shape):
        return nc.dram_tensor(name, list(shape), F32, kind="ExternalInput").ap()

    def dout(name, shape):
        return nc.dram_tensor(name, list(shape), F32, kind="ExternalOutput").ap()

    xT = din("xT", [128, 8, NT])
    condT = din("condT", [128, 8, 2])
    s0ssd = din("s0ssd", [128, 80, 64])
    s0ret = din("s0ret", [64, 80, 64])
    ckT = din("ckT", [64, 4, 512])
    cvt = din("cvt", [128, 8, 128])
    ptab_d = din("ptab", [128, PT.n])
    ftab_d = din("ftab", [128, FT.n])
    cf32_d = din("cf32", [128, CF.n])
    cbf_d = din("cbf", [128, CB.n])
    featsA_d = din("featsA", [33, 1024])
    featsB_d = din("featsB", [33, 256])
    dec_d = din("dec", [128, 20, 256])
    w_mod = din("w_mod", [2, 18, 128, 4096])
    ffn_w_in = din("ffn_w_in", [2, 2, 11, 128, 4096])
    ffn_w_out = din("ffn_w_out", [2, 2, 11, 128, 2048])
    mix_w_in = din("mix_w_in", [2, D, 3336])
    mix_w_out = din("mix_w_out", [2, D, D])
    hy_w1 = din("hy_w1", [2, 33, 64])
    hy_w2 = din("hy_w2", [2, 64, 64])
    hy_w3 = din("hy_w3", [2, 64, 1024])

    yT = dout("yT", [128, 8, NT])
    nk_o = dout("nk", [2, NT, 128])
    nv_o = dout("nv", [2, NT, 128])
    nssd_o = dout("nssd", [128, 80, 64])
    nret_o = dout("nret", [64, 80, 64])

    K = KB(nc)
    V, A, T, G = K.V, K.A, K.T, K.G

    x = K.sb("x", [128, 8, NT])
    XB = [[Buf(f"x{m}_{t}") for t in range(3)] for m in range(8)]
    modt = K.sb("modt", [128, 2, 2, 72])
    MODB = Buf("mod")
    ptab = K.sb("ptab", [128, PT.n])
    ftab = K.sb("ftab", [128, FT.n])
    cf = K.sb("cf", [128, CF.n])
    cb = K.sb("cb", [128, CB.n], BF16)
    CONSTB = Buf("const")
    CONSTB2 = Buf("constb")
    CONSTB3 = Buf("constb_late")
    WA = [K.sb(f"WA{i}", [128, 8, 512], BF16) for i in range(2)]
    WAB = bufs(2, "WA")
    WB = [K.sb(f"WB{i}", [128, 4096], BF16) for i in range(2)]
    WBB = bufs(2, "WB")
    wctr = {"a": 0, "b": 0}
    PS = [K.ps(f"ps{i}", [128, 512]) for i in range(8)]
    PSBK = [Buf(f"psb{i}", excl=True) for i in range(8)]
    PSR = [PSBK[i // 4] for i in range(32)]
    PSB = [[PSBK[i]] for i in range(8)]
    psctr = [0, 0, 8]
    prc = [0]

    def bank():
        lo, hi = psctr[1], psctr[2]
        i = psctr[0]
        if i < lo or i >= hi:
            i = lo
        psctr[0] = i + 1 if i + 1 < hi else lo
        return PS[i], PSB[i]

    def set_banks(lo, hi):
        psctr[1], psctr[2] = lo, hi

    def pr(ncols):
        n = (ncols + 127) // 128
        i = prc[0]
        if (i % 4) + n > 4:
            i = (i // 4 + 1) * 4
        if i + n > 32:
            i = 0
        prc[0] = (i + n) % 32
        b, r = divmod(i, 4)
        return PS[b][:, r * 128:r * 128 + n * 128], [PSBK[b]]

    def interleave(gens):
        gens = list(gens)
        while gens:
            nxt = []
            for g in gens:
                try:
                    next(g)
                    nxt.append(g)
                except StopIteration:
                    pass
            gens = nxt

    def nextA():
        i = wctr["a"] % 2
        wctr["a"] += 1
        return WA[i], WAB[i], f"wa{i}"

    def nextB():
        i = wctr["b"] % 2
        wctr["b"] += 1
        return WB[i], WBB[i], f"wb{i}"

    def pcol(name, off=0, w=1, rows=128):
        c0 = PT[name][0] + off
        return ptab[0:rows, c0:c0 + w]

    def fcol(name, off=0, w=1, rows=128):
        c0 = FT[name][0] + off
        return ftab[0:rows, c0:c0 + w]

    def ccol(name, off=0, w=None, rows=128):
        c0, ww = CF[name]
        if w is None:
            w = ww
        return cf[0:rows, c0 + off:c0 + off + w]

    def bcol(name, off=0, w=None, rows=128):
        c0, ww = CB[name]
        if w is None:
            w = ww
        return cb[0:rows, c0 + off:c0 + off + w]

    K.dma("sync", lambda e: e.dma_start(out=x[:], in_=xT), writes=[b for row in XB for b in row], semkey="ldx")
    K.dma("sync", lambda e: e.dma_start(out=ptab[:], in_=ptab_d), writes=[CONSTB], semkey="ldc")
    K.dma("sync", lambda e: e.dma_start(out=ftab[:], in_=ftab_d), writes=[CONSTB], semkey="ldc")
    K.dma("sync", lambda e: e.dma_start(out=cf[:], in_=cf32_d), writes=[CONSTB], semkey="ldc")
    K.dma("gpsimd", lambda e: e.dma_start(out=cb[:, 0:256], in_=cbf_d[:, 0:256]), writes=[CONSTB2], semkey="ldcb")
    deferred_loads = [lambda: K.dma("gpsimd", lambda e: e.dma_start(out=cb[:, 256:CB.n], in_=cbf_d[:, 256:CB.n], max_dma_last_dim=2048),
                                    writes=[CONSTB3], semkey="ldcb2")]

    K.barrier()

    flag = ccol("flag")
    ident_b = bcol("ident")
    ones_b = bcol("ones")
    ones_f = ccol("ones")
    trif = ccol("trif")
    trib = ccol("trib")

    condf = K.sb("condf", [128, 8, 2])
    condb = K.sb("condb", [128, 8, 2], BF16)
    CB_ = Buf("cond")
    MODL = [Buf("mod0"), Buf("mod1")]
    K.dma("sync", lambda e: e.dma_start(out=condf[:], in_=condT), writes=[CB_], semkey="ldcond")
    A(lambda e: e.activation(out=condb[:], in_=condf[:], func=AF.Silu), [CB_], [CB_])

    def mod_dma(l, ci, wa, wab, sk):
        K.dma("gpsimd", lambda e: e.dma_start(out=wa[:].rearrange("p k n -> p (k n)"), in_=w_mod[l, ci], max_dma_last_dim=8192),
              writes=[wab], semkey=sk)

    def mod_chunk(l, ci, wa, wab, sk, dma=True):
        if dma:
            mod_dma(l, ci, wa, wab, sk)
        pb, pbb = bank()
        for mb in range(4):
            for k in range(8):
                T(lambda e: e.matmul(pb[:, mb * 2:mb * 2 + 2], lhsT=wa[:, k, mb * 128:(mb + 1) * 128], rhs=condb[:, k, :],
                                     start=(k == 0), stop=(k == 7)), [wab, CB_], [pbb])
        for cnd in range(2):
            V(lambda e: e.tensor_tensor(out=modt[:, l, cnd, ci * 4:ci * 4 + 4], in0=pb[:, 0:8].rearrange("p (m c) -> p m c", c=2)[:, :, cnd],
                                        in1=pcol("b_mod", l * 72 + ci * 4, 4), op=ALU.add), [pbb, CONSTB], [MODL[l]])

    def mod_finish_j(l, j):
        for cnd in range(2):
            V(lambda e: e.scalar_tensor_tensor(
                out=modt[:, l, cnd, (3 * j + 1) * 8:(3 * j + 2) * 8], in0=modt[:, l, cnd, (3 * j + 1) * 8:(3 * j + 2) * 8],
                scalar=1.0, op0=ALU.add, in1=pcol("norm_w", (l * 3 + j) * 8, 8), op1=ALU.mult), [MODL[l], CONSTB], [MODL[l]])
            if j in (0, 2):
                V(lambda e: e.tensor_scalar(
                    out=modt[:, l, cnd, (3 * j + 2) * 8:(3 * j + 3) * 8], in0=modt[:, l, cnd, (3 * j + 2) * 8:(3 * j + 3) * 8],
                    scalar1=0.5, scalar2=None, op0=ALU.mult), [MODL[l]], [MODL[l]])

    mod_done = {}

    def mod_mark(l, ci):
        j = ci // 6
        mod_done[(l, j)] = mod_done.get((l, j), 0) + 1
        if mod_done[(l, j)] == 6:
            mod_finish_j(l, j)

    for ci in range(6):
        wa, wab, sk = nextA()
        mod_chunk(0, ci, wa, wab, sk)
        mod_mark(0, ci)
    K.barrier()
    pending_mod = [(0, ci) for ci in range(6, 18)] + ([(1, ci) for ci in range(18)] if nlayers > 1 else [])
    inflight_mod = []
    ffn_gi = [0]

    def modc(l, cnd, j, m):
        return modt[:, l, cnd, j * 8 + m:j * 8 + m + 1]

    def make_h(l, j, hbuf, HB, tiles=(0, 1, 2), rs_pool=None):
        for tt in tiles:
            t0, w, cnd = TT[tt]
            pb, pbb = bank()
            for c in range(8):
                sq, sqb = rs_pool["sq"].next()
                A(lambda e, sq=sq, c=c, t0=t0, w=w: e.activation(out=sq[:, 0:w], in_=x[:, c, t0:t0 + w], func=AF.Square),
                  [XB[c][tt]], [sqb])
                T(lambda e, pb=pb, sq=sq, c=c, w=w: e.matmul(pb[:, 0:w], lhsT=ones_b, rhs=sq[:, 0:w],
                                                               start=(c == 0), stop=(c == 7)), [sqb, CONSTB], [pbb])
            rs, rsb = rs_pool["rs"].next()
            A(lambda e, rs=rs, pb=pb, w=w: e.activation(out=rs[:, 0:w], in_=pb[:, 0:w], func=AF.Sqrt, bias=EPS, scale=1.0 / D),
              [pbb], [rsb])
            V(lambda e, rs=rs, w=w: e.reciprocal(rs[:, 0:w], rs[:, 0:w]), [rsb], [rsb])
            for c in range(8):
                tm, tmb = rs_pool["tm"].next()
                V(lambda e, tm=tm, rs=rs, c=c, t0=t0, w=w: e.tensor_tensor(out=tm[:, 0:w], in0=x[:, c, t0:t0 + w], in1=rs[:, 0:w], op=ALU.mult),
                  [XB[c][tt], rsb], [tmb])
                A(lambda e, tm=tm, c=c, t0=t0, w=w, cnd=cnd: e.activation(
                    out=hbuf[:, c, t0:t0 + w], in_=tm[:, 0:w], func=AF.Identity,
                    scale=modc(l, cnd, 3 * j + 1, c), bias=modc(l, cnd, 3 * j, c)), [tmb, MODL[l]], [HB[tt]])

    def resid_update(pb, pbb, m, tt, gate_ap, extra_reads=()):
        t0, w, cnd = TT[tt]
        V(lambda e: e.scalar_tensor_tensor(out=x[:, m, t0:t0 + w], in0=pb[:, 0:w], scalar=gate_ap, op0=ALU.mult,
                                           in1=x[:, m, t0:t0 + w], op1=ALU.add), [pbb, MODL[0], MODL[1]] + list(extra_reads), [XB[m][tt]])

    def ffn(l, f):
        j = 0 if f == 0 else 2
        mk = K.mark()
        hbuf = K.sb("h", [128, 8, NT], BF16)
        HB = bufs(3, "h")
        hid = [K.sb(f"hid{i}", [128, 2, NT], BF16) for i in range(2)]
        HIDB = [bufs(3, "hid0_"), bufs(3, "hid1_")]
        pool = {"sq": Rot(K, "sq", [128, 512], BF16, 3), "rs": Rot(K, "rs", [128, 512], F32, 2),
                "tm": Rot(K, "tm", [128, 512], F32, 3)}
        sgp = Rot(K, "sg", [128, 512], F32, 3)
        make_h(l, j, hbuf, HB, rs_pool=pool)
        stream_mod = (l == 0 and (len(pending_mod) > 0 or len(inflight_mod) > 0))
        if stream_mod:
            WM = [K.sb(f"WM{i}", [128, 8, 512], BF16) for i in range(4)]
            WMB = bufs(4, "WM")
            assert not inflight_mod
        for g in range(11):
            if stream_mod:
                while inflight_mod:
                    (ml, ci, slot) = inflight_mod.pop(0)
                    mod_chunk(ml, ci, WM[slot], WMB[slot], f"wm{slot}", dma=False)
                    mod_mark(ml, ci)
                ffn_gi[0] += 1
            wa, wab, ska = nextA()
            wb, wbb, skb = nextB()
            K.dma("gpsimd", lambda e, wa=wa, g=g: e.dma_start(out=wa[:].rearrange("p k n -> p (k n)"), in_=ffn_w_in[l, f, g], max_dma_last_dim=8192),
                  writes=[wab], semkey=ska)
            K.dma("gpsimd", lambda e, wb=wb, g=g: e.dma_start(out=wb[:, 0:2048], in_=ffn_w_out[l, f, g], max_dma_last_dim=8192),
                  writes=[wbb], semkey=skb)
            while deferred_loads:
                deferred_loads.pop(0)()
            if stream_mod and g < 10:
                nper = 2 if ffn_gi[0] <= 10 else 1
                for i in range(nper):
                    if pending_mod:
                        slot = 2 * (g % 2) + i
                        (ml, ci) = pending_mod.pop(0)
                        mod_dma(ml, ci, WM[slot], WMB[slot], f"wm{slot}")
                        inflight_mod.append((ml, ci, slot))
            hd = hid[g % 2]
            hdb = HIDB[g % 2]
            for jj in range(2):
                for tt in range(3):
                    t0, w, cnd = TT[tt]
                    pg, pgb = bank()
                    pu, pub = bank()
                    for k in range(8):
                        T(lambda e, pg=pg, wa=wa, k=k, jj=jj, t0=t0, w=w: e.matmul(
                            pg[:, 0:w], lhsT=wa[:, k, jj * 128:(jj + 1) * 128], rhs=hbuf[:, k, t0:t0 + w], start=(k == 0), stop=(k == 7)),
                            [wab, HB[tt]], [pgb])
                    for k in range(8):
                        T(lambda e, pu=pu, wa=wa, k=k, jj=jj, t0=t0, w=w: e.matmul(
                            pu[:, 0:w], lhsT=wa[:, k, 256 + jj * 128:256 + (jj + 1) * 128], rhs=hbuf[:, k, t0:t0 + w], start=(k == 0), stop=(k == 7)),
                            [wab, HB[tt]], [pub])
                    sg, sgb = sgp.next()
                    A(lambda e, sg=sg, pg=pg, w=w: e.activation(out=sg[:, 0:w], in_=pg[:, 0:w], func=AF.Silu), [pgb], [sgb])
                    V(lambda e, sg=sg, pu=pu, hd=hd, jj=jj, t0=t0, w=w: e.tensor_tensor(
                        out=hd[:, jj, t0:t0 + w], in0=sg[:, 0:w], in1=pu[:, 0:w], op=ALU.mult), [sgb, pub], [hdb[tt]])
            wbv = wb[:, 0:2048].rearrange("p (j n) -> p j n", j=2)
            for m in range(8):
                for tt in range(3):
                    t0, w, cnd = TT[tt]
                    po, pob = bank()
                    for jj in range(2):
                        T(lambda e, po=po, wbv=wbv, jj=jj, m=m, hd=hd, t0=t0, w=w: e.matmul(
                            po[:, 0:w], lhsT=wbv[:, jj, m * 128:(m + 1) * 128], rhs=hd[:, jj, t0:t0 + w], start=(jj == 0), stop=(jj == 1)),
                            [wbb, hdb[tt]], [pob])
                    resid_update(po, pob, m, tt, modc(l, cnd, 3 * j + 2, m))
        if stream_mod:
            while inflight_mod:
                (ml, ci, slot) = inflight_mod.pop(0)
                mod_chunk(ml, ci, WM[slot], WMB[slot], f"wm{slot}", dma=False)
                mod_mark(ml, ci)
        K.barrier()
        K.release(mk)

    def mixer(l):
        mk_all = K.mark()
        wmi = mix_w_in[l]
        wmo = mix_w_out[l]
        cvx = {}

        def load_wa(col0, ncols):
            wa, wab, sk = nextA()
            K.dma("gpsimd", lambda e: e.dma_start(out=wa[:, :, 0:ncols],
                                                  in_=wmi[:, col0:col0 + ncols].rearrange("(k p) n -> p k n", p=128)),
                  writes=[wab], semkey=sk)
            return wa, wab

        def proj_fm(pb, pbb, wa, wab, c0, M, hbuf, HB, tt):
            t0, w, cnd = TT[tt]
            for k in range(8):
                T(lambda e, k=k: e.matmul(pb[0:M, 0:w], lhsT=wa[:, k, c0:c0 + M], rhs=hbuf[:, k, t0:t0 + w],
                                          start=(k == 0), stop=(k == 7)), [wab, HB[tt]], [pbb])

        yreg = []

        def out_proj(ychunks, YB, wrow0, kp, tile=None):
            yreg.append((ychunks, YB, wrow0, kp, tile))

        def out_proj_all():
            packed = []
            for (ychunks, YB, wrow0, kp, tile) in yreg:
                if kp == 64 and tile is not None:
                    yp = K.sb("ypair", [128, 2, NT], BF16)
                    YPB = Buf("ypair")
                    tv = tile[:].rearrange("p (j two) t -> p j two t", two=2)
                    K.dma("sync", lambda e, yp=yp, tv=tv: e.dma_start(out=yp[0:64, :, :], in_=tv[:, :, 0, :]), reads=YB, writes=[YPB], semkey=f"ldyp{len(packed)}")
                    K.dma("sync", lambda e, yp=yp, tv=tv: e.dma_start(out=yp[64:128, :, :], in_=tv[:, :, 1, :]), reads=YB, writes=[YPB], semkey=f"ldyp{len(packed)}")
                    packed.append(([yp[:, 0, :], yp[:, 1, :]], [YPB], wrow0, 128))
                else:
                    packed.append((ychunks, YB, wrow0, kp))
            yreg[:] = packed
            slots = []
            for (ychunks, YB, wrow0, kp) in yreg:
                nk_ = len(ychunks)
                if len(slots) % 2 == 0:
                    wt_, wtb_, sk = nextB()
                    flat = wt_[:, :]
                else:
                    wt_, wtb_, sk = nextA()
                    flat = wt_[:].rearrange("p k n -> p (k n)")
                wv = flat[0:kp, 0:nk_ * 1024].rearrange("p (j n) -> p j n", j=nk_)
                K.dma("gpsimd", lambda e, wv=wv, wrow0=wrow0, nk_=nk_, kp=kp: e.dma_start(
                    out=wv, in_=wmo[wrow0:wrow0 + nk_ * kp, :].rearrange("(j p) n -> p j n", p=kp)), writes=[wtb_], semkey=sk)
                slots.append((wv, wtb_))
            total = sum(len(y[0]) for y in yreg)
            for m in range(8):
                for tt in range(3):
                    t0, w, cnd = TT[tt]
                    po, pob = bank()
                    i = 0
                    for (ychunks, YB, wrow0, kp), (wv, wtb_) in zip(yreg, slots):
                        for jj, yc in enumerate(ychunks):
                            T(lambda e, wv=wv, jj=jj, yc=yc, i=i: e.matmul(po[:, 0:w], lhsT=wv[:, jj, m * 128:(m + 1) * 128],
                                                                        rhs=yc[:, t0:t0 + w], start=(i == 0), stop=(i == total - 1)),
                              [wtb_] + YB, [pob])
                            i += 1
                    resid_update(po, pob, m, tt, modc(l, cnd, 5, m))

        def conv_chunk(raw, rawb, P, pc0, out_ap, outb, silu):
            hl = ccol("hl")
            hr = ccol("hr")
            V(lambda e: e.tensor_tensor(out=raw[0:P, 1:5, 0:1], in0=raw[0:P, 0:4, 256:257], in1=hl[0:P, 1:5].unsqueeze(2), op=ALU.mult),
              [rawb, CONSTB], [rawb])
            V(lambda e: e.tensor_tensor(out=raw[0:P, 0:4, 257:258], in0=raw[0:P, 1:5, 1:2], in1=hr[0:P, 0:4].unsqueeze(2), op=ALU.mult),
              [rawb, CONSTB], [rawb])
            acc, accb = cvx["convacc"].next()
            accv = acc[0:P, :].rearrange("p (b t) -> p b t", t=256)
            w0 = ptab[0:P, pc0:pc0 + 1]
            w1 = ptab[0:P, pc0 + 1:pc0 + 2]
            w2 = ptab[0:P, pc0 + 2:pc0 + 3]
            bb = ptab[0:P, pc0 + 3:pc0 + 4]
            V(lambda e: e.tensor_scalar(out=accv, in0=raw[0:P, :, 1:257], scalar1=w1, scalar2=None, op0=ALU.mult), [rawb, CONSTB], [accb])
            V(lambda e: e.scalar_tensor_tensor(out=accv, in0=raw[0:P, :, 0:256], scalar=w0, op0=ALU.mult, in1=accv, op1=ALU.add),
              [rawb, CONSTB, accb], [accb])
            V(lambda e: e.scalar_tensor_tensor(out=accv, in0=raw[0:P, :, 2:258], scalar=w2, op0=ALU.mult, in1=accv, op1=ALU.add),
              [rawb, CONSTB, accb], [accb])
            A(lambda e: e.activation(out=out_ap, in_=acc[0:P, :], func=(AF.Silu if silu else AF.Identity), bias=bb), [accb, CONSTB], [outb])

        def raw_fill(raw, rawb, P, pb, pbb, tt):
            t0, w, cnd = TT[tt]
            b0 = t0 // 256
            nb = w // 256
            A(lambda e: e.activation(out=raw[0:P, b0:b0 + nb, 1:257], in_=pb[0:P, 0:w].rearrange("p (b t) -> p b t", t=256), func=AF.Copy),
              [pbb], [rawb])

        hbuf_m = K.sb("hmix", [128, 8, NT], BF16)
        HB_m = bufs(3, "hmix")
        mkp = K.mark()
        pool_m = {"sq": Rot(K, "sq", [128, 512], BF16, 3), "rs": Rot(K, "rs", [128, 512], F32, 2),
                  "tm": Rot(K, "tm", [128, 512], F32, 2)}
        make_h(l, 1, hbuf_m, HB_m, rs_pool=pool_m)
        K.barrier()
        K.release(mkp)

        def with_h(fn, conv=False):
            mk = K.mark()
            if conv:
                cvx["convacc"] = Rot(K, "cacc", [128, NT], F32, 1)
                raws = Rot(K, "raw", [128, 5, 258], F32, 2)
                cvx["raws"] = raws
                for i in range(2):
                    V(lambda e, i=i: e.memset(raws.t[i][:], 0.0), [], [raws.b[i]])
            fn(hbuf_m, HB_m)
            K.barrier()
            K.release(mk)

        def ssd(szb, SZB):
            mk = K.mark()
            xsf = K.sb("xsf", [64, 4, NT], BF16)
            XSF = Buf("xsf")
            bcf = K.sb("bcf", [128, 4, NT], BF16)
            BCF = Buf("bcf")
            dtr = K.sb("dtr", [128, 10, 8])
            dtt = K.sb("dtt", [128, 10, 8])
            lat = K.sb("lat", [128, 10, 8])
            DTB = Buf("dt")

            def inproj(hbuf, HB):
                wa0, wab0 = load_wa(0, 512)
                wa1, wab1 = load_wa(512, 512)
                zxp = K.sb("zxpair", [128, 2, 2, NT], BF16)
                ZXB = bufs(2, "zxpair")
                for j in range(2):
                    for tt in range(3):
                        t0, w, cnd = TT[tt]
                        pb, pbb = bank()
                        proj_fm(pb, pbb, wa0, wab0, j * 128, 128, hbuf, HB, tt)
                        A(lambda e, pb=pb, j=j, t0=t0, w=w: e.activation(out=zxp[:, 0, j, t0:t0 + w], in_=pb[:, 0:w], func=AF.Silu), [pbb], [ZXB[0]])
                for j in range(2):
                    raw, rawb = cvx["raws"].next()
                    for tt in range(3):
                        pb, pbb = bank()
                        proj_fm(pb, pbb, wa0, wab0, 256 + j * 128, 128, hbuf, HB, tt)
                        raw_fill(raw, rawb, 128, pb, pbb, tt)
                    conv_chunk(raw, rawb, 128, PT["ssd_convp"][0] + (l * 2 + j) * 4, zxp[:, 1, j, :], ZXB[1], True)
                for slot, (dst, dstb) in enumerate(((szb, SZB), (xsf, XSF))):
                    dv = dst[:].rearrange("p (j two) t -> p j two t", two=2)
                    for half in range(2):
                        K.dma("sync", lambda e, dv=dv, slot=slot, half=half: e.dma_start(out=dv[:, :, half, :], in_=zxp[half * 64:(half + 1) * 64, slot, :, :]),
                              reads=[ZXB[slot]], writes=[dstb], semkey=f"ldzx{slot}")
                for q in range(4, 8):
                    raw, rawb = cvx["raws"].next()
                    P = 64 if q < 4 else 128
                    for tt in range(3):
                        pb, pbb = bank()
                        if q < 4:
                            proj_fm(pb, pbb, wa0, wab0, 256 + q * 64, 64, hbuf, HB, tt)
                        else:
                            proj_fm(pb, pbb, wa1, wab1, (q - 4) * 128, 128, hbuf, HB, tt)
                        raw_fill(raw, rawb, P, pb, pbb, tt)
                    pc0 = PT["ssd_conv"][0] + (l * 8 + q) * 4
                    if q < 4:
                        conv_chunk(raw, rawb, 64, pc0, xsf[:, q, :], XSF, True)
                    else:
                        conv_chunk(raw, rawb, 128, pc0, bcf[:, q - 4, :], BCF, True)
                wdt = K.sb("wdt", [128, 8, 8], BF16)
                WDT = Buf("wdt")
                K.dma("gpsimd", lambda e: e.dma_start(out=wdt[:], in_=wmi[:, 1024:1032].rearrange("(k p) n -> p k n", p=128)),
                      writes=[WDT], semkey="wdt")
                pb, pbb = bank()
                for tc in range(NCH):
                    tt = 0 if tc < 4 else (1 if tc < 8 else 2)
                    for k in range(8):
                        T(lambda e, tc=tc, k=k: e.matmul(pb[:, tc * 8:tc * 8 + 8], lhsT=hbuf[:, k, tc * 128:(tc + 1) * 128], rhs=wdt[:, k, :],
                                                         start=(k == 0), stop=(k == 7)), [HB[tt], WDT], [pbb])
                V(lambda e: e.tensor_tensor(out=dtr[:].rearrange("p a b -> p (a b)"), in0=pb[:, 0:80], in1=fcol("dt_bias", l * 80, 80), op=ALU.add),
                  [pbb, CONSTB], [DTB])
            with_h(inproj, conv=True)
            A(lambda e: e.activation(out=dtt[:], in_=dtr[:], func=AF.Exp), [DTB], [DTB])
            A(lambda e: e.activation(out=dtt[:], in_=dtt[:], func=AF.Ln, bias=1.0), [DTB], [DTB])
            A(lambda e: e.activation(out=dtr[:].rearrange("p a b -> p (a b)"), in_=fcol("a_log", l * 80, 80), func=AF.Exp), [DTB, CONSTB], [DTB])
            V(lambda e: e.scalar_tensor_tensor(out=lat[:], in0=dtr[:], scalar=-1.0, op0=ALU.mult, in1=dtt[:], op1=ALU.mult), [DTB], [DTB])
            xbtm = K.sb("xbtm", [128, NCH, 512], BF16)
            XBT = bufs(NCH, "xbtm")
            for tc in range(NCH):
                pb, pbb = bank()
                pbv = pb[:].bitcast(BF16)
                for hh in range(4):
                    T(lambda e, hh=hh, tc=tc: e.transpose(pbv[:, hh * 64:(hh + 1) * 64], xsf[:, hh, tc * 128:(tc + 1) * 128], ident_b[0:64, 0:64]),
                      [XSF, CONSTB], [pbb])
                for g in range(2):
                    T(lambda e, g=g, tc=tc: e.transpose(pbv[:, 256 + g * 128:256 + (g + 1) * 128], bcf[:, g, tc * 128:(tc + 1) * 128], ident_b),
                      [BCF, CONSTB], [pbb])
                A(lambda e, tc=tc, pbv=pbv: e.activation(out=xbtm[:, tc, :], in_=pbv[:, 0:512], func=AF.Copy), [pbb], [XBT[tc]])
            cst = K.sb("cst", [128, NCH, 16])
            wBt = K.sb("wBt", [128, NCH, 8])
            edt = K.sb("edt", [128, NCH, 8])
            pb, pbb = bank()
            for tc in range(NCH):
                T(lambda e, tc=tc: e.matmul(pb[:, tc * 16:tc * 16 + 4], lhsT=trif, rhs=lat[:, tc, 0:4], start=True, stop=True), [DTB, CONSTB], [pbb])
                T(lambda e, tc=tc: e.matmul(pb[:, tc * 16 + 4:tc * 16 + 8], lhsT=trib, rhs=lat[:, tc, 4:8], start=True, stop=True), [DTB, CONSTB], [pbb])
                T(lambda e, tc=tc: e.matmul(pb[:, tc * 16 + 8:tc * 16 + 16], lhsT=ones_f, rhs=lat[:, tc, 0:8], start=True, stop=True), [DTB, CONSTB], [pbb])
            V(lambda e: e.tensor_copy(cst[:].rearrange("p a b -> p (a b)"), pb[:, 0:160]), [pbb], [DTB])
            V(lambda e: e.tensor_tensor(out=wBt[:], in0=cst[:, :, 8:16], in1=cst[:, :, 0:8], op=ALU.subtract), [DTB], [DTB])
            A(lambda e: e.activation(out=wBt[:], in_=wBt[:], func=AF.Exp), [DTB], [DTB])
            V(lambda e: e.tensor_tensor(out=wBt[:], in0=wBt[:], in1=dtt[:], op=ALU.mult), [DTB], [DTB])
            A(lambda e: e.activation(out=edt[:], in_=cst[:, :, 8:16], func=AF.Exp), [DTB], [DTB])
            S32 = K.sb("S32", [128, 8, 64])
            SB_ = bufs(8, "S32")
            SP = K.sb("SP", [128, NCH, 8, 64], BF16)
            SPB = [bufs(8, f"SP{c}_") for c in range(NCH)]
            mk_st = K.mark()
            s0t = K.sb("s0t", [128, 40, 64])
            S0B = Buf("s0")
            K.dma("sync", lambda e: e.dma_start(out=s0t[:], in_=s0ssd[:, l * 40:(l + 1) * 40, :]), writes=[S0B], semkey="lds0")
            V(lambda e: e.memset(S32[:], 0.0), [], SB_)
            hl = ccol("hl")
            hr = ccol("hr")
            bsp = Rot(K, "bs", [128, 128], BF16, 12)

            def state_chain(d, hh):
                col = d * 4 + hh
                g = hh // 2
                for k in range(NCH):
                    c = k if d == 0 else NCH - 1 - k
                    blk = c // 2
                    first = (c % 2 == 0) if d == 0 else (c % 2 == 1)
                    sidx = (blk * 2 + d) * 4 + hh
                    if first:
                        fl = hl[:, blk:blk + 1] if d == 0 else hr[:, blk:blk + 1]
                        V(lambda e: e.scalar_tensor_tensor(out=S32[:, col, :], in0=S32[:, col, :], scalar=fl, op0=ALU.mult, in1=s0t[:, sidx, :], op1=ALU.add),
                          [SB_[col], S0B, CONSTB], [SB_[col]])
                    A(lambda e: e.activation(out=SP[:, c, col, :], in_=S32[:, col, :], func=AF.Copy), [SB_[col]], [SPB[c][col]])
                    bs, bsb = bsp.next()
                    V(lambda e: e.tensor_scalar(out=bs[:], in0=xbtm[:, c, 256 + g * 128:256 + (g + 1) * 128], scalar1=wBt[:, c, col:col + 1], scalar2=None, op0=ALU.mult),
                      [XBT[c], DTB], [bsb])
                    yield
                    pb, pbb = pr(64)
                    T(lambda e: e.matmul(pb[:, 0:64], lhsT=bs[:], rhs=xbtm[:, c, hh * 64:(hh + 1) * 64], start=True, stop=True), [bsb, XBT[c]], [pbb])
                    yield
                    V(lambda e: e.scalar_tensor_tensor(out=S32[:, col, :], in0=S32[:, col, :], scalar=edt[:, c, col:col + 1], op0=ALU.mult, in1=pb[:, 0:64], op1=ALU.add),
                      [SB_[col], DTB, pbb], [SB_[col]])
                    if not first:
                        oidx = l * 40 + sidx
                        sslot = sstp.i
                        stg, stgb = sstp.next()
                        A(lambda e: e.activation(out=stg[:], in_=S32[:, col, :], func=AF.Copy), [SB_[col]], [stgb])
                        K.dma("sync", lambda e: e.dma_start(out=nssd_o[:, oidx, :], in_=stg[:]), reads=[stgb], semkey=f"stS{sslot}")
                    yield
            sstp = Rot(K, "sstg", [128, 64], F32, 8)
            interleave([state_chain(d, hh) for d in range(2) for hh in range(4)])
            K.barrier()
            K.release(mk_st)

            dsk = pcol("ssd_d", l * 4, 4, rows=64)
            nws = pcol("ssd_nw", l * 4, 4, rows=64)
            ygp = Rot(K, "yg", [64, 4, 128], F32, 2)
            wtp = Rot(K, "wt", [128, 128], F32, 16)
            sgp2 = Rot(K, "sg2", [128, 128], F32, 16)
            csp = Rot(K, "cs", [128, 128], BF16, 16)
            sqp = Rot(K, "sq4", [64, 128], BF16, 8)

            def ssd_chain(c, hh, d, psc, pscb, res):
                cs = slice(c * 128, (c + 1) * 128)
                col = d * 4 + hh
                g = hh // 2
                U = trif if d == 0 else trib
                NB = ccol("nbf") if d == 0 else ccol("nbb")
                wt, wtb = wtp.next()
                G(lambda e: e.tensor_scalar(out=wt[:], in0=U, scalar1=lat[:, c, col:col + 1], scalar2=0.0, op0=ALU.mult, op1=ALU.add), [CONSTB, DTB], [wtb])
                yield
                pc, pcb = pr(128)
                T(lambda e: e.matmul(pc[:, 0:128], lhsT=ones_f, rhs=wt[:], start=True, stop=True), [wtb, CONSTB], [pcb])
                yield
                sg, sgb = sgp2.next()
                V(lambda e: e.scalar_tensor_tensor(out=sg[:], in0=pc[:, 0:128], scalar=cst[:, c, col:col + 1], op0=ALU.subtract, in1=NB, op1=ALU.add),
                  [pcb, DTB, CONSTB], [sgb])
                yield
                ec, ecb = wt, wtb
                A(lambda e: e.activation(out=ec[:], in_=pc[:, 0:128], func=AF.Exp), [pcb], [ecb])
                yield
                A(lambda e: e.activation(out=sg[:], in_=sg[:], func=AF.Exp), [sgb], [sgb])
                csb, csbb = csp.next()
                G(lambda e: e.tensor_tensor(out=csb[:], in0=bcf[:, 2 + g, cs], in1=ec[:], op=ALU.mult), [BCF, ecb], [csbb])
                yield
                yield
                st, stb = wt[:].bitcast(BF16)[:, 0:128], wtb
                V(lambda e: e.scalar_tensor_tensor(out=st, in0=psc[:, g * 128:(g + 1) * 128], scalar=dtt[:, c, col:col + 1], op0=ALU.mult, in1=sg[:], op1=ALU.mult),
                  [pscb, sgb, DTB], [stb])
                res[(hh, d)] = (st, stb, csb, csbb)
                yield

            def ssd_chunk(c):
                cs = slice(c * 128, (c + 1) * 128)
                psc, pscb = pr(256)
                for g in range(2):
                    T(lambda e: e.matmul(psc[:, g * 128:(g + 1) * 128], lhsT=bcf[:, g, cs], rhs=bcf[:, 2 + g, cs], start=True, stop=True), [BCF], [pscb])
                res = {}
                chains = [ssd_chain(c, hh, d, psc, pscb, res) for hh in range(4) for d in range(2)]
                while chains:
                    nxt = []
                    for gch in chains:
                        try:
                            next(gch)
                            nxt.append(gch)
                        except StopIteration:
                            pass
                    chains = nxt
                    yield
                yg, ygb = ygp.next()
                pys = []
                for hh in range(4):
                    py, pyb = pr(128)
                    pys.append((py, pyb))
                    for d in range(2):
                        col = d * 4 + hh
                        st, stb, csb, csbb = res[(hh, d)]
                        T(lambda e: e.matmul(py[0:64, 0:128], lhsT=xbtm[:, c, hh * 64:(hh + 1) * 64], rhs=st, start=(d == 0), stop=False), [XBT[c], stb], [pyb])
                        T(lambda e: e.matmul(py[0:64, 0:128], lhsT=SP[:, c, col, :], rhs=csb[:], start=False, stop=(d == 1)), [SPB[c][col], csbb], [pyb])
                yield
                sqs = []
                for hh in range(4):
                    py, pyb = pys[hh]
                    V(lambda e: e.scalar_tensor_tensor(out=yg[:, hh, :], in0=xsf[:, hh, cs], scalar=dsk[:, hh:hh + 1], op0=ALU.mult, in1=py[0:64, 0:128], op1=ALU.add),
                      [XSF, CONSTB, pyb], [ygb])
                    V(lambda e: e.tensor_tensor(out=yg[:, hh, :], in0=yg[:, hh, :], in1=szb[:, hh, cs], op=ALU.mult), [ygb, SZB], [ygb])
                    sq, sqb = sqp.next()
                    A(lambda e: e.activation(out=sq[:], in_=yg[:, hh, :], func=AF.Square), [ygb], [sqb])
                    sqs.append((sq, sqb))
                    yield
                pq, pqb = pr(128)
                for hh in range(4):
                    sq, sqb = sqs[hh]
                    T(lambda e: e.matmul(pq[0:64, 0:128], lhsT=ones_b[0:64, 0:64], rhs=sq[:], start=(hh == 0), stop=(hh == 3)), [sqb, CONSTB], [pqb])
                yield
                rs, rsb = rsp2.next()
                A(lambda e: e.activation(out=rs[:], in_=pq[0:64, 0:128], func=AF.Sqrt, bias=EPS, scale=1.0 / 256), [pqb], [rsb])
                yield
                V(lambda e: e.reciprocal(rs[:], rs[:]), [rsb], [rsb])
                yield
                for hh in range(4):
                    V(lambda e: e.scalar_tensor_tensor(out=szb[:, hh, cs], in0=yg[:, hh, :], scalar=nws[:, hh:hh + 1], op0=ALU.mult, in1=rs[:], op1=ALU.mult),
                      [ygb, rsb, CONSTB], [SZB])
                    yield
            rsp2 = Rot(K, "rs2", [64, 128], F32, 2)
            for c0 in range(0, NCH, 2):
                interleave([ssd_chunk(c0), ssd_chunk(c0 + 1)])
            out_proj([szb[:, hh, :] for hh in range(4)], [SZB], 0, 64, tile=szb)
            K.barrier()
            K.release(mk)

        def ret(sgf, SGF):
            mk = K.mark()
            qf = K.sb("qf", [64, 4, NT], BF16)
            kf = K.sb("kf", [64, 4, NT], BF16)
            kvt = K.sb("kvt", [128, NCH, 256], BF16)
            QF, KF = Buf("qf"), Buf("kf")
            KVT = bufs(NCH, "kvt")
            KST = Buf("kst")
            lg = K.sb("lg", [128, 8])
            LG = Buf("lg")
            A(lambda e: e.activation(out=lg[:], in_=fcol("ret_logit", l * 8, 8), func=AF.Exp, scale=-1.0), [CONSTB], [LG])
            A(lambda e: e.activation(out=lg[:], in_=lg[:], func=AF.Ln, bias=1.0), [LG], [LG])
            V(lambda e: e.tensor_scalar(out=lg[:], in0=lg[:], scalar1=-1.0, scalar2=None, op0=ALU.mult), [LG], [LG])
            Tl = K.sb("Tl", [128, 8])
            G128 = K.sb("G128", [128, 8])
            RC = Buf("retc")
            for hh in range(4):
                A(lambda e, hh=hh: e.activation(out=Tl[:, hh:hh + 1], in_=ccol("tailf"), func=AF.Exp, scale=lg[:, hh:hh + 1]), [LG, CONSTB], [RC])
                A(lambda e, hh=hh: e.activation(out=Tl[:, 4 + hh:5 + hh], in_=ccol("tailb"), func=AF.Exp, scale=lg[:, 4 + hh:5 + hh]), [LG, CONSTB], [RC])
            V(lambda e: e.tensor_scalar(out=Tl[:], in0=Tl[:], scalar1=0.125, scalar2=None, op0=ALU.mult), [RC], [RC])
            A(lambda e: e.activation(out=G128[:], in_=lg[:], func=AF.Exp, scale=128.0), [LG], [RC])
            S32 = K.sb("R32", [64, 8, 64])
            SB_ = bufs(8, "R32")
            SP = K.sb("RSP", [64, NCH, 8, 64], BF16)
            SPB = [bufs(8, f"RSP{c}_") for c in range(NCH)]
            mk_k = K.mark()
            kst = K.sb("kst", [128, 2, NCH, 256], BF16)

            def inproj(hbuf, HB):
                wa0, wab0 = load_wa(1800, 512)
                wa1, wab1 = load_wa(2312, 512)
                qp = K.sb("qpair", [128, 2, 2, NT], BF16)
                QPB = bufs(2, "qpair")
                specs = [(wa0, wab0, 0, AF.Copy, None, qf, QF), (wa0, wab0, 256, AF.Copy, 0.125, kf, KF), (wa1, wab1, 256, AF.Silu, None, sgf, SGF)]
                for i, (wsrc, wsrcb, c0, fn_, scl, dst, dstb) in enumerate(specs):
                    slot = i % 2
                    for j in range(2):
                        for tt in range(3):
                            t0, w, cnd = TT[tt]
                            pb, pbb = bank()
                            proj_fm(pb, pbb, wsrc, wsrcb, c0 + j * 128, 128, hbuf, HB, tt)
                            if scl is None:
                                A(lambda e: e.activation(out=qp[:, slot, j, t0:t0 + w], in_=pb[:, 0:w], func=fn_), [pbb], [QPB[slot]])
                            else:
                                A(lambda e: e.activation(out=qp[:, slot, j, t0:t0 + w], in_=pb[:, 0:w], func=fn_, scale=scl), [pbb], [QPB[slot]])
                    dv = dst[:].rearrange("p (j two) t -> p j two t", two=2)
                    for half in range(2):
                        K.dma("sync", lambda e: e.dma_start(out=dv[:, :, half, :], in_=qp[half * 64:(half + 1) * 64, slot, :, :]),
                              reads=[QPB[slot]], writes=[dstb], semkey=f"ldrp{i}")
                for tc in range(NCH):
                    tt = 0 if tc < 4 else (1 if tc < 8 else 2)
                    pb, pbb = bank()
                    for k in range(8):
                        T(lambda e, pb=pb, tc=tc, k=k: e.matmul(pb[:, 0:256], lhsT=hbuf[:, k, tc * 128:(tc + 1) * 128], rhs=wa0[:, k, 256:512],
                                                                 start=(k == 0), stop=(k == 7)), [HB[tt], wab0], [pbb])
                    for k in range(8):
                        T(lambda e, pb=pb, tc=tc, k=k: e.matmul(pb[:, 256:512], lhsT=hbuf[:, k, tc * 128:(tc + 1) * 128], rhs=wa1[:, k, 0:256],
                                                                 start=(k == 0), stop=(k == 7)), [HB[tt], wab1], [pbb])
                    for d in range(2):
                        V(lambda e, pb=pb, tc=tc, d=d: e.tensor_tensor(out=kst[:, d, tc, :].rearrange("p (h n) -> p h n", h=4),
                                                                       in0=pb[:, 0:256].rearrange("p (h n) -> p h n", h=4),
                                                                       in1=Tl[:, d * 4:(d + 1) * 4].unsqueeze(2).broadcast_to([128, 4, 64]), op=ALU.mult),
                          [pbb, RC], [KST])
                    A(lambda e, pb=pb, tc=tc: e.activation(out=kvt[:, tc, :], in_=pb[:, 256:512], func=AF.Copy), [pbb], [KVT[tc]])
            with_h(inproj)
            mk_st = K.mark()
            s0t = K.sb("rs0t", [64, 40, 64])
            S0B = Buf("rs0")
            K.dma("sync", lambda e: e.dma_start(out=s0t[:], in_=s0ret[:, l * 40:(l + 1) * 40, :]), writes=[S0B], semkey="lds0")
            V(lambda e: e.memset(S32[:], 0.0), [], SB_)
            hl = ccol("hl")
            hr = ccol("hr")
            def rstate_chain(d, hh):
                col = d * 4 + hh
                for k in range(NCH):
                    c = k if d == 0 else NCH - 1 - k
                    blk = c // 2
                    first = (c % 2 == 0) if d == 0 else (c % 2 == 1)
                    sidx = (blk * 2 + d) * 4 + hh
                    if first:
                        fl = hl[0:64, blk:blk + 1] if d == 0 else hr[0:64, blk:blk + 1]
                        V(lambda e: e.scalar_tensor_tensor(out=S32[:, col, :], in0=S32[:, col, :], scalar=fl, op0=ALU.mult, in1=s0t[:, sidx, :], op1=ALU.add),
                          [SB_[col], S0B, CONSTB], [SB_[col]])
                    A(lambda e: e.activation(out=SP[:, c, col, :], in_=S32[:, col, :], func=AF.Copy), [SB_[col]], [SPB[c][col]])
                    yield
                    pb, pbb = pr(64)
                    T(lambda e: e.matmul(pb[0:64, 0:64], lhsT=kst[:, d, c, hh * 64:(hh + 1) * 64], rhs=kvt[:, c, hh * 64:(hh + 1) * 64],
                                         start=True, stop=True), [KST, KVT[c]], [pbb])
                    yield
                    V(lambda e: e.scalar_tensor_tensor(out=S32[:, col, :], in0=S32[:, col, :], scalar=G128[0:64, col:col + 1], op0=ALU.mult, in1=pb[0:64, 0:64], op1=ALU.add),
                      [SB_[col], RC, pbb], [SB_[col]])
                    if not first:
                        oidx = l * 40 + sidx
                        sslot = rstp.i
                        stg, stgb = rstp.next()
                        A(lambda e: e.activation(out=stg[:], in_=S32[:, col, :], func=AF.Copy), [SB_[col]], [stgb])
                        K.dma("sync", lambda e: e.dma_start(out=nret_o[:, oidx, :], in_=stg[:]), reads=[stgb], semkey=f"stR{sslot}")
                    yield
            rstp = Rot(K, "rstg", [64, 64], F32, 8)
            interleave([rstate_chain(d, hh) for d in range(2) for hh in range(4)])
            K.barrier()
            K.release(mk_k)

            gnw = pcol("ret_gn", l * 4, 4, rows=64)
            Dm = K.sb("Dm", [128, 4, 128])
            Ef = K.sb("Ef", [64, 8, 128])
            tmpf = Rot(K, "tmpf", [128, 128], F32, 2)
            for hh in range(4):
                t1, t1b = tmpf.next()
                A(lambda e, t1=t1, hh=hh: e.activation(out=t1[:], in_=ccol("relu_f"), func=AF.Exp, scale=lg[:, hh:hh + 1]), [LG, CONSTB], [t1b])
                V(lambda e, t1=t1, hh=hh: e.tensor_tensor(out=Dm[:, hh, :], in0=t1[:], in1=trif, op=ALU.mult), [t1b, CONSTB], [RC])
                t2, t2b = tmpf.next()
                A(lambda e, t2=t2, hh=hh: e.activation(out=t2[:], in_=ccol("relu_b"), func=AF.Exp, scale=lg[:, 4 + hh:5 + hh]), [LG, CONSTB], [t2b])
                V(lambda e, t2=t2: e.tensor_tensor(out=t2[:], in0=t2[:], in1=trib, op=ALU.mult), [t2b, CONSTB], [t2b])
                V(lambda e, t2=t2, hh=hh: e.tensor_tensor(out=Dm[:, hh, :], in0=Dm[:, hh, :], in1=t2[:], op=ALU.add), [t2b, RC], [RC])
                A(lambda e, hh=hh: e.activation(out=Ef[:, hh, :], in_=ccol("ip1", rows=64), func=AF.Exp, scale=lg[0:64, hh:hh + 1]), [LG, CONSTB], [RC])
                A(lambda e, hh=hh: e.activation(out=Ef[:, 4 + hh, :], in_=ccol("rmi", rows=64), func=AF.Exp, scale=lg[0:64, 4 + hh:5 + hh]), [LG, CONSTB], [RC])
            qsp = Rot(K, "qs4", [64, 2, 4, 128], BF16, 2)
            st4p = Rot(K, "st4", [128, 4, 128], BF16, 2)
            yvp = Rot(K, "yv4", [64, 4, 128], F32, 2)
            sq4p = Rot(K, "sq4", [64, 4, 128], F32, 2)

            def ret_chunk(c):
                cs = slice(c * 128, (c + 1) * 128)
                ps_, psb = bank()
                for hh in range(4):
                    T(lambda e: e.matmul(ps_[:, hh * 128:(hh + 1) * 128], lhsT=kf[:, hh, cs], rhs=qf[:, hh, cs], start=True, stop=True), [KF, QF], [psb])
                yield
                st, stb = st4p.next()
                V(lambda e: e.tensor_tensor(out=st[:], in0=ps_[:, 0:512].rearrange("p (h t) -> p h t", h=4), in1=Dm[:], op=ALU.mult), [psb, RC], [stb])
                qs, qsb = qsp.next()
                V(lambda e: e.tensor_tensor(out=qs[:], in0=qf[:, :, cs].unsqueeze(1).broadcast_to([64, 2, 4, 128]),
                                            in1=Ef[:].rearrange("p (d h) t -> p d h t", d=2), op=ALU.mult), [QF, RC], [qsb])
                yield
                py, pyb = bank()
                for hh in range(4):
                    T(lambda e: e.matmul(py[0:64, hh * 128:(hh + 1) * 128], lhsT=kvt[:, c, hh * 64:(hh + 1) * 64], rhs=st[:, hh, :],
                                         start=True, stop=False), [KVT[c], stb], [pyb])
                    T(lambda e: e.matmul(py[0:64, hh * 128:(hh + 1) * 128], lhsT=SP[:, c, hh, :], rhs=qs[:, 0, hh, :], start=False, stop=False),
                      [SPB[c][hh], qsb], [pyb])
                    T(lambda e: e.matmul(py[0:64, hh * 128:(hh + 1) * 128], lhsT=SP[:, c, 4 + hh, :], rhs=qs[:, 1, hh, :], start=False, stop=True),
                      [SPB[c][4 + hh], qsb], [pyb])
                yield
                yv, yvb = yvp.next()
                yvf = yv[:].rearrange("p h t -> p (h t)")
                V(lambda e: e.tensor_copy(yvf, py[0:64, 0:512]), [pyb], [yvb])
                yield
                pm, pmb = bank()
                T(lambda e: e.matmul(pm[0:64, 0:512], lhsT=ones_f[0:64, 0:64], rhs=yvf, start=True, stop=True), [yvb, CONSTB], [pmb])
                yield
                V(lambda e: e.scalar_tensor_tensor(out=yvf, in0=pm[0:64, 0:512], scalar=-1.0 / 64, op0=ALU.mult, in1=yvf, op1=ALU.add), [pmb, yvb], [yvb])
                yield
                sq, sqb = sq4p.next()
                sqf = sq[:].rearrange("p h t -> p (h t)")
                A(lambda e: e.activation(out=sqf, in_=yvf, func=AF.Square), [yvb], [sqb])
                yield
                pv_, pvb = bank()
                T(lambda e: e.matmul(pv_[0:64, 0:512], lhsT=ones_f[0:64, 0:64], rhs=sqf, start=True, stop=True), [sqb, CONSTB], [pvb])
                yield
                A(lambda e: e.activation(out=sqf, in_=pv_[0:64, 0:512], func=AF.Sqrt, bias=EPS, scale=1.0 / 64), [pvb], [sqb])
                yield
                V(lambda e: e.reciprocal(sqf, sqf), [sqb], [sqb])
                yield
                V(lambda e: e.tensor_tensor(out=yvf, in0=yvf, in1=sqf, op=ALU.mult), [yvb, sqb], [yvb])
                V(lambda e: e.tensor_tensor(out=yv[:], in0=yv[:], in1=gnw.unsqueeze(2).broadcast_to([64, 4, 128]), op=ALU.mult), [yvb, CONSTB], [yvb])
                yield
                V(lambda e: e.tensor_tensor(out=sgf[:, :, cs], in0=yv[:], in1=sgf[:, :, cs], op=ALU.mult), [yvb, SGF], [SGF])
                yield
            for c0 in range(0, NCH, 2):
                interleave([ret_chunk(c0), ret_chunk(c0 + 1)])
            out_proj([sgf[:, hh, :] for hh in range(4)], [SGF], 512, 64, tile=sgf)
            K.barrier()
            K.release(mk)

        def att(yat, YA):
            mk = K.mark()
            qfm = K.sb("aq", [64, 4, NT], BF16)
            kfm = K.sb("ak", [64, 2, NT], BF16)
            vtm = K.sb("av", [128, NCH, 128], BF16)
            QF, KF = Buf("aq"), Buf("ak")
            VT = bufs(NCH, "av")
            ckf = K.sb("ckf", [64, 2, 512], BF16)
            cvs = K.sb("cvs", [128, 4, 128], BF16)
            CK = Buf("ck")
            K.dma("gpsimd", lambda e: e.dma_start(out=ckf[:], in_=ckT[:, l * 2:l * 2 + 2, :]), writes=[CK], semkey="ldck")
            K.dma("gpsimd", lambda e: e.dma_start(out=cvs[:], in_=cvt[:, l * 4:l * 4 + 4, :]), writes=[CK], semkey="ldck")
            es = K.sb("es", [64, 4])
            A(lambda e: e.activation(out=es[:], in_=fcol("sink", l * 4, 4, rows=64), func=AF.Exp), [CONSTB], [CK])
            mk_in = K.mark()
            qkp = Rot(K, "qk", [128, 512], F32, 3)
            qnp = Rot(K, "qn", [128, 384], F32, 3)
            qbp = Rot(K, "qb", [128, 384], BF16, 3)
            smp = Rot(K, "sm", [128, 8], F32, 3)
            rtp = Rot(K, "rt", [128, 6, 2, 16], F32, 8)
            kvo = Rot(K, "kvo", [128, 256], F32, 3)

            def inproj(hbuf, HB):
                wa, wab = load_wa(2824, 512)

                def in_chunk(tc):
                    tt = 0 if tc < 4 else (1 if tc < 8 else 2)
                    pb, pbb = bank()
                    for k in range(8):
                        T(lambda e: e.matmul(pb[:, 0:512], lhsT=hbuf[:, k, tc * 128:(tc + 1) * 128], rhs=wa[:, k, :], start=(k == 0), stop=(k == 7)),
                          [HB[tt], wab], [pbb])
                    yield
                    qk, qkb = qkp.next()
                    A(lambda e: e.activation(out=qk[:], in_=pb[:, 0:512], func=AF.Copy), [pbb], [qkb])
                    yield
                    qn, qnb = qnp.next()
                    sm, smb = smp.next()
                    A(lambda e: e.activation(out=qn[:], in_=qk[:, 0:384], func=AF.Square), [qkb], [qnb])
                    A(lambda e: e.activation(out=vtm[:, tc, :], in_=qk[:, 384:512], func=AF.Copy), [qkb], [VT[tc]])
                    yield
                    V(lambda e: e.tensor_reduce(out=sm[:, 0:6], in_=qn[:].rearrange("p (h d) -> p h d", d=64), axis=AX.X, op=ALU.add), [qnb], [smb])
                    yield
                    A(lambda e: e.activation(out=sm[:, 0:6], in_=sm[:, 0:6], func=AF.Sqrt, bias=EPS, scale=1.0 / 64), [smb], [smb])
                    yield
                    V(lambda e: e.reciprocal(sm[:, 0:6], sm[:, 0:6]), [smb], [smb])
                    yield
                    V(lambda e: e.tensor_tensor(out=qn[:].rearrange("p (h d) -> p h d", d=64), in0=qk[:, 0:384].rearrange("p (h d) -> p h d", d=64),
                                                in1=sm[:, 0:6].unsqueeze(2).broadcast_to([128, 6, 64]), op=ALU.mult), [qkb, smb], [qnb])
                    yield
                    V(lambda e: e.tensor_tensor(out=qn[:], in0=qn[:], in1=fcol("qkw", l * 384, 384), op=ALU.mult), [qnb, CONSTB], [qnb])
                    yield
                    qv = qn[:].rearrange("p (h a b f) -> p h a b f", h=6, a=2, b=2)
                    cosv = ccol("cos", tc * 32, 32).rearrange("p (a f) -> p a f", a=2).unsqueeze(1).broadcast_to([128, 6, 2, 16])
                    sinv = ccol("sin", tc * 32, 32).rearrange("p (a f) -> p a f", a=2).unsqueeze(1).broadcast_to([128, 6, 2, 16])
                    x1 = qv[:, :, :, 0, :]
                    x2 = qv[:, :, :, 1, :]
                    t1, t1b = rtp.next()
                    t2, t2b = rtp.next()
                    t3, t3b = rtp.next()
                    t4, t4b = rtp.next()
                    V(lambda e: e.tensor_tensor(out=t1[:], in0=x1, in1=cosv, op=ALU.mult), [qnb, CONSTB], [t1b])
                    V(lambda e: e.tensor_tensor(out=t3[:], in0=x1, in1=sinv, op=ALU.mult), [qnb, CONSTB], [t3b])
                    yield
                    V(lambda e: e.tensor_tensor(out=t2[:], in0=x2, in1=sinv, op=ALU.mult), [qnb, CONSTB], [t2b])
                    V(lambda e: e.tensor_tensor(out=t4[:], in0=x2, in1=cosv, op=ALU.mult), [qnb, CONSTB], [t4b])
                    yield
                    V(lambda e: e.tensor_tensor(out=x1, in0=t1[:], in1=t2[:], op=ALU.subtract), [t1b, t2b], [qnb])
                    yield
                    V(lambda e: e.tensor_tensor(out=x2, in0=t3[:], in1=t4[:], op=ALU.add), [t3b, t4b], [qnb])
                    yield
                    kslot = kvo.i
                    ko, kob = kvo.next()
                    V(lambda e: e.tensor_copy(ko[:, 0:128], qn[:, 256:384]), [qnb], [kob])
                    V(lambda e: e.tensor_copy(ko[:, 128:256], qk[:, 384:512]), [qkb], [kob])
                    qb, qbb = qbp.next()
                    A(lambda e: e.activation(out=qb[:], in_=qn[:], func=AF.Copy), [qnb], [qbb])
                    yield
                    K.dma("sync", lambda e: e.dma_start(out=nk_o[l, tc * 128:(tc + 1) * 128, :], in_=ko[:, 0:128]), reads=[kob], semkey=f"stk{kslot}")
                    K.dma("sync", lambda e: e.dma_start(out=nv_o[l, tc * 128:(tc + 1) * 128, :], in_=ko[:, 128:256]), reads=[kob], semkey=f"stk{kslot}")
                    pt, ptb = bank()
                    ptv = pt[:].bitcast(BF16)
                    for hh in range(6):
                        T(lambda e: e.transpose(ptv[0:64, hh * 128:(hh + 1) * 128], qb[:, hh * 64:(hh + 1) * 64], ident_b), [qbb, CONSTB], [ptb])
                    yield
                    V(lambda e: e.tensor_copy(qfm[:, :, tc * 128:(tc + 1) * 128], ptv[0:64, 0:512].rearrange("p (h t) -> p h t", h=4)), [ptb], [QF])
                    V(lambda e: e.tensor_copy(kfm[:, :, tc * 128:(tc + 1) * 128], ptv[0:64, 512:768].rearrange("p (h t) -> p h t", h=2)), [ptb], [KF])
                    yield
                for c0 in range(0, NCH, 2):
                    interleave([in_chunk(c0), in_chunk(c0 + 1)])
            with_h(inproj)
            K.release(mk_in)
            ptp = Rot(K, "pt", [128, 7, 2, 128], BF16, 3)
            rcp = Rot(K, "rc", [64, 2, 128], F32, 3)
            am = bcol("am").rearrange("p (c a t) -> p c a t", c=NCH, a=2)
            cfb = ccol("cfb")

            def att_chain(qc, kv):
                qs = slice(qc * 128, (qc + 1) * 128)
                pc_ = max(qc - 1, 0)
                nc_ = min(qc + 1, NCH - 1)
                qrhs = qfm[:, 2 * kv:2 * kv + 2, qs]
                pa, pab = bank()
                pb_, pbb_ = bank()
                for sc in range(4):
                    dst, dstb = (pa, pab) if sc < 2 else (pb_, pbb_)
                    T(lambda e: e.matmul(dst[:, (sc % 2) * 256:(sc % 2) * 256 + 256], lhsT=ckf[:, kv, sc * 128:(sc + 1) * 128], rhs=qrhs, start=True, stop=True),
                      [CK, QF], [dstb])
                yield
                pt_, ptb_ = ptp.next()
                A(lambda e: e.activation(out=pt_[:, 0:2, :, :], in_=pa[:, 0:512].rearrange("p (c h t) -> p c h t", c=2, h=2), func=AF.Exp,
                                         scale=0.125, bias=cfb[:, qc:qc + 1]), [pab, CONSTB], [ptb_])
                A(lambda e: e.activation(out=pt_[:, 2:4, :, :], in_=pb_[:, 0:512].rearrange("p (c h t) -> p c h t", c=2, h=2), func=AF.Exp,
                                         scale=0.125, bias=cfb[:, qc:qc + 1]), [pbb_, CONSTB], [ptb_])
                pc2, pc2b = bank()
                pd2, pd2b = bank()
                for i, kc in enumerate((pc_, nc_, qc)):
                    dst, dstb = (pc2, pc2b) if i < 2 else (pd2, pd2b)
                    T(lambda e: e.matmul(dst[:, (i % 2) * 256:(i % 2) * 256 + 256], lhsT=kfm[:, kv, kc * 128:(kc + 1) * 128], rhs=qrhs, start=True, stop=True),
                      [KF, QF], [dstb])
                yield
                A(lambda e: e.activation(out=pt_[:, 4:6, :, :], in_=pc2[:, 0:512].rearrange("p (c h t) -> p c h t", c=2, h=2), func=AF.Exp, scale=0.125),
                  [pc2b], [ptb_])
                A(lambda e: e.activation(out=pt_[:, 6, :, :], in_=pd2[:, 0:256].rearrange("p (h t) -> p h t", h=2), func=AF.Exp, scale=0.125),
                  [pd2b], [ptb_])
                yield
                V(lambda e: e.tensor_tensor(out=pt_[:, 4:6, :, :], in0=pt_[:, 4:6, :, :], in1=am[:, qc, :, :].unsqueeze(2).broadcast_to([128, 2, 2, 128]), op=ALU.mult),
                  [ptb_, CONSTB], [ptb_])
                yield
                po, pob = bank()
                vlist = [(cvs[:, sc, kv * 64:(kv + 1) * 64], CK) for sc in range(4)] + \
                        [(vtm[:, kc, kv * 64:(kv + 1) * 64], VT[kc]) for kc in (pc_, nc_, qc)]
                for i, (vap, vb) in enumerate(vlist):
                    T(lambda e: e.matmul(po[0:64, 0:256], lhsT=vap, rhs=pt_[:, i, :, :], start=(i == 0), stop=(i == 6)), [vb, ptb_], [pob])
                for i in range(7):
                    T(lambda e: e.matmul(po[0:64, 256:512], lhsT=ones_b[:, 0:64], rhs=pt_[:, i, :, :], start=(i == 0), stop=(i == 6)), [CONSTB, ptb_], [pob])
                yield
                rc, rcb = rcp.next()
                V(lambda e: e.tensor_tensor(out=rc[:], in0=po[0:64, 256:512].rearrange("p (h t) -> p h t", h=2),
                                            in1=es[:, 2 * kv:2 * kv + 2].unsqueeze(2).broadcast_to([64, 2, 128]), op=ALU.add), [pob, CK], [rcb])
                V(lambda e: e.reciprocal(rc[:], rc[:]), [rcb], [rcb])
                V(lambda e: e.tensor_tensor(out=yat[:, 2 * kv:2 * kv + 2, qs], in0=po[0:64, 0:256].rearrange("p (h t) -> p h t", h=2), in1=rc[:], op=ALU.mult),
                  [pob, rcb], [YA])
                yield
            for qc in range(NCH):
                interleave([att_chain(qc, 0), att_chain(qc, 1)])
            out_proj([yat[:, hh, :] for hh in range(4)], [YA], 768, 64, tile=yat)
            K.barrier()
            K.release(mk)

        def hyena(yhy, YHB):
            mk = K.mark()
            vb_ = K.sb("hv", [128, 2, NT], BF16)
            x1f = K.sb("hx1", [128, 2, NT], BF16)
            x2f = K.sb("hx2", [128, 2, NT], BF16)
            HVB, HX1, HX2 = Buf("hv"), Buf("hx1"), Buf("hx2")

            def inproj(hbuf, HB):
                wa0, wab0 = load_wa(1032, 512)
                wa1, wab1 = load_wa(1544, 256)
                dests = [(vb_, HVB), (vb_, HVB), (x1f, HX1), (x1f, HX1), (x2f, HX2), (x2f, HX2)]
                for q in range(6):
                    raw, rawb = cvx["raws"].next()
                    for tt in range(3):
                        pb, pbb = bank()
                        if q < 4:
                            proj_fm(pb, pbb, wa0, wab0, q * 128, 128, hbuf, HB, tt)
                        else:
                            proj_fm(pb, pbb, wa1, wab1, (q - 4) * 128, 128, hbuf, HB, tt)
                        raw_fill(raw, rawb, 128, pb, pbb, tt)
                    pc0 = PT["hy_conv"][0] + (l * 6 + q) * 4
                    dt_, db_ = dests[q]
                    conv_chunk(raw, rawb, 128, pc0, dt_[:, q % 2, :], db_, False)
            with_h(inproj, conv=True)
            FEB = Buf("feats")
            w3s = K.sb("hw3", [64, 1024])
            K.dma("sync", lambda e: e.dma_start(out=w3s[:], in_=hy_w3[l]), writes=[FEB], semkey="ldf")
            h2 = K.sb("hh2", [64, NT])
            H2B = Buf("h2")
            mk2 = K.mark()
            feats = K.sb("feats", [33, NT])
            K.dma("sync", lambda e: e.dma_start(out=feats[:, 0:1024], in_=featsA_d), writes=[FEB], semkey="ldf")
            K.dma("sync", lambda e: e.dma_start(out=feats[:, 1024:1280], in_=featsB_d), writes=[FEB], semkey="ldf")
            w1s = K.sb("hw1", [33, 64])
            w2s = K.sb("hw2", [64, 64])
            K.dma("sync", lambda e: e.dma_start(out=w1s[:], in_=hy_w1[l]), writes=[FEB], semkey="ldf")
            K.dma("sync", lambda e: e.dma_start(out=w2s[:], in_=hy_w2[l]), writes=[FEB], semkey="ldf")
            hp = pcol("hyp", l * 3, 3, rows=64)
            fb = K.sb("fb", [64, 2])
            V(lambda e: e.tensor_tensor(out=fb[:, 0:1], in0=hp[:, 0:1], in1=hp[:, 1:2], op=ALU.mult), [CONSTB], [FEB])
            V(lambda e: e.tensor_tensor(out=fb[:, 1:2], in0=hp[:, 2:3], in1=hp[:, 1:2], op=ALU.mult), [CONSTB], [FEB])
            h1 = K.sb("hh1", [64, NT])
            H1B = Buf("h1")
            MAGIC = 12582912.0
            argp = Rot(K, "arg", [64, 512], F32, 2)
            nrp = Rot(K, "nr", [64, 512], F32, 2)

            def sin_layer(lhsT, src, srcb, KK, dst, dstb, fbcol):
                for tt in range(3):
                    t0, w, cnd = TT[tt]
                    pb, pbb = bank()
                    T(lambda e, pb=pb, t0=t0, w=w: e.matmul(pb[0:64, 0:w], lhsT=lhsT, rhs=src[0:KK, t0:t0 + w], start=True, stop=True), [FEB, srcb], [pbb])
                    ar, arb = argp.next()
                    nr, nrb = nrp.next()
                    V(lambda e, ar=ar, pb=pb, w=w: e.tensor_scalar(out=ar[:, 0:w], in0=pb[0:64, 0:w], scalar1=hp[:, 1:2], scalar2=fb[:, fbcol:fbcol + 1],
                                                                   op0=ALU.mult, op1=ALU.add), [pbb, CONSTB, FEB], [arb])
                    V(lambda e, ar=ar, nr=nr, w=w: e.tensor_scalar(out=nr[:, 0:w], in0=ar[:, 0:w], scalar1=float(1 / (2 * math.pi)), scalar2=MAGIC,
                                                                   op0=ALU.mult, op1=ALU.add), [arb], [nrb])
                    V(lambda e, nr=nr, w=w: e.tensor_scalar(out=nr[:, 0:w], in0=nr[:, 0:w], scalar1=MAGIC, scalar2=None, op0=ALU.subtract), [nrb], [nrb])
                    V(lambda e, ar=ar, nr=nr, w=w: e.scalar_tensor_tensor(out=ar[:, 0:w], in0=nr[:, 0:w], scalar=float(-2 * math.pi), op0=ALU.mult,
                                                                          in1=ar[:, 0:w], op1=ALU.add), [arb, nrb], [arb])
                    V(lambda e, ar=ar, w=w: e.tensor_scalar(out=ar[:, 0:w], in0=ar[:, 0:w], scalar1=3.1415925, scalar2=-3.1415925, op0=ALU.min, op1=ALU.max), [arb], [arb])
                    A(lambda e, ar=ar, t0=t0, w=w: e.activation(out=dst[:, t0:t0 + w], in_=ar[:, 0:w], func=AF.Sin), [arb], [dstb])
            sin_layer(w1s[:], feats, FEB, 33, h1, H1B, 0)
            sin_layer(w2s[:], h1, H1B, 64, h2, H2B, 1)
            K.barrier()
            K.release(mk2)

            gA = K.sb("gA", [128, 2, 2, 7, 256], BF16)
            gB = K.sb("gB", [128, 2, 2, 256], BF16)
            GB_ = Buf("g")
            hfa = K.sb("hfa", [128, 10, 2, 256], BF16)
            HFB = Buf("hf")
            ztm = K.sb("ztm", [128, NCH, 128], BF16)
            ZTB = bufs(NCH, "ztm")
            Yb = K.sb("Yb", [128, 2, 2, 5, 128], BF16)
            YBB = Buf("Yb")
            ytp = Rot(K, "yt", [128, 4, 128], F32, 2)
            tqp = Rot(K, "tqr", [128, 4, 128], F32, 4)
            identr = K.sb("identr", [128, 2, 128])
            IDR = Buf("identr")
            V(lambda e: e.tensor_copy(identr[:, 0, :].bitcast(F32R), ident_b), [CONSTB], [IDR])
            V(lambda e: e.tensor_scalar(out=identr[:, 1, :].bitcast(F32R), in0=ident_b, scalar1=-1.0, scalar2=None, op0=ALU.mult), [CONSTB], [IDR])
            dftb = bcol("dft").rearrange("p (t r f) -> p t r f", t=8, r=2)
            idft = bcol("idft").rearrange("p (a r t) -> p a r t", a=2, r=2)

            decs = K.sb("decs", [128, 20, 128])
            DCB = Buf("decs")
            for o in range(2):
                zin, zinb = (vb_, HVB) if o == 0 else (x1f, HX1)
                gate, gateb = (x1f, HX1) if o == 0 else (x2f, HX2)
                zout, zoutb = (x1f, HX1) if o == 0 else (yhy, YHB)
                for cc in range(2):
                    K.dma("sync", lambda e, cc=cc: e.dma_start(out=decs[:], in_=dec_d[:, :, cc * 128:(cc + 1) * 128]), writes=[DCB], semkey="lddec")
                    for pk in range(10):
                        pb, pbb = bank()
                        pos0 = pk * 128
                        for sd in range(2):
                            wc0 = sd * 512 + o * 256 + cc * 128
                            T(lambda e, pb=pb, sd=sd, wc0=wc0, pos0=pos0: e.matmul(pb[:, sd * 128:(sd + 1) * 128], lhsT=h2[:, pos0:pos0 + 128],
                                                                                   rhs=w3s[:, wc0:wc0 + 128], start=True, stop=True), [H2B, FEB], [pbb])
                        if pk < 8:
                            di = [pk, 8 + pk]
                        else:
                            di = [16 + pk - 8, 18 + pk - 8]
                        for sd in range(2):
                            V(lambda e, pb=pb, sd=sd, pk=pk, di=di, cc=cc: e.tensor_tensor(out=hfa[:, pk, sd, cc * 128:(cc + 1) * 128], in0=pb[:, sd * 128:(sd + 1) * 128],
                                                                                          in1=decs[:, di[sd], :], op=ALU.mult), [pbb, DCB], [HFB])
                for delta in range(-3, 4):
                    ents = spectrum_entries(delta, 8)
                    for fch in range(2):
                        pb, pbb = bank()
                        for r in range(2):
                            for i, (src, k, ty) in enumerate(ents):
                                T(lambda e, pb=pb, r=r, i=i, src=src, k=k, ty=ty, fch=fch: e.matmul(
                                    pb[:, r * 256:(r + 1) * 256], lhsT=dftb[:, ty, r, fch * 128:(fch + 1) * 128], rhs=hfa[:, k, src, :],
                                    start=(i == 0), stop=(i == len(ents) - 1)), [CONSTB, HFB], [pbb])
                        if delta == 0:
                            A(lambda e, pb=pb, fch=fch, delta=delta: e.activation(out=gA[:, fch, :, delta + 3, :], in_=pb[:, 0:512].rearrange("p (r c) -> p r c", r=2),
                                                                                  func=AF.Copy), [pbb], [GB_])
                        else:
                            A(lambda e, pb=pb, fch=fch, delta=delta: e.activation(out=gA[:, fch, :, delta + 3, :], in_=pb[:, 0:512].rearrange("p (r c) -> p r c", r=2),
                                                                                  func=AF.Copy, scale=flag), [pbb, CONSTB], [GB_])
                entsB = spectrum_entries(0, 2)
                for fch in range(2):
                    pb, pbb = bank()
                    for r in range(2):
                        for i, (src, k, ty) in enumerate(entsB):
                            T(lambda e, pb=pb, r=r, i=i, src=src, k=k, ty=ty, fch=fch: e.matmul(
                                pb[:, r * 256:(r + 1) * 256], lhsT=dftb[:, ty, r, fch * 128:(fch + 1) * 128], rhs=hfa[:, 8 + k, src, :],
                                start=(i == 0), stop=(i == len(entsB) - 1)), [CONSTB, HFB], [pbb])
                    A(lambda e, pb=pb, fch=fch: e.activation(out=gB[:, fch, :, :], in_=pb[:, 0:512].rearrange("p (r c) -> p r c", r=2), func=AF.Copy), [pbb], [GB_])
                for cc in range(2):
                    gcs = slice(cc * 128, (cc + 1) * 128)
                    for tc in range(NCH):
                        pb, pbb = bank()
                        pbv = pb[:].bitcast(BF16)
                        T(lambda e, pbv=pbv, tc=tc: e.transpose(pbv[:, 0:128], zin[:, cc, tc * 128:(tc + 1) * 128], ident_b), [zinb, CONSTB], [pbb])
                        A(lambda e, pbv=pbv, tc=tc: e.activation(out=ztm[:, tc, :], in_=pbv[:, 0:128], func=AF.Copy), [pbb], [ZTB[tc]])
                    ty_f = [dft_type(0, 0), dft_type(0, 128)]
                    set_banks(6, 8)
                    for fch in range(2):
                        for r in range(2):
                            for blk in range(NB):
                                if blk < 4:
                                    dst, dstb = PS[r][:, blk * 128:(blk + 1) * 128], PSB[r]
                                else:
                                    dst, dstb = PS[2][:, r * 128:(r + 1) * 128], PSB[2]
                                for tk in range(2):
                                    T(lambda e: e.matmul(dst, lhsT=dftb[:, ty_f[tk], r, fch * 128:(fch + 1) * 128], rhs=ztm[:, 2 * blk + tk, :],
                                                         start=(tk == 0), stop=(tk == 1)), [CONSTB, ZTB[2 * blk + tk]], [dstb])
                        terms = []
                        for delta in [0, 1, -1, 2, -2, 3, -3]:
                            nbk = 4 - abs(delta)
                            tb0 = max(delta, 0)
                            sb0 = tb0 - delta
                            for (acc, gr, zr, sgn) in ((3, 0, 0, 0), (3, 1, 1, 1), (4, 0, 1, 0), (4, 1, 0, 0)):
                                g_ap = gA[:, fch, gr, delta + 3, gcs].unsqueeze(1).broadcast_to([128, nbk, 128])
                                z_ap = PS[zr][:, sb0 * 128:(sb0 + nbk) * 128].rearrange("p (b c) -> p b c", b=nbk)
                                terms.append((acc, tb0 * 128, nbk * 128, sgn, z_ap, PSB[zr], g_ap, nbk))
                        cnt = {3: 0, 4: 0}
                        tot = {3: 14, 4: 14}
                        for (acc, c0, ncol, sgn, z_ap, zb, g_ap, nbk) in terms:
                            tq, tqb = tqp.next()
                            V(lambda e: e.tensor_tensor(out=tq[:, 0:nbk, :].bitcast(F32R), in0=z_ap, in1=g_ap, op=ALU.mult), [zb, GB_], [tqb])
                            T(lambda e: e.matmul(PS[acc][:, c0:c0 + ncol], lhsT=identr[:, sgn, :].bitcast(F32R),
                                                 rhs=tq[:, 0:nbk, :].rearrange("p b c -> p (b c)").bitcast(F32R),
                                                 start=(cnt[acc] == 0), stop=(cnt[acc] == tot[acc] - 1)), [tqb, IDR], [PSB[acc]])
                            cnt[acc] += 1
                        tqs = []
                        for (gr, zr, sgn) in ((0, 0, 0), (1, 1, 1), (0, 1, 0), (1, 0, 0)):
                            tq, tqb = tqp.next()
                            V(lambda e: e.tensor_tensor(out=tq[:, 0, :].bitcast(F32R), in0=PS[2][:, zr * 128:(zr + 1) * 128], in1=gB[:, fch, gr, gcs], op=ALU.mult),
                              [PSB[2], GB_], [tqb])
                            tqs.append((tq, tqb, sgn))
                        for i, (tq, tqb, sgn) in enumerate(tqs):
                            ro = i // 2
                            T(lambda e: e.matmul(PS[5][:, ro * 128:(ro + 1) * 128], lhsT=identr[:, sgn, :].bitcast(F32R), rhs=tq[:, 0, :].bitcast(F32R),
                                                 start=(i % 2 == 0), stop=(i % 2 == 1)), [tqb, IDR], [PSB[5]])
                        for r in range(2):
                            A(lambda e: e.activation(out=Yb[:, fch, r, 0:4, :], in_=PS[3 + r][:, 0:512].rearrange("p (b c) -> p b c", b=4), func=AF.Copy),
                              [PSB[3 + r]], [YBB])
                        A(lambda e: e.activation(out=Yb[:, fch, :, 4, :], in_=PS[5][:, 0:256].rearrange("p (r c) -> p r c", r=2), func=AF.Copy), [PSB[5]], [YBB])
                    set_banks(0, 8)
                    hbcol = pcol("hy_bias", (l * 2 + o) * 2 + cc, 1)
                    for blk in range(NB):
                        pb, pbb = bank()
                        i = 0
                        for fch in range(2):
                            for r in range(2):
                                T(lambda e, pb=pb, fch=fch, r=r, blk=blk, i=i: e.matmul(pb[:, 0:256], lhsT=Yb[:, fch, r, blk, :], rhs=idft[:, fch, r, :],
                                                                                       start=(i == 0), stop=(i == 3)), [YBB, CONSTB], [pbb])
                                i += 1
                        ts = slice(blk * 256, (blk + 1) * 256)
                        tq, tqb = ytp.next()
                        tqv = tq[:].rearrange("p a b -> p (a b)")[:, 0:256]
                        V(lambda e, tqv=tqv, pb=pb, ts=ts: e.scalar_tensor_tensor(out=tqv, in0=zin[:, cc, ts], scalar=hbcol, op0=ALU.mult, in1=pb[:, 0:256], op1=ALU.add),
                          [zinb, CONSTB, pbb], [tqb])
                        V(lambda e, tqv=tqv, ts=ts: e.tensor_tensor(out=zout[:, cc, ts], in0=tqv, in1=gate[:, cc, ts], op=ALU.mult), [tqb, gateb], [zoutb])
            out_proj([yhy[:, 0, :], yhy[:, 1, :]], [YHB], 256, 128)
            K.barrier()
            K.release(mk)

        yhy = K.sb("yhy", [128, 2, NT], BF16)
        YHB = Buf("yhy")
        hyena(yhy, YHB)
        yssd = K.sb("yssd", [64, 4, NT], BF16)
        YSB = Buf("yssd")
        ssd(yssd, YSB)
        yret = K.sb("yret", [64, 4, NT], BF16)
        YRB = Buf("yret")
        ret(yret, YRB)
        yatt = K.sb("yatt", [64, 4, NT], BF16)
        YAB_ = Buf("yatt")
        att(yatt, YAB_)
        out_proj_all()
        K.barrier()
        K.release(mk_all)

    for l in range(nlayers):
        if l == 1:
            assert not pending_mod and not inflight_mod, (pending_mod, inflight_mod)
        ffn(l, 0)
        mixer(l)
        ffn(l, 1)

    K.dma("sync", lambda e: e.dma_start(out=yT, in_=x[:]), reads=[b for row in XB for b in row], semkey="sty")
    K.barrier()
    K.release(0)
    return nc, K


def _consts(is_s):
    f32 = np.float32
    cfv = np.zeros((128, CF.n), f32)

    def put(name, arr):
        c0, w = CF[name]
        cfv[:, c0:c0 + w] = np.asarray(arr, f32).reshape(128, w) if np.asarray(arr).ndim > 1 or w == 1 else np.broadcast_to(np.asarray(arr, f32), (128, w))
    j = np.arange(128)[:, None]
    i = np.arange(128)[None, :]
    put("trif", (i >= j).astype(f32))
    put("trib", (j >= i).astype(f32))
    put("ones", np.ones((128, 128), f32))
    put("relu_f", np.maximum(i - j, 0).astype(f32))
    put("relu_b", np.maximum(j - i, 0).astype(f32))
    put("ip1", np.broadcast_to((i + 1).astype(f32), (128, 128)))
    put("rmi", np.broadcast_to((128 - i).astype(f32), (128, 128)))
    put("tailf", (127 - j).astype(f32))
    put("tailb", j.astype(f32))
    nf = 16
    inv = (10000.0 ** (-np.arange(nf, dtype=f32) / nf)).astype(f32)
    cos = np.ones((NT, 32), f32)
    sin = np.zeros((NT, 32), f32)
    if is_s:
        t = np.arange(1024)
        rows = (t // 64).astype(f32)
        cols = (t % 64).astype(f32)
        angr = rows[:, None] * inv[None, :]
        angc = cols[:, None] * inv[None, :]
        cos[:1024, 0:16] = np.cos(angr)
        cos[:1024, 16:32] = np.cos(angc)
        sin[:1024, 0:16] = np.sin(angr)
        sin[:1024, 16:32] = np.sin(angc)
    put("cos", cos.reshape(NCH, 128, 32).transpose(1, 0, 2).reshape(128, 320))
    put("sin", sin.reshape(NCH, 128, 32).transpose(1, 0, 2).reshape(128, 320))
    cfb = np.full((NCH,), -30000.0, f32)
    hl = np.zeros((5,), f32)
    hr = np.zeros((5,), f32)
    if is_s:
        cfb[0:8] = 0.0
        hl[1:4] = 1.0
        hr[0:3] = 1.0
    put("cfb", cfb)
    put("hl", hl)
    put("hr", hr)
    put("flag", np.full((128, 1), 1.0 if is_s else 0.0, f32))
    put("negpi", np.full((128, 1), -math.pi, f32))
    put("nbf", ((i >= j).astype(f32) - 1.0) * 30000.0)
    put("nbb", ((j >= i).astype(f32) - 1.0) * 30000.0)

    cbv = np.zeros((128, CB.n), f32)

    def putb(name, arr):
        c0, w = CB[name]
        cbv[:, c0:c0 + w] = np.asarray(arr, f32).reshape(128, w)
    putb("ident", np.eye(128, dtype=f32))
    putb("ones", np.ones((128, 128), f32))
    am = np.zeros((128, NCH, 2, 128), f32)
    band_prev = (j >= i).astype(f32)
    band_next = (j <= i).astype(f32)
    for qc in range(NCH):
        if is_s and qc < 8:
            if qc >= 1:
                am[:, qc, 0, :] = band_prev
            if qc <= 6:
                am[:, qc, 1, :] = band_next
        else:
            if qc % 2 == 1:
                am[:, qc, 0, :] = 1.0
            else:
                am[:, qc, 1, :] = 1.0
    putb("am", am)
    om = 2 * np.pi * (np.arange(256) + 0.5) / 512.0
    row = np.arange(128)
    dft = np.zeros((128, 8, 2, 256), np.float64)
    for ty in range(8):
        if ty < 4:
            e = FWD_E0[ty] + row
        else:
            e = BWD_E0[ty - 4] - row
        valid = (np.abs(e) <= 255).astype(np.float64)
        ang = e[:, None] * om[None, :]
        dft[:, ty, 0, :] = np.cos(ang) * valid[:, None]
        dft[:, ty, 1, :] = -np.sin(ang) * valid[:, None]
    putb("dft", dft)
    tt = np.arange(256)
    idft = np.zeros((128, 2, 2, 256), np.float64)
    for fch in range(2):
        omf = om[fch * 128:(fch + 1) * 128]
        ang = omf[:, None] * tt[None, :]
        idft[:, fch, 0, :] = (2.0 / 512) * np.cos(ang)
        idft[:, fch, 1, :] = -(2.0 / 512) * np.sin(ang)
    putb("idft", idft)
    return cfv, cbv


def _hy_consts(LA):
    f32 = np.float32
    l = LA
    pos = np.arange(l, dtype=f32)
    t = pos / f32(l - 1)
    bands = np.linspace(1e-4, 15, 16, dtype=f32)
    ang = (f32(2.0 * math.pi / l)) * pos[:, None] * bands[None, :]
    feats = np.concatenate([t[:, None], np.cos(ang), -np.sin(ang)], axis=-1).astype(f32)
    max_decay = math.log(1e-2) / 0.3
    min_decay = math.log(1e-2) / 1.5
    deltas = np.abs(np.linspace(min_decay, max_decay, 256, dtype=f32))
    dec = np.exp(-t[:, None] * deltas[None, :]).astype(f32)
    return feats, dec


def _prepare(inp):
    f32 = np.float32
    g = lambda k: np.asarray(inp[k], dtype=f32)
    x_prompt, x_sample = g("x_prompt"), g("x_sample")
    cache_k, cache_v = g("cache_k"), g("cache_v")
    state_ssd, state_ret = g("state_ssd"), g("state_ret")
    c, c_ctx = g("c"), g("c_ctx")
    pt = np.zeros((128, PT.n), f32)

    def putp(name, off, arr):
        arr = np.asarray(arr, f32)
        c0 = PT[name][0] + off
        pt[0:arr.shape[0], c0:c0 + arr.shape[1]] = arr
    nw = g("norm_w")
    for l in range(2):
        for j in range(3):
            putp("norm_w", (l * 3 + j) * 8, nw[l, j].reshape(8, 128).T)
        putp("b_mod", l * 72, g("b_mod")[l].reshape(72, 128).T)
        cw, cbias = g("ssd_conv_w")[l], g("ssd_conv_b")[l]
        for q in range(8):
            if q < 4:
                f0, fs = q * 64, 64
            else:
                f0, fs = 256 + (q - 4) * 128, 128
            arr = np.stack([cw[0, f0:f0 + fs], cw[1, f0:f0 + fs], cw[2, f0:f0 + fs], cbias[f0:f0 + fs]], axis=1)
            putp("ssd_conv", (l * 8 + q) * 4, arr)
        for j in range(2):
            f0 = j * 128
            arr = np.stack([cw[0, f0:f0 + 128], cw[1, f0:f0 + 128], cw[2, f0:f0 + 128], cbias[f0:f0 + 128]], axis=1)
            putp("ssd_convp", (l * 2 + j) * 4, arr)
        hw, hb = g("hy_conv_w")[l], g("hy_conv_b")[l]
        for q in range(6):
            f0 = q * 128
            arr = np.stack([hw[0, f0:f0 + 128], hw[1, f0:f0 + 128], hw[2, f0:f0 + 128], hb[f0:f0 + 128]], axis=1)
            putp("hy_conv", (l * 6 + q) * 4, arr)
        hbias = g("hy_bias")[l]
        for o in range(2):
            putp("hy_bias", (l * 2 + o) * 2, hbias[o].reshape(2, 128).T)
        putp("ssd_nw", l * 4, g("ssd_norm_w")[l].reshape(4, 64).T)
        putp("ssd_d", l * 4, np.broadcast_to(g("ssd_d")[l][None, :], (64, 4)))
        putp("ret_gn", l * 4, g("ret_gn_w")[l].reshape(4, 64).T)
        putp("hyp", l * 3, np.stack([g("hy_b1")[l], g("hy_freq")[l], g("hy_b2")[l]], axis=1))
    ft = np.zeros((128, FT.n), f32)

    def putf(name, off, vec):
        vec = np.asarray(vec, f32).reshape(-1)
        c0 = FT[name][0] + off
        ft[:, c0:c0 + vec.size] = vec[None, :]
    for l in range(2):
        putf("dt_bias", l * 80, np.tile(g("ssd_dt_bias")[l].reshape(8), NCH))
        putf("a_log", l * 80, np.tile(g("ssd_a_log")[l].reshape(8), NCH))
        putf("ret_logit", l * 8, g("ret_decay_logit")[l].reshape(8))
        putf("qkw", l * 384, np.concatenate([np.tile(g("attn_q_norm")[l], 4), np.tile(g("attn_k_norm")[l], 2)]))
        putf("sink", l * 4, g("attn_sink")[l])
    featsB, decB = _hy_consts(256)
    featsA_s, decA_s = _hy_consts(1024)
    wm = g("w_mod").reshape(2, 8, 128, 18, 512).transpose(0, 3, 2, 1, 4).reshape(2, 18, 128, 4096)
    wi = g("ffn_w_in").reshape(2, 2, 8, 128, 2, 11, 256)
    wi = wi.transpose(0, 1, 5, 3, 2, 4, 6).reshape(2, 2, 11, 128, 4096)
    wo = g("ffn_w_out").reshape(2, 2, 11, 2, 128, 1024).transpose(0, 1, 2, 4, 3, 5).reshape(2, 2, 11, 128, 2048)
    shared = dict(ptab=pt, ftab=ft, w_mod=np.ascontiguousarray(wm), ffn_w_in=np.ascontiguousarray(wi), ffn_w_out=np.ascontiguousarray(wo), mix_w_in=g("mix_w_in"),
                  mix_w_out=g("mix_w_out"), hy_w1=g("hy_w1"), hy_w2=g("hy_w2"), hy_w3=g("hy_w3"),
                  featsB=np.ascontiguousarray(featsB.T))
    consts = {True: _consts(True), False: _consts(False)}
    in_maps = []
    plan = []
    for cid in range(NCORE):
        is_s = cid < 2
        if is_s:
            xs = np.concatenate([x_sample[cid], x_prompt[30 + cid]], axis=0)
            seqs = [30 + cid]
            condA = c[cid]
        else:
            seqs = list(range(5 * (cid - 2), 5 * (cid - 2) + 5))
            xs = x_prompt[seqs].reshape(NT, D)
            condA = c_ctx
        plan.append((is_s, seqs))
        xTm = np.ascontiguousarray(xs.T.reshape(8, 128, NT).transpose(1, 0, 2))
        cond = np.stack([condA, c_ctx], axis=-1)
        condTm = np.ascontiguousarray(cond.reshape(8, 128, 2).transpose(1, 0, 2))
        s0s = np.zeros((2, 5, 2, 4, 128, 64), f32)
        s0r = np.zeros((2, 5, 2, 4, 64, 64), f32)
        ck = np.zeros((2, 2, 64, 512), f32)
        cv = np.zeros((2, 512, 128), f32)
        if is_s:
            for l in range(2):
                s0s[l, 0, 0] = state_ssd[cid, l, 0]
                s0s[l, 3, 1] = state_ssd[cid, l, 1]
                s0r[l, 0, 0] = state_ret[cid, l, 0]
                s0r[l, 3, 1] = state_ret[cid, l, 1]
                ck[l] = cache_k[cid, l].transpose(1, 2, 0)
                cv[l] = cache_v[cid, l].reshape(512, 128)
            featsA = featsA_s
            decFA = decA_s.copy()
        else:
            featsA = np.zeros((1024, 33), f32)
            featsA[:256] = featsB
            decFA = np.zeros((1024, 256), f32)
            decFA[:256] = decB
        decBA = decFA.copy()
        decBA[0] = 0.0
        decFB = decB.copy()
        decBB = decB.copy()
        decBB[0] = 0.0
        dec = np.concatenate([decFA.reshape(8, 128, 256), decBA.reshape(8, 128, 256), decFB.reshape(2, 128, 256), decBB.reshape(2, 128, 256)], axis=0)
        cfv, cbv = consts[is_s]
        m = dict(shared)
        m.update(xT=xTm, condT=condTm,
                 s0ssd=np.ascontiguousarray(s0s.reshape(80, 128, 64).transpose(1, 0, 2)),
                 s0ret=np.ascontiguousarray(s0r.reshape(80, 64, 64).transpose(1, 0, 2)),
                 ckT=np.ascontiguousarray(ck.reshape(4, 64, 512).transpose(1, 0, 2)),
                 cvt=np.ascontiguousarray(cv.reshape(8, 128, 128).transpose(1, 0, 2)),
                 cf32=cfv, cbf=cbv, featsA=np.ascontiguousarray(featsA.T), dec=np.ascontiguousarray(dec.transpose(1, 0, 2)))
        in_maps.append(m)
    return in_maps, plan


_CACHE = {}


def kernel(**inputs):
    in_maps, plan = _prepare(inputs)
    if "nc" not in _CACHE:
        _CACHE["nc"] = build_program()[0]
    nc = _CACHE["nc"]
    res = run_bass_kernel_spmd(nc, in_maps, core_ids=list(range(NCORE)))
    f32 = np.float32
    y_prompt = np.zeros((32, 256, D), f32)
    y_sample = np.zeros((2, 1024, D), f32)
    nck = np.zeros((32, 2, 256, 2, 64), f32)
    ncv = np.zeros((32, 2, 256, 2, 64), f32)
    nssd = np.zeros((32, 2, 2, 4, 128, 64), f32)
    nret = np.zeros((32, 2, 2, 4, 64, 64), f32)
    for cid, (is_s, seqs) in enumerate(plan):
        r = res.results[cid]
        y = np.asarray(r["yT"]).transpose(1, 0, 2).reshape(D, NT).T
        nk = np.asarray(r["nk"])
        nv = np.asarray(r["nv"])
        ss = np.asarray(r["nssd"]).transpose(1, 0, 2).reshape(2, 5, 2, 4, 128, 64)
        sr = np.asarray(r["nret"]).transpose(1, 0, 2).reshape(2, 5, 2, 4, 64, 64)
        if is_s:
            y_sample[cid] = y[:1024]
            blks = [(4, seqs[0])]
        else:
            blks = list(enumerate(seqs))
        for blk, b in blks:
            ts = slice(blk * 256, (blk + 1) * 256)
            y_prompt[b] = y[ts]
            for l in range(2):
                nck[b, l] = nk[l, ts].reshape(256, 2, 64)
                ncv[b, l] = nv[l, ts].reshape(256, 2, 64)
                nssd[b, l] = ss[l, blk]
                nret[b, l] = sr[l, blk]
    return (y_prompt, y_sample, nck, ncv, nssd, nret)
```
